# Optimizing a Trainium2 kernel written in Bass

```python
import math
import jax, jax.numpy as jnp
from jax import lax
import numpy as np

D_MODEL = 1024
BATCH = 2
SEQ = 8192
DEPTH = 1

N_HEADS_A = 8
HEAD_DIM_A = 64
D_LATENT = 256
N_HEADS_IDX = 8
HEAD_DIM_IDX = 64
TOPK_MAX = 256
Q_BLOCK = 128
N_BUCKETS = 32
MAX_DISTANCE = 128
N_HEADS_M = 4
HEAD_DIM_M = 128
CHUNK = 64
CONV_WIDTH = 4
F_BIAS_INIT = 3.0
D_FF = 2816
EPS = 1e-6

W_A = N_HEADS_A * HEAD_DIM_A
W_M = N_HEADS_M * HEAD_DIM_M
IDX_SCALE = (N_HEADS_IDX ** -0.5) * (HEAD_DIM_IDX ** -0.5)
SPLITS = (W_A, D_LATENT, N_HEADS_IDX * HEAD_DIM_IDX, HEAD_DIM_IDX, N_HEADS_IDX,
          W_M, W_M, W_M, N_HEADS_M, N_HEADS_M, W_M, D_MODEL, D_MODEL)
D_IN = sum(SPLITS)

kernel_name = "hybrid_dsa_mlstm_macaron_block"


def rms_norm(x, g):
    xf = x.astype(jnp.float32)
    y = xf * lax.rsqrt(jnp.mean(xf * xf, axis=-1, keepdims=True) + EPS)
    return (y * g.astype(jnp.float32)).astype(x.dtype)


def modulate(h, shift, scale):
    return h * (1.0 + scale[:, None, :]) + shift[:, None, :]


def swiglu(h, w1, w3, w2):
    return (jax.nn.silu(h @ w1) * (h @ w3)) @ w2


def t5_bucket(dist):
    n = jnp.maximum(dist, 0)
    max_exact = N_BUCKETS // 2
    nf = jnp.maximum(n, 1).astype(jnp.float32)
    large = max_exact + (jnp.log(nf / max_exact) / math.log(MAX_DISTANCE / max_exact)
                         * (N_BUCKETS - max_exact)).astype(jnp.int32)
    large = jnp.minimum(large, N_BUCKETS - 1)
    return jnp.where(n < max_exact, n, large)


def dsa_attention(q, c_kv, q_idx, k_idx, w_idx, w_uk, w_uv, rel_bias):
    B, S = q.shape[:2]
    nb = S // Q_BLOCK
    topk = min(TOPK_MAX, S // 4)
    kpos = jnp.arange(S)
    scale = HEAD_DIM_A ** -0.5

    def blocks(a):
        return jnp.moveaxis(a.reshape(B, nb, Q_BLOCK, *a.shape[2:]), 1, 0)

    def one_block(args):
        blk, qb, qib, wb = args
        tpos = blk * Q_BLOCK + jnp.arange(Q_BLOCK)
        idx_logits = jax.nn.relu(jnp.einsum('bthd,bsd->bths', qib, k_idx).astype(jnp.float32))
        score = jnp.einsum('bth,bths->bts', wb.astype(jnp.float32), idx_logits)
        causal = kpos[None, :] <= tpos[:, None]
        score = jnp.where(causal[None], score, -jnp.inf)
        _, sel = lax.top_k(score, topk)
        ckv_sel = jax.vmap(lambda cb, ib: cb[ib])(c_kv, sel)
        q_abs = jnp.einsum('bthd,hcd->bthc', qb, w_uk)
        logits = jnp.einsum('bthc,btkc->bthk', q_abs, ckv_sel).astype(jnp.float32) * scale
        dist = tpos[None, :, None] - sel
        bias = jnp.moveaxis(rel_bias[t5_bucket(dist)], -1, 2)
        logits = logits + bias.astype(jnp.float32)
        logits = jnp.where((dist >= 0)[:, :, None, :], logits, -jnp.inf)
        p = jax.nn.softmax(logits, axis=-1).astype(ckv_sel.dtype)
        o_lat = jnp.einsum('bthk,btkc->bthc', p, ckv_sel)
        return jnp.einsum('bthc,hcd->bthd', o_lat, w_uv)

    out = lax.map(one_block, (jnp.arange(nb), blocks(q), blocks(q_idx), blocks(w_idx)))
    return jnp.moveaxis(out, 0, 1).reshape(B, S, W_A)


def mlstm_chunkwise(q, k, v, i_pre, logf):
    B, H, S, dk = q.shape
    dv = v.shape[-1]
    nc = S // CHUNK
    tri = jnp.tril(jnp.ones((CHUNK, CHUNK), dtype=bool))

    def to_chunks(a):
        return jnp.moveaxis(a.reshape(B, H, nc, CHUNK, *a.shape[3:]), 2, 0)

    def step(carry, inp):
        C, n, m = carry
        qc, kc, vc, ic, fc = inp
        b = jnp.cumsum(fc, axis=-1)
        d_log = jnp.where(tri, b[..., :, None] - b[..., None, :] + ic[..., None, :], -jnp.inf)
        inter_log = b + m[..., None]
        m_j = jnp.maximum(inter_log, jnp.max(d_log, axis=-1))
        s = jnp.einsum('bhjd,bhsd->bhjs', qc, kc) * jnp.exp(d_log - m_j[..., None])
        w_inter = jnp.exp(inter_log - m_j)
        num = (w_inter[..., None] * jnp.einsum('bhjd,bhde->bhje', qc, C)
               + jnp.einsum('bhjs,bhse->bhje', s, vc))
        den = w_inter * jnp.einsum('bhjd,bhd->bhj', qc, n) + jnp.sum(s, axis=-1)
        h = num / jnp.maximum(jnp.abs(den), jnp.exp(-m_j))[..., None]
        b_last = b[..., -1]
        g = b_last[..., None] - b + ic
        m_new = jnp.maximum(b_last + m, jnp.max(g, axis=-1))
        w = jnp.exp(g - m_new[..., None])
        decay = jnp.exp(b_last + m - m_new)
        C_new = decay[..., None, None] * C + jnp.einsum('bhs,bhsd,bhse->bhde', w, kc, vc)
        n_new = decay[..., None] * n + jnp.einsum('bhs,bhsd->bhd', w, kc)
        return (C_new, n_new, m_new), h

    init = (jnp.zeros((B, H, dk, dv), jnp.float32), jnp.zeros((B, H, dk), jnp.float32),
            jnp.zeros((B, H), jnp.float32))
    _, hs = lax.scan(step, init, (to_chunks(q), to_chunks(k), to_chunks(v),
                                  to_chunks(i_pre), to_chunks(logf)))
    return jnp.moveaxis(hs, 0, 2).reshape(B, H, S, dv)


def causal_conv(x, w, b):
    C = x.shape[-1]
    y = lax.conv_general_dilated(x, w[:, None, :].astype(x.dtype), window_strides=(1,),
                                 padding=[(CONV_WIDTH - 1, 0)],
                                 dimension_numbers=('NWC', 'WIO', 'NWC'),
                                 feature_group_count=C)
    return y + b


def hybrid_mixer(h, w_in, conv_w, conv_b, kv_norm, w_uk, w_uv, gate_bias, head_norm,
                 rel_bias, w_branch_attn, w_branch_mlstm, w_out):
    B, S, _ = h.shape
    split_points = np.cumsum(SPLITS)[:-1].tolist()
    (q_a, c_kv, q_idx, k_idx, w_idx, q_m, k_m, v_m, i_pre, f_pre, o_pre,
     gate_a, gate_m) = jnp.split(h @ w_in, split_points, axis=-1)

    c_kv = rms_norm(c_kv, kv_norm)
    y_a = dsa_attention(q_a.reshape(B, S, N_HEADS_A, HEAD_DIM_A), c_kv,
                        q_idx.reshape(B, S, N_HEADS_IDX, HEAD_DIM_IDX), k_idx,
                        w_idx * IDX_SCALE, w_uk, w_uv, rel_bias)

    qk = jax.nn.silu(causal_conv(jnp.concatenate([q_m, k_m], axis=-1), conv_w, conv_b))
    q_m, k_m = jnp.split(qk, 2, axis=-1)

    def heads(a):
        return a.reshape(B, S, N_HEADS_M, -1).transpose(0, 2, 1, 3).astype(jnp.float32)

    i_g = (i_pre + gate_bias[:N_HEADS_M]).astype(jnp.float32).transpose(0, 2, 1)
    logf = jax.nn.log_sigmoid((f_pre + gate_bias[N_HEADS_M:]).astype(jnp.float32)).transpose(0, 2, 1)
    h_m = mlstm_chunkwise(heads(q_m), heads(k_m) * (HEAD_DIM_M ** -0.5), heads(v_m), i_g, logf)
    mu = jnp.mean(h_m, axis=-1, keepdims=True)
    var = jnp.mean(jnp.square(h_m - mu), axis=-1, keepdims=True)
    h_m = (h_m - mu) * lax.rsqrt(var + EPS)
    h_m = h_m.transpose(0, 2, 1, 3).reshape(B, S, W_M) * head_norm.astype(jnp.float32)
    h_m = h_m.astype(h.dtype) * jax.nn.sigmoid(o_pre)

    merged = (jax.nn.sigmoid(gate_a) * (y_a @ w_branch_attn)
              + jax.nn.sigmoid(gate_m) * (h_m @ w_branch_mlstm))
    return merged @ w_out


def setup_inputs(seed: int = 0) -> dict:
    key = jax.random.key(seed)
    ks = iter(jax.random.split(key, 40))
    nrm = lambda shape, s: jax.random.normal(next(ks), shape, jnp.float32) * s
    gain = lambda shape: 1.0 + nrm(shape, 0.05)
    L = DEPTH
    gate_bias = jnp.concatenate([nrm((L, N_HEADS_M), 0.1),
                                 F_BIAS_INIT + nrm((L, N_HEADS_M), 0.5)], axis=-1)
    return {
        "x": nrm((BATCH, SEQ, D_MODEL), 1.0),
        "c": nrm((BATCH, D_MODEL), 1.0),
        "ada_w": nrm((L, D_MODEL, 9 * D_MODEL), 0.5 * D_MODEL ** -0.5),
        "ada_b": nrm((L, 9 * D_MODEL), 0.01),
        "ffn1_norm": gain((L, D_MODEL)),
        "ffn1_w1": nrm((L, D_MODEL, D_FF), D_MODEL ** -0.5),
        "ffn1_w3": nrm((L, D_MODEL, D_FF), D_MODEL ** -0.5),
        "ffn1_w2": nrm((L, D_FF, D_MODEL), D_FF ** -0.5),
        "mix_norm": gain((L, D_MODEL)),
        "w_in": nrm((L, D_MODEL, D_IN), D_MODEL ** -0.5),
        "conv_w": nrm((L, CONV_WIDTH, 2 * W_M), CONV_WIDTH ** -0.5),
        "conv_b": nrm((L, 2 * W_M), 0.01),
        "kv_norm": gain((L, D_LATENT)),
        "w_uk": nrm((L, N_HEADS_A, D_LATENT, HEAD_DIM_A), D_LATENT ** -0.5),
        "w_uv": nrm((L, N_HEADS_A, D_LATENT, HEAD_DIM_A), D_LATENT ** -0.5),
        "mlstm_gate_bias": gate_bias,
        "mlstm_head_norm": gain((L, W_M)),
        "rel_bias": nrm((N_BUCKETS, N_HEADS_A), 0.5),
        "w_branch_attn": nrm((L, W_A, D_MODEL), W_A ** -0.5),
        "w_branch_mlstm": nrm((L, W_M, D_MODEL), W_M ** -0.5),
        "w_out": nrm((L, D_MODEL, D_MODEL), D_MODEL ** -0.5),
        "ffn2_norm": gain((L, D_MODEL)),
        "ffn2_w1": nrm((L, D_MODEL, D_FF), D_MODEL ** -0.5),
        "ffn2_w3": nrm((L, D_MODEL, D_FF), D_MODEL ** -0.5),
        "ffn2_w2": nrm((L, D_FF, D_MODEL), D_FF ** -0.5),
        "final_norm": gain((D_MODEL,)),
    }


def reference(x, c, ada_w, ada_b, ffn1_norm, ffn1_w1, ffn1_w3, ffn1_w2, mix_norm, w_in,
              conv_w, conv_b, kv_norm, w_uk, w_uv, mlstm_gate_bias, mlstm_head_norm,
              rel_bias, w_branch_attn, w_branch_mlstm, w_out, ffn2_norm, ffn2_w1, ffn2_w3,
              ffn2_w2, final_norm):
    cond = jax.nn.silu(c)
    for l in range(DEPTH):
        mod = cond @ ada_w[l] + ada_b[l]
        sh1, sc1, g1, sh2, sc2, g2, sh3, sc3, g3 = jnp.split(mod, 9, axis=-1)
        h = modulate(rms_norm(x, ffn1_norm[l]), sh1, sc1)
        x = x + 0.5 * g1[:, None, :] * swiglu(h, ffn1_w1[l], ffn1_w3[l], ffn1_w2[l])
        h = modulate(rms_norm(x, mix_norm[l]), sh2, sc2)
        x = x + g2[:, None, :] * hybrid_mixer(h, w_in[l], conv_w[l], conv_b[l], kv_norm[l],
                                              w_uk[l], w_uv[l], mlstm_gate_bias[l],
                                              mlstm_head_norm[l], rel_bias, w_branch_attn[l],
                                              w_branch_mlstm[l], w_out[l])
        h = modulate(rms_norm(x, ffn2_norm[l]), sh3, sc3)
        x = x + 0.5 * g3[:, None, :] * swiglu(h, ffn2_w1[l], ffn2_w3[l], ffn2_w2[l])
    return rms_norm(x, final_norm)
```

```python
import numpy as np
import ml_dtypes
import concourse.bass as bass
import concourse.mybir as mybir
from concourse.bass_utils import run_bass_kernel_spmd
from contextlib import ExitStack

F32 = mybir.dt.float32
BF16 = mybir.dt.bfloat16
ALU = mybir.AluOpType
AF = mybir.ActivationFunctionType
AX = mybir.AxisListType

D = 1024
DFF = 2816
NF = DFF // 128
DIN = 5456
NT = 16
TOK = NT * 128
S_LEN = 8192
EPS = 1e-6
NCORES = 8
DBG_STOP = 0
EPI_ENG = "dve"
DBG_EVAC = 0

P_QA, P_CKV, P_QIDX, P_KIDX, P_QM, P_KM, P_VM, P_WIDX, P_IF, NCOL1 = 0, 512, 768, 1280, 1408, 1920, 2432, 2944, 2952, 2960
C_QA, C_CKV, C_QIDX, C_KIDX, C_WIDX, C_QM, C_KM, C_VM, C_I, C_F, C_O, C_GA, C_GM = (
    0, 512, 768, 1280, 1344, 1352, 1864, 2376, 2888, 2892, 2896, 3408, 4432)


class Buf:
    __slots__ = ("name", "w", "r", "ps")

    def __init__(self, name="", ps=False):
        self.name = name
        self.w = None
        self.r = []
        self.ps = ps


class Sched:
    NS = 8

    def __init__(self, nc, es):
        self.nc = nc
        self.names = ["pe", "act", "dve", "pool", "sp"]
        self.ops = {k: [] for k in self.names}
        self.sem = {k: es.enter_context(nc.semaphore("s_" + k)) for k in ["pe", "act", "dve", "pool"]}
        self.cnt = {k: 0 for k in self.sem}
        self.dsem = {q: [es.enter_context(nc.semaphore("d_%s%d" % (q, i))) for i in range(self.NS)]
                     for q in ["sp", "act", "pool"]}
        self.dcnt = {q: 0 for q in self.dsem}
        self.seen = {k: {} for k in self.names}
        self.dlast = {}
        self.ninst = 0

    def _wait(self, e, tok):
        key, val, _ = tok
        if key == ("c", "pe") and getattr(self, "_pend", None) is not None and val > self.cnt["pe"]:
            self._pe_close(inc=True)
        if self.seen[e].get(key, 0) >= val:
            return
        self.seen[e][key] = val
        sem = self.sem[key[1]] if key[0] == "c" else self.dsem[key[1]][key[2]]
        self.ops[e].append(lambda eng, sem=sem, val=val: eng.wait_ge(sem, val))

    def _deps(self, e, reads, writes, is_dma):
        deps = []
        for b in reads:
            t = b.w
            if t is not None and not (t[2] == e and t[0][0] == "c" and e == "pe"):
                deps.append(t)
            if b.ps:
                for t in b.r:
                    if t[2] != e:
                        deps.append(t)
        for b in writes:
            t = b.w
            if t is not None and not (t[2] == e and t[0][0] == "c" and e == "pe"):
                deps.append(t)
            for t in b.r:
                if t[2] == e and t[0][0] == "c" and e == "pe":
                    continue
                deps.append(t)
        return deps

    def _pe_close(self, inc=True):
        p = getattr(self, "_pend", None)
        if p is None:
            return
        self._pend = None
        fn, wset = p
        if inc:
            self.cnt["pe"] += 1
            sem = self.sem["pe"]
            self.ops["pe"].append(lambda eng, fn=fn, sem=sem: fn(eng).then_inc(sem, 1))
        else:
            self.ops["pe"].append(lambda eng, fn=fn: fn(eng))

    def op(self, e, fn, reads=(), writes=()):
        if e == "pe":
            p = getattr(self, "_pend", None)
            wset = frozenset(id(b) for b in writes)
            if p is not None:
                self._pe_close(inc=(p[1] != wset))
            for t in self._deps(e, reads, writes, False):
                self._wait(e, t)
            self.ninst += 1
            tok = (("c", e), self.cnt[e] + 1, e)
            self._pend = (fn, wset)
        else:
            for t in self._deps(e, reads, writes, False):
                self._wait(e, t)
            self.cnt[e] += 1
            self.ninst += 1
            tok = (("c", e), self.cnt[e], e)
            sem = self.sem[e]
            self.ops[e].append(lambda eng, fn=fn, sem=sem: fn(eng).then_inc(sem, 1))
        for b in reads:
            b.r = [t for t in b.r if t[0] != tok[0]] + [tok]
        for b in writes:
            b.w = tok
            b.r = []
        return tok

    def dma(self, q, fn, reads=(), writes=()):
        for t in self._deps(q, reads, writes, True):
            self._wait(q, t)
        n = self.dcnt[q]
        self.dcnt[q] += 1
        self.ninst += 1
        slot, rnd = n % self.NS, n // self.NS
        key = ("d", q, slot)
        if rnd > 0:
            self._wait(q, (key, 16 * rnd, q))
        tok = (key, 16 * (rnd + 1), q)
        sem = self.dsem[q][slot]
        self.ops[q].append(lambda eng, fn=fn, sem=sem: fn(eng).then_inc(sem, 16))
        for b in reads:
            b.r = b.r + [tok]
        for b in writes:
            b.w = tok
            b.r = []
        self.dlast[key] = tok[1]
        return tok

    def barrier(self):
        self._pe_close(inc=True)
        for e in self.names:
            for k in self.sem:
                if k != e and self.cnt[k] > 0:
                    self._wait(e, (("c", k), self.cnt[k], None))
            for key, val in self.dlast.items():
                self._wait(e, (key, val, None))

    def finish(self):
        self._pe_close(inc=True)
        for key, val in self.dlast.items():
            self._wait("sp", (key, val, None))
        for k in self.sem:
            if self.cnt[k] > 0:
                self._wait("sp", (("c", k), self.cnt[k], None))

    def flush(self):
        nc = self.nc
        self._pe_close(inc=True)
        if not hasattr(self, "marks"):
            self.marks = []
        self.marks.append(dict(self.cnt))
        ops = self.ops
        self.ops = {k: [] for k in self.names}
        with nc.Block() as block:
            @block.sync
            def _(eng):
                for f in ops["sp"]:
                    f(eng)

            @block.tensor
            def _(eng):
                for f in ops["pe"]:
                    f(eng)

            @block.scalar
            def _(eng):
                for f in ops["act"]:
                    f(eng)

            @block.vector
            def _(eng):
                for f in ops["dve"]:
                    f(eng)

            @block.gpsimd
            def _(eng):
                for f in ops["pool"]:
                    f(eng)


class Ctx:
    def __init__(self, nc, es):
        self.nc = nc
        self.es = es
        self.S = Sched(nc, es)
        self.n = 0

    def sb(self, es, shape, dt, name=None):
        self.n += 1
        t = es.enter_context(self.nc.sbuf_tensor("%s_%d" % (name or "t", self.n), list(shape), dt))
        return t, Buf(name or "t")

    def ps(self, es, shape, dt, name=None):
        self.n += 1
        t = es.enter_context(self.nc.psum_tensor("%s_%d" % (name or "p", self.n), list(shape), dt))
        return t, Buf(name or "p", ps=True)


def bc_last(ap2d, n):
    p, c = ap2d.shape
    return ap2d.unsqueeze(2).to_broadcast([p, c, n])


def emit_mod(K, ph, ada_w, abT_sb, cT_sb, modT, b_modT, cols, rows, ada_b_dram, ident_f, b_ident):
    S = K.S
    cond, b_cond = K.sb(ph, [128, 8], F32, "cond")
    condB, b_condB = K.sb(ph, [128, 8, 128], F32, "condB")
    pieces = [K.sb(ph, [128, 8, 512], F32, "adapc") for _ in range(2)]
    abrow, b_abrow = K.sb(ph, [1, 512], F32, "abrow")
    ones1, b_ones1 = K.sb(ph, [1, 128], F32, "ones1")
    S.op("dve", lambda e: e.memset(ones1[:], 1.0), writes=[b_ones1])
    ab2 = ada_b_dram.rearrange("(a n) -> a n", a=1)
    pCol, b_pCol = K.ps(ph, [128, 512], F32, "pCol")
    pRow, b_pRow = K.ps(ph, [128, 512], F32, "pRow")
    b_cT = Buf("cT")
    S.op("act", lambda e: e.activation(out=cond[:], in_=cT_sb[:], func=AF.Silu), reads=[b_cT], writes=[b_cond])
    for kc in range(8):
        S.op("act", lambda e, kc=kc: e.activation(out=condB[:, kc, :], in_=cT_sb[:, kc:kc + 1].to_broadcast([128, 128]),
                                                  func=AF.Silu), reads=[b_cT], writes=[b_condB])
    aw = ada_w.rearrange("(kc p) n -> p kc n", p=128)
    order = sorted(set(cols) | set(rows.keys()))
    i = 0
    for v in order:
        for hh in range(2):
            pc, b_pc = pieces[i % 2]
            i += 1
            c0 = v * 1024 + hh * 512
            S.dma("sp", lambda e, pc=pc, c0=c0: e.dma_start(out=pc[:], in_=aw[:, :, c0:c0 + 512]), writes=[b_pc])
            if v in cols:
                for c in range(4):
                    cc = v * 8 + hh * 4 + c
                    for kc in range(8):
                        S.op("pe", lambda e, pc=pc, cc=cc, c=c, kc=kc: e.matmul(
                            pCol[:, cc:cc + 1], lhsT=pc[:, kc, c * 128:(c + 1) * 128], rhs=cond[:, kc:kc + 1],
                            start=(kc == 0), stop=(kc == 7)), reads=[b_pc, b_cond], writes=[b_pCol])
                q0 = v * 8 + hh * 4
                S.op("dve", lambda e, q0=q0: e.tensor_tensor(out=modT[:, q0:q0 + 4], in0=pCol[:, q0:q0 + 4], in1=abT_sb[:, q0:q0 + 4], op=ALU.add),
                     reads=[b_pCol], writes=[b_modT])
            if v in rows:
                row_sb, b_row, factor = rows[v]
                S.dma("sp", lambda e, c0=c0: e.dma_start(out=abrow[:], in_=ab2[:, c0:c0 + 512]), writes=[b_abrow])
                for kc in range(8):
                    S.op("pe", lambda e, pc=pc, kc=kc: e.matmul(pRow[:, :], lhsT=condB[:, kc, :], rhs=pc[:, kc, :], start=(kc == 0), stop=False),
                         reads=[b_pc, b_condB], writes=[b_pRow])
                S.op("pe", lambda e: e.matmul(pRow[:, :], lhsT=ones1[0:1, :], rhs=abrow[0:1, :], start=False, stop=True),
                     reads=[b_abrow, b_ones1], writes=[b_pRow])
                S.op("dve", lambda e, row_sb=row_sb, factor=factor, hh=hh: e.tensor_scalar(
                    out=row_sb[:, hh * 512:(hh + 1) * 512], in0=pRow[:], scalar1=float(factor), scalar2=None, op0=ALU.mult),
                    reads=[b_pRow], writes=[b_row])


def emit_rstd(K, ssq, b_ssq, rstd, b_rstd, n):
    S = K.S
    S.op("dve", lambda e: e.tensor_scalar(out=rstd[:], in0=ssq[:], scalar1=1.0 / n, scalar2=EPS, op0=ALU.mult, op1=ALU.add),
         reads=[b_ssq], writes=[b_rstd])
    S.op("act", lambda e: e.activation(out=rstd[:], in_=rstd[:], func=AF.Sqrt), reads=[b_rstd], writes=[b_rstd])
    S.op("dve", lambda e: e.reciprocal(out=rstd[:], in_=rstd[:]), reads=[b_rstd], writes=[b_rstd])


def emit_norm_T(K, xt, b_xt, rstd_col, b_rstd, xn, b_xn, pT, b_pT, idb, b_idb, tmpH, b_tmpH, A, Bv, b_AB, dst, b_dst):
    S = K.S
    S.op("dve", lambda e: e.tensor_scalar(out=xn[:], in0=xt[:], scalar1=rstd_col, scalar2=None, op0=ALU.mult),
         reads=[b_xt, b_rstd], writes=[b_xn])
    for c in range(8):
        S.op("pe", lambda e, c=c: e.transpose(out=pT[:, c, :], in_=xn[:, c * 128:(c + 1) * 128], identity=idb[:]),
             reads=[b_xn, b_idb], writes=[b_pT])
    S.op("dve", lambda e: e.tensor_tensor(out=tmpH[:], in0=pT[:], in1=bc_last(A, 128), op=ALU.mult),
         reads=[b_pT, b_AB], writes=[b_tmpH])
    S.op("dve", lambda e: e.tensor_tensor(out=dst, in0=tmpH[:], in1=bc_last(Bv, 128), op=ALU.add),
         reads=[b_tmpH, b_AB], writes=[b_dst])


def alloc_ffn_weights(K, ph, w1, w3, w2):
    S = K.S
    w1b, b_w1 = K.sb(ph, [128, 8, DFF], BF16, "w1b")
    w3b, b_w3 = K.sb(ph, [128, 8, DFF], BF16, "w3b")
    w2b, b_w2 = K.sb(ph, [128, NF, D], BF16, "w2b")
    S.dma("pool", lambda e: e.dma_start(out=w1b[:], in_=w1.rearrange("(kc p) n -> p kc n", p=128)), writes=[b_w1])
    S.dma("pool", lambda e: e.dma_start(out=w3b[:], in_=w3.rearrange("(kc p) n -> p kc n", p=128)), writes=[b_w3])
    S.dma("pool", lambda e: e.dma_start(out=w2b[:], in_=w2.rearrange("(fc p) n -> p fc n", p=128)), writes=[b_w2])
    return (w1b, b_w1, w3b, b_w3, w2b, b_w2)


def emit_ffn(K, ph, x_src, w1, w3, w2, A, Bv, b_AB, grow, b_grow, idb, b_idb, epilogue, src_bufs=None, ntiles=None, wts=None):
    S = K.S
    NT = ntiles or globals()["NT"]
    if wts is None:
        wts = alloc_ffn_weights(K, ph, w1, w3, w2)
    w1b, b_w1, w3b, b_w3, w2b, b_w2 = wts

    ssq, b_ssq = K.sb(ph, [128, NT], F32, "ssq")
    rstd, b_rstd = K.sb(ph, [128, NT], F32, "rstd")
    xts = [K.sb(ph, [128, D], F32, "xt") for _ in range(4)]
    xns = [K.sb(ph, [128, D], BF16, "xn") for _ in range(2)]
    junk, b_junk = xns[0]
    tmpH, b_tmpH = K.sb(ph, [128, 8, 128], F32, "tmpH")
    hTs = [K.sb(ph, [128, 2, 8, 128], BF16, "hT") for _ in range(2)]
    actTs = [(K.sb(ph, [128, NF, 256], BF16, "actT")[0], [Buf("actT%d" % f) for f in range(NF)]) for _ in range(2)]
    sil = [K.sb(ph, [128, 256], F32, "sil") for _ in range(2)]
    tmpO = [K.sb(ph, [128, D], F32, "tmpO")] * 2
    pAB = [K.ps(ph, [128, 512], F32, "pAB") for _ in range(3)]
    pO = [K.ps(ph, [128, D], F32, "pO") for _ in range(2)]
    pT, b_pT = K.ps(ph, [128, 8, 128], BF16, "pT")

    for t in range(NT):
        xt, b_xt = xts[t % 4]
        S.dma("sp", lambda e, xt=xt, t=t: e.dma_start(out=xt[:], in_=x_src(t)), reads=([src_bufs[t]] if src_bufs else []), writes=[b_xt])
        S.op("act", lambda e, xt=xt, t=t: e.activation(out=junk[:], in_=xt[:], func=AF.Square, accum_out=ssq[:, t:t + 1]),
             reads=[b_xt], writes=[b_junk, b_ssq])
    emit_rstd(K, ssq, b_ssq, rstd, b_rstd, D)
    if DBG_STOP == 1:
        return

    NG = NT // 2

    def prep(g):
        hT, b_hT = hTs[g % 2]
        for tt in range(2):
            t = 2 * g + tt
            xt, b_xt = xts[t % 4]
            xn, b_xn = xns[tt]
            S.dma("sp", lambda e, xt=xt, t=t: e.dma_start(out=xt[:], in_=x_src(t)), reads=([src_bufs[t]] if src_bufs else []), writes=[b_xt])
            emit_norm_T(K, xt, b_xt, rstd[:, t:t + 1], b_rstd, xn, b_xn, pT, b_pT, idb, b_idb, tmpH, b_tmpH,
                        A, Bv, b_AB, hT[:, tt, :, :], b_hT)

    def up(g, f):
        hT, b_hT = hTs[g % 2]
        pab, b_pab = pAB[f % 3]
        for wi, (wb, b_w) in enumerate(((w1b, b_w1), (w3b, b_w3))):
            for kc in range(8):
                S.op("pe", lambda e, wb=wb, kc=kc, f=f, wi=wi, hT=hT, pab=pab: e.matmul(
                    pab[:, wi * 256:(wi + 1) * 256].rearrange("p (a b) -> p a b", a=2),
                    lhsT=wb[:, kc, f * 128:(f + 1) * 128], rhs=hT[:, :, kc, :],
                    start=(kc == 0), stop=(kc == 7)), reads=[b_w, b_hT], writes=[b_pab])
        sl, b_sl = sil[f % 2]
        actT, b_actTs = actTs[g % 2]
        b_actT = b_actTs[f]
        S.op("act", lambda e, pab=pab, sl=sl: e.activation(out=sl[:], in_=pab[:, 0:256], func=AF.Silu),
             reads=[b_pab], writes=[b_sl])
        S.op("dve", lambda e, pab=pab, sl=sl, actT=actT, f=f: e.tensor_tensor(out=actT[:, f, :], in0=sl[:], in1=pab[:, 256:512],
                                                                               op=ALU.mult),
             reads=[b_sl, b_pab], writes=[b_actT])

    def down(g, f):
        actT, b_actTs = actTs[g % 2]
        b_actT = b_actTs[f]
        for tt in range(2):
            po, b_po = pO[tt]
            for dh in range(2):
                S.op("pe", lambda e, actT=actT, tt=tt, f=f, dh=dh, po=po: e.matmul(
                    po[:, dh * 512:(dh + 1) * 512], lhsT=actT[:, f, tt * 128:(tt + 1) * 128],
                    rhs=w2b[:, f, dh * 512:(dh + 1) * 512], start=(f == 0), stop=(f == NF - 1)),
                    reads=[b_actT, b_w2], writes=[b_po])

    def epi(g):
        for tt in range(2):
            t = 2 * g + tt
            xt, b_xt = xts[t % 4]
            po, b_po = pO[tt]
            to, b_to = tmpO[tt]
            S.op("dve", lambda e, to=to, po=po: e.tensor_tensor(out=to[:], in0=po[:], in1=grow[:], op=ALU.mult),
                 reads=[b_po, b_grow], writes=[b_to])
            S.op(EPI_ENG, lambda e, to=to, xt=xt: e.tensor_tensor(out=to[:], in0=to[:], in1=xt[:], op=ALU.add),
                 reads=[b_to, b_xt], writes=[b_to])
            epilogue(t, to, b_to)

    prep(0)
    for g in range(NG):
        for f in range(NF):
            up(g, f)
            if f >= 2:
                down(g, f - 2)
            if f == 6 and g + 1 < NG:
                prep(g + 1)
        down(g, NF - 2)
        down(g, NF - 1)
        epi(g)


def build_p1():
    nc = bass.Bass("TRN2", target_bir_lowering=False)

    def din(name, shape, dt=F32):
        return nc.dram_tensor(name, list(shape), dt, kind="ExternalInput").ap()

    def dout(name, shape, dt=F32):
        return nc.dram_tensor(name, list(shape), dt, kind="ExternalOutput").ap()

    x = din("x", [TOK, D])
    cT = din("cT", [128, 8])
    ada_w = din("ada_w", [D, 9 * D])
    ada_b = din("ada_b", [9 * D])
    ada_bT = din("ada_bT", [128, 72])
    n1T = din("n1T", [128, 8])
    n2T = din("n2T", [128, 8])
    w1 = din("w1", [D, DFF])
    w3 = din("w3", [D, DFF])
    w2 = din("w2", [DFF, D])
    w_in = din("w_in", [D, NCOL1])
    ident = din("ident", [128, 128])

    x1 = dout("x1", [TOK, D])
    h2T_o = dout("h2T", [NT, 128, 8, 128], BF16)
    qaT_o = dout("qaT", [128, 4, TOK], BF16)
    qidxT_o = dout("qidxT", [128, 4, TOK], BF16)
    kidxT_o = dout("kidxT", [64, TOK], BF16)
    ckvT_o = dout("ckvT", [128, 2, TOK], BF16)
    rstdkv_o = dout("rstdkv", [128, NT])
    small_o = dout("small", [128, NT, 16])
    qkmT_o = dout("qkmT", [128, 8, TOK])
    vm_o = dout("vm", [128, NT, 512], BF16)

    with ExitStack() as es:
        K = Ctx(nc, es)
        S = K.S
        idf, b_idf = K.sb(es, [128, 128], F32, "idf")
        idb, b_idb = K.sb(es, [128, 128], BF16, "idb")
        cT_sb, b_cT = K.sb(es, [128, 8], F32, "cT")
        abT_sb, b_abT = K.sb(es, [128, 72], F32, "abT")
        n1_sb, b_n1 = K.sb(es, [128, 8], F32, "n1")
        n2_sb, b_n2 = K.sb(es, [128, 8], F32, "n2")
        modT, b_modT = K.sb(es, [128, 72], F32, "modT")
        g1row, b_g1row = K.sb(es, [128, D], F32, "g1row")
        A1, b_A1 = K.sb(es, [128, 8], F32, "A1")
        A2, b_A2 = K.sb(es, [128, 8], F32, "A2")
        ssq2, b_ssq2 = K.sb(es, [128, NT], F32, "ssq2")
        rstd2, b_rstd2 = K.sb(es, [128, NT], F32, "rstd2")
        junk2, b_junk2 = K.sb(es, [128, D], BF16, "junk2")
        S.dma("sp", lambda e: e.dma_start(out=idf[:], in_=ident[:, :]), writes=[b_idf])
        S.dma("sp", lambda e: e.dma_start(out=cT_sb[:], in_=cT[:, :]), writes=[b_cT])
        S.dma("sp", lambda e: e.dma_start(out=abT_sb[:], in_=ada_bT[:, :]), writes=[b_abT])
        S.dma("sp", lambda e: e.dma_start(out=n1_sb[:], in_=n1T[:, :]), writes=[b_n1])
        S.dma("sp", lambda e: e.dma_start(out=n2_sb[:], in_=n2T[:, :]), writes=[b_n2])
        S.op("dve", lambda e: e.tensor_copy(out=idb[:], in_=idf[:]), reads=[b_idf], writes=[b_idb])
        with ExitStack() as ph:
            S.barrier()
            emit_mod(K, ph, ada_w, abT_sb, cT_sb, modT, b_modT, cols=[0, 1, 3, 4],
                     rows={2: (g1row, b_g1row, 0.5)}, ada_b_dram=ada_b, ident_f=idf, b_ident=b_idf)
            S.op("dve", lambda e: e.scalar_tensor_tensor(out=A1[:], in0=modT[:, 8:16], scalar=1.0, in1=n1_sb[:],
                                                         op0=ALU.add, op1=ALU.mult), reads=[b_modT, b_n1], writes=[b_A1])
            S.op("dve", lambda e: e.scalar_tensor_tensor(out=A2[:], in0=modT[:, 32:40], scalar=1.0, in1=n2_sb[:],
                                                         op0=ALU.add, op1=ALU.mult), reads=[b_modT, b_n2], writes=[b_A2])
            S.barrier()
            S.flush()
        b_x1 = [Buf("x1_%d" % t) for t in range(NT)]
        with ExitStack() as ph:
            b_AB = Buf("AB1")

            def epilogue(t, xo, b_xo):
                S.dma("sp", lambda e: e.dma_start(out=x1[t * 128:(t + 1) * 128, :], in_=xo[:]), reads=[b_xo], writes=[b_x1[t]])
                S.op("act", lambda e: e.activation(out=junk2[:], in_=xo[:], func=AF.Square, accum_out=ssq2[:, t:t + 1]),
                     reads=[b_xo], writes=[b_junk2, b_ssq2])

            emit_ffn(K, ph, lambda t: x[t * 128:(t + 1) * 128, :], w1, w3, w2, A1[:, :], modT[:, 0:8], b_AB,
                     g1row, b_g1row, idb, b_idb, epilogue)
            S.barrier()
            S.flush()
        with ExitStack() as ph:
            emit_proj(K, ph, x1, b_x1, w_in, ssq2, b_ssq2, rstd2, b_rstd2, A2, modT[:, 24:32], idb, b_idb,
                      dict(h2T=h2T_o, qaT=qaT_o, qidxT=qidxT_o, kidxT=kidxT_o, ckvT=ckvT_o, rstdkv=rstdkv_o,
                           small=small_o, qkmT=qkmT_o, vm=vm_o), ncols=NCOL1)
            S.barrier()
            S.finish()
            S.flush()
    return nc


def emit_proj(K, ph, x1, b_x1, w_in, ssq2, b_ssq2, rstd2, b_rstd2, A2, B2, idb, b_idb, outs, ncols):
    S = K.S
    wb, b_wb = K.sb(ph, [128, 8, ncols], BF16, "winb")
    S.dma("pool", lambda e: e.dma_start(out=wb[:], in_=w_in.rearrange("(kc p) n -> p kc n", p=128)), writes=[b_wb])
    emit_rstd(K, ssq2, b_ssq2, rstd2, b_rstd2, D)
    xts = [K.sb(ph, [128, D], F32, "xt") for _ in range(4)]
    xns = [K.sb(ph, [128, D], BF16, "xn") for _ in range(2)]
    tmpH, b_tmpH = K.sb(ph, [128, 8, 128], F32, "tmpH")
    hTs = [K.sb(ph, [128, 2, 8, 128], BF16, "hT") for _ in range(2)]
    ones_b, b_ones = K.sb(ph, [128, 1], F32, "ones")
    S.op("dve", lambda e: e.memset(ones_b[:], 1.0), writes=[b_ones])
    ssqkv, b_ssqkv = K.sb(ph, [128, NT], F32, "ssqkv")
    rstdkv, b_rstdkv = K.sb(ph, [128, NT], F32, "rstdkv")
    small, b_small = K.sb(ph, [128, NT, 16], F32, "small")
    fm16 = [K.sb(ph, [128, 11, 256], BF16, "fm16") for _ in range(2)]
    fm32 = [K.sb(ph, [128, 8, 256], F32, "fm32") for _ in range(2)]
    sq = [K.sb(ph, [128, 2, 256], F32, "sq") for _ in range(2)]
    vms = [K.sb(ph, [128, 512], BF16, "vms") for _ in range(2)]
    pF = [K.ps(ph, [128, 512], F32, "pF") for _ in range(3)]
    pV = [K.ps(ph, [128, 512], F32, "pV") for _ in range(2)]
    pS, b_pS = K.ps(ph, [128, 512], F32, "pS")
    pT, b_pT = K.ps(ph, [128, 8, 128], BF16, "pT")
    b_AB = Buf("AB2")
    NG = NT // 2
    b_h2T_d = Buf("h2T_d")

    def prep(g):
        hT, b_hT = hTs[g % 2]
        for tt in range(2):
            t = 2 * g + tt
            xt, b_xt = xts[t % 4]
            xn, b_xn = xns[tt]
            S.dma("sp", lambda e, xt=xt, t=t: e.dma_start(out=xt[:], in_=x1[t * 128:(t + 1) * 128, :]),
                  reads=[b_x1[t]], writes=[b_xt])
            emit_norm_T(K, xt, b_xt, rstd2[:, t:t + 1], b_rstd2, xn, b_xn, pT, b_pT, idb, b_idb, tmpH, b_tmpH,
                        A2[:, :], B2, b_AB, hT[:, tt, :, :], b_hT)
            S.dma("sp", lambda e, hT=hT, tt=tt, t=t: e.dma_start(out=outs["h2T"][t], in_=hT[:, tt, :, :]),
                  reads=[b_hT], writes=[b_h2T_d])

    chunks = []
    for i in range(4):
        chunks.append((P_QA + 128 * i, 128, "b", i))
    for i in range(2):
        chunks.append((P_CKV + 128 * i, 128, "b", 4 + i))
    for i in range(4):
        chunks.append((P_QIDX + 128 * i, 128, "b", 6 + i))
    chunks.append((P_KIDX, 128, "b", 10))
    for i in range(8):
        chunks.append((P_QM + 128 * i, 128, "f", i))

    cnt = [0]

    def body(g):
        hT, b_hT = hTs[g % 2]
        f16, b_f16 = fm16[g % 2]
        f32, b_f32 = fm32[g % 2]
        sqt, b_sq = sq[g % 2]
        for (c0, M, kind, di) in chunks:
            pf, b_pf = pF[cnt[0] % 3]
            cnt[0] += 1
            for kc in range(8):
                S.op("pe", lambda e, pf=pf, c0=c0, M=M, kc=kc, hT=hT: e.matmul(
                    pf[0:M, 0:256].rearrange("p (a b) -> p a b", a=2), lhsT=wb[:, kc, c0:c0 + M], rhs=hT[:, :, kc, :],
                    start=(kc == 0), stop=(kc == 7)), reads=[b_wb, b_hT], writes=[b_pf])
            if DBG_EVAC == 1:
                dst = f16[:, di, :] if kind == "b" else f32[:, di, :]
                S.op("dve", lambda e, pf=pf, dst=dst: e.tensor_copy(out=dst, in_=pf[:, 0:256]),
                     reads=[b_pf], writes=[b_f16 if kind == "b" else b_f32])
            elif kind == "b":
                if di in (4, 5):
                    S.op("act", lambda e, pf=pf, di=di, sqt=sqt: e.activation(out=sqt[:, di - 4, :], in_=pf[:, 0:256], func=AF.Square),
                         reads=[b_pf], writes=[b_sq])
                    S.op("dve", lambda e, pf=pf, di=di, f16=f16: e.tensor_copy(out=f16[:, di, :], in_=pf[:, 0:256]),
                         reads=[b_pf, b_sq], writes=[b_f16])
                else:
                    S.op("dve", lambda e, pf=pf, di=di, f16=f16, M=M: e.tensor_copy(out=f16[0:M, di, :], in_=pf[0:M, 0:256]),
                         reads=[b_pf], writes=[b_f16])
            else:
                S.op("act", lambda e, pf=pf, di=di, f32=f32: e.activation(out=f32[:, di, :], in_=pf[:, 0:256], func=AF.Identity),
                     reads=[b_pf], writes=[b_f32])
        if DBG_STOP == 12:
            return
        for tt in range(2):
            t = 2 * g + tt
            pv, b_pv = pV[tt]
            for kc in range(8):
                S.op("pe", lambda e, pv=pv, kc=kc, hT=hT, tt=tt: e.matmul(
                    pv[:, :], lhsT=hT[:, tt, kc, :], rhs=wb[:, kc, P_VM:P_VM + 512], start=(kc == 0), stop=(kc == 7)),
                    reads=[b_wb, b_hT], writes=[b_pv])
            vmt, b_vmt = vms[tt]
            S.op("act", lambda e, pv=pv, vmt=vmt: e.activation(out=vmt[:], in_=pv[:], func=AF.Identity), reads=[b_pv], writes=[b_vmt])
            S.dma("sp", lambda e, vmt=vmt, t=t: e.dma_start(out=outs["vm"][:, t, :], in_=vmt[:]), reads=[b_vmt])
            for kc in range(8):
                S.op("pe", lambda e, kc=kc, hT=hT, tt=tt: e.matmul(
                    pS[:, 0:8], lhsT=hT[:, tt, kc, :], rhs=wb[:, kc, P_WIDX:P_WIDX + 8], start=(kc == 0), stop=(kc == 7)),
                    reads=[b_wb, b_hT], writes=[b_pS])
            for kc in range(8):
                S.op("pe", lambda e, kc=kc, hT=hT, tt=tt: e.matmul(
                    pS[:, 8:16], lhsT=hT[:, tt, kc, :], rhs=wb[:, kc, P_IF:P_IF + 8], start=(kc == 0), stop=(kc == 7)),
                    reads=[b_wb, b_hT], writes=[b_pS])
            for c in range(2):
                S.op("pe", lambda e, c=c, tt=tt, sqt=sqt: e.matmul(
                    pS[:, 16:17], lhsT=sqt[:, c, tt * 128:(tt + 1) * 128], rhs=ones_b[:, 0:1], start=(c == 0), stop=(c == 1)),
                    reads=[b_sq, b_ones], writes=[b_pS])
            S.op("dve", lambda e, t=t: e.tensor_copy(out=small[:, t, :], in_=pS[:, 0:16]), reads=[b_pS], writes=[b_small])
            S.op("dve", lambda e, t=t: e.tensor_copy(out=ssqkv[:, t:t + 1], in_=pS[:, 16:17]), reads=[b_pS], writes=[b_ssqkv])
        tok0 = g * 256
        if DBG_STOP == 13:
            return
        S.dma("sp", lambda e, f16=f16: e.dma_start(out=outs["qaT"][:, :, tok0:tok0 + 256], in_=f16[:, 0:4, :]), reads=[b_f16])
        S.dma("sp", lambda e, f16=f16: e.dma_start(out=outs["ckvT"][:, :, tok0:tok0 + 256], in_=f16[:, 4:6, :]), reads=[b_f16])
        S.dma("sp", lambda e, f16=f16: e.dma_start(out=outs["qidxT"][:, :, tok0:tok0 + 256], in_=f16[:, 6:10, :]), reads=[b_f16])
        S.dma("sp", lambda e, f16=f16: e.dma_start(out=outs["kidxT"][:, tok0:tok0 + 256], in_=f16[0:64, 10, :]), reads=[b_f16])
        S.dma("sp", lambda e, f32=f32: e.dma_start(out=outs["qkmT"][:, :, tok0:tok0 + 256], in_=f32[:, :, :]), reads=[b_f32])

    prep(0)
    if DBG_STOP == 11:
        return
    for g in range(NG):
        if g + 1 < NG:
            prep(g + 1)
        body(g)
    if DBG_STOP == 14:
        return
    emit_rstd(K, ssqkv, b_ssqkv, rstdkv, b_rstdkv, 256)
    S.dma("sp", lambda e: e.dma_start(out=outs["rstdkv"][:, :], in_=rstdkv[:]), reads=[b_rstdkv])
    S.dma("sp", lambda e: e.dma_start(out=outs["small"][:, :, :], in_=small[:]), reads=[b_small])


def colT(v, n):
    return np.ascontiguousarray(np.asarray(v, np.float32).reshape(n, 128).T)


def core_tokens(x_b, j):
    S_, Dd = x_b.shape
    return np.ascontiguousarray(x_b.reshape(S_ // 512, 4, 128, Dd)[:, j].reshape(-1, Dd))


def p1_inputs(inp, core):
    b, j = divmod(core, 4)
    return {
        "x": core_tokens(np.asarray(inp["x"][b], np.float32), j),
        "cT": colT(inp["c"][b], 8),
        "ada_w": np.ascontiguousarray(inp["ada_w"][0], np.float32),
        "ada_b": np.ascontiguousarray(inp["ada_b"][0], np.float32),
        "ada_bT": colT(inp["ada_b"][0], 72),
        "n1T": colT(inp["ffn1_norm"][0], 8),
        "n2T": colT(inp["mix_norm"][0], 8),
        "w1": np.ascontiguousarray(inp["ffn1_w1"][0], np.float32),
        "w3": np.ascontiguousarray(inp["ffn1_w3"][0], np.float32),
        "w2": np.ascontiguousarray(inp["ffn1_w2"][0], np.float32),
        "w_in": pack_w_in(inp["w_in"][0]),
        "ident": np.eye(128, dtype=np.float32),
    }


def pack_w_in(w):
    w = np.asarray(w, np.float32)
    cols = np.concatenate([np.arange(0, 1344), np.arange(1280, 1344), np.arange(C_QM, C_VM + 512),
                           np.arange(C_WIDX, C_WIDX + 8), np.arange(C_I, C_I + 8)])
    assert cols.size == NCOL1
    return np.ascontiguousarray(w[:, cols])


NCH = S_LEN // 64


def emit_mlstm(K, ph, d, ident_b, b_identb, fused=None):
    S = K.S
    triu, b_triu = K.sb(ph, [64, 64], F32, "triu")
    ones64, b_ones64 = K.sb(ph, [64, 128], F32, "ones64")
    Ig, b_Ig = K.sb(ph, [64, NCH], F32, "Ig")
    Fg, b_Fg = K.sb(ph, [64, NCH], F32, "Fg")
    gb, b_gb = K.sb(ph, [64, 2], F32, "gb")
    ngb, b_ngb = K.sb(ph, [64, 1], F32, "ngb")
    cw, b_cw = K.sb(ph, [128, 8], F32, "cw")
    cb, b_cb = K.sb(ph, [128, 2], F32, "cb")
    LOGF, b_LOGF = K.sb(ph, [64, NCH], F32, "LOGF")
    Am, b_A = K.sb(ph, [64, NCH], F32, "Am")
    Gm, b_G = K.sb(ph, [64, NCH], F32, "Gm")
    GL, b_GL = K.sb(ph, [128, NCH], F32, "GL")
    QT, b_QT = K.sb(ph, [128, S_LEN], BF16, "QT")
    KT, b_KT = K.sb(ph, [128, S_LEN], BF16, "KT")
    Ktok, b_Ktok = K.sb(ph, [64, NCH, 128], BF16, "Ktok")
    Vaug, b_Vaug = K.sb(ph, [64, NCH, 129], BF16, "Vaug")
    aV, b_aV = K.sb(ph, [64, NCH, 129], BF16, "aV")
    if fused is None:
        col = lambda c: c
        for (t, b, src) in ((triu, b_triu, "triu"), (Ig, b_Ig, "ig"), (Fg, b_Fg, "fg"), (gb, b_gb, "gb"), (cw, b_cw, "cw"),
                            (cb, b_cb, "cb"), (Vaug, b_Vaug, "vaug")):
            S.dma("sp", lambda e, t=t, src=src: e.dma_start(out=t[:], in_=d[src]), writes=[b])
    else:
        col = lambda c: (c % 2) * 64 + c // 2
        hd, IFt, b_IFt, vm_s = fused["hd"], fused["IFt"], fused["b_IFt"], fused["vm_s"]
        for (t, b, src) in ((triu, b_triu, "triu"), (gb, b_gb, "gb"), (cw, b_cw, "cw"), (cb, b_cb, "cb")):
            S.dma("sp", lambda e, t=t, src=src: e.dma_start(out=t[:], in_=d[src]), writes=[b])
        for (t, b, q) in ((Ig, b_Ig, hd), (Fg, b_Fg, 4 + hd)):
            S.op("dve", lambda e, t=t, q=q: e.tensor_copy(out=t[:, 0:64], in_=IFt[0:64, q, :]), reads=[b_IFt], writes=[b])
            S.dma("sp", lambda e, t=t, q=q: e.dma_start(out=t[:, 64:128], in_=IFt[64:128, q, :]), reads=[b_IFt], writes=[b])
        for h2 in range(2):
            S.dma("sp", lambda e, h2=h2: e.dma_start(
                out=Vaug[:, h2 * 64:(h2 + 1) * 64, 0:128],
                in_=vm_s[:, h2 * 64:(h2 + 1) * 64, hd * 128:(hd + 1) * 128].rearrange("t s c -> s t c")), writes=[b_Vaug])
    S.op("dve", lambda e: e.memset(ones64[:], 1.0), writes=[b_ones64])
    S.op("dve", lambda e: e.memset(Vaug[:, :, 128:129], 1.0), reads=[b_Vaug], writes=[b_Vaug])
    S.op("dve", lambda e: e.tensor_scalar(out=ngb[:], in0=gb[:, 1:2], scalar1=-1.0, scalar2=None, op0=ALU.mult),
         reads=[b_gb], writes=[b_ngb])
    pG1, b_pG1 = K.ps(ph, [128, 512], F32, "pG1")
    pG2, b_pG2 = K.ps(ph, [128, 512], F32, "pG2")
    S.op("act", lambda e: e.activation(out=LOGF[:], in_=Fg[:], func=AF.Exp, scale=-1.0, bias=ngb[:, 0:1]),
         reads=[b_Fg, b_ngb], writes=[b_LOGF])
    S.op("act", lambda e: e.activation(out=LOGF[:], in_=LOGF[:], func=AF.Ln, bias=1.0), reads=[b_LOGF], writes=[b_LOGF])
    S.op("dve", lambda e: e.tensor_scalar(out=LOGF[:], in0=LOGF[:], scalar1=-1.0, scalar2=None, op0=ALU.mult),
         reads=[b_LOGF], writes=[b_LOGF])
    S.op("pe", lambda e: e.matmul(pG1[0:64, 0:NCH], lhsT=triu[:, :], rhs=LOGF[:, :], start=True, stop=True),
         reads=[b_triu, b_LOGF], writes=[b_pG1])
    S.op("pe", lambda e: e.matmul(pG2[:, 0:NCH], lhsT=ones64[:, :], rhs=LOGF[:, :], start=True, stop=True),
         reads=[b_ones64, b_LOGF], writes=[b_pG2])
    S.op("dve", lambda e: e.scalar_tensor_tensor(out=Am[:], in0=Ig[:], scalar=gb[:, 0:1], in1=pG1[0:64, 0:NCH],
                                                 op0=ALU.add, op1=ALU.subtract), reads=[b_Ig, b_gb, b_pG1], writes=[b_A])
    S.op("act", lambda e: e.activation(out=Am[:], in_=Am[:], func=AF.Exp), reads=[b_A], writes=[b_A])
    S.op("act", lambda e: e.activation(out=Gm[:], in_=pG1[0:64, 0:NCH], func=AF.Exp), reads=[b_pG1], writes=[b_G])
    S.op("act", lambda e: e.activation(out=GL[:], in_=pG2[:, 0:NCH], func=AF.Exp), reads=[b_pG2], writes=[b_GL])
    S.op("dve", lambda e: e.tensor_tensor(out=aV[:], in0=Vaug[:], in1=bc_last(Am[:, :], 129), op=ALU.mult),
         reads=[b_Vaug, b_A], writes=[b_aV])

    SEG = 1024
    xs = [K.sb(ph, [128, SEG + 3], F32, "xs") for _ in range(2)]
    accs = [K.sb(ph, [128, SEG], F32, "acc") for _ in range(2)]
    kscale = float(128 ** -0.5)
    n = 0
    for seg in range(S_LEN // SEG):
        for which, (src, dst, b_dst) in enumerate((("qpad", QT, b_QT), ("kpad", KT, b_KT))):
            x_, b_x = xs[n % 2]
            a_, b_a = accs[n % 2]
            n += 1
            S.dma("sp", lambda e, x_=x_, src=src, seg=seg: e.dma_start(out=x_[:], in_=d[src][:, seg * SEG:seg * SEG + SEG + 3]),
                  writes=[b_x])
            S.op("dve", lambda e, x_=x_, a_=a_, which=which: e.tensor_scalar(
                out=a_[:], in0=x_[:, 0:SEG], scalar1=cw[:, 4 * which:4 * which + 1], scalar2=None, op0=ALU.mult),
                reads=[b_x, b_cw], writes=[b_a])
            for w in range(1, 4):
                S.op("dve", lambda e, x_=x_, a_=a_, which=which, w=w: e.scalar_tensor_tensor(
                    out=a_[:], in0=x_[:, w:w + SEG], scalar=cw[:, 4 * which + w:4 * which + w + 1], in1=a_[:],
                    op0=ALU.mult, op1=ALU.add), reads=[b_x, b_cw, b_a], writes=[b_a])
            if which == 0:
                S.op("act", lambda e, a_=a_, dst=dst, seg=seg: e.activation(
                    out=dst[:, seg * SEG:(seg + 1) * SEG], in_=a_[:], func=AF.Silu, bias=cb[:, 0:1]),
                    reads=[b_a, b_cb], writes=[b_dst])
            else:
                S.op("act", lambda e, a_=a_: e.activation(out=a_[:], in_=a_[:], func=AF.Silu, bias=cb[:, 1:2]),
                     reads=[b_a, b_cb], writes=[b_a])
                S.op("dve", lambda e, a_=a_, dst=dst, seg=seg: e.tensor_scalar(
                    out=dst[:, seg * SEG:(seg + 1) * SEG], in0=a_[:], scalar1=kscale, scalar2=None, op0=ALU.mult),
                    reads=[b_a], writes=[b_dst])
    pKt, b_pKt = K.ps(ph, [64, 8, 128], BF16, "pKt")
    for g8 in range(NCH // 8):
        for cc in range(8):
            c = g8 * 8 + cc
            S.op("pe", lambda e, c=c, cc=cc: e.transpose(out=pKt[:, cc, :], in_=KT[:, c * 64:(c + 1) * 64], identity=ident_b[:]),
                 reads=[b_KT, b_identb], writes=[b_pKt])
        S.op("dve", lambda e, g8=g8: e.tensor_copy(out=Ktok[:, g8 * 8:(g8 + 1) * 8, :], in_=pKt[:]), reads=[b_pKt], writes=[b_Ktok])

    Cs = [K.sb(ph, [128, 129], F32, "Cst") for _ in range(2)]
    Cbs = [K.sb(ph, [128, 129], BF16, "Cbf") for _ in range(2)]
    tmpC, b_tmpC = K.sb(ph, [128, 129], F32, "tmpC")
    Sps = [K.sb(ph, [64, 64], BF16, "Sp") for _ in range(2)]
    t1s = [K.sb(ph, [64, 2], F32, "t1") for _ in range(2)]
    GSEG = 16
    Hs, b_Hs = K.sb(ph, [64, GSEG, 128], F32, "Hs")
    Hq, b_Hq = K.sb(ph, [64, GSEG, 128], F32, "Hq")
    Hn, b_Hn = K.sb(ph, [64, GSEG, 128], BF16, "Hn")
    mu, b_mu = K.sb(ph, [64, GSEG], F32, "mu")
    var, b_var = K.sb(ph, [64, GSEG], F32, "var")
    hseg, b_hseg = K.sb(ph, [128, GSEG * 64], BF16, "hseg")
    pSt = [K.ps(ph, [64, 512], F32, "pSt") for _ in range(2)]
    pU = [(pG1, b_pG1), (pG2, b_pG2)]
    pND = [K.ps(ph, [64, 512], F32, "pND") for _ in range(2)]
    pTr, b_pTr = pKt, b_pKt
    pTr2, b_pTr2 = K.ps(ph, [128, GSEG * 64], BF16, "pTr2")
    S.op("dve", lambda e: e.memset(Cs[1][0][:], 0.0), writes=[Cs[1][1]])
    for c in range(NCH):
        st_, b_st = pSt[c % 2]
        sp_, b_sp = Sps[c % 2]
        u_, b_u = pU[c % 2]
        nd_, b_nd = pND[c % 2]
        Cp, b_Cp = Cs[(c + 1) % 2]
        Cn, b_Cn = Cs[c % 2]
        Cbp, b_Cbp = Cbs[(c + 1) % 2]
        Cbn, b_Cbn = Cbs[c % 2]
        t1, b_t1 = t1s[c % 2]
        cs = slice(c * 64, (c + 1) * 64)
        S.op("pe", lambda e, st_=st_, cs=cs: e.matmul(st_[:, 0:64], lhsT=KT[:, cs], rhs=QT[:, cs], start=True, stop=True),
             reads=[b_KT, b_QT], writes=[b_st])
        S.op("dve", lambda e, st_=st_, sp_=sp_, c=c: e.scalar_tensor_tensor(
            out=sp_[:], in0=st_[:, 0:64], scalar=Am[:, col(c):col(c) + 1], in1=triu[:, :], op0=ALU.mult, op1=ALU.mult),
            reads=[b_st, b_A, b_triu], writes=[b_sp])
        S.op("pe", lambda e, u_=u_, c=c: e.matmul(u_[:, 0:129], lhsT=Ktok[:, c, :], rhs=aV[:, col(c), :], start=True, stop=True),
             reads=[b_Ktok, b_aV], writes=[b_u])
        if c > 0:
            S.op("pe", lambda e, nd_=nd_, cs=cs, Cbp=Cbp: e.matmul(nd_[:, 0:129], lhsT=QT[:, cs], rhs=Cbp[:, :], start=True, stop=False),
                 reads=[b_QT, b_Cbp], writes=[b_nd])
        S.op("pe", lambda e, nd_=nd_, sp_=sp_, c=c: e.matmul(nd_[:, 0:129], lhsT=sp_[:, :], rhs=Vaug[:, col(c), :], start=(c == 0), stop=True),
             reads=[b_sp, b_Vaug], writes=[b_nd])
        S.op("dve", lambda e, u_=u_, Cp=Cp: e.tensor_tensor(out=tmpC[:], in0=u_[:, 0:129], in1=Cp[:], op=ALU.add),
             reads=[b_u, b_Cp], writes=[b_tmpC])
        S.op("dve", lambda e, Cn=Cn, c=c: e.tensor_scalar(out=Cn[:], in0=tmpC[:], scalar1=GL[:, col(c):col(c) + 1], scalar2=None, op0=ALU.mult),
             reads=[b_tmpC, b_GL], writes=[b_Cn])
        S.op("act", lambda e, Cn=Cn, Cbn=Cbn: e.activation(out=Cbn[:], in_=Cn[:], func=AF.Identity), reads=[b_Cn], writes=[b_Cbn])
        S.op("act", lambda e, nd_=nd_, t1=t1, c=c: e.activation(out=t1[:, 0:1], in_=nd_[:, 128:129], func=AF.Abs, scale=Gm[:, col(c):col(c) + 1]),
             reads=[b_nd, b_G], writes=[b_t1])
        S.op("dve", lambda e, t1=t1: e.tensor_scalar(out=t1[:, 0:1], in0=t1[:, 0:1], scalar1=1.0, scalar2=None, op0=ALU.max),
             reads=[b_t1], writes=[b_t1])
        S.op("dve", lambda e, t1=t1: e.reciprocal(out=t1[:, 0:1], in_=t1[:, 0:1]), reads=[b_t1], writes=[b_t1])
        S.op("dve", lambda e, t1=t1, c=c: e.tensor_tensor(out=t1[:, 1:2], in0=t1[:, 0:1], in1=Gm[:, col(c):col(c) + 1], op=ALU.mult),
             reads=[b_t1, b_G], writes=[b_t1])
        S.op("dve", lambda e, nd_=nd_, t1=t1, c=c: e.tensor_scalar(out=Hs[:, c % GSEG, :], in0=nd_[:, 0:128], scalar1=t1[:, 1:2], scalar2=None,
                                                                    op0=ALU.mult), reads=[b_nd, b_t1], writes=[b_Hs])
        if c % GSEG == GSEG - 1:
            sg = c // GSEG
            S.op("dve", lambda e: e.tensor_reduce(out=mu[:], in_=Hs[:], axis=AX.X, op=ALU.add), reads=[b_Hs], writes=[b_mu])
            S.op("dve", lambda e: e.tensor_scalar(out=mu[:], in0=mu[:], scalar1=1.0 / 128, scalar2=None, op0=ALU.mult),
                 reads=[b_mu], writes=[b_mu])
            S.op("dve", lambda e: e.tensor_tensor(out=Hs[:], in0=Hs[:], in1=bc_last(mu[:, :], 128), op=ALU.subtract),
                 reads=[b_Hs, b_mu], writes=[b_Hs])
            S.op("dve", lambda e: e.tensor_tensor(out=Hq[:], in0=Hs[:], in1=Hs[:], op=ALU.mult), reads=[b_Hs], writes=[b_Hq])
            S.op("dve", lambda e: e.tensor_reduce(out=var[:], in_=Hq[:], axis=AX.X, op=ALU.add), reads=[b_Hq], writes=[b_var])
            emit_rstd(K, var, b_var, var, b_var, 128)
            S.op("dve", lambda e: e.tensor_tensor(out=Hn[:], in0=Hs[:], in1=bc_last(var[:, :], 128), op=ALU.mult),
                 reads=[b_Hs, b_var], writes=[b_Hn])
            for cc in range(GSEG):
                S.op("pe", lambda e, cc=cc: e.transpose(out=pTr2[:, cc * 64:(cc + 1) * 64], in_=Hn[:, cc, :], identity=ident_b[0:64, 0:64]),
                     reads=[b_Hn, b_identb], writes=[b_pTr2])
            S.op("dve", lambda e: e.tensor_copy(out=hseg[:], in_=pTr2[:]), reads=[b_pTr2], writes=[b_hseg])
            S.dma("sp", lambda e, sg=sg: e.dma_start(out=d["hmT"][:, sg * GSEG * 64:(sg + 1) * GSEG * 64], in_=hseg[:]), reads=[b_hseg])


def mlstm_inputs_from(qT, kT, v, ig, fg, inp, hd):
    z3 = np.zeros((128, 3), np.float32)
    vaug = np.zeros((64, NCH, 129), ml_dtypes.bfloat16)
    vaug[:, :, 0:128] = np.asarray(v).reshape(NCH, 64, 128).transpose(1, 0, 2)
    gbv = np.asarray(inp["mlstm_gate_bias"][0], np.float32)
    cwv = np.asarray(inp["conv_w"][0], np.float32)
    cbv = np.asarray(inp["conv_b"][0], np.float32)
    cw = np.concatenate([cwv[:, hd * 128:(hd + 1) * 128].T, cwv[:, 512 + hd * 128:512 + (hd + 1) * 128].T], axis=1)
    cb = np.stack([cbv[hd * 128:(hd + 1) * 128], cbv[512 + hd * 128:512 + (hd + 1) * 128]], axis=1)
    return {
        "qpad": np.ascontiguousarray(np.concatenate([z3, qT], axis=1), np.float32),
        "kpad": np.ascontiguousarray(np.concatenate([z3, kT], axis=1), np.float32),
        "vaug": vaug,
        "ig": np.ascontiguousarray(np.asarray(ig, np.float32).reshape(NCH, 64).T),
        "fg": np.ascontiguousarray(np.asarray(fg, np.float32).reshape(NCH, 64).T),
        "gb": np.ascontiguousarray(np.broadcast_to(np.array([gbv[hd], gbv[4 + hd]], np.float32)[None, :], (64, 2))),
        "cw": np.ascontiguousarray(cw, np.float32),
        "cb": np.ascontiguousarray(cb, np.float32),
        "triu": np.triu(np.ones((64, 64), np.float32)),
    }


NIT_BISECT = 16
NEG_MASK = -32768.0


def emit_attn(K, ph, d, ident_f, b_identf, ident_b, b_identb, nslots=NT):
    S = K.S
    nc = K.nc
    KpT, b_KpT = K.sb(ph, [128, 4, S_LEN], BF16, "KpT")
    kidx2, b_kidx2 = K.sb(ph, [128, S_LEN], BF16, "kidx2")
    rstd, b_rstd = K.sb(ph, [128, 64], F32, "rstdk")
    widx, b_widx = K.sb(ph, [128, NT, 8], F32, "widx")
    wabs, b_wabs = K.sb(ph, [128, NT, 8], F32, "wabs")
    wsgn, b_wsgn = K.sb(ph, [128, NT, 8], F32, "wsgn")
    E4, b_E4 = K.sb(ph, [128, 512], BF16, "E4")
    zer, b_zer = K.sb(ph, [128, 65], BF16, "zer")
    onesr, b_onesr = K.sb(ph, [128, 64], F32, "onesr")
    cm, b_cm = K.sb(ph, [128, 512], F32, "cm")
    btb, b_btb = K.sb(ph, [128, 8, 640], BF16, "btb")
    b31, b_b31 = K.sb(ph, [128, 8], F32, "b31")
    ring = [K.ps(ph, [128, 512], F32, "ring") for _ in range(4)]
    pY = [K.ps(ph, [128, 512], F32, "pY") for _ in range(2)]
    pT, b_pT = K.ps(ph, [128, 4, 128], BF16, "pTa")
    S.dma("sp", lambda e: e.dma_start(out=kidx2[:], in_=d["kidx2"]), writes=[b_kidx2])
    S.dma("sp", lambda e: e.dma_start(out=rstd[:], in_=d["rstd"]), writes=[b_rstd])
    S.dma("sp", lambda e: e.dma_start(out=widx[:], in_=d["widx"]), writes=[b_widx])
    S.dma("sp", lambda e: e.dma_start(out=cm[:], in_=d["cm"]), writes=[b_cm])
    S.dma("sp", lambda e: e.dma_start(out=b31[:], in_=d["b31"]), writes=[b_b31])
    S.op("dve", lambda e: e.memset(zer[:], 0.0), writes=[b_zer])
    S.op("dve", lambda e: e.memset(onesr[:], 1.0), writes=[b_onesr])
    for hh in range(4):
        S.op("dve", lambda e, hh=hh: e.tensor_copy(out=E4[:, hh * 128:(hh + 1) * 128], in_=ident_f[:]), reads=[b_identf], writes=[b_E4])
    S.op("act", lambda e: e.activation(out=wabs[:], in_=widx[:], func=AF.Abs), reads=[b_widx], writes=[b_wabs])
    S.op("act", lambda e: e.activation(out=wsgn[:], in_=widx[:], func=AF.Sign), reads=[b_widx], writes=[b_wsgn])

    with ExitStack() as pp:
        ckvT, b_ckvT = K.sb(pp, [128, 2, S_LEN], BF16, "ckvT")
        wst, b_wst = K.sb(pp, [128, 2, 512], F32, "wst")
        gkv, b_gkv = K.sb(pp, [128, 2], F32, "gkv")
        wukb, b_wukb = K.sb(pp, [128, 2, 512], BF16, "wukb")
        wuvb, b_wuvb = K.sb(pp, [128, 2, 512], BF16, "wuvb")
        bst, b_bst = K.sb(pp, [128, 640], F32, "bst")
        S.dma("sp", lambda e: e.dma_start(out=ckvT[:], in_=d["ckvT"]), writes=[b_ckvT])
        S.dma("sp", lambda e: e.dma_start(out=gkv[:], in_=d["gkv"]), writes=[b_gkv])
        for (src, dst, b_dst, fac) in (("wuk", wukb, b_wukb, 0.125), ("wuv", wuvb, b_wuvb, 1.0)):
            S.dma("sp", lambda e, src=src: e.dma_start(out=wst[:], in_=d[src]), writes=[b_wst])
            for cc in range(2):
                S.op("dve", lambda e, cc=cc, dst=dst, fac=fac: e.tensor_scalar(
                    out=dst[:, cc, :], in0=wst[:, cc, :], scalar1=gkv[:, cc:cc + 1], scalar2=float(fac), op0=ALU.mult, op1=ALU.mult),
                    reads=[b_wst, b_gkv], writes=[b_dst])
        for h in range(8):
            S.dma("sp", lambda e, h=h: e.dma_start(out=bst[:], in_=d["bt"][:, h, :]), writes=[b_bst])
            S.op("dve", lambda e, h=h: e.tensor_scalar(out=btb[:, h, :], in0=bst[:], scalar1=b31[:, h:h + 1], scalar2=None, op0=ALU.subtract),
                 reads=[b_bst, b_b31], writes=[b_btb])
        ktoks = [K.sb(pp, [128, 512], BF16, "ktok") for _ in range(2)]
        vts = [K.sb(pp, [128, 8, 65], BF16, "vt") for _ in range(3)]
        for (vt, b_vt) in vts:
            S.op("dve", lambda e, vt=vt: e.memset(vt[:], 1.0), writes=[b_vt])
        b_vscr = [Buf("vscr%d" % T) for T in range(64)]
        for T in range(64):
            pk, b_pk = ring[(2 * T) % 4]
            pv, b_pv = ring[(2 * T + 1) % 4]
            kt, b_kt = ktoks[T % 2]
            vt, b_vt = vts[T % 3]
            ts = slice(T * 128, (T + 1) * 128)
            for cc in range(2):
                S.op("pe", lambda e, pk=pk, cc=cc, ts=ts: e.matmul(pk[:, :], lhsT=ckvT[:, cc, ts], rhs=wukb[:, cc, :], start=(cc == 0), stop=(cc == 1)),
                     reads=[b_ckvT, b_wukb], writes=[b_pk])
            for cc in range(2):
                S.op("pe", lambda e, pv=pv, cc=cc, ts=ts: e.matmul(pv[:, :], lhsT=ckvT[:, cc, ts], rhs=wuvb[:, cc, :], start=(cc == 0), stop=(cc == 1)),
                     reads=[b_ckvT, b_wuvb], writes=[b_pv])
            S.op("dve", lambda e, pk=pk, kt=kt, T=T: e.tensor_scalar(out=kt[:], in0=pk[:], scalar1=rstd[:, T:T + 1], scalar2=None, op0=ALU.mult),
                 reads=[b_pk, b_rstd], writes=[b_kt])
            S.op("dve", lambda e, pv=pv, vt=vt, T=T: e.tensor_scalar(
                out=vt[:, :, 0:64], in0=pv[:].rearrange("p (h d) -> p h d", h=8), scalar1=rstd[:, T:T + 1], scalar2=None, op0=ALU.mult),
                reads=[b_pv, b_rstd], writes=[b_vt])
            S.dma("sp", lambda e, vt=vt, T=T: e.dma_start(out=d["vscr"][T], in_=vt[:].rearrange("p h d -> p (h d)")),
                  reads=[b_vt], writes=[b_vscr[T]])
            for q in range(4):
                S.op("pe", lambda e, q=q, kt=kt: e.transpose(out=pT[:, q, :], in_=kt[:, q * 128:(q + 1) * 128], identity=ident_b[:]),
                     reads=[b_kt, b_identb], writes=[b_pT])
            S.op("act", lambda e, ts=ts: e.copy(out=KpT[:, :, ts], in_=pT[:]), reads=[b_pT], writes=[b_KpT])
        S.barrier()

    sc, b_sc = K.sb(ph, [128, S_LEN], F32, "sc")
    nms = [K.sb(ph, [128, S_LEN], BF16, "nm") for _ in range(2)]
    nmbs = [K.sb(ph, [128, 8, 640], BF16, "nmb") for _ in range(2)]
    nmA_bufs = [Buf("nmA0"), Buf("nmA1")]
    rts = [K.sb(ph, [128, 512], F32, "rt") for _ in range(2)]
    qas = [K.sb(ph, [128, 4, 256], BF16, "qa") for _ in range(2)]
    for (qa_, b_qa_) in qas:
        S.op("dve", lambda e, qa_=qa_: e.memset(qa_[:], 0.0), writes=[b_qa_])
    qis = [K.sb(ph, [128, 4, 128], BF16, "qi") for _ in range(2)]
    vbufs = [K.sb(ph, [128, 520], BF16, "vbuf") for _ in range(3)]
    PTs = [K.sb(ph, [128, 512], BF16, "PT") for _ in range(4)]
    bs = {n_: K.sb(ph, [128, 1], F32, n_) for n_ in ("amax", "w0", "lo", "mid", "cnt", "gw", "nmid", "sa")}
    rd, b_rd = K.sb(ph, [128, 512], F32, "rd")
    bsb, b_bsb = K.sb(ph, [64, 512], F32, "bsb")
    yo, b_yo = K.sb(ph, [64, 1024], BF16, "yo")
    rcnt = [0]
    vcnt = [0]
    pcnt = [0]

    def stage_a(i):
        nk = (i + 1) * 512
        qa, b_qa = qas[i % 2]
        qi, b_qi = qis[i % 2]
        nm, b_nm = nms[i % 2]
        nmb, b_nmb = nmbs[i % 2]
        b_nmA = nmA_bufs[i % 2]
        S.dma("sp", lambda e: e.dma_start(out=qa[0:64, :, 0:128], in_=d["qaT"][0:64, :, i * 128:(i + 1) * 128]), writes=[b_qa])
        S.dma("sp", lambda e: e.dma_start(out=qa[64:128, :, 128:256], in_=d["qaT"][64:128, :, i * 128:(i + 1) * 128]), writes=[b_qa])
        S.dma("sp", lambda e: e.dma_start(out=qi[:], in_=d["qidxT"][:, :, i * 128:(i + 1) * 128]), writes=[b_qi])
        for kc in range(i + 1):
            ks = slice(kc * 512, (kc + 1) * 512)
            for h in range(8):
                pz, b_pz = ring[rcnt[0] % 4]
                rt, b_rt = rts[rcnt[0] % 2]
                rcnt[0] += 1
                hp = slice((h % 2) * 64, (h % 2) * 64 + 64)
                S.op("pe", lambda e, pz=pz, hp=hp, h=h, ks=ks: e.matmul(pz[:, :], lhsT=qi[hp, h // 2, :], rhs=kidx2[hp, ks], start=True, stop=True),
                     reads=[b_qi, b_kidx2], writes=[b_pz])
                S.op("act", lambda e, pz=pz, rt=rt, h=h: e.activation(out=rt[:], in_=pz[:], func=AF.Relu, scale=wabs[:, i, h:h + 1]),
                     reads=[b_pz, b_wabs], writes=[b_rt])
                if h == 0:
                    S.op("dve", lambda e, rt=rt, ks=ks, h=h: e.tensor_scalar(out=sc[:, ks], in0=rt[:], scalar1=wsgn[:, i, h:h + 1], scalar2=None, op0=ALU.mult),
                         reads=[b_rt, b_wsgn], writes=[b_sc])
                else:
                    S.op("dve", lambda e, rt=rt, ks=ks, h=h: e.scalar_tensor_tensor(out=sc[:, ks], in0=rt[:], scalar=wsgn[:, i, h:h + 1], in1=sc[:, ks],
                                                                                     op0=ALU.mult, op1=ALU.add), reads=[b_rt, b_wsgn, b_sc], writes=[b_sc])
        amax, b_amax = bs["amax"]
        w0, b_w0 = bs["w0"]
        lo, b_lo = bs["lo"]
        mid, b_mid = bs["mid"]
        cnt, b_cnt = bs["cnt"]
        gw, b_gw = bs["gw"]
        S.op("dve", lambda e: e.tensor_reduce(out=amax[:], in_=sc[:, 0:nk], axis=AX.X, op=ALU.max, apply_absolute_value=True),
             reads=[b_sc], writes=[b_amax])
        S.op("dve", lambda e: e.tensor_tensor(out=sc[:, nk - 512:nk], in0=sc[:, nk - 512:nk], in1=cm[:], op=ALU.add),
             reads=[b_sc, b_cm], writes=[b_sc])
        S.op("dve", lambda e: e.tensor_scalar(out=lo[:], in0=amax[:], scalar1=-1.0, scalar2=-1.0, op0=ALU.mult, op1=ALU.add),
             reads=[b_amax], writes=[b_lo])
        S.op("dve", lambda e: e.tensor_scalar(out=w0[:], in0=amax[:], scalar1=2.0, scalar2=2.0, op0=ALU.mult, op1=ALU.add),
             reads=[b_amax], writes=[b_w0])
        split = False
        nd_ = (nk * 9 // 16) // 512 * 512 if split else nk
        na_ = nk - nd_
        thr = 255.5 - 0.5 * na_
        nmid, b_nmid = bs["nmid"]
        sa, b_sa = bs["sa"]
        for it in range(1, NIT_BISECT + 1):
            f = float(2.0 ** -it)
            S.op("dve", lambda e, f=f: e.scalar_tensor_tensor(out=mid[:], in0=w0[:], scalar=f, in1=lo[:], op0=ALU.mult, op1=ALU.add),
                 reads=[b_w0, b_lo], writes=[b_mid])
            if split:
                S.op("dve", lambda e: e.tensor_scalar(out=nmid[:], in0=mid[:], scalar1=-1.0, scalar2=None, op0=ALU.mult), reads=[b_mid], writes=[b_nmid])
                S.op("act", lambda e: e.activation(out=nm[:, nd_:nk], in_=sc[:, nd_:nk], func=AF.Sign, bias=nmid[:, 0:1], accum_out=sa[:]),
                     reads=[b_sc, b_nmid], writes=[b_nmA, b_sa])
            S.op("dve", lambda e: e.tensor_scalar(out=nm[:, 0:nd_], in0=sc[:, 0:nd_], scalar1=mid[:, 0:1], scalar2=0.0, op0=ALU.is_ge, op1=ALU.add,
                                                  accum_out=cnt[:]), reads=[b_sc, b_mid], writes=[b_nm, b_cnt])
            if split:
                S.op("dve", lambda e: e.scalar_tensor_tensor(out=cnt[:], in0=sa[:], scalar=0.5, in1=cnt[:], op0=ALU.mult, op1=ALU.add),
                     reads=[b_sa, b_cnt], writes=[b_cnt])
            S.op("dve", lambda e: e.scalar_tensor_tensor(out=gw[:], in0=cnt[:], scalar=float(thr), in1=w0[:], op0=ALU.is_ge, op1=ALU.mult),
                 reads=[b_cnt, b_w0], writes=[b_gw])
            S.op("dve", lambda e, f=f: e.scalar_tensor_tensor(out=lo[:], in0=gw[:], scalar=f, in1=lo[:], op0=ALU.mult, op1=ALU.add),
                 reads=[b_gw, b_lo], writes=[b_lo])
        S.op("dve", lambda e: e.tensor_scalar(out=nm[:, 0:nk], in0=sc[:, 0:nk], scalar1=lo[:, 0:1], scalar2=NEG_MASK, op0=ALU.is_lt, op1=ALU.mult),
             reads=[b_sc, b_lo], writes=[b_nm, b_nmA])
        t0 = 1 if i == 0 else 0
        for h in range(8):
            S.op("dve", lambda e, h=h: e.tensor_tensor(
                out=nmb[:, h, t0 * 128:640], in0=nm[:, nk - 640 + t0 * 128:nk], in1=btb[:, h, t0 * 128:640], op=ALU.add),
                reads=[b_nm, b_btb], writes=[b_nmb])

    def stage_b(i):
        ntile = 4 * i + 4
        qa, b_qa = qas[i % 2]
        nm, b_nm = nms[i % 2]
        nmb, b_nmb = nmbs[i % 2]
        for g in range(2):
            py, b_py = pY[g]
            S.op("pe", lambda e, py=py: e.matmul(py[0:65, :], lhsT=zer[:, :], rhs=E4[:, :], start=True, stop=False),
                 reads=[b_zer, b_E4], writes=[b_py])
        units = [(st, g) for st in range(ntile) for g in range(2)]
        state = {}

        def qk(u):
            st, g = units[u]
            if g == 0:
                vb, b_vb = vbufs[vcnt[0] % 3]
                vcnt[0] += 1
                S.dma("sp", lambda e: e.dma_start(out=vb[:], in_=d["vscr"][st]), reads=[b_vscr[st]], writes=[b_vb])
                state[("vb", st)] = (vb, b_vb)
            ti = st - (4 * i - 1)
            ss = slice(st * 128, (st + 1) * 128)
            pl, b_pl = ring[rcnt[0] % 4]
            rcnt[0] += 1
            pt, b_pt = PTs[pcnt[0] % 4]
            pcnt[0] += 1
            if ti < 0:
                S.op("pe", lambda e: e.matmul(pl[:, :], lhsT=nm[:, ss], rhs=E4[:, :], start=True, stop=False),
                     reads=[b_nm, b_E4], writes=[b_pl])
                for pp in range(2):
                    pair = 2 * g + pp
                    S.op("pe", lambda e, pp=pp, pair=pair: e.matmul(pl[:, pp * 256:(pp + 1) * 256], lhsT=KpT[:, pair, ss], rhs=qa[:, pair, :],
                                                                    start=False, stop=(pp == 1)), reads=[b_KpT, b_qa], writes=[b_pl])
            for hh in (range(4) if ti >= 0 else ()):
                h = 4 * g + hh
                hp = slice((h % 2) * 64, (h % 2) * 64 + 64)
                os_ = slice(hh * 128, (hh + 1) * 128)
                if ti >= 0:
                    S.op("pe", lambda e, os_=os_, h=h: e.matmul(pl[:, os_], lhsT=nmb[:, h, ti * 128:(ti + 1) * 128], rhs=ident_b[:, :],
                                                                start=True, stop=False), reads=[b_nmb, b_identb], writes=[b_pl])
                qc = slice((h % 2) * 128, (h % 2) * 128 + 128)
                S.op("pe", lambda e, os_=os_, hp=hp, h=h, qc=qc: e.matmul(pl[:, os_], lhsT=KpT[hp, h // 2, ss], rhs=qa[hp, h // 2, qc],
                                                                            start=False, stop=True), reads=[b_KpT, b_qa], writes=[b_pl])
            S.op("act", lambda e: e.activation(out=pt[:], in_=pl[:], func=AF.Exp), reads=[b_pl], writes=[b_pt])
            state[u] = (pt, b_pt)

        def pv(u):
            st, g = units[u]
            pt, b_pt = state.pop(u)
            vb, b_vb = state[("vb", st)]
            py, b_py = pY[g]
            for hh in range(4):
                h = 4 * g + hh
                os_ = slice(hh * 128, (hh + 1) * 128)
                S.op("pe", lambda e, os_=os_, h=h, hh=hh: e.matmul(
                    py[0:65, os_], lhsT=vb[:, h * 65:(h + 1) * 65], rhs=pt[:, os_], start=False, stop=(st == ntile - 1 and hh == 3)),
                    reads=[b_vb, b_pt], writes=[b_py])

        qk(0)
        for u in range(len(units)):
            if u + 1 < len(units):
                qk(u + 1)
            pv(u)
        for g in range(2):
            pb, b_pb = ring[rcnt[0] % 4]
            rcnt[0] += 1
            py, b_py = pY[g]
            S.op("dve", lambda e, py=py: e.reciprocal(out=rd[64:65, :], in_=py[64:65, :]), reads=[b_py], writes=[b_rd])
            S.op("pe", lambda e, pb=pb: e.matmul(pb[0:64, :], lhsT=onesr[64:65, 0:64], rhs=rd[64:65, :], start=True, stop=True),
                 reads=[b_onesr, b_rd], writes=[b_pb])
            S.op("act", lambda e, pb=pb: e.activation(out=bsb[:, :], in_=pb[0:64, :], func=AF.Identity), reads=[b_pb], writes=[b_bsb])
            S.op("dve", lambda e, py=py, g=g: e.tensor_tensor(out=yo[:, g * 512:(g + 1) * 512], in0=py[0:64, :], in1=bsb[:, :], op=ALU.mult),
                 reads=[b_py, b_bsb], writes=[b_yo])
        S.dma("sp", lambda e: e.dma_start(out=d["yaT"][:, :, i * 128:(i + 1) * 128], in_=yo[:].rearrange("p (h t) -> p h t", h=8)),
              reads=[b_yo])

    stage_a(0)
    for i in range(nslots):
        if i + 1 < nslots:
            stage_a(i + 1)
        stage_b(i)


def t5_bucket_np(dist):
    n = np.maximum(dist, 0)
    nf = np.maximum(n, 1).astype(np.float32)
    large = 16 + (np.log(nf / np.float32(16)) / np.float32(np.log(128 / 16)) * np.float32(16)).astype(np.int32)
    large = np.minimum(large, 31)
    return np.where(n < 16, n, large)


def attn_consts(inp, j):
    rb = np.asarray(inp["rel_bias"], np.float32)
    t = np.arange(128)[:, None, None]
    ti = np.arange(5)[None, :, None]
    sl = np.arange(128)[None, None, :]
    dist = (j + 1 - ti) * 128 + t - sl
    bk = t5_bucket_np(dist)
    bt = rb[bk]
    bt = np.ascontiguousarray(bt.transpose(0, 3, 1, 2).reshape(128, 8, 640), np.float32)
    b31 = np.ascontiguousarray(np.broadcast_to(rb[31][None, :], (128, 8)), np.float32)
    jp = np.arange(4)[None, :, None]
    t2 = np.arange(128)[:, None, None]
    valid = ((jp - j) * 128 + sl - t2) <= 0
    cm = np.where(valid, 0.0, -1e30).astype(np.float32).reshape(128, 512)
    wuk = np.asarray(inp["w_uk"][0], np.float32).transpose(1, 0, 2).reshape(256, 512)
    wuv = np.asarray(inp["w_uv"][0], np.float32).transpose(1, 0, 2).reshape(256, 512)
    return {
        "bt": bt, "b31": b31, "cm": cm,
        "wuk": np.ascontiguousarray(wuk.reshape(2, 128, 512).transpose(1, 0, 2)),
        "wuv": np.ascontiguousarray(wuv.reshape(2, 128, 512).transpose(1, 0, 2)),
        "gkv": colT(inp["kv_norm"][0], 2),
    }


def emit_merge(K, ph, d, g2row, b_g2row, x2_d, b_x2, sel=None):
    S = K.S
    wg, b_wg = K.sb(ph, [128, 8, 2560], BF16, "wg")
    wa, b_wa = K.sb(ph, [64, 8, D], BF16, "wa")
    wm, b_wm = K.sb(ph, [128, 4, D], BF16, "wm")
    wo, b_wo = K.sb(ph, [128, 8, D], BF16, "wo")
    wms, b_wms = K.sb(ph, [128, 4, D], F32, "wms")
    hn, b_hn = K.sb(ph, [128, 4], F32, "hn")
    S.dma("pool", lambda e: e.dma_start(out=wg[:], in_=d["wg"].rearrange("(kc p) n -> p kc n", p=128)), writes=[b_wg])
    S.dma("pool", lambda e: e.dma_start(out=wa[:], in_=d["wa"]), writes=[b_wa])
    S.dma("pool", lambda e: e.dma_start(out=wo[:], in_=d["wout"].rearrange("(kc p) n -> p kc n", p=128)), writes=[b_wo])
    S.dma("sp", lambda e: e.dma_start(out=wms[:], in_=d["wm"]), writes=[b_wms])
    S.dma("sp", lambda e: e.dma_start(out=hn[:], in_=d["hnT"]), writes=[b_hn])
    for hd in range(4):
        S.op("dve", lambda e, hd=hd: e.tensor_scalar(out=wm[:, hd, :], in0=wms[:, hd, :], scalar1=hn[:, hd:hd + 1], scalar2=None, op0=ALU.mult),
             reads=[b_wms, b_hn], writes=[b_wm])
    h2s = [K.sb(ph, [128, 2, 8, 128], BF16, "h2g") for _ in range(2)]
    yas = [K.sb(ph, [64, 8, 256], BF16, "yag") for _ in range(2)]
    hms = [K.sb(ph, [128, 4, 256], BF16, "hmg") for _ in range(2)]
    x1s = [K.sb(ph, [128, D], F32, "x1t") for _ in range(4)]
    sig, b_sig = K.sb(ph, [128, 20, 256], BF16, "sig")
    hmo, b_hmo = K.sb(ph, [128, 4, 256], BF16, "hmo")
    t1s = [K.sb(ph, [128, 256], F32, "mt1") for _ in range(2)]
    t2s = [K.sb(ph, [128, 256], F32, "mt2") for _ in range(2)]
    mgs = [K.sb(ph, [128, 8, 256], BF16, "mg") for _ in range(2)]
    tos = [K.sb(ph, [128, D], F32, "mto") for _ in range(2)]
    pG = [K.ps(ph, [128, 512], F32, "pGm") for _ in range(2)]
    pZ = [K.ps(ph, [128, 512], F32, "pZ") for _ in range(2)]
    pO = [K.ps(ph, [128, D], F32, "pOm") for _ in range(2)]
    NG = NT // 2
    for g in range(NG):
        h2, b_h2 = h2s[g % 2]
        ya, b_ya = yas[g % 2]
        hm, b_hm = hms[g % 2]
        mg, b_mg = mgs[g % 2]
        tk = slice(g * 256, (g + 1) * 256)
        for tt in range(2):
            S.dma("sp", lambda e, h2=h2, tt=tt, g=g: e.dma_start(out=h2[:, tt, :, :], in_=d["h2T"][2 * g + tt]), writes=[b_h2])
        S.dma("sp", lambda e, ya=ya, tk=tk: e.dma_start(out=ya[:], in_=d["yaT"][:, :, tk]), writes=[b_ya])
        if sel is None:
            S.dma("sp", lambda e, hm=hm, tk=tk: e.dma_start(out=hm[:], in_=d["hmT"][:, :, tk]), writes=[b_hm])
        else:
            oh, b_oh, hm_s, cands = sel
            for tt in range(2):
                i = 2 * g + tt
                cd, b_cd = cands[tt]
                S.dma("sp", lambda e, cd=cd, i=i: e.dma_start(out=cd[:], in_=hm_s.rearrange("h e t -> e h t")[:, :, i * 512:(i + 1) * 512]),
                      writes=[b_cd])
                ts_ = slice(tt * 128, (tt + 1) * 128)
                S.op("dve", lambda e, cd=cd, hm=hm, ts_=ts_: e.tensor_scalar(out=hm[:, :, ts_], in0=cd[:, :, 0:128], scalar1=oh[:, 0:1], scalar2=None,
                                                                            op0=ALU.mult), reads=[b_cd, b_oh], writes=[b_hm])
                for jj in range(1, 4):
                    S.op("dve", lambda e, cd=cd, hm=hm, ts_=ts_, jj=jj: e.scalar_tensor_tensor(
                        out=hm[:, :, ts_], in0=cd[:, :, jj * 128:(jj + 1) * 128], scalar=oh[:, jj:jj + 1], in1=hm[:, :, ts_],
                        op0=ALU.mult, op1=ALU.add), reads=[b_cd, b_oh, b_hm], writes=[b_hm])
        for c in range(20):
            pg, b_pg = pG[c % 2]
            for kc in range(8):
                S.op("pe", lambda e, pg=pg, c=c, kc=kc, h2=h2: e.matmul(
                    pg[:, 0:256].rearrange("p (a b) -> p a b", a=2), lhsT=wg[:, kc, c * 128:(c + 1) * 128], rhs=h2[:, :, kc, :],
                    start=(kc == 0), stop=(kc == 7)), reads=[b_wg, b_h2], writes=[b_pg])
            S.op("act", lambda e, pg=pg, c=c: e.activation(out=sig[:, c, :], in_=pg[:, 0:256], func=AF.Sigmoid), reads=[b_pg], writes=[b_sig])
        S.op("dve", lambda e, hm=hm: e.tensor_tensor(out=hmo[:], in0=hm[:], in1=sig[:, 0:4, :], op=ALU.mult), reads=[b_hm, b_sig], writes=[b_hmo])
        for n in range(8):
            pz, b_pz = pZ[n % 2]
            t1, b_t1 = t1s[n % 2]
            t2, b_t2 = t2s[n % 2]
            ns = slice(n * 128, (n + 1) * 128)
            for h in range(8):
                S.op("pe", lambda e, pz=pz, h=h, ns=ns, ya=ya: e.matmul(pz[:, 0:256], lhsT=wa[:, h, ns], rhs=ya[:, h, :], start=(h == 0), stop=(h == 7)),
                     reads=[b_wa, b_ya], writes=[b_pz])
            for hd in range(4):
                S.op("pe", lambda e, pz=pz, hd=hd, ns=ns: e.matmul(pz[:, 256:512], lhsT=wm[:, hd, ns], rhs=hmo[:, hd, :], start=(hd == 0), stop=(hd == 3)),
                     reads=[b_wm, b_hmo], writes=[b_pz])
            S.op("dve", lambda e, pz=pz, t1=t1, n=n: e.tensor_tensor(out=t1[:], in0=pz[:, 0:256], in1=sig[:, 4 + n, :], op=ALU.mult),
                 reads=[b_pz, b_sig], writes=[b_t1])
            S.op("dve", lambda e, pz=pz, t2=t2, n=n: e.tensor_tensor(out=t2[:], in0=pz[:, 256:512], in1=sig[:, 12 + n, :], op=ALU.mult),
                 reads=[b_pz, b_sig], writes=[b_t2])
            S.op("dve", lambda e, t1=t1, t2=t2, mg=mg, n=n: e.tensor_tensor(out=mg[:, n, :], in0=t1[:], in1=t2[:], op=ALU.add),
                 reads=[b_t1, b_t2], writes=[b_mg])
        for tt in range(2):
            t = 2 * g + tt
            po, b_po = pO[tt]
            x1t, b_x1t = x1s[t % 4]
            to, b_to = tos[tt]
            S.dma("sp", lambda e, x1t=x1t, t=t: e.dma_start(out=x1t[:], in_=d["x1"][t * 128:(t + 1) * 128, :]), writes=[b_x1t])
            for dh in range(2):
                for kc in range(8):
                    S.op("pe", lambda e, po=po, dh=dh, kc=kc, mg=mg, tt=tt: e.matmul(
                        po[:, dh * 512:(dh + 1) * 512], lhsT=mg[:, kc, tt * 128:(tt + 1) * 128], rhs=wo[:, kc, dh * 512:(dh + 1) * 512],
                        start=(kc == 0), stop=(kc == 7)), reads=[b_mg, b_wo], writes=[b_po])
            S.op("dve", lambda e, po=po, to=to: e.tensor_tensor(out=to[:], in0=po[:], in1=g2row[:], op=ALU.mult), reads=[b_po, b_g2row], writes=[b_to])
            S.op("dve", lambda e, to=to, x1t=x1t: e.tensor_tensor(out=to[:], in0=to[:], in1=x1t[:], op=ALU.add), reads=[b_to, b_x1t], writes=[b_to])
            S.dma("sp", lambda e, to=to, t=t: e.dma_start(out=x2_d[t * 128:(t + 1) * 128, :], in_=to[:]), reads=[b_to], writes=[b_x2[t]])


def build_p3():
    nc = bass.Bass("TRN2", target_bir_lowering=False)

    def din(name, shape, dt=F32):
        return nc.dram_tensor(name, list(shape), dt, kind="ExternalInput").ap()

    d = dict(x1=din("x1", [TOK, D]), h2T=din("h2T", [NT, 128, 8, 128], BF16), yaT=din("yaT", [64, 8, TOK], BF16),
             hmT=din("hmT", [128, 4, TOK], BF16), wg=din("wg", [D, 2560]), wa=din("wa", [64, 8, D]), wm=din("wm", [128, 4, D]),
             hnT=din("hnT", [128, 4]), wout=din("wout", [D, D]))
    cT = din("cT", [128, 8])
    ada_w = din("ada_w", [D, 9 * D])
    ada_b = din("ada_b", [9 * D])
    ada_bT = din("ada_bT", [128, 72])
    n3T = din("n3T", [128, 8])
    fn = din("fn", [D])
    w1 = din("w1", [D, DFF])
    w3 = din("w3", [D, DFF])
    w2 = din("w2", [DFF, D])
    ident = din("ident", [128, 128])
    out = nc.dram_tensor("out", [TOK, D], F32, kind="ExternalOutput").ap()
    x2_d = nc.dram_tensor("x2s", [TOK, D], F32, kind="Internal").ap()
    x3_d = nc.dram_tensor("x3s", [TOK, D], F32, kind="Internal").ap()
    with ExitStack() as es:
        K = Ctx(nc, es)
        S = K.S
        idf, b_idf = K.sb(es, [128, 128], F32, "idf")
        idb, b_idb = K.sb(es, [128, 128], BF16, "idb")
        cT_sb, b_cT = K.sb(es, [128, 8], F32, "cT")
        abT_sb, b_abT = K.sb(es, [128, 72], F32, "abT")
        n3_sb, b_n3 = K.sb(es, [128, 8], F32, "n3")
        modT, b_modT = K.sb(es, [128, 72], F32, "modT")
        g3row, b_g3row = K.sb(es, [128, D], F32, "g3row")
        A3, b_A3 = K.sb(es, [128, 8], F32, "A3")
        ssqF, b_ssqF = K.sb(es, [128, NT], F32, "ssqF")
        rstdF, b_rstdF = K.sb(es, [128, NT], F32, "rstdF")
        junkF, b_junkF = K.sb(es, [128, D], BF16, "junkF")
        st_g2 = ExitStack()
        g2row, b_g2row = K.sb(st_g2, [128, D], F32, "g2row")
        S.dma("sp", lambda e: e.dma_start(out=idf[:], in_=ident[:, :]), writes=[b_idf])
        S.dma("sp", lambda e: e.dma_start(out=cT_sb[:], in_=cT[:, :]), writes=[b_cT])
        S.dma("sp", lambda e: e.dma_start(out=abT_sb[:], in_=ada_bT[:, :]), writes=[b_abT])
        S.dma("sp", lambda e: e.dma_start(out=n3_sb[:], in_=n3T[:, :]), writes=[b_n3])
        S.op("dve", lambda e: e.tensor_copy(out=idb[:], in_=idf[:]), reads=[b_idf], writes=[b_idb])
        with ExitStack() as ph:
            S.barrier()
            emit_mod(K, ph, ada_w, abT_sb, cT_sb, modT, b_modT, cols=[6, 7],
                     rows={5: (g2row, b_g2row, 1.0), 8: (g3row, b_g3row, 0.5)}, ada_b_dram=ada_b, ident_f=idf, b_ident=b_idf)
            S.op("dve", lambda e: e.scalar_tensor_tensor(out=A3[:], in0=modT[:, 56:64], scalar=1.0, in1=n3_sb[:],
                                                         op0=ALU.add, op1=ALU.mult), reads=[b_modT, b_n3], writes=[b_A3])
            S.barrier()
            S.flush()
        b_x2 = [Buf("x2_%d" % t) for t in range(NT)]
        b_x3 = [Buf("x3_%d" % t) for t in range(NT)]
        with ExitStack() as ph:
            emit_merge(K, ph, d, g2row, b_g2row, x2_d, b_x2)
            S.barrier()
            S.flush()
        st_g2.close()
        with ExitStack() as ph:
            b_AB = Buf("AB3")

            def epilogue(t, xo, b_xo):
                S.dma("sp", lambda e: e.dma_start(out=x3_d[t * 128:(t + 1) * 128, :], in_=xo[:]), reads=[b_xo], writes=[b_x3[t]])
                S.op("act", lambda e: e.activation(out=junkF[:], in_=xo[:], func=AF.Square, accum_out=ssqF[:, t:t + 1]),
                     reads=[b_xo], writes=[b_junkF, b_ssqF])

            emit_ffn(K, ph, lambda t: x2_d[t * 128:(t + 1) * 128, :], w1, w3, w2, A3[:, :], modT[:, 48:56], b_AB,
                     g3row, b_g3row, idb, b_idb, epilogue, src_bufs=b_x2)
            S.barrier()
            S.flush()
        with ExitStack() as ph:
            fnrow, b_fnrow = K.sb(ph, [128, D], F32, "fnrow")
            fn1, b_fn1 = K.sb(ph, [1, D], F32, "fn1")
            on1, b_on1 = K.sb(ph, [1, 128], F32, "on1")
            pF, b_pF = K.ps(ph, [128, D], F32, "pFn")
            S.op("dve", lambda e: e.memset(on1[:], 1.0), writes=[b_on1])
            S.dma("sp", lambda e: e.dma_start(out=fn1[:], in_=fn.rearrange("(a n) -> a n", a=1)), writes=[b_fn1])
            for nt in range(2):
                S.op("pe", lambda e, nt=nt: e.matmul(pF[:, nt * 512:(nt + 1) * 512], lhsT=on1[0:1, :], rhs=fn1[0:1, nt * 512:(nt + 1) * 512],
                                                    start=True, stop=True), reads=[b_on1, b_fn1], writes=[b_pF])
            S.op("dve", lambda e: e.tensor_copy(out=fnrow[:], in_=pF[:]), reads=[b_pF], writes=[b_fnrow])
            emit_rstd(K, ssqF, b_ssqF, rstdF, b_rstdF, D)
            xf = [K.sb(ph, [128, D], F32, "xf") for _ in range(3)]
            for t in range(NT):
                xt, b_xt = xf[t % 3]
                S.dma("sp", lambda e, xt=xt, t=t: e.dma_start(out=xt[:], in_=x3_d[t * 128:(t + 1) * 128, :]), reads=[b_x3[t]], writes=[b_xt])
                S.op("dve", lambda e, xt=xt, t=t: e.scalar_tensor_tensor(out=xt[:], in0=xt[:], scalar=rstdF[:, t:t + 1], in1=fnrow[:],
                                                                          op0=ALU.mult, op1=ALU.mult), reads=[b_xt, b_rstdF, b_fnrow], writes=[b_xt])
                S.dma("sp", lambda e, xt=xt, t=t: e.dma_start(out=out[t * 128:(t + 1) * 128, :], in_=xt[:]), reads=[b_xt])
            S.barrier()
            S.finish()
            S.flush()
    return nc


def build_p2():
    nc = bass.Bass("TRN2", target_bir_lowering=False)

    def din(name, shape, dt=F32):
        return nc.dram_tensor(name, list(shape), dt, kind="ExternalInput").ap()

    def dout(name, shape, dt=F32):
        return nc.dram_tensor(name, list(shape), dt, kind="ExternalOutput").ap()

    dm = dict(qpad=din("qpad", [128, S_LEN + 3]), kpad=din("kpad", [128, S_LEN + 3]), vaug=din("vaug", [64, NCH, 129], BF16),
              ig=din("ig", [64, NCH]), fg=din("fg", [64, NCH]), gb=din("gb", [64, 2]), cw=din("cw", [128, 8]), cb=din("cb", [128, 2]),
              triu=din("triu", [64, 64]), hmT=dout("hmT", [128, S_LEN], BF16))
    da = dict(ckvT=din("ckvT", [128, 2, S_LEN], BF16), rstd=din("rstd", [128, 64]), kidx2=din("kidx2", [128, S_LEN], BF16),
              qaT=din("qaT", [128, 4, TOK], BF16), qidxT=din("qidxT", [128, 4, TOK], BF16), widx=din("widx", [128, NT, 8]),
              wuk=din("wuk", [128, 2, 512]), wuv=din("wuv", [128, 2, 512]), gkv=din("gkv", [128, 2]),
              bt=din("bt", [128, 8, 640]), b31=din("b31", [128, 8]), cm=din("cm", [128, 512]),
              yaT=dout("yaT", [64, 8, TOK], BF16))
    da["vscr"] = nc.dram_tensor("vscr", [64, 128, 520], BF16, kind="Internal").ap()
    ident = din("ident", [128, 128])
    with ExitStack() as es:
        K = Ctx(nc, es)
        S = K.S
        idf, b_idf = K.sb(es, [128, 128], F32, "idf")
        idb, b_idb = K.sb(es, [128, 128], BF16, "idb")
        S.dma("sp", lambda e: e.dma_start(out=idf[:], in_=ident[:, :]), writes=[b_idf])
        S.op("dve", lambda e: e.tensor_copy(out=idb[:], in_=idf[:]), reads=[b_idf], writes=[b_idb])
        with ExitStack() as ph:
            emit_mlstm(K, ph, dm, idb, b_idb)
            S.barrier()
            S.flush()
        with ExitStack() as ph:
            emit_attn(K, ph, da, idf, b_idf, idb, b_idb)
            S.barrier()
            S.finish()
            S.flush()
    return nc


def gather_tokens(parts, axis):
    outs = []
    for jj in range(4):
        a = np.moveaxis(np.asarray(parts[jj]), axis, -1)
        outs.append(a.reshape(a.shape[:-1] + (NT, 1, 128)))
    g = np.concatenate(outs, axis=-2)
    g = g.reshape(g.shape[:-3] + (S_LEN,))
    return np.moveaxis(g, -1, axis)


def p2_inputs(inp, r1, core):
    b, j = divmod(core, 4)
    grp = [r1[b * 4 + jj] for jj in range(4)]
    hd = j
    qk = gather_tokens([g["qkmT"] for g in grp], 2)
    vm = gather_tokens([np.asarray(g["vm"]).transpose(1, 0, 2).reshape(TOK, 512) for g in grp], 0)
    sm = gather_tokens([np.asarray(g["small"]).transpose(1, 0, 2).reshape(TOK, 16) for g in grp], 0)
    m = mlstm_inputs_from(qk[:, hd, :], qk[:, 4 + hd, :], vm[:, hd * 128:(hd + 1) * 128], sm[:, 8 + hd], sm[:, 12 + hd], inp, hd)
    rs = gather_tokens([np.asarray(g["rstdkv"]).T.reshape(TOK) for g in grp], 0)
    kid = gather_tokens([g["kidxT"] for g in grp], 1)
    own = r1[core]
    a = {
        "ckvT": np.ascontiguousarray(gather_tokens([g["ckvT"] for g in grp], 2)),
        "rstd": np.ascontiguousarray(rs.reshape(64, 128).T, np.float32),
        "kidx2": np.ascontiguousarray(np.concatenate([kid, kid], axis=0)),
        "qaT": np.ascontiguousarray(own["qaT"]),
        "qidxT": np.ascontiguousarray(own["qidxT"]),
        "widx": np.ascontiguousarray(np.asarray(own["small"])[:, :, 0:8], np.float32),
        "ident": np.eye(128, dtype=np.float32),
    }
    a.update(attn_consts(inp, j))
    a.update(m)
    return a


def p3_inputs(inp, r1, r2, core):
    b, j = divmod(core, 4)
    hm = []
    for hd in range(4):
        h = np.asarray(r2[b * 4 + hd]["hmT"])
        hm.append(h.reshape(128, NT, 4, 128)[:, :, j, :].reshape(128, TOK))
    w_in = np.asarray(inp["w_in"][0], np.float32)
    wa = np.asarray(inp["w_branch_attn"][0], np.float32).reshape(8, 64, D).transpose(1, 0, 2)
    wm = np.asarray(inp["w_branch_mlstm"][0], np.float32).reshape(4, 128, D).transpose(1, 0, 2)
    return {
        "x1": np.ascontiguousarray(r1[core]["x1"]),
        "h2T": np.ascontiguousarray(r1[core]["h2T"]),
        "yaT": np.ascontiguousarray(r2[core]["yaT"]),
        "hmT": np.ascontiguousarray(np.stack(hm, axis=1)),
        "wg": np.ascontiguousarray(w_in[:, C_O:DIN]),
        "wa": np.ascontiguousarray(wa),
        "wm": np.ascontiguousarray(wm),
        "hnT": colT(inp["mlstm_head_norm"][0], 4),
        "wout": np.ascontiguousarray(inp["w_out"][0], np.float32),
        "cT": colT(inp["c"][b], 8),
        "ada_w": np.ascontiguousarray(inp["ada_w"][0], np.float32),
        "ada_b": np.ascontiguousarray(inp["ada_b"][0], np.float32),
        "ada_bT": colT(inp["ada_b"][0], 72),
        "n3T": colT(inp["ffn2_norm"][0], 8),
        "fn": np.ascontiguousarray(inp["final_norm"], np.float32),
        "w1": np.ascontiguousarray(inp["ffn2_w1"][0], np.float32),
        "w3": np.ascontiguousarray(inp["ffn2_w3"][0], np.float32),
        "w2": np.ascontiguousarray(inp["ffn2_w2"][0], np.float32),
        "ident": np.eye(128, dtype=np.float32),
    }


_NC_CACHE = {}


def _prog(name, builder):
    if name not in _NC_CACHE:
        _NC_CACHE[name] = builder()
    return _NC_CACHE[name]


def kernel(**inputs):
    inp = {k: np.asarray(v) for k, v in inputs.items()}
    cores = list(range(NCORES))
    r1 = run_bass_kernel_spmd(_prog("p1", build_p1), [p1_inputs(inp, c) for c in cores], core_ids=cores).results
    r2 = run_bass_kernel_spmd(_prog("p2", build_p2), [p2_inputs(inp, r1, c) for c in cores], core_ids=cores).results
    r3 = run_bass_kernel_spmd(_prog("p3", build_p3), [p3_inputs(inp, r1, r2, c) for c in cores], core_ids=cores).results
    out = np.zeros((2, S_LEN, D), np.float32)
    for c in cores:
        b, j = divmod(c, 4)
        out[b].reshape(NT, 4, 128, D)[:, j] = np.asarray(r3[c]["out"], np.float32).reshape(NT, 128, D)
    return out


NTA = S_LEN // 128
NCA = 1928
NCO = 1032


def emit_proj2(K, ph, ntiles, x1_of, b_x1, wpk, ncols, ssq2, b_ssq2, col0, A2, B2, idb, b_idb, spec):
    S = K.S
    wb, b_wb = K.sb(ph, [128, 8, ncols], BF16, "winb")
    S.dma("pool", lambda e: e.dma_start(out=wb[:], in_=wpk.rearrange("(kc p) n -> p kc n", p=128)), writes=[b_wb])
    rstd2, b_rstd2 = K.sb(ph, [128, ntiles], F32, "rstd2")
    S.op("dve", lambda e: e.tensor_scalar(out=rstd2[:], in0=ssq2[:, col0:col0 + ntiles], scalar1=1.0 / D, scalar2=EPS, op0=ALU.mult, op1=ALU.add),
         reads=[b_ssq2], writes=[b_rstd2])
    S.op("act", lambda e: e.activation(out=rstd2[:], in_=rstd2[:], func=AF.Sqrt), reads=[b_rstd2], writes=[b_rstd2])
    S.op("dve", lambda e: e.reciprocal(out=rstd2[:], in_=rstd2[:]), reads=[b_rstd2], writes=[b_rstd2])
    xts = [K.sb(ph, [128, D], F32, "xt") for _ in range(4)]
    xns = [K.sb(ph, [128, D], BF16, "xn") for _ in range(2)]
    tmpH, b_tmpH = K.sb(ph, [128, 8, 128], F32, "tmpH")
    hTs = [K.sb(ph, [128, 2, 8, 128], BF16, "hT") for _ in range(2)]
    nb, nf = spec["nb"], spec["nf"]
    fm16 = [K.sb(ph, [128, nb, 256], BF16, "fm16") for _ in range(2)]
    fm32 = [K.sb(ph, [128, max(nf, 1), 256], F32, "fm32") for _ in range(2)]
    has_sq = bool(spec.get("sq"))
    if has_sq:
        ones_f, b_ones = K.sb(ph, [128, 1], F32, "ones")
        S.op("dve", lambda e: e.memset(ones_f[:], 1.0), writes=[b_ones])
        sq = [K.sb(ph, [128, 2, 256], F32, "sq") for _ in range(2)]
        ssqkv, b_ssqkv = K.sb(ph, [128, ntiles], F32, "ssqkv")
    vms = [K.sb(ph, [128, 512], BF16, "vms") for _ in range(2)]
    pF = [K.ps(ph, [128, 512], F32, "pF") for _ in range(3)]
    pV = [K.ps(ph, [128, 512], F32, "pV") for _ in range(2)]
    pS, b_pS = K.ps(ph, [128, 512], F32, "pS")
    pT, b_pT = K.ps(ph, [128, 8, 128], BF16, "pT")
    b_AB = Buf("AB2")
    NG = ntiles // 2
    small, b_small = spec["small_sb"]
    cnt = [0]

    def prep(g):
        hT, b_hT = hTs[g % 2]
        for tt in range(2):
            t = 2 * g + tt
            xt, b_xt = xts[t % 4]
            xn, b_xn = xns[tt]
            if spec.get("xsel"):
                spec["xsel"](t, xt, b_xt)
            else:
                S.dma("sp", lambda e, xt=xt, t=t: e.dma_start(out=xt[:], in_=x1_of(t)), reads=[b_x1[t]], writes=[b_xt])
            emit_norm_T(K, xt, b_xt, rstd2[:, t:t + 1], b_rstd2, xn, b_xn, pT, b_pT, idb, b_idb, tmpH, b_tmpH,
                        A2, B2, b_AB, hT[:, tt, :, :], b_hT)
            if spec.get("out_h2T"):
                spec["out_h2T"](t, hT, tt, b_hT)

    def body(g):
        hT, b_hT = hTs[g % 2]
        f16, b_f16 = fm16[g % 2]
        f32, b_f32 = fm32[g % 2]
        if has_sq:
            sqt, b_sq = sq[g % 2]
        for (c0, kind, di) in spec["fm"]:
            pf, b_pf = pF[cnt[0] % 3]
            cnt[0] += 1
            for kc in range(8):
                S.op("pe", lambda e, pf=pf, c0=c0, kc=kc, hT=hT: e.matmul(
                    pf[:, 0:256].rearrange("p (a b) -> p a b", a=2), lhsT=wb[:, kc, c0:c0 + 128], rhs=hT[:, :, kc, :],
                    start=(kc == 0), stop=(kc == 7)), reads=[b_wb, b_hT], writes=[b_pf])
            if kind == "b":
                rd = [b_pf]
                if has_sq and di in spec["sq"]:
                    S.op("act", lambda e, pf=pf, di=di, sqt=sqt: e.activation(out=sqt[:, spec["sq"].index(di), :], in_=pf[:, 0:256], func=AF.Square),
                         reads=[b_pf], writes=[b_sq])
                    rd = [b_pf, b_sq]
                S.op("dve", lambda e, pf=pf, di=di, f16=f16: e.tensor_copy(out=f16[:, di, :], in_=pf[:, 0:256]), reads=rd, writes=[b_f16])
            else:
                S.op("act", lambda e, pf=pf, di=di, f32=f32: e.activation(out=f32[:, di, :], in_=pf[:, 0:256], func=AF.Identity),
                     reads=[b_pf], writes=[b_f32])
        if g + 1 < NG:
            prep(g + 1)
        for tt in range(2):
            t = 2 * g + tt
            if spec.get("vm_col") is not None:
                pv, b_pv = pV[tt]
                vc = spec["vm_col"]
                for kc in range(8):
                    S.op("pe", lambda e, pv=pv, kc=kc, hT=hT, tt=tt, vc=vc: e.matmul(
                        pv[:, :], lhsT=hT[:, tt, kc, :], rhs=wb[:, kc, vc:vc + 512], start=(kc == 0), stop=(kc == 7)),
                        reads=[b_wb, b_hT], writes=[b_pv])
                vmt, b_vmt = vms[tt]
                S.op("act", lambda e, pv=pv, vmt=vmt: e.activation(out=vmt[:], in_=pv[:], func=AF.Identity), reads=[b_pv], writes=[b_vmt])
                spec["out_vm"](t, vmt, b_vmt)
            sc0, sn = spec["small"]
            for kc in range(8):
                S.op("pe", lambda e, kc=kc, hT=hT, tt=tt, sc0=sc0, sn=sn: e.matmul(
                    pS[:, 0:sn], lhsT=hT[:, tt, kc, :], rhs=wb[:, kc, sc0:sc0 + sn], start=(kc == 0), stop=(kc == 7)),
                    reads=[b_wb, b_hT], writes=[b_pS])
            if has_sq:
                for c in range(2):
                    S.op("pe", lambda e, c=c, tt=tt, sqt=sqt: e.matmul(
                        pS[:, 16:17], lhsT=sqt[:, c, tt * 128:(tt + 1) * 128], rhs=ones_f[:, 0:1], start=(c == 0), stop=(c == 1)),
                        reads=[b_sq, b_ones], writes=[b_pS])
            S.op("dve", lambda e, t=t, sn=sn: e.tensor_copy(out=small[:, t, 0:sn], in_=pS[:, 0:sn]), reads=[b_pS], writes=[b_small])
            if has_sq:
                S.op("dve", lambda e, t=t: e.tensor_copy(out=ssqkv[:, t:t + 1], in_=pS[:, 16:17]), reads=[b_pS], writes=[b_ssqkv])
        spec["out_fm"](g, f16, b_f16, f32, b_f32)

    prep(0)
    for g in range(NG):
        body(g)
    if has_sq:
        rk, b_rk = spec["rstdkv"]
        emit_rstd(K, ssqkv, b_ssqkv, rk, b_rk, 256)


def build_fused():
    nc = bass.Bass("TRN2", target_bir_lowering=False)

    def din(name, shape, dt=F32):
        return nc.dram_tensor(name, list(shape), dt, kind="ExternalInput").ap()

    def scr(name, shape, dt=F32):
        return nc.dram_tensor(name, list(shape), dt, kind="Internal").ap()

    x_all = din("x_all", [S_LEN, D])
    cT = din("cT", [128, 8])
    ada_w = din("ada_w", [D, 9 * D])
    ada_b = din("ada_b", [9 * D])
    ada_bT = din("ada_bT", [128, 72])
    n1T, n2T, n3T = din("n1T", [128, 8]), din("n2T", [128, 8]), din("n3T", [128, 8])
    fn = din("fn", [D])
    w1, w3, w2 = din("w1", [D, DFF]), din("w3", [D, DFF]), din("w2", [DFF, D])
    f2w1, f2w3, f2w2 = din("f2w1", [D, DFF]), din("f2w3", [D, DFF]), din("f2w2", [DFF, D])
    wA, wO = din("wA", [D, NCA]), din("wO", [D, NCO])
    ident = din("ident", [128, 128])
    oh_d = din("oh", [128, 4])
    dm_c = dict(gb4=din("gb4", [64, 8]), cw4=din("cw4", [128, 4, 8]), cb4=din("cb4", [128, 4, 2]), triu=din("triu", [64, 64]))
    da = dict(wuk=din("wuk", [128, 2, 512]), wuv=din("wuv", [128, 2, 512]), gkv=din("gkv", [128, 2]),
              bt=din("bt", [128, 8, 640]), b31=din("b31", [128, 8]), cm=din("cm", [128, 512]))
    d3 = dict(wg=din("wg", [D, 2560]), wa=din("wa", [64, 8, D]), wm=din("wm", [128, 4, D]), hnT=din("hnT", [128, 4]), wout=din("wout", [D, D]))
    out = nc.dram_tensor("out", [TOK, D], F32, kind="ExternalOutput").ap()

    NTF = NTA + NT
    x1s = scr("x1s", [NTF * 128, D])
    ckvT_s = scr("ckvT_s", [128, 2, S_LEN], BF16)
    kidx2_s = scr("kidx2_s", [128, S_LEN], BF16)
    qkm_s = scr("qkm_s", [128, 8, S_LEN + 3], BF16)
    vm_s = scr("vm_s", [NTA, 128, 512], BF16)
    hm_s = scr("hm_s", [4, 128, S_LEN], BF16)
    qaT_s = scr("qaT_s", [128, 4, TOK], BF16)
    qidxT_s = scr("qidxT_s", [128, 4, TOK], BF16)
    h2T_s = scr("h2T_s", [NT, 128, 8, 128], BF16)
    ya_s = scr("ya_s", [64, 8, TOK], BF16)
    x2_d = scr("x2s", [TOK, D])
    x3_d = scr("x3s", [TOK, D])
    da["vscr"] = scr("vscr", [64, 128, 520], BF16)

    with ExitStack() as es:
        K = Ctx(nc, es)
        S = K.S
        idf, b_idf = K.sb(es, [128, 128], F32, "idf")
        idb, b_idb = K.sb(es, [128, 128], BF16, "idb")
        cT_sb, b_cT = K.sb(es, [128, 8], F32, "cT")
        abT_sb, b_abT = K.sb(es, [128, 72], F32, "abT")
        nsb = [K.sb(es, [128, 8], F32, "nrm") for _ in range(3)]
        modT, b_modT = K.sb(es, [128, 72], F32, "modT")
        As = [K.sb(es, [128, 8], F32, "Amod") for _ in range(3)]
        ssq2, b_ssq2 = K.sb(es, [128, NTF], F32, "ssq2")
        oh, b_oh = K.sb(es, [128, 4], F32, "oh")
        ssqF, b_ssqF = K.sb(es, [128, NT], F32, "ssqF")
        rstdF, b_rstdF = K.sb(es, [128, NT], F32, "rstdF")
        for (t_, b_, src) in ((idf, b_idf, ident), (cT_sb, b_cT, cT), (abT_sb, b_abT, ada_bT), (nsb[0][0], nsb[0][1], n1T),
                              (nsb[1][0], nsb[1][1], n2T), (nsb[2][0], nsb[2][1], n3T), (oh, b_oh, oh_d)):
            S.dma("sp", lambda e, t_=t_, src=src: e.dma_start(out=t_[:], in_=src), writes=[b_])
        S.op("dve", lambda e: e.tensor_copy(out=idb[:], in_=idf[:]), reads=[b_idf], writes=[b_idb])

        st_g1 = ExitStack()
        g1row, b_g1row = K.sb(st_g1, [128, D], F32, "g1row")
        S.barrier()
        wts1 = alloc_ffn_weights(K, st_g1, w1, w3, w2)
        with ExitStack() as ph:
            emit_mod(K, ph, ada_w, abT_sb, cT_sb, modT, b_modT, cols=[0, 1, 3, 4, 6, 7],
                     rows={2: (g1row, b_g1row, 0.5)}, ada_b_dram=ada_b, ident_f=idf, b_ident=b_idf)
            for q, (sc0, _) in enumerate(((8, 0), (32, 0), (56, 0))):
                S.op("dve", lambda e, q=q, sc0=sc0: e.scalar_tensor_tensor(out=As[q][0][:], in0=modT[:, sc0:sc0 + 8], scalar=1.0, in1=nsb[q][0][:],
                                                                            op0=ALU.add, op1=ALU.mult), reads=[b_modT, nsb[q][1]], writes=[As[q][1]])
            S.flush()
        b_x1 = [Buf("x1_%d" % t) for t in range(NTF)]
        with ExitStack() as ph:
            junk2, b_junk2 = K.sb(ph, [128, D], BF16, "junk2")

            def epi1(t, xo, b_xo):
                S.dma("sp", lambda e: e.dma_start(out=x1s[t * 128:(t + 1) * 128, :], in_=xo[:]), reads=[b_xo], writes=[b_x1[t]])
                S.op("act", lambda e: e.activation(out=junk2[:], in_=xo[:], func=AF.Square, accum_out=ssq2[:, t:t + 1]),
                     reads=[b_xo], writes=[b_junk2, b_ssq2])

            emit_ffn(K, ph, lambda t: x_all[t * 128:(t + 1) * 128, :], w1, w3, w2, As[0][0][:, :], modT[:, 0:8], Buf("AB1"),
                     g1row, b_g1row, idb, b_idb, epi1, ntiles=NTA, wts=wts1)
            S.barrier()
            S.flush()
        st_g1.close()

        IFt, b_IFt = K.sb(es, [128, 8, NTA], F32, "IFt")
        IFall, b_IFall = IFt[:, :, :].rearrange("p c t -> p t c"), b_IFt
        widx_sb, b_widx_sb = K.sb(es, [128, NT, 8], F32, "widx_sb")
        rstdkv, b_rstdkv = K.sb(es, [128, NTA], F32, "rstdkv")
        zpad, b_zpad = K.sb(es, [128, 8, 3], BF16, "zpad")
        S.op("dve", lambda e: e.memset(zpad[:], 0.0), writes=[b_zpad])
        S.dma("sp", lambda e: e.dma_start(out=qkm_s[:, :, 0:3], in_=zpad[:]), reads=[b_zpad])

        with ExitStack() as ph:
            spec = dict(fm=[(0, "b", 0), (128, "b", 1), (256, "b", 2)] + [(384 + 128 * i, "b", 3 + i) for i in range(8)],
                        nb=11, nf=0, sq=[0, 1], vm_col=1408, small=(1920, 8), small_sb=(IFall, b_IFall), rstdkv=(rstdkv, b_rstdkv))

            def out_fm(g, f16, b_f16, f32, b_f32):
                tk = slice(g * 256, (g + 1) * 256)
                S.dma("sp", lambda e: e.dma_start(out=ckvT_s[:, :, tk], in_=f16[:, 0:2, :]), reads=[b_f16])
                S.dma("sp", lambda e: e.dma_start(out=kidx2_s[:, tk], in_=f16[:, 2, :]), reads=[b_f16])
                S.dma("sp", lambda e: e.dma_start(out=qkm_s[:, :, 3 + g * 256:3 + (g + 1) * 256], in_=f16[:, 3:11, :]), reads=[b_f16])

            def out_vm(t, vmt, b_vmt):
                S.dma("sp", lambda e: e.dma_start(out=vm_s[t], in_=vmt[:]), reads=[b_vmt])

            spec["out_fm"], spec["out_vm"] = out_fm, out_vm
            emit_proj2(K, ph, NTA, lambda t: x1s[t * 128:(t + 1) * 128, :], b_x1[0:NTA], wA, NCA, ssq2, b_ssq2, 0,
                       As[1][0][:, :], modT[:, 24:32], idb, b_idb, spec)
            S.barrier()
            S.flush()
        with ExitStack() as ph:
            spec = dict(fm=[(128 * i, "b", i) for i in range(8)], nb=8, nf=0, vm_col=None, small=(1024, 8), small_sb=(widx_sb, b_widx_sb))

            def out_fm2(g, f16, b_f16, f32, b_f32):
                tk = slice(g * 256, (g + 1) * 256)
                S.dma("sp", lambda e: e.dma_start(out=qaT_s[:, :, tk], in_=f16[:, 0:4, :]), reads=[b_f16])
                S.dma("sp", lambda e: e.dma_start(out=qidxT_s[:, :, tk], in_=f16[:, 4:8, :]), reads=[b_f16])

            def out_h2T(t, hT, tt, b_hT):
                S.dma("sp", lambda e: e.dma_start(out=h2T_s[t], in_=hT[:, tt, :, :]), reads=[b_hT])

            cands1 = [K.sb(ph, [128, 4, D], F32, "x1cand") for _ in range(2)]
            for jj in range(4):
                v = ssq2[:, 0:NTA].rearrange("p (t j) -> p t j", j=4)[:, :, jj]
                if jj == 0:
                    S.op("dve", lambda e, v=v: e.tensor_scalar(out=ssq2[:, NTA:NTF], in0=v, scalar1=oh[:, 0:1], scalar2=None, op0=ALU.mult),
                         reads=[b_ssq2, b_oh], writes=[b_ssq2])
                else:
                    S.op("dve", lambda e, v=v, jj=jj: e.scalar_tensor_tensor(out=ssq2[:, NTA:NTF], in0=v, scalar=oh[:, jj:jj + 1], in1=ssq2[:, NTA:NTF],
                                                                              op0=ALU.mult, op1=ALU.add), reads=[b_ssq2, b_oh], writes=[b_ssq2])

            def xsel(t, xt, b_xt):
                cd, b_cd = cands1[t % 2]
                S.dma("sp", lambda e: e.dma_start(out=cd[:], in_=x1s[4 * t * 128:(4 * t + 4) * 128, :].rearrange("(j p) d -> p j d", p=128)),
                      reads=b_x1[4 * t:4 * t + 4], writes=[b_cd])
                S.op("dve", lambda e: e.tensor_scalar(out=xt[:], in0=cd[:, 0, :], scalar1=oh[:, 0:1], scalar2=None, op0=ALU.mult),
                     reads=[b_cd, b_oh], writes=[b_xt])
                for jj in range(1, 4):
                    S.op("dve", lambda e, jj=jj: e.scalar_tensor_tensor(out=xt[:], in0=cd[:, jj, :], scalar=oh[:, jj:jj + 1], in1=xt[:],
                                                                         op0=ALU.mult, op1=ALU.add), reads=[b_cd, b_oh, b_xt], writes=[b_xt])
                S.dma("sp", lambda e: e.dma_start(out=x1s[(NTA + t) * 128:(NTA + t + 1) * 128, :], in_=xt[:]), reads=[b_xt], writes=[b_x1[NTA + t]])

            spec["xsel"] = xsel
            spec["out_fm"], spec["out_h2T"] = out_fm2, out_h2T
            emit_proj2(K, ph, NT, lambda t: x1s[(NTA + t) * 128:(NTA + t + 1) * 128, :], b_x1[NTA:NTF], wO, NCO, ssq2, b_ssq2, NTA,
                       As[1][0][:, :], modT[:, 24:32], idb, b_idb, spec)
            S.barrier()
            S.flush()

        with ExitStack() as ph:
            emit_mlstm4(K, ph, dm_c, qkm_s, vm_s, hm_s, IFt, b_IFt, idb, b_idb)
            S.barrier()
            S.flush()

        with ExitStack() as ph:
            da.update(ckvT=ckvT_s, rstd=rstdkv[:, :], kidx2=kidx2_s, qaT=qaT_s, qidxT=qidxT_s, widx=widx_sb[:, :, 0:8], yaT=ya_s)
            emit_attn(K, ph, da, idf, b_idf, idb, b_idb)
            S.barrier()
            S.flush()

        g2row, b_g2row = K.sb(es, [128, D], F32, "g2row")
        g3row, b_g3row = K.sb(es, [128, D], F32, "g3row")
        with ExitStack() as ph:
            emit_mod(K, ph, ada_w, abT_sb, cT_sb, modT, b_modT, cols=[],
                     rows={5: (g2row, b_g2row, 1.0), 8: (g3row, b_g3row, 0.5)}, ada_b_dram=ada_b, ident_f=idf, b_ident=b_idf)
            S.barrier()
            S.flush()
        b_x2 = [Buf("x2_%d" % t) for t in range(NT)]
        b_x3 = [Buf("x3_%d" % t) for t in range(NT)]
        with ExitStack() as ph:
            d3.update(x1=x1s[NTA * 128:NTF * 128, :], h2T=h2T_s, yaT=ya_s)
            cands = [K.sb(ph, [128, 4, 512], BF16, "cand") for _ in range(2)]
            emit_merge(K, ph, d3, g2row, b_g2row, x2_d, b_x2, sel=(oh, b_oh, hm_s, cands))
            S.barrier()
            S.flush()
        with ExitStack() as ph:
            junk2, b_junk2 = K.sb(ph, [128, D], BF16, "junk2")

            def epi2(t, xo, b_xo):
                S.dma("sp", lambda e: e.dma_start(out=x3_d[t * 128:(t + 1) * 128, :], in_=xo[:]), reads=[b_xo], writes=[b_x3[t]])
                S.op("act", lambda e: e.activation(out=junk2[:], in_=xo[:], func=AF.Square, accum_out=ssqF[:, t:t + 1]),
                     reads=[b_xo], writes=[b_junk2, b_ssqF])

            emit_ffn(K, ph, lambda t: x2_d[t * 128:(t + 1) * 128, :], f2w1, f2w3, f2w2, As[2][0][:, :], modT[:, 48:56], Buf("AB3"),
                     g3row, b_g3row, idb, b_idb, epi2, src_bufs=b_x2)
            S.barrier()
            S.flush()
        with ExitStack() as ph:
            fnrow, b_fnrow = K.sb(ph, [128, D], F32, "fnrow")
            fn1, b_fn1 = K.sb(ph, [1, D], F32, "fn1")
            on1, b_on1 = K.sb(ph, [1, 128], F32, "on1")
            pF, b_pF = K.ps(ph, [128, D], F32, "pFn")
            S.op("dve", lambda e: e.memset(on1[:], 1.0), writes=[b_on1])
            S.dma("sp", lambda e: e.dma_start(out=fn1[:], in_=fn.rearrange("(a n) -> a n", a=1)), writes=[b_fn1])
            for nt in range(2):
                S.op("pe", lambda e, nt=nt: e.matmul(pF[:, nt * 512:(nt + 1) * 512], lhsT=on1[0:1, :], rhs=fn1[0:1, nt * 512:(nt + 1) * 512],
                                                    start=True, stop=True), reads=[b_on1, b_fn1], writes=[b_pF])
            S.op("dve", lambda e: e.tensor_copy(out=fnrow[:], in_=pF[:]), reads=[b_pF], writes=[b_fnrow])
            emit_rstd(K, ssqF, b_ssqF, rstdF, b_rstdF, D)
            xf = [K.sb(ph, [128, D], F32, "xf") for _ in range(3)]
            for t in range(NT):
                xt, b_xt = xf[t % 3]
                S.dma("sp", lambda e, xt=xt, t=t: e.dma_start(out=xt[:], in_=x3_d[t * 128:(t + 1) * 128, :]), reads=[b_x3[t]], writes=[b_xt])
                S.op("dve", lambda e, xt=xt, t=t: e.scalar_tensor_tensor(out=xt[:], in0=xt[:], scalar=rstdF[:, t:t + 1], in1=fnrow[:],
                                                                          op0=ALU.mult, op1=ALU.mult), reads=[b_xt, b_rstdF, b_fnrow], writes=[b_xt])
                S.dma("sp", lambda e, xt=xt, t=t: e.dma_start(out=out[t * 128:(t + 1) * 128, :], in_=xt[:]), reads=[b_xt])
            S.barrier()
            S.finish()
            S.flush()
        print("fused program: %d instructions" % S.ninst, {k: S.cnt[k] for k in S.cnt})
        nc._phase_marks = S.marks
    return nc


def fused_inputs(inp, core):
    b, j = divmod(core, 4)
    w_in = np.asarray(inp["w_in"][0], np.float32)
    colsA = np.concatenate([np.arange(C_CKV, C_CKV + 256), np.arange(C_KIDX, C_KIDX + 64), np.arange(C_KIDX, C_KIDX + 64),
                            np.arange(C_QM, C_VM + 512), np.arange(C_I, C_I + 8)])
    colsO = np.concatenate([np.arange(C_QA, C_QA + 512), np.arange(C_QIDX, C_QIDX + 512), np.arange(C_WIDX, C_WIDX + 8)])
    assert colsA.size == NCA and colsO.size == NCO
    gbv = np.asarray(inp["mlstm_gate_bias"][0], np.float32)
    cwv = np.asarray(inp["conv_w"][0], np.float32)
    cbv = np.asarray(inp["conv_b"][0], np.float32)
    gb4 = np.zeros((64, 8), np.float32)
    cw4 = np.zeros((128, 4, 8), np.float32)
    cb4 = np.zeros((128, 4, 2), np.float32)
    for hd in range(4):
        gb4[:, 2 * hd] = gbv[hd]
        gb4[:, 2 * hd + 1] = gbv[4 + hd]
        cw4[:, hd, 0:4] = cwv[:, hd * 128:(hd + 1) * 128].T
        cw4[:, hd, 4:8] = cwv[:, 512 + hd * 128:512 + (hd + 1) * 128].T
        cb4[:, hd, 0] = cbv[hd * 128:(hd + 1) * 128]
        cb4[:, hd, 1] = cbv[512 + hd * 128:512 + (hd + 1) * 128]
    ohv = np.zeros((128, 4), np.float32)
    ohv[:, j] = 1.0
    wa = np.asarray(inp["w_branch_attn"][0], np.float32).reshape(8, 64, D).transpose(1, 0, 2)
    wm = np.asarray(inp["w_branch_mlstm"][0], np.float32).reshape(4, 128, D).transpose(1, 0, 2)
    xb = np.ascontiguousarray(inp["x"][b], np.float32)
    r = {
        "x_all": xb,
        "cT": colT(inp["c"][b], 8),
        "ada_w": np.ascontiguousarray(inp["ada_w"][0], np.float32),
        "ada_b": np.ascontiguousarray(inp["ada_b"][0], np.float32),
        "ada_bT": colT(inp["ada_b"][0], 72),
        "n1T": colT(inp["ffn1_norm"][0], 8), "n2T": colT(inp["mix_norm"][0], 8), "n3T": colT(inp["ffn2_norm"][0], 8),
        "fn": np.ascontiguousarray(inp["final_norm"], np.float32),
        "w1": np.ascontiguousarray(inp["ffn1_w1"][0], np.float32), "w3": np.ascontiguousarray(inp["ffn1_w3"][0], np.float32),
        "w2": np.ascontiguousarray(inp["ffn1_w2"][0], np.float32),
        "f2w1": np.ascontiguousarray(inp["ffn2_w1"][0], np.float32), "f2w3": np.ascontiguousarray(inp["ffn2_w3"][0], np.float32),
        "f2w2": np.ascontiguousarray(inp["ffn2_w2"][0], np.float32),
        "wA": np.ascontiguousarray(w_in[:, colsA]), "wO": np.ascontiguousarray(w_in[:, colsO]),
        "wg": np.ascontiguousarray(w_in[:, C_O:DIN]),
        "ident": np.eye(128, dtype=np.float32), "oh": ohv,
        "gb4": gb4, "cw4": cw4, "cb4": cb4, "triu": np.triu(np.ones((64, 64), np.float32)),
        "wa": np.ascontiguousarray(wa), "wm": np.ascontiguousarray(wm), "hnT": colT(inp["mlstm_head_norm"][0], 4),
        "wout": np.ascontiguousarray(inp["w_out"][0], np.float32),
    }
    r.update(attn_consts(inp, j))
    return r


def kernel(**inputs):
    inp = {k: np.asarray(v) for k, v in inputs.items()}
    cores = list(range(NCORES))
    res = run_bass_kernel_spmd(_prog("fused", build_fused), [fused_inputs(inp, c) for c in cores], core_ids=cores).results
    out = np.zeros((2, S_LEN, D), np.float32)
    for c in cores:
        b, j = divmod(c, 4)
        out[b].reshape(NT, 4, 128, D)[:, j] = np.asarray(res[c]["out"], np.float32).reshape(NT, 128, D)
    return out


def emit_mlstm4(K, ph, dmc, qkm_s, vm_s, hm_s, IFt, b_IFt, ident_b, b_identb):
    S = K.S
    GS = 16
    NSEG = NCH // GS
    SEG = GS * 64
    col = lambda c: (c % 2) * 64 + c // 2
    lcol = lambda cl: (cl % 2) * 8 + cl // 2
    triu, b_triu = K.sb(ph, [64, 64], F32, "triu")
    ones64, b_ones64 = K.sb(ph, [64, 128], F32, "ones64")
    S.dma("sp", lambda e: e.dma_start(out=triu[:], in_=dmc["triu"]), writes=[b_triu])
    S.op("dve", lambda e: e.memset(ones64[:], 1.0), writes=[b_ones64])
    gb4, b_gb4 = K.sb(ph, [64, 8], F32, "gb4")
    cw4, b_cw4 = K.sb(ph, [128, 4, 8], F32, "cw4")
    cb4, b_cb4 = K.sb(ph, [128, 4, 2], F32, "cb4")
    for (t, b, src) in ((gb4, b_gb4, "gb4"), (cw4, b_cw4, "cw4"), (cb4, b_cb4, "cb4")):
        S.dma("sp", lambda e, t=t, src=src: e.dma_start(out=t[:], in_=dmc[src]), writes=[b])
    lnk, b_lnk = K.sb(ph, [64, 1], F32, "lnk")
    S.op("dve", lambda e: e.memset(lnk[:], float(np.log(128 ** -0.5))), writes=[b_lnk])
    diagW, b_diagW = K.sb(ph, [128, 32, 128], BF16, "diagW")
    pG1, b_pG1 = K.ps(ph, [128, 512], F32, "pG1")
    pG2, b_pG2 = K.ps(ph, [128, 512], F32, "pG2")
    pSt = [K.ps(ph, [64, 512], F32, "pSt")] * 2
    pCv, b_pCv = K.ps(ph, [128, 512], F32, "pCv")
    pND = [K.ps(ph, [64, 512], F32, "pND") for _ in range(2)]
    pU = [(pG1, b_pG1), (pG2, b_pG2)]
    pKt, b_pKt = K.ps(ph, [64, 8, 128], BF16, "pKt")
    pTr2, b_pTr2 = K.ps(ph, [128, GS * 64], BF16, "pTr2")
    H = []
    for hd in range(4):
        h = {}
        for n_ in ("Ig", "Fg", "Am", "Gm"):
            h[n_] = K.sb(ph, [64, NCH], F32, n_)
        h["GL"] = K.sb(ph, [128, NCH], F32, "GL")
        h["ngb"] = K.sb(ph, [64, 1], F32, "ngb")
        h["QT"] = [K.sb(ph, [128, SEG], BF16, "QTs") for _ in range(2)]
        h["KT"] = [K.sb(ph, [128, SEG], BF16, "KTs") for _ in range(2)]
        h["Ktok"] = K.sb(ph, [64, GS, 128], BF16, "Ktok")
        h["Vaug"] = K.sb(ph, [64, GS, 129], BF16, "Vaug")
        h["aV"] = K.sb(ph, [64, GS, 129], BF16, "aV")
        h["Hs"] = K.sb(ph, [64, GS, 128], F32, "Hs")
        h["C"] = [K.sb(ph, [128, 129], F32, "Cst") for _ in range(2)]
        h["Cb"] = [K.sb(ph, [128, 129], BF16, "Cbf") for _ in range(2)]
        h["t1"] = [K.sb(ph, [64, 2], F32, "t1") for _ in range(2)]
        H.append(h)
        Ig, b_Ig = h["Ig"]
        Fg, b_Fg = h["Fg"]
        Am, b_A = h["Am"]
        Gm, b_G = h["Gm"]
        GL, b_GL = h["GL"]
        ngb, b_ngb = h["ngb"]
        Vaug, b_Vaug = h["Vaug"]
        for (t, b, q) in ((Ig, b_Ig, hd), (Fg, b_Fg, 4 + hd)):
            S.op("dve", lambda e, t=t, q=q: e.tensor_copy(out=t[:, 0:64], in_=IFt[0:64, q, :]), reads=[b_IFt], writes=[b])
            S.dma("sp", lambda e, t=t, q=q: e.dma_start(out=t[:, 64:128], in_=IFt[64:128, q, :]), reads=[b_IFt], writes=[b])
        S.op("dve", lambda e, Vaug=Vaug: e.memset(Vaug[:], 1.0), writes=[b_Vaug])
        S.op("dve", lambda e, ngb=ngb, hd=hd: e.tensor_scalar(out=ngb[:], in0=gb4[:, 2 * hd + 1:2 * hd + 2], scalar1=-1.0, scalar2=None, op0=ALU.mult),
             reads=[b_gb4], writes=[b_ngb])
        S.op("act", lambda e, Fg=Fg, ngb=ngb: e.activation(out=Fg[:], in_=Fg[:], func=AF.Exp, scale=-1.0, bias=ngb[:, 0:1]),
             reads=[b_Fg, b_ngb], writes=[b_Fg])
        S.op("act", lambda e, Fg=Fg: e.activation(out=Fg[:], in_=Fg[:], func=AF.Ln, bias=1.0), reads=[b_Fg], writes=[b_Fg])
        S.op("dve", lambda e, Fg=Fg: e.tensor_scalar(out=Fg[:], in0=Fg[:], scalar1=-1.0, scalar2=None, op0=ALU.mult), reads=[b_Fg], writes=[b_Fg])
        S.op("pe", lambda e, Fg=Fg: e.matmul(pG1[0:64, 0:NCH], lhsT=triu[:, :], rhs=Fg[:, :], start=True, stop=True),
             reads=[b_triu, b_Fg], writes=[b_pG1])
        S.op("pe", lambda e, Fg=Fg: e.matmul(pG2[:, 0:NCH], lhsT=ones64[:, :], rhs=Fg[:, :], start=True, stop=True),
             reads=[b_ones64, b_Fg], writes=[b_pG2])
        S.op("dve", lambda e, Am=Am, Ig=Ig, hd=hd: e.scalar_tensor_tensor(out=Am[:], in0=Ig[:], scalar=gb4[:, 2 * hd:2 * hd + 1], in1=pG1[0:64, 0:NCH],
                                                                          op0=ALU.add, op1=ALU.subtract), reads=[b_Ig, b_gb4, b_pG1], writes=[b_A])
        S.op("act", lambda e, Am=Am: e.activation(out=Am[:], in_=Am[:], func=AF.Exp, bias=lnk[:, 0:1]), reads=[b_A, b_lnk], writes=[b_A])
        S.op("act", lambda e, Gm=Gm: e.activation(out=Gm[:], in_=pG1[0:64, 0:NCH], func=AF.Exp), reads=[b_pG1], writes=[b_G])
        S.op("act", lambda e, GL=GL: e.activation(out=GL[:], in_=pG2[:, 0:NCH], func=AF.Exp), reads=[b_pG2], writes=[b_GL])
        S.op("dve", lambda e, h=h: e.memset(h["C"][1][0][:], 0.0), writes=[h["C"][1][1]])

    for hd in range(4):
        for q in range(8):
            S.op("dve", lambda e, hd=hd, q=q: e.tensor_scalar(out=diagW[:, hd * 8 + q, :], in0=ident_b[:, :], scalar1=cw4[:, hd, q:q + 1], scalar2=None,
                                                             op0=ALU.mult), reads=[b_identb, b_cw4], writes=[b_diagW])
    xs = [K.sb(ph, [128, SEG + 3], BF16, "xs") for _ in range(3)]
    Hq, b_Hq = K.sb(ph, [64, GS, 128], F32, "Hq")
    Hn, b_Hn = K.sb(ph, [64, GS, 128], BF16, "Hn")
    mu, b_mu = K.sb(ph, [64, GS], F32, "mu")
    var, b_var = K.sb(ph, [64, GS], F32, "var")
    hseg, b_hseg = K.sb(ph, [128, GS * 64], BF16, "hseg")
    n = [0]

    def prep_seg(sg):
        par = sg % 2
        for hd in range(4):
            h = H[hd]
            for which in range(2):
                dst, b_dst = (h["QT"] if which == 0 else h["KT"])[par]
                x_, b_x = xs[n[0] % 3]
                n[0] += 1
                S.dma("sp", lambda e, x_=x_, which=which, hd=hd: e.dma_start(out=x_[:], in_=qkm_s[:, 4 * which + hd, sg * SEG:sg * SEG + SEG + 3]),
                      writes=[b_x])
                for hf in range(SEG // 512):
                    for w in range(4):
                        S.op("pe", lambda e, x_=x_, which=which, hd=hd, w=w, hf=hf: e.matmul(
                            pCv[:, :], lhsT=diagW[:, hd * 8 + 4 * which + w, :], rhs=x_[:, hf * 512 + w:hf * 512 + w + 512],
                            start=(w == 0), stop=(w == 3)), reads=[b_diagW, b_x], writes=[b_pCv])
                    S.op("act", lambda e, dst=dst, hd=hd, which=which, hf=hf: e.activation(
                        out=dst[:, hf * 512:(hf + 1) * 512], in_=pCv[:, :], func=AF.Silu, bias=cb4[:, hd, which:which + 1]),
                        reads=[b_pCv, b_cb4], writes=[b_dst])

    def load_seg(sg):
        par = sg % 2
        for hd in range(4):
            h = H[hd]
            KT, b_KT = h["KT"][par]
            Ktok, b_Ktok = h["Ktok"]
            Vaug, b_Vaug = h["Vaug"]
            aV, b_aV = h["aV"]
            Am, b_A = h["Am"]
            for g8 in range(GS // 8):
                for cc in range(8):
                    cl = g8 * 8 + cc
                    S.op("pe", lambda e, cl=cl, cc=cc, KT=KT: e.transpose(out=pKt[:, cc, :], in_=KT[:, cl * 64:(cl + 1) * 64], identity=ident_b[:]),
                         reads=[b_KT, b_identb], writes=[b_pKt])
                S.op("dve", lambda e, g8=g8, Ktok=Ktok: e.tensor_copy(out=Ktok[:, g8 * 8:(g8 + 1) * 8, :], in_=pKt[:]), reads=[b_pKt], writes=[b_Ktok])
            for h2 in range(2):
                S.dma("sp", lambda e, h2=h2, Vaug=Vaug, hd=hd: e.dma_start(
                    out=Vaug[:, h2 * 8:(h2 + 1) * 8, 0:128],
                    in_=vm_s[sg * 8:(sg + 1) * 8, h2 * 64:(h2 + 1) * 64, hd * 128:(hd + 1) * 128].rearrange("t s c -> s t c")), writes=[b_Vaug])
            for h2 in range(2):
                S.op("dve", lambda e, h2=h2, aV=aV, Vaug=Vaug, Am=Am: e.tensor_tensor(
                    out=aV[:, h2 * 8:(h2 + 1) * 8, :], in0=Vaug[:, h2 * 8:(h2 + 1) * 8, :],
                    in1=bc_last(Am[:, h2 * 64 + sg * 8:h2 * 64 + sg * 8 + 8], 129), op=ALU.mult), reads=[b_Vaug, b_A], writes=[b_aV])

    scnt = [0]
    Gm4, b_Gm4 = K.sb(ph, [64, NCH, 4], F32, "Gm4")
    for hd in range(4):
        S.op("dve", lambda e, hd=hd: e.tensor_copy(out=Gm4[:, :, hd], in_=H[hd]["Gm"][0][:, :]), reads=[H[hd]["Gm"][1]], writes=[b_Gm4])
    t1all, b_t1all = K.sb(ph, [64, 2, 2, 4], F32, "t1all")
    Sps = [K.sb(ph, [64, 64], BF16, "Sp8") for _ in range(8)]
    tmpCs = [K.sb(ph, [128, 129], F32, "tmpC4") for _ in range(4)]

    def step4(c, cl, par):
        gc = col(c)
        vc = lcol(cl)
        cs = slice(cl * 64, (cl + 1) * 64)
        st_, b_st = pSt[c % 2]
        sps = []
        for hd in range(4):
            h = H[hd]
            QT, b_QT = h["QT"][par]
            KT, b_KT = h["KT"][par]
            S.op("pe", lambda e, hd=hd, QT=QT, KT=KT: e.matmul(st_[:, hd * 64:(hd + 1) * 64], lhsT=KT[:, cs], rhs=QT[:, cs], start=True, stop=True),
                 reads=[b_KT, b_QT], writes=[b_st])
        for hd in range(4):
            h = H[hd]
            Am, b_A = h["Am"]
            sp_, b_sp = Sps[scnt[0] % 8]
            scnt[0] += 1
            sps.append((sp_, b_sp))
            S.op("dve", lambda e, hd=hd, sp_=sp_, Am=Am: e.scalar_tensor_tensor(out=sp_[:], in0=st_[:, hd * 64:(hd + 1) * 64], scalar=Am[:, gc:gc + 1],
                                                                                 in1=triu[:, :], op0=ALU.mult, op1=ALU.mult),
                 reads=[b_st, b_A, b_triu], writes=[b_sp])
        for hd in range(4):
            h = H[hd]
            u_, b_u = pU[hd // 2]
            us = slice((hd % 2) * 129, (hd % 2) * 129 + 129)
            Ktok, b_Ktok = h["Ktok"]
            aV, b_aV = h["aV"]
            S.op("pe", lambda e, u_=u_, us=us, Ktok=Ktok, aV=aV: e.matmul(u_[:, us], lhsT=Ktok[:, cl, :], rhs=aV[:, vc, :], start=True, stop=True),
                 reads=[b_Ktok, b_aV], writes=[b_u])
        for hd in range(4):
            h = H[hd]
            nd_, b_nd = pND[hd // 2]
            us = slice((hd % 2) * 129, (hd % 2) * 129 + 129)
            QT, b_QT = h["QT"][par]
            Vaug, b_Vaug = h["Vaug"]
            Cbp, b_Cbp = h["Cb"][(c + 1) % 2]
            sp_, b_sp = sps[hd]
            if c > 0:
                S.op("pe", lambda e, nd_=nd_, us=us, QT=QT, Cbp=Cbp: e.matmul(nd_[:, us], lhsT=QT[:, cs], rhs=Cbp[:, :], start=True, stop=False),
                     reads=[b_QT, b_Cbp], writes=[b_nd])
            S.op("pe", lambda e, nd_=nd_, us=us, sp_=sp_, Vaug=Vaug: e.matmul(nd_[:, us], lhsT=sp_[:, :], rhs=Vaug[:, vc, :], start=(c == 0), stop=True),
                 reads=[b_sp, b_Vaug], writes=[b_nd])
        for hd in range(4):
            h = H[hd]
            u_, b_u = pU[hd // 2]
            us = slice((hd % 2) * 129, (hd % 2) * 129 + 129)
            GL, b_GL = h["GL"]
            Cp, b_Cp = h["C"][(c + 1) % 2]
            Cn, b_Cn = h["C"][c % 2]
            tmpC, b_tmpC = tmpCs[hd]
            S.op("dve", lambda e, u_=u_, us=us, Cp=Cp, tmpC=tmpC: e.tensor_tensor(out=tmpC[:], in0=u_[:, us], in1=Cp[:], op=ALU.add),
                 reads=[b_u, b_Cp], writes=[b_tmpC])
            S.op("dve", lambda e, Cn=Cn, tmpC=tmpC, GL=GL: e.tensor_scalar(out=Cn[:], in0=tmpC[:], scalar1=GL[:, gc:gc + 1], scalar2=None, op0=ALU.mult),
                 reads=[b_tmpC, b_GL], writes=[b_Cn])
        for hd in range(4):
            h = H[hd]
            Cn, b_Cn = h["C"][c % 2]
            Cbn, b_Cbn = h["Cb"][c % 2]
            S.op("act", lambda e, Cn=Cn, Cbn=Cbn: e.activation(out=Cbn[:], in_=Cn[:], func=AF.Identity), reads=[b_Cn], writes=[b_Cbn])
        pr = c % 2
        for hd in range(4):
            h = H[hd]
            nd_, b_nd = pND[hd // 2]
            Gm, b_G = h["Gm"]
            o = (hd % 2) * 129 + 128
            S.op("act", lambda e, nd_=nd_, Gm=Gm, o=o, hd=hd: e.activation(out=t1all[:, pr, 0, hd:hd + 1], in_=nd_[:, o:o + 1], func=AF.Abs,
                                                                          scale=Gm[:, gc:gc + 1]), reads=[b_nd, b_G], writes=[b_t1all])
        S.op("dve", lambda e: e.tensor_scalar(out=t1all[:, pr, 0, :], in0=t1all[:, pr, 0, :], scalar1=1.0, scalar2=None, op0=ALU.max),
             reads=[b_t1all], writes=[b_t1all])
        S.op("dve", lambda e: e.reciprocal(out=t1all[:, pr, 0, :], in_=t1all[:, pr, 0, :]), reads=[b_t1all], writes=[b_t1all])
        S.op("dve", lambda e: e.tensor_tensor(out=t1all[:, pr, 1, :], in0=t1all[:, pr, 0, :], in1=Gm4[:, gc, :], op=ALU.mult),
             reads=[b_t1all, b_Gm4], writes=[b_t1all])
        for hd in range(4):
            h = H[hd]
            nd_, b_nd = pND[hd // 2]
            Hs, b_Hs = h["Hs"]
            o = (hd % 2) * 129
            S.op("dve", lambda e, nd_=nd_, Hs=Hs, o=o, hd=hd: e.tensor_scalar(out=Hs[:, cl, :], in0=nd_[:, o:o + 128], scalar1=t1all[:, pr, 1, hd:hd + 1],
                                                                             scalar2=None, op0=ALU.mult), reads=[b_nd, b_t1all], writes=[b_Hs])

    def finish_seg(sg):
        for hd in range(4):
            Hs, b_Hs = H[hd]["Hs"]
            S.op("dve", lambda e, Hs=Hs: e.tensor_reduce(out=mu[:], in_=Hs[:], axis=AX.X, op=ALU.add), reads=[b_Hs], writes=[b_mu])
            S.op("dve", lambda e: e.tensor_scalar(out=mu[:], in0=mu[:], scalar1=1.0 / 128, scalar2=None, op0=ALU.mult), reads=[b_mu], writes=[b_mu])
            S.op("dve", lambda e, Hs=Hs: e.tensor_tensor(out=Hs[:], in0=Hs[:], in1=bc_last(mu[:, :], 128), op=ALU.subtract),
                 reads=[b_Hs, b_mu], writes=[b_Hs])
            S.op("dve", lambda e, Hs=Hs: e.tensor_tensor(out=Hq[:], in0=Hs[:], in1=Hs[:], op=ALU.mult), reads=[b_Hs], writes=[b_Hq])
            S.op("dve", lambda e: e.tensor_reduce(out=var[:], in_=Hq[:], axis=AX.X, op=ALU.add), reads=[b_Hq], writes=[b_var])
            emit_rstd(K, var, b_var, var, b_var, 128)
            S.op("dve", lambda e, Hs=Hs: e.tensor_tensor(out=Hn[:], in0=Hs[:], in1=bc_last(var[:, :], 128), op=ALU.mult),
                 reads=[b_Hs, b_var], writes=[b_Hn])
            for cc in range(GS):
                S.op("pe", lambda e, cc=cc: e.transpose(out=pTr2[:, cc * 64:(cc + 1) * 64], in_=Hn[:, cc, :], identity=ident_b[0:64, 0:64]),
                     reads=[b_Hn, b_identb], writes=[b_pTr2])
            S.op("dve", lambda e: e.tensor_copy(out=hseg[:], in_=pTr2[:]), reads=[b_pTr2], writes=[b_hseg])
            S.dma("sp", lambda e, hd=hd: e.dma_start(out=hm_s[hd][:, sg * SEG:(sg + 1) * SEG], in_=hseg[:]), reads=[b_hseg])

    prep_seg(0)
    for sg in range(NSEG):
        load_seg(sg)
        if sg + 1 < NSEG:
            prep_seg(sg + 1)
        for cl in range(GS):
            step4(sg * GS + cl, cl, sg % 2)
        finish_seg(sg)
```

```python
import numpy as np
import ml_dtypes
import concourse.bass as bass
import concourse.mybir as mybir
from concourse.bass_utils import run_bass_kernel_spmd
from contextlib import ExitStack

F32 = mybir.dt.float32
BF16 = mybir.dt.bfloat16
ALU = mybir.AluOpType
AF = mybir.ActivationFunctionType
AX = mybir.AxisListType

D = 1024
DFF = 2816
NF = DFF // 128
DIN = 5456
NT = 16
TOK = NT * 128
S_LEN = 8192
EPS = 1e-6
NCORES = 8
DBG_STOP = 0
EPI_ENG = "dve"
DBG_EVAC = 0

P_QA, P_CKV, P_QIDX, P_KIDX, P_QM, P_KM, P_VM, P_WIDX, P_IF, NCOL1 = 0, 512, 768, 1280, 1408, 1920, 2432, 2944, 2952, 2960
C_QA, C_CKV, C_QIDX, C_KIDX, C_WIDX, C_QM, C_KM, C_VM, C_I, C_F, C_O, C_GA, C_GM = (
    0, 512, 768, 1280, 1344, 1352, 1864, 2376, 2888, 2892, 2896, 3408, 4432)


class Buf:
    __slots__ = ("name", "w", "r", "ps")

    def __init__(self, name="", ps=False):
        self.name = name
        self.w = None
        self.r = []
        self.ps = ps


class Sched:
    NS = 8

    def __init__(self, nc, es):
        self.nc = nc
        self.names = ["pe", "act", "dve", "pool", "sp"]
        self.ops = {k: [] for k in self.names}
        self.sem = {k: es.enter_context(nc.semaphore("s_" + k)) for k in ["pe", "act", "dve", "pool"]}
        self.cnt = {k: 0 for k in self.sem}
        self.dsem = {q: [es.enter_context(nc.semaphore("d_%s%d" % (q, i))) for i in range(self.NS)]
                     for q in ["sp", "act", "pool"]}
        self.dcnt = {q: 0 for q in self.dsem}
        self.seen = {k: {} for k in self.names}
        self.dlast = {}
        self.ninst = 0

    def _wait(self, e, tok):
        key, val, _ = tok
        if key == ("c", "pe") and getattr(self, "_pend", None) is not None and val > self.cnt["pe"]:
            self._pe_close(inc=True)
        if self.seen[e].get(key, 0) >= val:
            return
        self.seen[e][key] = val
        sem = self.sem[key[1]] if key[0] == "c" else self.dsem[key[1]][key[2]]
        self.ops[e].append(lambda eng, sem=sem, val=val: eng.wait_ge(sem, val))

    def _deps(self, e, reads, writes, is_dma):
        deps = []
        for b in reads:
            t = b.w
            if t is not None and not (t[2] == e and t[0][0] == "c" and e == "pe"):
                deps.append(t)
            if b.ps:
                for t in b.r:
                    if t[2] != e:
                        deps.append(t)
        for b in writes:
            t = b.w
            if t is not None and not (t[2] == e and t[0][0] == "c" and e == "pe"):
                deps.append(t)
            for t in b.r:
                if t[2] == e and t[0][0] == "c" and e == "pe":
                    continue
                deps.append(t)
        return deps

    def _pe_close(self, inc=True):
        p = getattr(self, "_pend", None)
        if p is None:
            return
        self._pend = None
        fn, wset = p
        if inc:
            self.cnt["pe"] += 1
            sem = self.sem["pe"]
            self.ops["pe"].append(lambda eng, fn=fn, sem=sem: fn(eng).then_inc(sem, 1))
        else:
            self.ops["pe"].append(lambda eng, fn=fn: fn(eng))

    def op(self, e, fn, reads=(), writes=()):
        if e == "pe":
            p = getattr(self, "_pend", None)
            wset = frozenset(id(b) for b in writes)
            if p is not None:
                self._pe_close(inc=(p[1] != wset))
            for t in self._deps(e, reads, writes, False):
                self._wait(e, t)
            self.ninst += 1
            tok = (("c", e), self.cnt[e] + 1, e)
            self._pend = (fn, wset)
        else:
            for t in self._deps(e, reads, writes, False):
                self._wait(e, t)
            self.cnt[e] += 1
            self.ninst += 1
            tok = (("c", e), self.cnt[e], e)
            sem = self.sem[e]
            self.ops[e].append(lambda eng, fn=fn, sem=sem: fn(eng).then_inc(sem, 1))
        for b in reads:
            b.r = [t for t in b.r if t[0] != tok[0]] + [tok]
        for b in writes:
            b.w = tok
            b.r = []
        return tok

    def dma(self, q, fn, reads=(), writes=()):
        for t in self._deps(q, reads, writes, True):
            self._wait(q, t)
        n = self.dcnt[q]
        self.dcnt[q] += 1
        self.ninst += 1
        slot, rnd = n % self.NS, n // self.NS
        key = ("d", q, slot)
        if rnd > 0:
            self._wait(q, (key, 16 * rnd, q))
        tok = (key, 16 * (rnd + 1), q)
        sem = self.dsem[q][slot]
        self.ops[q].append(lambda eng, fn=fn, sem=sem: fn(eng).then_inc(sem, 16))
        for b in reads:
            b.r = b.r + [tok]
        for b in writes:
            b.w = tok
            b.r = []
        self.dlast[key] = tok[1]
        return tok

    def barrier(self):
        self._pe_close(inc=True)
        for e in self.names:
            for k in self.sem:
                if k != e and self.cnt[k] > 0:
                    self._wait(e, (("c", k), self.cnt[k], None))
            for key, val in self.dlast.items():
                self._wait(e, (key, val, None))

    def finish(self):
        self._pe_close(inc=True)
        for key, val in self.dlast.items():
            self._wait("sp", (key, val, None))
        for k in self.sem:
            if self.cnt[k] > 0:
                self._wait("sp", (("c", k), self.cnt[k], None))

    def flush(self):
        nc = self.nc
        self._pe_close(inc=True)
        if not hasattr(self, "marks"):
            self.marks = []
        self.marks.append(dict(self.cnt))
        ops = self.ops
        self.ops = {k: [] for k in self.names}
        with nc.Block() as block:
            @block.sync
            def _(eng):
                for f in ops["sp"]:
                    f(eng)

            @block.tensor
            def _(eng):
                for f in ops["pe"]:
                    f(eng)

            @block.scalar
            def _(eng):
                for f in ops["act"]:
                    f(eng)

            @block.vector
            def _(eng):
                for f in ops["dve"]:
                    f(eng)

            @block.gpsimd
            def _(eng):
                for f in ops["pool"]:
                    f(eng)


class Ctx:
    def __init__(self, nc, es):
        self.nc = nc
        self.es = es
        self.S = Sched(nc, es)
        self.n = 0

    def sb(self, es, shape, dt, name=None):
        self.n += 1
        t = es.enter_context(self.nc.sbuf_tensor("%s_%d" % (name or "t", self.n), list(shape), dt))
        return t, Buf(name or "t")

    def ps(self, es, shape, dt, name=None):
        self.n += 1
        t = es.enter_context(self.nc.psum_tensor("%s_%d" % (name or "p", self.n), list(shape), dt))
        return t, Buf(name or "p", ps=True)


def bc_last(ap2d, n):
    p, c = ap2d.shape
    return ap2d.unsqueeze(2).to_broadcast([p, c, n])


def emit_mod(K, ph, ada_w, abT_sb, cT_sb, modT, b_modT, cols, rows, ada_b_dram, ident_f, b_ident):
    S = K.S
    cond, b_cond = K.sb(ph, [128, 8], F32, "cond")
    condB, b_condB = K.sb(ph, [128, 8, 128], F32, "condB")
    pieces = [K.sb(ph, [128, 8, 512], F32, "adapc") for _ in range(2)]
    abrow, b_abrow = K.sb(ph, [1, 512], F32, "abrow")
    ones1, b_ones1 = K.sb(ph, [1, 128], F32, "ones1")
    S.op("dve", lambda e: e.memset(ones1[:], 1.0), writes=[b_ones1])
    ab2 = ada_b_dram.rearrange("(a n) -> a n", a=1)
    pCol, b_pCol = K.ps(ph, [128, 512], F32, "pCol")
    pRow, b_pRow = K.ps(ph, [128, 512], F32, "pRow")
    b_cT = Buf("cT")
    S.op("act", lambda e: e.activation(out=cond[:], in_=cT_sb[:], func=AF.Silu), reads=[b_cT], writes=[b_cond])
    for kc in range(8):
        S.op("act", lambda e, kc=kc: e.activation(out=condB[:, kc, :], in_=cT_sb[:, kc:kc + 1].to_broadcast([128, 128]),
                                                  func=AF.Silu), reads=[b_cT], writes=[b_condB])
    aw = ada_w.rearrange("(kc p) n -> p kc n", p=128)
    order = sorted(set(cols) | set(rows.keys()))
    i = 0
    for v in order:
        for hh in range(2):
            pc, b_pc = pieces[i % 2]
            i += 1
            c0 = v * 1024 + hh * 512
            S.dma("sp", lambda e, pc=pc, c0=c0: e.dma_start(out=pc[:], in_=aw[:, :, c0:c0 + 512]), writes=[b_pc])
            if v in cols:
                for c in range(4):
                    cc = v * 8 + hh * 4 + c
                    for kc in range(8):
                        S.op("pe", lambda e, pc=pc, cc=cc, c=c, kc=kc: e.matmul(
                            pCol[:, cc:cc + 1], lhsT=pc[:, kc, c * 128:(c + 1) * 128], rhs=cond[:, kc:kc + 1],
                            start=(kc == 0), stop=(kc == 7)), reads=[b_pc, b_cond], writes=[b_pCol])
                q0 = v * 8 + hh * 4
                S.op("dve", lambda e, q0=q0: e.tensor_tensor(out=modT[:, q0:q0 + 4], in0=pCol[:, q0:q0 + 4], in1=abT_sb[:, q0:q0 + 4], op=ALU.add),
                     reads=[b_pCol], writes=[b_modT])
            if v in rows:
                row_sb, b_row, factor = rows[v]
                S.dma("sp", lambda e, c0=c0: e.dma_start(out=abrow[:], in_=ab2[:, c0:c0 + 512]), writes=[b_abrow])
                for kc in range(8):
                    S.op("pe", lambda e, pc=pc, kc=kc: e.matmul(pRow[:, :], lhsT=condB[:, kc, :], rhs=pc[:, kc, :], start=(kc == 0), stop=False),
                         reads=[b_pc, b_condB], writes=[b_pRow])
                S.op("pe", lambda e: e.matmul(pRow[:, :], lhsT=ones1[0:1, :], rhs=abrow[0:1, :], start=False, stop=True),
                     reads=[b_abrow, b_ones1], writes=[b_pRow])
                S.op("dve", lambda e, row_sb=row_sb, factor=factor, hh=hh: e.tensor_scalar(
                    out=row_sb[:, hh * 512:(hh + 1) * 512], in0=pRow[:], scalar1=float(factor), scalar2=None, op0=ALU.mult),
                    reads=[b_pRow], writes=[b_row])


def emit_rstd(K, ssq, b_ssq, rstd, b_rstd, n):
    S = K.S
    S.op("dve", lambda e: e.tensor_scalar(out=rstd[:], in0=ssq[:], scalar1=1.0 / n, scalar2=EPS, op0=ALU.mult, op1=ALU.add),
         reads=[b_ssq], writes=[b_rstd])
    S.op("act", lambda e: e.activation(out=rstd[:], in_=rstd[:], func=AF.Sqrt), reads=[b_rstd], writes=[b_rstd])
    S.op("dve", lambda e: e.reciprocal(out=rstd[:], in_=rstd[:]), reads=[b_rstd], writes=[b_rstd])


def emit_norm_T(K, xt, b_xt, rstd_col, b_rstd, xn, b_xn, pT, b_pT, idb, b_idb, tmpH, b_tmpH, A, Bv, b_AB, dst, b_dst):
    S = K.S
    S.op("dve", lambda e: e.tensor_scalar(out=xn[:], in0=xt[:], scalar1=rstd_col, scalar2=None, op0=ALU.mult),
         reads=[b_xt, b_rstd], writes=[b_xn])
    for c in range(8):
        S.op("pe", lambda e, c=c: e.transpose(out=pT[:, c, :], in_=xn[:, c * 128:(c + 1) * 128], identity=idb[:]),
             reads=[b_xn, b_idb], writes=[b_pT])
    S.op("dve", lambda e: e.tensor_tensor(out=tmpH[:], in0=pT[:], in1=bc_last(A, 128), op=ALU.mult),
         reads=[b_pT, b_AB], writes=[b_tmpH])
    S.op("dve", lambda e: e.tensor_tensor(out=dst, in0=tmpH[:], in1=bc_last(Bv, 128), op=ALU.add),
         reads=[b_tmpH, b_AB], writes=[b_dst])


def alloc_ffn_weights(K, ph, w1, w3, w2):
    S = K.S
    w1b, b_w1 = K.sb(ph, [128, 8, DFF], BF16, "w1b")
    w3b, b_w3 = K.sb(ph, [128, 8, DFF], BF16, "w3b")
    w2b, b_w2 = K.sb(ph, [128, NF, D], BF16, "w2b")
    S.dma("pool", lambda e: e.dma_start(out=w1b[:], in_=w1.rearrange("(kc p) n -> p kc n", p=128)), writes=[b_w1])
    S.dma("pool", lambda e: e.dma_start(out=w3b[:], in_=w3.rearrange("(kc p) n -> p kc n", p=128)), writes=[b_w3])
    S.dma("pool", lambda e: e.dma_start(out=w2b[:], in_=w2.rearrange("(fc p) n -> p fc n", p=128)), writes=[b_w2])
    return (w1b, b_w1, w3b, b_w3, w2b, b_w2)


def emit_ffn(K, ph, x_src, w1, w3, w2, A, Bv, b_AB, grow, b_grow, idb, b_idb, epilogue, src_bufs=None, ntiles=None, wts=None):
    S = K.S
    NT = ntiles or globals()["NT"]
    if wts is None:
        wts = alloc_ffn_weights(K, ph, w1, w3, w2)
    w1b, b_w1, w3b, b_w3, w2b, b_w2 = wts

    ssq, b_ssq = K.sb(ph, [128, NT], F32, "ssq")
    rstd, b_rstd = K.sb(ph, [128, NT], F32, "rstd")
    xts = [K.sb(ph, [128, D], F32, "xt") for _ in range(4)]
    xns = [K.sb(ph, [128, D], BF16, "xn") for _ in range(2)]
    junk, b_junk = xns[0]
    tmpH, b_tmpH = K.sb(ph, [128, 8, 128], F32, "tmpH")
    hTs = [K.sb(ph, [128, 2, 8, 128], BF16, "hT") for _ in range(2)]
    actTs = [(K.sb(ph, [128, NF, 256], BF16, "actT")[0], [Buf("actT%d" % f) for f in range(NF)]) for _ in range(2)]
    sil = [K.sb(ph, [128, 256], F32, "sil") for _ in range(2)]
    tmpO = [K.sb(ph, [128, D], F32, "tmpO")] * 2
    pAB = [K.ps(ph, [128, 512], F32, "pAB") for _ in range(3)]
    pO = [K.ps(ph, [128, D], F32, "pO") for _ in range(2)]
    pT, b_pT = K.ps(ph, [128, 8, 128], BF16, "pT")

    for t in range(NT):
        xt, b_xt = xts[t % 4]
        S.dma("sp", lambda e, xt=xt, t=t: e.dma_start(out=xt[:], in_=x_src(t)), reads=([src_bufs[t]] if src_bufs else []), writes=[b_xt])
        S.op("act", lambda e, xt=xt, t=t: e.activation(out=junk[:], in_=xt[:], func=AF.Square, accum_out=ssq[:, t:t + 1]),
             reads=[b_xt], writes=[b_junk, b_ssq])
    emit_rstd(K, ssq, b_ssq, rstd, b_rstd, D)
    if DBG_STOP == 1:
        return

    NG = NT // 2

    def prep(g):
        hT, b_hT = hTs[g % 2]
        for tt in range(2):
            t = 2 * g + tt
            xt, b_xt = xts[t % 4]
            xn, b_xn = xns[tt]
            S.dma("sp", lambda e, xt=xt, t=t: e.dma_start(out=xt[:], in_=x_src(t)), reads=([src_bufs[t]] if src_bufs else []), writes=[b_xt])
            emit_norm_T(K, xt, b_xt, rstd[:, t:t + 1], b_rstd, xn, b_xn, pT, b_pT, idb, b_idb, tmpH, b_tmpH,
                        A, Bv, b_AB, hT[:, tt, :, :], b_hT)

    def up(g, f):
        hT, b_hT = hTs[g % 2]
        pab, b_pab = pAB[f % 3]
        for wi, (wb, b_w) in enumerate(((w1b, b_w1), (w3b, b_w3))):
            for kc in range(8):
                S.op("pe", lambda e, wb=wb, kc=kc, f=f, wi=wi, hT=hT, pab=pab: e.matmul(
                    pab[:, wi * 256:(wi + 1) * 256].rearrange("p (a b) -> p a b", a=2),
                    lhsT=wb[:, kc, f * 128:(f + 1) * 128], rhs=hT[:, :, kc, :],
                    start=(kc == 0), stop=(kc == 7)), reads=[b_w, b_hT], writes=[b_pab])
        sl, b_sl = sil[f % 2]
        actT, b_actTs = actTs[g % 2]
        b_actT = b_actTs[f]
        S.op("act", lambda e, pab=pab, sl=sl: e.activation(out=sl[:], in_=pab[:, 0:256], func=AF.Silu),
             reads=[b_pab], writes=[b_sl])
        S.op("dve", lambda e, pab=pab, sl=sl, actT=actT, f=f: e.tensor_tensor(out=actT[:, f, :], in0=sl[:], in1=pab[:, 256:512],
                                                                               op=ALU.mult),
             reads=[b_sl, b_pab], writes=[b_actT])

    def down(g, f):
        actT, b_actTs = actTs[g % 2]
        b_actT = b_actTs[f]
        for tt in range(2):
            po, b_po = pO[tt]
            for dh in range(2):
                S.op("pe", lambda e, actT=actT, tt=tt, f=f, dh=dh, po=po: e.matmul(
                    po[:, dh * 512:(dh + 1) * 512], lhsT=actT[:, f, tt * 128:(tt + 1) * 128],
                    rhs=w2b[:, f, dh * 512:(dh + 1) * 512], start=(f == 0), stop=(f == NF - 1)),
                    reads=[b_actT, b_w2], writes=[b_po])

    def epi(g):
        for tt in range(2):
            t = 2 * g + tt
            xt, b_xt = xts[t % 4]
            po, b_po = pO[tt]
            to, b_to = tmpO[tt]
            S.op("dve", lambda e, to=to, po=po: e.tensor_tensor(out=to[:], in0=po[:], in1=grow[:], op=ALU.mult),
                 reads=[b_po, b_grow], writes=[b_to])
            S.op(EPI_ENG, lambda e, to=to, xt=xt: e.tensor_tensor(out=to[:], in0=to[:], in1=xt[:], op=ALU.add),
                 reads=[b_to, b_xt], writes=[b_to])
            epilogue(t, to, b_to)

    prep(0)
    for g in range(NG):
        for f in range(NF):
            up(g, f)
            if f >= 2:
                down(g, f - 2)
            if f == 6 and g + 1 < NG:
                prep(g + 1)
        down(g, NF - 2)
        down(g, NF - 1)
        epi(g)


def build_p1():
    nc = bass.Bass("TRN2", target_bir_lowering=False)

    def din(name, shape, dt=F32):
        return nc.dram_tensor(name, list(shape), dt, kind="ExternalInput").ap()

    def dout(name, shape, dt=F32):
        return nc.dram_tensor(name, list(shape), dt, kind="ExternalOutput").ap()

    x = din("x", [TOK, D])
    cT = din("cT", [128, 8])
    ada_w = din("ada_w", [D, 9 * D])
    ada_b = din("ada_b", [9 * D])
    ada_bT = din("ada_bT", [128, 72])
    n1T = din("n1T", [128, 8])
    n2T = din("n2T", [128, 8])
    w1 = din("w1", [D, DFF])
    w3 = din("w3", [D, DFF])
    w2 = din("w2", [DFF, D])
    w_in = din("w_in", [D, NCOL1])
    ident = din("ident", [128, 128])

    x1 = dout("x1", [TOK, D])
    h2T_o = dout("h2T", [NT, 128, 8, 128], BF16)
    qaT_o = dout("qaT", [128, 4, TOK], BF16)
    qidxT_o = dout("qidxT", [128, 4, TOK], BF16)
    kidxT_o = dout("kidxT", [64, TOK], BF16)
    ckvT_o = dout("ckvT", [128, 2, TOK], BF16)
    rstdkv_o = dout("rstdkv", [128, NT])
    small_o = dout("small", [128, NT, 16])
    qkmT_o = dout("qkmT", [128, 8, TOK])
    vm_o = dout("vm", [128, NT, 512], BF16)

    with ExitStack() as es:
        K = Ctx(nc, es)
        S = K.S
        idf, b_idf = K.sb(es, [128, 128], F32, "idf")
        idb, b_idb = K.sb(es, [128, 128], BF16, "idb")
        cT_sb, b_cT = K.sb(es, [128, 8], F32, "cT")
        abT_sb, b_abT = K.sb(es, [128, 72], F32, "abT")
        n1_sb, b_n1 = K.sb(es, [128, 8], F32, "n1")
        n2_sb, b_n2 = K.sb(es, [128, 8], F32, "n2")
        modT, b_modT = K.sb(es, [128, 72], F32, "modT")
        g1row, b_g1row = K.sb(es, [128, D], F32, "g1row")
        A1, b_A1 = K.sb(es, [128, 8], F32, "A1")
        A2, b_A2 = K.sb(es, [128, 8], F32, "A2")
        ssq2, b_ssq2 = K.sb(es, [128, NT], F32, "ssq2")
        rstd2, b_rstd2 = K.sb(es, [128, NT], F32, "rstd2")
        junk2, b_junk2 = K.sb(es, [128, D], BF16, "junk2")
        S.dma("sp", lambda e: e.dma_start(out=idf[:], in_=ident[:, :]), writes=[b_idf])
        S.dma("sp", lambda e: e.dma_start(out=cT_sb[:], in_=cT[:, :]), writes=[b_cT])
        S.dma("sp", lambda e: e.dma_start(out=abT_sb[:], in_=ada_bT[:, :]), writes=[b_abT])
        S.dma("sp", lambda e: e.dma_start(out=n1_sb[:], in_=n1T[:, :]), writes=[b_n1])
        S.dma("sp", lambda e: e.dma_start(out=n2_sb[:], in_=n2T[:, :]), writes=[b_n2])
        S.op("dve", lambda e: e.tensor_copy(out=idb[:], in_=idf[:]), reads=[b_idf], writes=[b_idb])
        with ExitStack() as ph:
            S.barrier()
            emit_mod(K, ph, ada_w, abT_sb, cT_sb, modT, b_modT, cols=[0, 1, 3, 4],
                     rows={2: (g1row, b_g1row, 0.5)}, ada_b_dram=ada_b, ident_f=idf, b_ident=b_idf)
            S.op("dve", lambda e: e.scalar_tensor_tensor(out=A1[:], in0=modT[:, 8:16], scalar=1.0, in1=n1_sb[:],
                                                         op0=ALU.add, op1=ALU.mult), reads=[b_modT, b_n1], writes=[b_A1])
            S.op("dve", lambda e: e.scalar_tensor_tensor(out=A2[:], in0=modT[:, 32:40], scalar=1.0, in1=n2_sb[:],
                                                         op0=ALU.add, op1=ALU.mult), reads=[b_modT, b_n2], writes=[b_A2])
            S.barrier()
            S.flush()
        b_x1 = [Buf("x1_%d" % t) for t in range(NT)]
        with ExitStack() as ph:
            b_AB = Buf("AB1")

            def epilogue(t, xo, b_xo):
                S.dma("sp", lambda e: e.dma_start(out=x1[t * 128:(t + 1) * 128, :], in_=xo[:]), reads=[b_xo], writes=[b_x1[t]])
                S.op("act", lambda e: e.activation(out=junk2[:], in_=xo[:], func=AF.Square, accum_out=ssq2[:, t:t + 1]),
                     reads=[b_xo], writes=[b_junk2, b_ssq2])

            emit_ffn(K, ph, lambda t: x[t * 128:(t + 1) * 128, :], w1, w3, w2, A1[:, :], modT[:, 0:8], b_AB,
                     g1row, b_g1row, idb, b_idb, epilogue)
            S.barrier()
            S.flush()
        with ExitStack() as ph:
            emit_proj(K, ph, x1, b_x1, w_in, ssq2, b_ssq2, rstd2, b_rstd2, A2, modT[:, 24:32], idb, b_idb,
                      dict(h2T=h2T_o, qaT=qaT_o, qidxT=qidxT_o, kidxT=kidxT_o, ckvT=ckvT_o, rstdkv=rstdkv_o,
                           small=small_o, qkmT=qkmT_o, vm=vm_o), ncols=NCOL1)
            S.barrier()
            S.finish()
            S.flush()
    return nc


def emit_proj(K, ph, x1, b_x1, w_in, ssq2, b_ssq2, rstd2, b_rstd2, A2, B2, idb, b_idb, outs, ncols):
    S = K.S
    wb, b_wb = K.sb(ph, [128, 8, ncols], BF16, "winb")
    S.dma("pool", lambda e: e.dma_start(out=wb[:], in_=w_in.rearrange("(kc p) n -> p kc n", p=128)), writes=[b_wb])
    emit_rstd(K, ssq2, b_ssq2, rstd2, b_rstd2, D)
    xts = [K.sb(ph, [128, D], F32, "xt") for _ in range(4)]
    xns = [K.sb(ph, [128, D], BF16, "xn") for _ in range(2)]
    tmpH, b_tmpH = K.sb(ph, [128, 8, 128], F32, "tmpH")
    hTs = [K.sb(ph, [128, 2, 8, 128], BF16, "hT") for _ in range(2)]
    ones_b, b_ones = K.sb(ph, [128, 1], F32, "ones")
    S.op("dve", lambda e: e.memset(ones_b[:], 1.0), writes=[b_ones])
    ssqkv, b_ssqkv = K.sb(ph, [128, NT], F32, "ssqkv")
    rstdkv, b_rstdkv = K.sb(ph, [128, NT], F32, "rstdkv")
    small, b_small = K.sb(ph, [128, NT, 16], F32, "small")
    fm16 = [K.sb(ph, [128, 11, 256], BF16, "fm16") for _ in range(2)]
    fm32 = [K.sb(ph, [128, 8, 256], F32, "fm32") for _ in range(2)]
    sq = [K.sb(ph, [128, 2, 256], F32, "sq") for _ in range(2)]
    vms = [K.sb(ph, [128, 512], BF16, "vms") for _ in range(2)]
    pF = [K.ps(ph, [128, 512], F32, "pF") for _ in range(3)]
    pV = [K.ps(ph, [128, 512], F32, "pV") for _ in range(2)]
    pS, b_pS = K.ps(ph, [128, 512], F32, "pS")
    pT, b_pT = K.ps(ph, [128, 8, 128], BF16, "pT")
    b_AB = Buf("AB2")
    NG = NT // 2
    b_h2T_d = Buf("h2T_d")

    def prep(g):
        hT, b_hT = hTs[g % 2]
        for tt in range(2):
            t = 2 * g + tt
            xt, b_xt = xts[t % 4]
            xn, b_xn = xns[tt]
            S.dma("sp", lambda e, xt=xt, t=t: e.dma_start(out=xt[:], in_=x1[t * 128:(t + 1) * 128, :]),
                  reads=[b_x1[t]], writes=[b_xt])
            emit_norm_T(K, xt, b_xt, rstd2[:, t:t + 1], b_rstd2, xn, b_xn, pT, b_pT, idb, b_idb, tmpH, b_tmpH,
                        A2[:, :], B2, b_AB, hT[:, tt, :, :], b_hT)
            S.dma("sp", lambda e, hT=hT, tt=tt, t=t: e.dma_start(out=outs["h2T"][t], in_=hT[:, tt, :, :]),
                  reads=[b_hT], writes=[b_h2T_d])

    chunks = []
    for i in range(4):
        chunks.append((P_QA + 128 * i, 128, "b", i))
    for i in range(2):
        chunks.append((P_CKV + 128 * i, 128, "b", 4 + i))
    for i in range(4):
        chunks.append((P_QIDX + 128 * i, 128, "b", 6 + i))
    chunks.append((P_KIDX, 128, "b", 10))
    for i in range(8):
        chunks.append((P_QM + 128 * i, 128, "f", i))

    cnt = [0]

    def body(g):
        hT, b_hT = hTs[g % 2]
        f16, b_f16 = fm16[g % 2]
        f32, b_f32 = fm32[g % 2]
        sqt, b_sq = sq[g % 2]
        for (c0, M, kind, di) in chunks:
            pf, b_pf = pF[cnt[0] % 3]
            cnt[0] += 1
            for kc in range(8):
                S.op("pe", lambda e, pf=pf, c0=c0, M=M, kc=kc, hT=hT: e.matmul(
                    pf[0:M, 0:256].rearrange("p (a b) -> p a b", a=2), lhsT=wb[:, kc, c0:c0 + M], rhs=hT[:, :, kc, :],
                    start=(kc == 0), stop=(kc == 7)), reads=[b_wb, b_hT], writes=[b_pf])
            if DBG_EVAC == 1:
                dst = f16[:, di, :] if kind == "b" else f32[:, di, :]
                S.op("dve", lambda e, pf=pf, dst=dst: e.tensor_copy(out=dst, in_=pf[:, 0:256]),
                     reads=[b_pf], writes=[b_f16 if kind == "b" else b_f32])
            elif kind == "b":
                if di in (4, 5):
                    S.op("act", lambda e, pf=pf, di=di, sqt=sqt: e.activation(out=sqt[:, di - 4, :], in_=pf[:, 0:256], func=AF.Square),
                         reads=[b_pf], writes=[b_sq])
                    S.op("dve", lambda e, pf=pf, di=di, f16=f16: e.tensor_copy(out=f16[:, di, :], in_=pf[:, 0:256]),
                         reads=[b_pf, b_sq], writes=[b_f16])
                else:
                    S.op("dve", lambda e, pf=pf, di=di, f16=f16, M=M: e.tensor_copy(out=f16[0:M, di, :], in_=pf[0:M, 0:256]),
                         reads=[b_pf], writes=[b_f16])
            else:
                S.op("act", lambda e, pf=pf, di=di, f32=f32: e.activation(out=f32[:, di, :], in_=pf[:, 0:256], func=AF.Identity),
                     reads=[b_pf], writes=[b_f32])
        if DBG_STOP == 12:
            return
        for tt in range(2):
            t = 2 * g + tt
            pv, b_pv = pV[tt]
            for kc in range(8):
                S.op("pe", lambda e, pv=pv, kc=kc, hT=hT, tt=tt: e.matmul(
                    pv[:, :], lhsT=hT[:, tt, kc, :], rhs=wb[:, kc, P_VM:P_VM + 512], start=(kc == 0), stop=(kc == 7)),
                    reads=[b_wb, b_hT], writes=[b_pv])
            vmt, b_vmt = vms[tt]
            S.op("act", lambda e, pv=pv, vmt=vmt: e.activation(out=vmt[:], in_=pv[:], func=AF.Identity), reads=[b_pv], writes=[b_vmt])
            S.dma("sp", lambda e, vmt=vmt, t=t: e.dma_start(out=outs["vm"][:, t, :], in_=vmt[:]), reads=[b_vmt])
            for kc in range(8):
                S.op("pe", lambda e, kc=kc, hT=hT, tt=tt: e.matmul(
                    pS[:, 0:8], lhsT=hT[:, tt, kc, :], rhs=wb[:, kc, P_WIDX:P_WIDX + 8], start=(kc == 0), stop=(kc == 7)),
                    reads=[b_wb, b_hT], writes=[b_pS])
            for kc in range(8):
                S.op("pe", lambda e, kc=kc, hT=hT, tt=tt: e.matmul(
                    pS[:, 8:16], lhsT=hT[:, tt, kc, :], rhs=wb[:, kc, P_IF:P_IF + 8], start=(kc == 0), stop=(kc == 7)),
                    reads=[b_wb, b_hT], writes=[b_pS])
            for c in range(2):
                S.op("pe", lambda e, c=c, tt=tt, sqt=sqt: e.matmul(
                    pS[:, 16:17], lhsT=sqt[:, c, tt * 128:(tt + 1) * 128], rhs=ones_b[:, 0:1], start=(c == 0), stop=(c == 1)),
                    reads=[b_sq, b_ones], writes=[b_pS])
            S.op("dve", lambda e, t=t: e.tensor_copy(out=small[:, t, :], in_=pS[:, 0:16]), reads=[b_pS], writes=[b_small])
            S.op("dve", lambda e, t=t: e.tensor_copy(out=ssqkv[:, t:t + 1], in_=pS[:, 16:17]), reads=[b_pS], writes=[b_ssqkv])
        tok0 = g * 256
        if DBG_STOP == 13:
            return
        S.dma("sp", lambda e, f16=f16: e.dma_start(out=outs["qaT"][:, :, tok0:tok0 + 256], in_=f16[:, 0:4, :]), reads=[b_f16])
        S.dma("sp", lambda e, f16=f16: e.dma_start(out=outs["ckvT"][:, :, tok0:tok0 + 256], in_=f16[:, 4:6, :]), reads=[b_f16])
        S.dma("sp", lambda e, f16=f16: e.dma_start(out=outs["qidxT"][:, :, tok0:tok0 + 256], in_=f16[:, 6:10, :]), reads=[b_f16])
        S.dma("sp", lambda e, f16=f16: e.dma_start(out=outs["kidxT"][:, tok0:tok0 + 256], in_=f16[0:64, 10, :]), reads=[b_f16])
        S.dma("sp", lambda e, f32=f32: e.dma_start(out=outs["qkmT"][:, :, tok0:tok0 + 256], in_=f32[:, :, :]), reads=[b_f32])

    prep(0)
    if DBG_STOP == 11:
        return
    for g in range(NG):
        if g + 1 < NG:
            prep(g + 1)
        body(g)
    if DBG_STOP == 14:
        return
    emit_rstd(K, ssqkv, b_ssqkv, rstdkv, b_rstdkv, 256)
    S.dma("sp", lambda e: e.dma_start(out=outs["rstdkv"][:, :], in_=rstdkv[:]), reads=[b_rstdkv])
    S.dma("sp", lambda e: e.dma_start(out=outs["small"][:, :, :], in_=small[:]), reads=[b_small])


def colT(v, n):
    return np.ascontiguousarray(np.asarray(v, np.float32).reshape(n, 128).T)


def core_tokens(x_b, j):
    S_, Dd = x_b.shape
    return np.ascontiguousarray(x_b.reshape(S_ // 512, 4, 128, Dd)[:, j].reshape(-1, Dd))


def p1_inputs(inp, core):
    b, j = divmod(core, 4)
    return {
        "x": core_tokens(np.asarray(inp["x"][b], np.float32), j),
        "cT": colT(inp["c"][b], 8),
        "ada_w": np.ascontiguousarray(inp["ada_w"][0], np.float32),
        "ada_b": np.ascontiguousarray(inp["ada_b"][0], np.float32),
        "ada_bT": colT(inp["ada_b"][0], 72),
        "n1T": colT(inp["ffn1_norm"][0], 8),
        "n2T": colT(inp["mix_norm"][0], 8),
        "w1": np.ascontiguousarray(inp["ffn1_w1"][0], np.float32),
        "w3": np.ascontiguousarray(inp["ffn1_w3"][0], np.float32),
        "w2": np.ascontiguousarray(inp["ffn1_w2"][0], np.float32),
        "w_in": pack_w_in(inp["w_in"][0]),
        "ident": np.eye(128, dtype=np.float32),
    }


def pack_w_in(w):
    w = np.asarray(w, np.float32)
    cols = np.concatenate([np.arange(0, 1344), np.arange(1280, 1344), np.arange(C_QM, C_VM + 512),
                           np.arange(C_WIDX, C_WIDX + 8), np.arange(C_I, C_I + 8)])
    assert cols.size == NCOL1
    return np.ascontiguousarray(w[:, cols])


NCH = S_LEN // 64


def emit_mlstm(K, ph, d, ident_b, b_identb, fused=None):
    S = K.S
    triu, b_triu = K.sb(ph, [64, 64], F32, "triu")
    ones64, b_ones64 = K.sb(ph, [64, 128], F32, "ones64")
    Ig, b_Ig = K.sb(ph, [64, NCH], F32, "Ig")
    Fg, b_Fg = K.sb(ph, [64, NCH], F32, "Fg")
    gb, b_gb = K.sb(ph, [64, 2], F32, "gb")
    ngb, b_ngb = K.sb(ph, [64, 1], F32, "ngb")
    cw, b_cw = K.sb(ph, [128, 8], F32, "cw")
    cb, b_cb = K.sb(ph, [128, 2], F32, "cb")
    LOGF, b_LOGF = K.sb(ph, [64, NCH], F32, "LOGF")
    Am, b_A = K.sb(ph, [64, NCH], F32, "Am")
    Gm, b_G = K.sb(ph, [64, NCH], F32, "Gm")
    GL, b_GL = K.sb(ph, [128, NCH], F32, "GL")
    QT, b_QT = K.sb(ph, [128, S_LEN], BF16, "QT")
    KT, b_KT = K.sb(ph, [128, S_LEN], BF16, "KT")
    Ktok, b_Ktok = K.sb(ph, [64, NCH, 128], BF16, "Ktok")
    Vaug, b_Vaug = K.sb(ph, [64, NCH, 129], BF16, "Vaug")
    aV, b_aV = K.sb(ph, [64, NCH, 129], BF16, "aV")
    if fused is None:
        col = lambda c: c
        for (t, b, src) in ((triu, b_triu, "triu"), (Ig, b_Ig, "ig"), (Fg, b_Fg, "fg"), (gb, b_gb, "gb"), (cw, b_cw, "cw"),
                            (cb, b_cb, "cb"), (Vaug, b_Vaug, "vaug")):
            S.dma("sp", lambda e, t=t, src=src: e.dma_start(out=t[:], in_=d[src]), writes=[b])
    else:
        col = lambda c: (c % 2) * 64 + c // 2
        hd, IFt, b_IFt, vm_s = fused["hd"], fused["IFt"], fused["b_IFt"], fused["vm_s"]
        for (t, b, src) in ((triu, b_triu, "triu"), (gb, b_gb, "gb"), (cw, b_cw, "cw"), (cb, b_cb, "cb")):
            S.dma("sp", lambda e, t=t, src=src: e.dma_start(out=t[:], in_=d[src]), writes=[b])
        for (t, b, q) in ((Ig, b_Ig, hd), (Fg, b_Fg, 4 + hd)):
            S.op("dve", lambda e, t=t, q=q: e.tensor_copy(out=t[:, 0:64], in_=IFt[0:64, q, :]), reads=[b_IFt], writes=[b])
            S.dma("sp", lambda e, t=t, q=q: e.dma_start(out=t[:, 64:128], in_=IFt[64:128, q, :]), reads=[b_IFt], writes=[b])
        for h2 in range(2):
            S.dma("sp", lambda e, h2=h2: e.dma_start(
                out=Vaug[:, h2 * 64:(h2 + 1) * 64, 0:128],
                in_=vm_s[:, h2 * 64:(h2 + 1) * 64, hd * 128:(hd + 1) * 128].rearrange("t s c -> s t c")), writes=[b_Vaug])
    S.op("dve", lambda e: e.memset(ones64[:], 1.0), writes=[b_ones64])
    S.op("dve", lambda e: e.memset(Vaug[:, :, 128:129], 1.0), reads=[b_Vaug], writes=[b_Vaug])
    S.op("dve", lambda e: e.tensor_scalar(out=ngb[:], in0=gb[:, 1:2], scalar1=-1.0, scalar2=None, op0=ALU.mult),
         reads=[b_gb], writes=[b_ngb])
    pG1, b_pG1 = K.ps(ph, [128, 512], F32, "pG1")
    pG2, b_pG2 = K.ps(ph, [128, 512], F32, "pG2")
    S.op("act", lambda e: e.activation(out=LOGF[:], in_=Fg[:], func=AF.Exp, scale=-1.0, bias=ngb[:, 0:1]),
         reads=[b_Fg, b_ngb], writes=[b_LOGF])
    S.op("act", lambda e: e.activation(out=LOGF[:], in_=LOGF[:], func=AF.Ln, bias=1.0), reads=[b_LOGF], writes=[b_LOGF])
    S.op("dve", lambda e: e.tensor_scalar(out=LOGF[:], in0=LOGF[:], scalar1=-1.0, scalar2=None, op0=ALU.mult),
         reads=[b_LOGF], writes=[b_LOGF])
    S.op("pe", lambda e: e.matmul(pG1[0:64, 0:NCH], lhsT=triu[:, :], rhs=LOGF[:, :], start=True, stop=True),
         reads=[b_triu, b_LOGF], writes=[b_pG1])
    S.op("pe", lambda e: e.matmul(pG2[:, 0:NCH], lhsT=ones64[:, :], rhs=LOGF[:, :], start=True, stop=True),
         reads=[b_ones64, b_LOGF], writes=[b_pG2])
    S.op("dve", lambda e: e.scalar_tensor_tensor(out=Am[:], in0=Ig[:], scalar=gb[:, 0:1], in1=pG1[0:64, 0:NCH],
                                                 op0=ALU.add, op1=ALU.subtract), reads=[b_Ig, b_gb, b_pG1], writes=[b_A])
    S.op("act", lambda e: e.activation(out=Am[:], in_=Am[:], func=AF.Exp), reads=[b_A], writes=[b_A])
    S.op("act", lambda e: e.activation(out=Gm[:], in_=pG1[0:64, 0:NCH], func=AF.Exp), reads=[b_pG1], writes=[b_G])
    S.op("act", lambda e: e.activation(out=GL[:], in_=pG2[:, 0:NCH], func=AF.Exp), reads=[b_pG2], writes=[b_GL])
    S.op("dve", lambda e: e.tensor_tensor(out=aV[:], in0=Vaug[:], in1=bc_last(Am[:, :], 129), op=ALU.mult),
         reads=[b_Vaug, b_A], writes=[b_aV])

    SEG = 1024
    xs = [K.sb(ph, [128, SEG + 3], F32, "xs") for _ in range(2)]
    accs = [K.sb(ph, [128, SEG], F32, "acc") for _ in range(2)]
    kscale = float(128 ** -0.5)
    n = 0
    for seg in range(S_LEN // SEG):
        for which, (src, dst, b_dst) in enumerate((("qpad", QT, b_QT), ("kpad", KT, b_KT))):
            x_, b_x = xs[n % 2]
            a_, b_a = accs[n % 2]
            n += 1
            S.dma("sp", lambda e, x_=x_, src=src, seg=seg: e.dma_start(out=x_[:], in_=d[src][:, seg * SEG:seg * SEG + SEG + 3]),
                  writes=[b_x])
            S.op("dve", lambda e, x_=x_, a_=a_, which=which: e.tensor_scalar(
                out=a_[:], in0=x_[:, 0:SEG], scalar1=cw[:, 4 * which:4 * which + 1], scalar2=None, op0=ALU.mult),
                reads=[b_x, b_cw], writes=[b_a])
            for w in range(1, 4):
                S.op("dve", lambda e, x_=x_, a_=a_, which=which, w=w: e.scalar_tensor_tensor(
                    out=a_[:], in0=x_[:, w:w + SEG], scalar=cw[:, 4 * which + w:4 * which + w + 1], in1=a_[:],
                    op0=ALU.mult, op1=ALU.add), reads=[b_x, b_cw, b_a], writes=[b_a])
            if which == 0:
                S.op("act", lambda e, a_=a_, dst=dst, seg=seg: e.activation(
                    out=dst[:, seg * SEG:(seg + 1) * SEG], in_=a_[:], func=AF.Silu, bias=cb[:, 0:1]),
                    reads=[b_a, b_cb], writes=[b_dst])
            else:
                S.op("act", lambda e, a_=a_: e.activation(out=a_[:], in_=a_[:], func=AF.Silu, bias=cb[:, 1:2]),
                     reads=[b_a, b_cb], writes=[b_a])
                S.op("dve", lambda e, a_=a_, dst=dst, seg=seg: e.tensor_scalar(
                    out=dst[:, seg * SEG:(seg + 1) * SEG], in0=a_[:], scalar1=kscale, scalar2=None, op0=ALU.mult),
                    reads=[b_a], writes=[b_dst])
    pKt, b_pKt = K.ps(ph, [64, 8, 128], BF16, "pKt")
    for g8 in range(NCH // 8):
        for cc in range(8):
            c = g8 * 8 + cc
            S.op("pe", lambda e, c=c, cc=cc: e.transpose(out=pKt[:, cc, :], in_=KT[:, c * 64:(c + 1) * 64], identity=ident_b[:]),
                 reads=[b_KT, b_identb], writes=[b_pKt])
        S.op("dve", lambda e, g8=g8: e.tensor_copy(out=Ktok[:, g8 * 8:(g8 + 1) * 8, :], in_=pKt[:]), reads=[b_pKt], writes=[b_Ktok])

    Cs = [K.sb(ph, [128, 129], F32, "Cst") for _ in range(2)]
    Cbs = [K.sb(ph, [128, 129], BF16, "Cbf") for _ in range(2)]
    tmpC, b_tmpC = K.sb(ph, [128, 129], F32, "tmpC")
    Sps = [K.sb(ph, [64, 64], BF16, "Sp") for _ in range(2)]
    t1s = [K.sb(ph, [64, 2], F32, "t1") for _ in range(2)]
    GSEG = 16
    Hs, b_Hs = K.sb(ph, [64, GSEG, 128], F32, "Hs")
    Hq, b_Hq = K.sb(ph, [64, GSEG, 128], F32, "Hq")
    Hn, b_Hn = K.sb(ph, [64, GSEG, 128], BF16, "Hn")
    mu, b_mu = K.sb(ph, [64, GSEG], F32, "mu")
    var, b_var = K.sb(ph, [64, GSEG], F32, "var")
    hseg, b_hseg = K.sb(ph, [128, GSEG * 64], BF16, "hseg")
    pSt = [K.ps(ph, [64, 512], F32, "pSt") for _ in range(2)]
    pU = [(pG1, b_pG1), (pG2, b_pG2)]
    pND = [K.ps(ph, [64, 512], F32, "pND") for _ in range(2)]
    pTr, b_pTr = pKt, b_pKt
    pTr2, b_pTr2 = K.ps(ph, [128, GSEG * 64], BF16, "pTr2")
    S.op("dve", lambda e: e.memset(Cs[1][0][:], 0.0), writes=[Cs[1][1]])
    for c in range(NCH):
        st_, b_st = pSt[c % 2]
        sp_, b_sp = Sps[c % 2]
        u_, b_u = pU[c % 2]
        nd_, b_nd = pND[c % 2]
        Cp, b_Cp = Cs[(c + 1) % 2]
        Cn, b_Cn = Cs[c % 2]
        Cbp, b_Cbp = Cbs[(c + 1) % 2]
        Cbn, b_Cbn = Cbs[c % 2]
        t1, b_t1 = t1s[c % 2]
        cs = slice(c * 64, (c + 1) * 64)
        S.op("pe", lambda e, st_=st_, cs=cs: e.matmul(st_[:, 0:64], lhsT=KT[:, cs], rhs=QT[:, cs], start=True, stop=True),
             reads=[b_KT, b_QT], writes=[b_st])
        S.op("dve", lambda e, st_=st_, sp_=sp_, c=c: e.scalar_tensor_tensor(
            out=sp_[:], in0=st_[:, 0:64], scalar=Am[:, col(c):col(c) + 1], in1=triu[:, :], op0=ALU.mult, op1=ALU.mult),
            reads=[b_st, b_A, b_triu], writes=[b_sp])
        S.op("pe", lambda e, u_=u_, c=c: e.matmul(u_[:, 0:129], lhsT=Ktok[:, c, :], rhs=aV[:, col(c), :], start=True, stop=True),
             reads=[b_Ktok, b_aV], writes=[b_u])
        if c > 0:
            S.op("pe", lambda e, nd_=nd_, cs=cs, Cbp=Cbp: e.matmul(nd_[:, 0:129], lhsT=QT[:, cs], rhs=Cbp[:, :], start=True, stop=False),
                 reads=[b_QT, b_Cbp], writes=[b_nd])
        S.op("pe", lambda e, nd_=nd_, sp_=sp_, c=c: e.matmul(nd_[:, 0:129], lhsT=sp_[:, :], rhs=Vaug[:, col(c), :], start=(c == 0), stop=True),
             reads=[b_sp, b_Vaug], writes=[b_nd])
        S.op("dve", lambda e, u_=u_, Cp=Cp: e.tensor_tensor(out=tmpC[:], in0=u_[:, 0:129], in1=Cp[:], op=ALU.add),
             reads=[b_u, b_Cp], writes=[b_tmpC])
        S.op("dve", lambda e, Cn=Cn, c=c: e.tensor_scalar(out=Cn[:], in0=tmpC[:], scalar1=GL[:, col(c):col(c) + 1], scalar2=None, op0=ALU.mult),
             reads=[b_tmpC, b_GL], writes=[b_Cn])
        S.op("act", lambda e, Cn=Cn, Cbn=Cbn: e.activation(out=Cbn[:], in_=Cn[:], func=AF.Identity), reads=[b_Cn], writes=[b_Cbn])
        S.op("act", lambda e, nd_=nd_, t1=t1, c=c: e.activation(out=t1[:, 0:1], in_=nd_[:, 128:129], func=AF.Abs, scale=Gm[:, col(c):col(c) + 1]),
             reads=[b_nd, b_G], writes=[b_t1])
        S.op("dve", lambda e, t1=t1: e.tensor_scalar(out=t1[:, 0:1], in0=t1[:, 0:1], scalar1=1.0, scalar2=None, op0=ALU.max),
             reads=[b_t1], writes=[b_t1])
        S.op("dve", lambda e, t1=t1: e.reciprocal(out=t1[:, 0:1], in_=t1[:, 0:1]), reads=[b_t1], writes=[b_t1])
        S.op("dve", lambda e, t1=t1, c=c: e.tensor_tensor(out=t1[:, 1:2], in0=t1[:, 0:1], in1=Gm[:, col(c):col(c) + 1], op=ALU.mult),
             reads=[b_t1, b_G], writes=[b_t1])
        S.op("dve", lambda e, nd_=nd_, t1=t1, c=c: e.tensor_scalar(out=Hs[:, c % GSEG, :], in0=nd_[:, 0:128], scalar1=t1[:, 1:2], scalar2=None,
                                                                    op0=ALU.mult), reads=[b_nd, b_t1], writes=[b_Hs])
        if c % GSEG == GSEG - 1:
            sg = c // GSEG
            S.op("dve", lambda e: e.tensor_reduce(out=mu[:], in_=Hs[:], axis=AX.X, op=ALU.add), reads=[b_Hs], writes=[b_mu])
            S.op("dve", lambda e: e.tensor_scalar(out=mu[:], in0=mu[:], scalar1=1.0 / 128, scalar2=None, op0=ALU.mult),
                 reads=[b_mu], writes=[b_mu])
            S.op("dve", lambda e: e.tensor_tensor(out=Hs[:], in0=Hs[:], in1=bc_last(mu[:, :], 128), op=ALU.subtract),
                 reads=[b_Hs, b_mu], writes=[b_Hs])
            S.op("dve", lambda e: e.tensor_tensor(out=Hq[:], in0=Hs[:], in1=Hs[:], op=ALU.mult), reads=[b_Hs], writes=[b_Hq])
            S.op("dve", lambda e: e.tensor_reduce(out=var[:], in_=Hq[:], axis=AX.X, op=ALU.add), reads=[b_Hq], writes=[b_var])
            emit_rstd(K, var, b_var, var, b_var, 128)
            S.op("dve", lambda e: e.tensor_tensor(out=Hn[:], in0=Hs[:], in1=bc_last(var[:, :], 128), op=ALU.mult),
                 reads=[b_Hs, b_var], writes=[b_Hn])
            for cc in range(GSEG):
                S.op("pe", lambda e, cc=cc: e.transpose(out=pTr2[:, cc * 64:(cc + 1) * 64], in_=Hn[:, cc, :], identity=ident_b[0:64, 0:64]),
                     reads=[b_Hn, b_identb], writes=[b_pTr2])
            S.op("dve", lambda e: e.tensor_copy(out=hseg[:], in_=pTr2[:]), reads=[b_pTr2], writes=[b_hseg])
            S.dma("sp", lambda e, sg=sg: e.dma_start(out=d["hmT"][:, sg * GSEG * 64:(sg + 1) * GSEG * 64], in_=hseg[:]), reads=[b_hseg])


def mlstm_inputs_from(qT, kT, v, ig, fg, inp, hd):
    z3 = np.zeros((128, 3), np.float32)
    vaug = np.zeros((64, NCH, 129), ml_dtypes.bfloat16)
    vaug[:, :, 0:128] = np.asarray(v).reshape(NCH, 64, 128).transpose(1, 0, 2)
    gbv = np.asarray(inp["mlstm_gate_bias"][0], np.float32)
    cwv = np.asarray(inp["conv_w"][0], np.float32)
    cbv = np.asarray(inp["conv_b"][0], np.float32)
    cw = np.concatenate([cwv[:, hd * 128:(hd + 1) * 128].T, cwv[:, 512 + hd * 128:512 + (hd + 1) * 128].T], axis=1)
    cb = np.stack([cbv[hd * 128:(hd + 1) * 128], cbv[512 + hd * 128:512 + (hd + 1) * 128]], axis=1)
    return {
        "qpad": np.ascontiguousarray(np.concatenate([z3, qT], axis=1), np.float32),
        "kpad": np.ascontiguousarray(np.concatenate([z3, kT], axis=1), np.float32),
        "vaug": vaug,
        "ig": np.ascontiguousarray(np.asarray(ig, np.float32).reshape(NCH, 64).T),
        "fg": np.ascontiguousarray(np.asarray(fg, np.float32).reshape(NCH, 64).T),
        "gb": np.ascontiguousarray(np.broadcast_to(np.array([gbv[hd], gbv[4 + hd]], np.float32)[None, :], (64, 2))),
        "cw": np.ascontiguousarray(cw, np.float32),
        "cb": np.ascontiguousarray(cb, np.float32),
        "triu": np.triu(np.ones((64, 64), np.float32)),
    }


NIT_BISECT = 16
NEG_MASK = -32768.0


def emit_attn(K, ph, d, ident_f, b_identf, ident_b, b_identb, nslots=NT):
    S = K.S
    nc = K.nc
    KpT, b_KpT = K.sb(ph, [128, 4, S_LEN], BF16, "KpT")
    kidx2, b_kidx2 = K.sb(ph, [128, S_LEN], BF16, "kidx2")
    rstd, b_rstd = K.sb(ph, [128, 64], F32, "rstdk")
    widx, b_widx = K.sb(ph, [128, NT, 8], F32, "widx")
    wabs, b_wabs = K.sb(ph, [128, NT, 8], F32, "wabs")
    wsgn, b_wsgn = K.sb(ph, [128, NT, 8], F32, "wsgn")
    E4, b_E4 = K.sb(ph, [128, 512], BF16, "E4")
    zer, b_zer = K.sb(ph, [128, 65], BF16, "zer")
    onesr, b_onesr = K.sb(ph, [128, 64], F32, "onesr")
    cm, b_cm = K.sb(ph, [128, 512], F32, "cm")
    btb, b_btb = K.sb(ph, [128, 8, 640], BF16, "btb")
    b31, b_b31 = K.sb(ph, [128, 8], F32, "b31")
    ring = [K.ps(ph, [128, 512], F32, "ring") for _ in range(4)]
    pY = [K.ps(ph, [128, 512], F32, "pY") for _ in range(2)]
    pT, b_pT = K.ps(ph, [128, 4, 128], BF16, "pTa")
    S.dma("sp", lambda e: e.dma_start(out=kidx2[:], in_=d["kidx2"]), writes=[b_kidx2])
    S.dma("sp", lambda e: e.dma_start(out=rstd[:], in_=d["rstd"]), writes=[b_rstd])
    S.dma("sp", lambda e: e.dma_start(out=widx[:], in_=d["widx"]), writes=[b_widx])
    S.dma("sp", lambda e: e.dma_start(out=cm[:], in_=d["cm"]), writes=[b_cm])
    S.dma("sp", lambda e: e.dma_start(out=b31[:], in_=d["b31"]), writes=[b_b31])
    S.op("dve", lambda e: e.memset(zer[:], 0.0), writes=[b_zer])
    S.op("dve", lambda e: e.memset(onesr[:], 1.0), writes=[b_onesr])
    for hh in range(4):
        S.op("dve", lambda e, hh=hh: e.tensor_copy(out=E4[:, hh * 128:(hh + 1) * 128], in_=ident_f[:]), reads=[b_identf], writes=[b_E4])
    S.op("act", lambda e: e.activation(out=wabs[:], in_=widx[:], func=AF.Abs), reads=[b_widx], writes=[b_wabs])
    S.op("act", lambda e: e.activation(out=wsgn[:], in_=widx[:], func=AF.Sign), reads=[b_widx], writes=[b_wsgn])

    with ExitStack() as pp:
        ckvT, b_ckvT = K.sb(pp, [128, 2, S_LEN], BF16, "ckvT")
        wst, b_wst = K.sb(pp, [128, 2, 512], F32, "wst")
        gkv, b_gkv = K.sb(pp, [128, 2], F32, "gkv")
        wukb, b_wukb = K.sb(pp, [128, 2, 512], BF16, "wukb")
        wuvb, b_wuvb = K.sb(pp, [128, 2, 512], BF16, "wuvb")
        bst, b_bst = K.sb(pp, [128, 640], F32, "bst")
        S.dma("sp", lambda e: e.dma_start(out=ckvT[:], in_=d["ckvT"]), writes=[b_ckvT])
        S.dma("sp", lambda e: e.dma_start(out=gkv[:], in_=d["gkv"]), writes=[b_gkv])
        for (src, dst, b_dst, fac) in (("wuk", wukb, b_wukb, 0.125), ("wuv", wuvb, b_wuvb, 1.0)):
            S.dma("sp", lambda e, src=src: e.dma_start(out=wst[:], in_=d[src]), writes=[b_wst])
            for cc in range(2):
                S.op("dve", lambda e, cc=cc, dst=dst, fac=fac: e.tensor_scalar(
                    out=dst[:, cc, :], in0=wst[:, cc, :], scalar1=gkv[:, cc:cc + 1], scalar2=float(fac), op0=ALU.mult, op1=ALU.mult),
                    reads=[b_wst, b_gkv], writes=[b_dst])
        for h in range(8):
            S.dma("sp", lambda e, h=h: e.dma_start(out=bst[:], in_=d["bt"][:, h, :]), writes=[b_bst])
            S.op("dve", lambda e, h=h: e.tensor_scalar(out=btb[:, h, :], in0=bst[:], scalar1=b31[:, h:h + 1], scalar2=None, op0=ALU.subtract),
                 reads=[b_bst, b_b31], writes=[b_btb])
        ktoks = [K.sb(pp, [128, 512], BF16, "ktok") for _ in range(2)]
        vts = [K.sb(pp, [128, 8, 65], BF16, "vt") for _ in range(3)]
        for (vt, b_vt) in vts:
            S.op("dve", lambda e, vt=vt: e.memset(vt[:], 1.0), writes=[b_vt])
        b_vscr = [Buf("vscr%d" % T) for T in range(64)]
        for T in range(64):
            pk, b_pk = ring[(2 * T) % 4]
            pv, b_pv = ring[(2 * T + 1) % 4]
            kt, b_kt = ktoks[T % 2]
            vt, b_vt = vts[T % 3]
            ts = slice(T * 128, (T + 1) * 128)
            for cc in range(2):
                S.op("pe", lambda e, pk=pk, cc=cc, ts=ts: e.matmul(pk[:, :], lhsT=ckvT[:, cc, ts], rhs=wukb[:, cc, :], start=(cc == 0), stop=(cc == 1)),
                     reads=[b_ckvT, b_wukb], writes=[b_pk])
            for cc in range(2):
                S.op("pe", lambda e, pv=pv, cc=cc, ts=ts: e.matmul(pv[:, :], lhsT=ckvT[:, cc, ts], rhs=wuvb[:, cc, :], start=(cc == 0), stop=(cc == 1)),
                     reads=[b_ckvT, b_wuvb], writes=[b_pv])
            S.op("dve", lambda e, pk=pk, kt=kt, T=T: e.tensor_scalar(out=kt[:], in0=pk[:], scalar1=rstd[:, T:T + 1], scalar2=None, op0=ALU.mult),
                 reads=[b_pk, b_rstd], writes=[b_kt])
            S.op("dve", lambda e, pv=pv, vt=vt, T=T: e.tensor_scalar(
                out=vt[:, :, 0:64], in0=pv[:].rearrange("p (h d) -> p h d", h=8), scalar1=rstd[:, T:T + 1], scalar2=None, op0=ALU.mult),
                reads=[b_pv, b_rstd], writes=[b_vt])
            S.dma("sp", lambda e, vt=vt, T=T: e.dma_start(out=d["vscr"][T], in_=vt[:].rearrange("p h d -> p (h d)")),
                  reads=[b_vt], writes=[b_vscr[T]])
            for q in range(4):
                S.op("pe", lambda e, q=q, kt=kt: e.transpose(out=pT[:, q, :], in_=kt[:, q * 128:(q + 1) * 128], identity=ident_b[:]),
                     reads=[b_kt, b_identb], writes=[b_pT])
            S.op("act", lambda e, ts=ts: e.copy(out=KpT[:, :, ts], in_=pT[:]), reads=[b_pT], writes=[b_KpT])
        S.barrier()

    sc, b_sc = K.sb(ph, [128, S_LEN], F32, "sc")
    nms = [K.sb(ph, [128, S_LEN], BF16, "nm") for _ in range(2)]
    nmbs = [K.sb(ph, [128, 8, 640], BF16, "nmb") for _ in range(2)]
    nmA_bufs = [Buf("nmA0"), Buf("nmA1")]
    rts = [K.sb(ph, [128, 512], F32, "rt") for _ in range(2)]
    qas = [K.sb(ph, [128, 4, 256], BF16, "qa") for _ in range(2)]
    for (qa_, b_qa_) in qas:
        S.op("dve", lambda e, qa_=qa_: e.memset(qa_[:], 0.0), writes=[b_qa_])
    qis = [K.sb(ph, [128, 4, 128], BF16, "qi") for _ in range(2)]
    vbufs = [K.sb(ph, [128, 520], BF16, "vbuf") for _ in range(3)]
    PTs = [K.sb(ph, [128, 512], BF16, "PT") for _ in range(4)]
    bs = {n_: K.sb(ph, [128, 1], F32, n_) for n_ in ("amax", "w0", "lo", "mid", "cnt", "gw", "nmid", "sa")}
    rd, b_rd = K.sb(ph, [128, 512], F32, "rd")
    bsb, b_bsb = K.sb(ph, [64, 512], F32, "bsb")
    yo, b_yo = K.sb(ph, [64, 1024], BF16, "yo")
    rcnt = [0]
    vcnt = [0]
    pcnt = [0]
    ftab, b_ftab = K.sb(ph, [128, NIT_BISECT + 2], F32, "ftab")
    Wtab, b_Wtab = K.sb(ph, [128, NIT_BISECT + 2], F32, "Wtab")
    W2tab, b_W2tab = K.sb(ph, [128, NIT_BISECT + 2], F32, "W2tab")
    for k in range(NIT_BISECT + 2):
        S.op("dve", lambda e, k=k: e.memset(ftab[:, k:k + 1], float(2.0 ** -k)), writes=[b_ftab])

    def stage_a(i):
        nk = (i + 1) * 512
        qa, b_qa = qas[i % 2]
        qi, b_qi = qis[i % 2]
        nm, b_nm = nms[i % 2]
        nmb, b_nmb = nmbs[i % 2]
        b_nmA = nmA_bufs[i % 2]
        S.dma("sp", lambda e: e.dma_start(out=qa[0:64, :, 0:128], in_=d["qaT"][0:64, :, i * 128:(i + 1) * 128]), writes=[b_qa])
        S.dma("sp", lambda e: e.dma_start(out=qa[64:128, :, 128:256], in_=d["qaT"][64:128, :, i * 128:(i + 1) * 128]), writes=[b_qa])
        S.dma("sp", lambda e: e.dma_start(out=qi[:], in_=d["qidxT"][:, :, i * 128:(i + 1) * 128]), writes=[b_qi])
        for kc in range(i + 1):
            ks = slice(kc * 512, (kc + 1) * 512)
            for h in range(8):
                pz, b_pz = ring[rcnt[0] % 4]
                rt, b_rt = rts[rcnt[0] % 2]
                rcnt[0] += 1
                hp = slice((h % 2) * 64, (h % 2) * 64 + 64)
                S.op("pe", lambda e, pz=pz, hp=hp, h=h, ks=ks: e.matmul(pz[:, :], lhsT=qi[hp, h // 2, :], rhs=kidx2[hp, ks], start=True, stop=True),
                     reads=[b_qi, b_kidx2], writes=[b_pz])
                S.op("act", lambda e, pz=pz, rt=rt, h=h: e.activation(out=rt[:], in_=pz[:], func=AF.Relu, scale=wabs[:, i, h:h + 1]),
                     reads=[b_pz, b_wabs], writes=[b_rt])
                if h == 0:
                    S.op("dve", lambda e, rt=rt, ks=ks, h=h: e.tensor_scalar(out=sc[:, ks], in0=rt[:], scalar1=wsgn[:, i, h:h + 1], scalar2=None, op0=ALU.mult),
                         reads=[b_rt, b_wsgn], writes=[b_sc])
                else:
                    S.op("dve", lambda e, rt=rt, ks=ks, h=h: e.scalar_tensor_tensor(out=sc[:, ks], in0=rt[:], scalar=wsgn[:, i, h:h + 1], in1=sc[:, ks],
                                                                                     op0=ALU.mult, op1=ALU.add), reads=[b_rt, b_wsgn, b_sc], writes=[b_sc])
        amax, b_amax = bs["amax"]
        w0, b_w0 = bs["w0"]
        lo, b_lo = bs["lo"]
        mid, b_mid = bs["mid"]
        cnt, b_cnt = bs["cnt"]
        gw, b_gw = bs["gw"]
        S.op("dve", lambda e: e.tensor_reduce(out=amax[:], in_=sc[:, 0:nk], axis=AX.X, op=ALU.max, apply_absolute_value=True),
             reads=[b_sc], writes=[b_amax])
        S.op("dve", lambda e: e.tensor_tensor(out=sc[:, nk - 512:nk], in0=sc[:, nk - 512:nk], in1=cm[:], op=ALU.add),
             reads=[b_sc, b_cm], writes=[b_sc])
        S.op("dve", lambda e: e.tensor_scalar(out=lo[:], in0=amax[:], scalar1=-1.0, scalar2=-1.0, op0=ALU.mult, op1=ALU.add),
             reads=[b_amax], writes=[b_lo])
        S.op("dve", lambda e: e.tensor_scalar(out=w0[:], in0=amax[:], scalar1=2.0, scalar2=2.0, op0=ALU.mult, op1=ALU.add),
             reads=[b_amax], writes=[b_w0])
        S.op("dve", lambda e: e.tensor_scalar(out=Wtab[:], in0=ftab[:], scalar1=w0[:, 0:1], scalar2=None, op0=ALU.mult),
             reads=[b_ftab, b_w0], writes=[b_Wtab])
        S.op("dve", lambda e: e.tensor_scalar(out=W2tab[:], in0=ftab[:], scalar1=w0[:, 0:1], scalar2=2.0, op0=ALU.mult, op1=ALU.mult),
             reads=[b_ftab, b_w0], writes=[b_W2tab])
        S.op("dve", lambda e: e.tensor_tensor(out=mid[:], in0=lo[:], in1=Wtab[:, 1:2], op=ALU.add), reads=[b_lo, b_Wtab], writes=[b_mid])
        for it in range(1, NIT_BISECT + 1):
            S.op("dve", lambda e: e.tensor_scalar(out=nm[:, 0:nk], in0=sc[:, 0:nk], scalar1=mid[:, 0:1], scalar2=0.0, op0=ALU.is_ge, op1=ALU.add,
                                                  accum_out=cnt[:]), reads=[b_sc, b_mid], writes=[b_nm, b_cnt])
            S.op("dve", lambda e, it=it: e.scalar_tensor_tensor(out=gw[:], in0=cnt[:], scalar=255.5, in1=W2tab[:, it + 1:it + 2], op0=ALU.is_ge, op1=ALU.mult),
                 reads=[b_cnt, b_W2tab], writes=[b_gw])
            S.op("dve", lambda e, it=it: e.scalar_tensor_tensor(out=mid[:], in0=gw[:], scalar=Wtab[:, it + 1:it + 2], in1=mid[:], op0=ALU.subtract, op1=ALU.add),
                 reads=[b_gw, b_Wtab, b_mid], writes=[b_mid])
        S.op("dve", lambda e: e.tensor_tensor(out=lo[:], in0=mid[:], in1=Wtab[:, NIT_BISECT + 1:NIT_BISECT + 2], op=ALU.subtract),
             reads=[b_mid, b_Wtab], writes=[b_lo])
        S.op("dve", lambda e: e.tensor_scalar(out=nm[:, 0:nk], in0=sc[:, 0:nk], scalar1=lo[:, 0:1], scalar2=NEG_MASK, op0=ALU.is_lt, op1=ALU.mult),
             reads=[b_sc, b_lo], writes=[b_nm, b_nmA])
        t0 = 1 if i == 0 else 0
        for h in range(8):
            S.op("dve", lambda e, h=h: e.tensor_tensor(
                out=nmb[:, h, t0 * 128:640], in0=nm[:, nk - 640 + t0 * 128:nk], in1=btb[:, h, t0 * 128:640], op=ALU.add),
                reads=[b_nm, b_btb], writes=[b_nmb])

    def stage_b(i):
        ntile = 4 * i + 4
        qa, b_qa = qas[i % 2]
        nm, b_nm = nms[i % 2]
        nmb, b_nmb = nmbs[i % 2]
        for g in range(2):
            py, b_py = pY[g]
            S.op("pe", lambda e, py=py: e.matmul(py[0:65, :], lhsT=zer[:, :], rhs=E4[:, :], start=True, stop=False),
                 reads=[b_zer, b_E4], writes=[b_py])
        units = [(st, g) for st in range(ntile) for g in range(2)]
        state = {}

        def qk(u):
            st, g = units[u]
            if g == 0:
                vb, b_vb = vbufs[vcnt[0] % 3]
                vcnt[0] += 1
                S.dma("sp", lambda e: e.dma_start(out=vb[:], in_=d["vscr"][st]), reads=[b_vscr[st]], writes=[b_vb])
                state[("vb", st)] = (vb, b_vb)
            ti = st - (4 * i - 1)
            ss = slice(st * 128, (st + 1) * 128)
            pl, b_pl = ring[rcnt[0] % 4]
            rcnt[0] += 1
            pt, b_pt = PTs[pcnt[0] % 4]
            pcnt[0] += 1
            if ti < 0:
                S.op("pe", lambda e: e.matmul(pl[:, :], lhsT=nm[:, ss], rhs=E4[:, :], start=True, stop=False),
                     reads=[b_nm, b_E4], writes=[b_pl])
                for pp in range(2):
                    pair = 2 * g + pp
                    S.op("pe", lambda e, pp=pp, pair=pair: e.matmul(pl[:, pp * 256:(pp + 1) * 256], lhsT=KpT[:, pair, ss], rhs=qa[:, pair, :],
                                                                    start=False, stop=(pp == 1)), reads=[b_KpT, b_qa], writes=[b_pl])
            for hh in (range(4) if ti >= 0 else ()):
                h = 4 * g + hh
                hp = slice((h % 2) * 64, (h % 2) * 64 + 64)
                os_ = slice(hh * 128, (hh + 1) * 128)
                if ti >= 0:
                    S.op("pe", lambda e, os_=os_, h=h: e.matmul(pl[:, os_], lhsT=nmb[:, h, ti * 128:(ti + 1) * 128], rhs=ident_b[:, :],
                                                                start=True, stop=False), reads=[b_nmb, b_identb], writes=[b_pl])
                qc = slice((h % 2) * 128, (h % 2) * 128 + 128)
                S.op("pe", lambda e, os_=os_, hp=hp, h=h, qc=qc: e.matmul(pl[:, os_], lhsT=KpT[hp, h // 2, ss], rhs=qa[hp, h // 2, qc],
                                                                            start=False, stop=True), reads=[b_KpT, b_qa], writes=[b_pl])
            S.op("act", lambda e: e.activation(out=pt[:], in_=pl[:], func=AF.Exp), reads=[b_pl], writes=[b_pt])
            state[u] = (pt, b_pt)

        def pv(u):
            st, g = units[u]
            pt, b_pt = state.pop(u)
            vb, b_vb = state[("vb", st)]
            py, b_py = pY[g]
            for hh in range(4):
                h = 4 * g + hh
                os_ = slice(hh * 128, (hh + 1) * 128)
                S.op("pe", lambda e, os_=os_, h=h, hh=hh: e.matmul(
                    py[0:65, os_], lhsT=vb[:, h * 65:(h + 1) * 65], rhs=pt[:, os_], start=False, stop=(st == ntile - 1 and hh == 3)),
                    reads=[b_vb, b_pt], writes=[b_py])

        qk(0)
        for u in range(len(units)):
            if u + 1 < len(units):
                qk(u + 1)
            pv(u)
        for g in range(2):
            pb, b_pb = ring[rcnt[0] % 4]
            rcnt[0] += 1
            py, b_py = pY[g]
            S.op("dve", lambda e, py=py: e.reciprocal(out=rd[64:65, :], in_=py[64:65, :]), reads=[b_py], writes=[b_rd])
            S.op("pe", lambda e, pb=pb: e.matmul(pb[0:64, :], lhsT=onesr[64:65, 0:64], rhs=rd[64:65, :], start=True, stop=True),
                 reads=[b_onesr, b_rd], writes=[b_pb])
            S.op("act", lambda e, pb=pb: e.activation(out=bsb[:, :], in_=pb[0:64, :], func=AF.Identity), reads=[b_pb], writes=[b_bsb])
            S.op("dve", lambda e, py=py, g=g: e.tensor_tensor(out=yo[:, g * 512:(g + 1) * 512], in0=py[0:64, :], in1=bsb[:, :], op=ALU.mult),
                 reads=[b_py, b_bsb], writes=[b_yo])
        S.dma("sp", lambda e: e.dma_start(out=d["yaT"][:, :, i * 128:(i + 1) * 128], in_=yo[:].rearrange("p (h t) -> p h t", h=8)),
              reads=[b_yo])

    stage_a(0)
    for i in range(nslots):
        if i + 1 < nslots:
            stage_a(i + 1)
        stage_b(i)


def t5_bucket_np(dist):
    n = np.maximum(dist, 0)
    nf = np.maximum(n, 1).astype(np.float32)
    large = 16 + (np.log(nf / np.float32(16)) / np.float32(np.log(128 / 16)) * np.float32(16)).astype(np.int32)
    large = np.minimum(large, 31)
    return np.where(n < 16, n, large)


def attn_consts(inp, j):
    rb = np.asarray(inp["rel_bias"], np.float32)
    t = np.arange(128)[:, None, None]
    ti = np.arange(5)[None, :, None]
    sl = np.arange(128)[None, None, :]
    dist = (j + 1 - ti) * 128 + t - sl
    bk = t5_bucket_np(dist)
    bt = rb[bk]
    bt = np.ascontiguousarray(bt.transpose(0, 3, 1, 2).reshape(128, 8, 640), np.float32)
    b31 = np.ascontiguousarray(np.broadcast_to(rb[31][None, :], (128, 8)), np.float32)
    jp = np.arange(4)[None, :, None]
    t2 = np.arange(128)[:, None, None]
    valid = ((jp - j) * 128 + sl - t2) <= 0
    cm = np.where(valid, 0.0, -1e30).astype(np.float32).reshape(128, 512)
    wuk = np.asarray(inp["w_uk"][0], np.float32).transpose(1, 0, 2).reshape(256, 512)
    wuv = np.asarray(inp["w_uv"][0], np.float32).transpose(1, 0, 2).reshape(256, 512)
    return {
        "bt": bt, "b31": b31, "cm": cm,
        "wuk": np.ascontiguousarray(wuk.reshape(2, 128, 512).transpose(1, 0, 2)),
        "wuv": np.ascontiguousarray(wuv.reshape(2, 128, 512).transpose(1, 0, 2)),
        "gkv": colT(inp["kv_norm"][0], 2),
    }


def emit_merge(K, ph, d, g2row, b_g2row, x2_d, b_x2, sel=None):
    S = K.S
    wg, b_wg = K.sb(ph, [128, 8, 2560], BF16, "wg")
    wa, b_wa = K.sb(ph, [64, 8, D], BF16, "wa")
    wm, b_wm = K.sb(ph, [128, 4, D], BF16, "wm")
    wo, b_wo = K.sb(ph, [128, 8, D], BF16, "wo")
    wms, b_wms = K.sb(ph, [128, 4, D], F32, "wms")
    hn, b_hn = K.sb(ph, [128, 4], F32, "hn")
    S.dma("pool", lambda e: e.dma_start(out=wg[:], in_=d["wg"].rearrange("(kc p) n -> p kc n", p=128)), writes=[b_wg])
    S.dma("pool", lambda e: e.dma_start(out=wa[:], in_=d["wa"]), writes=[b_wa])
    S.dma("pool", lambda e: e.dma_start(out=wo[:], in_=d["wout"].rearrange("(kc p) n -> p kc n", p=128)), writes=[b_wo])
    S.dma("sp", lambda e: e.dma_start(out=wms[:], in_=d["wm"]), writes=[b_wms])
    S.dma("sp", lambda e: e.dma_start(out=hn[:], in_=d["hnT"]), writes=[b_hn])
    for hd in range(4):
        S.op("dve", lambda e, hd=hd: e.tensor_scalar(out=wm[:, hd, :], in0=wms[:, hd, :], scalar1=hn[:, hd:hd + 1], scalar2=None, op0=ALU.mult),
             reads=[b_wms, b_hn], writes=[b_wm])
    h2s = [K.sb(ph, [128, 2, 8, 128], BF16, "h2g") for _ in range(2)]
    yas = [K.sb(ph, [64, 8, 256], BF16, "yag") for _ in range(2)]
    hms = [K.sb(ph, [128, 4, 256], BF16, "hmg") for _ in range(2)]
    x1s = [K.sb(ph, [128, D], F32, "x1t") for _ in range(4)]
    sig, b_sig = K.sb(ph, [128, 20, 256], BF16, "sig")
    hmo, b_hmo = K.sb(ph, [128, 4, 256], BF16, "hmo")
    t1s = [K.sb(ph, [128, 256], F32, "mt1") for _ in range(2)]
    t2s = [K.sb(ph, [128, 256], F32, "mt2") for _ in range(2)]
    mgs = [K.sb(ph, [128, 8, 256], BF16, "mg") for _ in range(2)]
    tos = [K.sb(ph, [128, D], F32, "mto") for _ in range(2)]
    pG = [K.ps(ph, [128, 512], F32, "pGm") for _ in range(2)]
    pZ = [K.ps(ph, [128, 512], F32, "pZ") for _ in range(2)]
    pO = [K.ps(ph, [128, D], F32, "pOm") for _ in range(2)]
    NG = NT // 2
    for g in range(NG):
        h2, b_h2 = h2s[g % 2]
        ya, b_ya = yas[g % 2]
        hm, b_hm = hms[g % 2]
        mg, b_mg = mgs[g % 2]
        tk = slice(g * 256, (g + 1) * 256)
        for tt in range(2):
            S.dma("sp", lambda e, h2=h2, tt=tt, g=g: e.dma_start(out=h2[:, tt, :, :], in_=d["h2T"][2 * g + tt]), writes=[b_h2])
        S.dma("sp", lambda e, ya=ya, tk=tk: e.dma_start(out=ya[:], in_=d["yaT"][:, :, tk]), writes=[b_ya])
        if sel is None:
            S.dma("sp", lambda e, hm=hm, tk=tk: e.dma_start(out=hm[:], in_=d["hmT"][:, :, tk]), writes=[b_hm])
        else:
            oh, b_oh, hm_s, cands = sel
            for tt in range(2):
                i = 2 * g + tt
                cd, b_cd = cands[tt]
                S.dma("sp", lambda e, cd=cd, i=i: e.dma_start(out=cd[:], in_=hm_s.rearrange("h e t -> e h t")[:, :, i * 512:(i + 1) * 512]),
                      writes=[b_cd])
                ts_ = slice(tt * 128, (tt + 1) * 128)
                S.op("dve", lambda e, cd=cd, hm=hm, ts_=ts_: e.tensor_scalar(out=hm[:, :, ts_], in0=cd[:, :, 0:128], scalar1=oh[:, 0:1], scalar2=None,
                                                                            op0=ALU.mult), reads=[b_cd, b_oh], writes=[b_hm])
                for jj in range(1, 4):
                    S.op("dve", lambda e, cd=cd, hm=hm, ts_=ts_, jj=jj: e.scalar_tensor_tensor(
                        out=hm[:, :, ts_], in0=cd[:, :, jj * 128:(jj + 1) * 128], scalar=oh[:, jj:jj + 1], in1=hm[:, :, ts_],
                        op0=ALU.mult, op1=ALU.add), reads=[b_cd, b_oh, b_hm], writes=[b_hm])
        for c in range(20):
            pg, b_pg = pG[c % 2]
            for kc in range(8):
                S.op("pe", lambda e, pg=pg, c=c, kc=kc, h2=h2: e.matmul(
                    pg[:, 0:256].rearrange("p (a b) -> p a b", a=2), lhsT=wg[:, kc, c * 128:(c + 1) * 128], rhs=h2[:, :, kc, :],
                    start=(kc == 0), stop=(kc == 7)), reads=[b_wg, b_h2], writes=[b_pg])
            S.op("act", lambda e, pg=pg, c=c: e.activation(out=sig[:, c, :], in_=pg[:, 0:256], func=AF.Sigmoid), reads=[b_pg], writes=[b_sig])
        S.op("dve", lambda e, hm=hm: e.tensor_tensor(out=hmo[:], in0=hm[:], in1=sig[:, 0:4, :], op=ALU.mult), reads=[b_hm, b_sig], writes=[b_hmo])
        for n in range(8):
            pz, b_pz = pZ[n % 2]
            t1, b_t1 = t1s[n % 2]
            t2, b_t2 = t2s[n % 2]
            ns = slice(n * 128, (n + 1) * 128)
            for h in range(8):
                S.op("pe", lambda e, pz=pz, h=h, ns=ns, ya=ya: e.matmul(pz[:, 0:256], lhsT=wa[:, h, ns], rhs=ya[:, h, :], start=(h == 0), stop=(h == 7)),
                     reads=[b_wa, b_ya], writes=[b_pz])
            for hd in range(4):
                S.op("pe", lambda e, pz=pz, hd=hd, ns=ns: e.matmul(pz[:, 256:512], lhsT=wm[:, hd, ns], rhs=hmo[:, hd, :], start=(hd == 0), stop=(hd == 3)),
                     reads=[b_wm, b_hmo], writes=[b_pz])
            S.op("dve", lambda e, pz=pz, t1=t1, n=n: e.tensor_tensor(out=t1[:], in0=pz[:, 0:256], in1=sig[:, 4 + n, :], op=ALU.mult),
                 reads=[b_pz, b_sig], writes=[b_t1])
            S.op("dve", lambda e, pz=pz, t2=t2, n=n: e.tensor_tensor(out=t2[:], in0=pz[:, 256:512], in1=sig[:, 12 + n, :], op=ALU.mult),
                 reads=[b_pz, b_sig], writes=[b_t2])
            S.op("dve", lambda e, t1=t1, t2=t2, mg=mg, n=n: e.tensor_tensor(out=mg[:, n, :], in0=t1[:], in1=t2[:], op=ALU.add),
                 reads=[b_t1, b_t2], writes=[b_mg])
        for tt in range(2):
            t = 2 * g + tt
            po, b_po = pO[tt]
            x1t, b_x1t = x1s[t % 4]
            to, b_to = tos[tt]
            S.dma("sp", lambda e, x1t=x1t, t=t: e.dma_start(out=x1t[:], in_=d["x1"][t * 128:(t + 1) * 128, :]), writes=[b_x1t])
            for dh in range(2):
                for kc in range(8):
                    S.op("pe", lambda e, po=po, dh=dh, kc=kc, mg=mg, tt=tt: e.matmul(
                        po[:, dh * 512:(dh + 1) * 512], lhsT=mg[:, kc, tt * 128:(tt + 1) * 128], rhs=wo[:, kc, dh * 512:(dh + 1) * 512],
                        start=(kc == 0), stop=(kc == 7)), reads=[b_mg, b_wo], writes=[b_po])
            S.op("dve", lambda e, po=po, to=to: e.tensor_tensor(out=to[:], in0=po[:], in1=g2row[:], op=ALU.mult), reads=[b_po, b_g2row], writes=[b_to])
            S.op("dve", lambda e, to=to, x1t=x1t: e.tensor_tensor(out=to[:], in0=to[:], in1=x1t[:], op=ALU.add), reads=[b_to, b_x1t], writes=[b_to])
            S.dma("sp", lambda e, to=to, t=t: e.dma_start(out=x2_d[t * 128:(t + 1) * 128, :], in_=to[:]), reads=[b_to], writes=[b_x2[t]])


def build_p3():
    nc = bass.Bass("TRN2", target_bir_lowering=False)

    def din(name, shape, dt=F32):
        return nc.dram_tensor(name, list(shape), dt, kind="ExternalInput").ap()

    d = dict(x1=din("x1", [TOK, D]), h2T=din("h2T", [NT, 128, 8, 128], BF16), yaT=din("yaT", [64, 8, TOK], BF16),
             hmT=din("hmT", [128, 4, TOK], BF16), wg=din("wg", [D, 2560]), wa=din("wa", [64, 8, D]), wm=din("wm", [128, 4, D]),
             hnT=din("hnT", [128, 4]), wout=din("wout", [D, D]))
    cT = din("cT", [128, 8])
    ada_w = din("ada_w", [D, 9 * D])
    ada_b = din("ada_b", [9 * D])
    ada_bT = din("ada_bT", [128, 72])
    n3T = din("n3T", [128, 8])
    fn = din("fn", [D])
    w1 = din("w1", [D, DFF])
    w3 = din("w3", [D, DFF])
    w2 = din("w2", [DFF, D])
    ident = din("ident", [128, 128])
    out = nc.dram_tensor("out", [TOK, D], F32, kind="ExternalOutput").ap()
    x2_d = nc.dram_tensor("x2s", [TOK, D], F32, kind="Internal").ap()
    x3_d = nc.dram_tensor("x3s", [TOK, D], F32, kind="Internal").ap()
    with ExitStack() as es:
        K = Ctx(nc, es)
        S = K.S
        idf, b_idf = K.sb(es, [128, 128], F32, "idf")
        idb, b_idb = K.sb(es, [128, 128], BF16, "idb")
        cT_sb, b_cT = K.sb(es, [128, 8], F32, "cT")
        abT_sb, b_abT = K.sb(es, [128, 72], F32, "abT")
        n3_sb, b_n3 = K.sb(es, [128, 8], F32, "n3")
        modT, b_modT = K.sb(es, [128, 72], F32, "modT")
        g3row, b_g3row = K.sb(es, [128, D], F32, "g3row")
        A3, b_A3 = K.sb(es, [128, 8], F32, "A3")
        ssqF, b_ssqF = K.sb(es, [128, NT], F32, "ssqF")
        rstdF, b_rstdF = K.sb(es, [128, NT], F32, "rstdF")
        junkF, b_junkF = K.sb(es, [128, D], BF16, "junkF")
        st_g2 = ExitStack()
        g2row, b_g2row = K.sb(st_g2, [128, D], F32, "g2row")
        S.dma("sp", lambda e: e.dma_start(out=idf[:], in_=ident[:, :]), writes=[b_idf])
        S.dma("sp", lambda e: e.dma_start(out=cT_sb[:], in_=cT[:, :]), writes=[b_cT])
        S.dma("sp", lambda e: e.dma_start(out=abT_sb[:], in_=ada_bT[:, :]), writes=[b_abT])
        S.dma("sp", lambda e: e.dma_start(out=n3_sb[:], in_=n3T[:, :]), writes=[b_n3])
        S.op("dve", lambda e: e.tensor_copy(out=idb[:], in_=idf[:]), reads=[b_idf], writes=[b_idb])
        with ExitStack() as ph:
            S.barrier()
            emit_mod(K, ph, ada_w, abT_sb, cT_sb, modT, b_modT, cols=[6, 7],
                     rows={5: (g2row, b_g2row, 1.0), 8: (g3row, b_g3row, 0.5)}, ada_b_dram=ada_b, ident_f=idf, b_ident=b_idf)
            S.op("dve", lambda e: e.scalar_tensor_tensor(out=A3[:], in0=modT[:, 56:64], scalar=1.0, in1=n3_sb[:],
                                                         op0=ALU.add, op1=ALU.mult), reads=[b_modT, b_n3], writes=[b_A3])
            S.barrier()
            S.flush()
        b_x2 = [Buf("x2_%d" % t) for t in range(NT)]
        b_x3 = [Buf("x3_%d" % t) for t in range(NT)]
        with ExitStack() as ph:
            emit_merge(K, ph, d, g2row, b_g2row, x2_d, b_x2)
            S.barrier()
            S.flush()
        st_g2.close()
        with ExitStack() as ph:
            b_AB = Buf("AB3")

            def epilogue(t, xo, b_xo):
                S.dma("sp", lambda e: e.dma_start(out=x3_d[t * 128:(t + 1) * 128, :], in_=xo[:]), reads=[b_xo], writes=[b_x3[t]])
                S.op("act", lambda e: e.activation(out=junkF[:], in_=xo[:], func=AF.Square, accum_out=ssqF[:, t:t + 1]),
                     reads=[b_xo], writes=[b_junkF, b_ssqF])

            emit_ffn(K, ph, lambda t: x2_d[t * 128:(t + 1) * 128, :], w1, w3, w2, A3[:, :], modT[:, 48:56], b_AB,
                     g3row, b_g3row, idb, b_idb, epilogue, src_bufs=b_x2)
            S.barrier()
            S.flush()
        with ExitStack() as ph:
            fnrow, b_fnrow = K.sb(ph, [128, D], F32, "fnrow")
            fn1, b_fn1 = K.sb(ph, [1, D], F32, "fn1")
            on1, b_on1 = K.sb(ph, [1, 128], F32, "on1")
            pF, b_pF = K.ps(ph, [128, D], F32, "pFn")
            S.op("dve", lambda e: e.memset(on1[:], 1.0), writes=[b_on1])
            S.dma("sp", lambda e: e.dma_start(out=fn1[:], in_=fn.rearrange("(a n) -> a n", a=1)), writes=[b_fn1])
            for nt in range(2):
                S.op("pe", lambda e, nt=nt: e.matmul(pF[:, nt * 512:(nt + 1) * 512], lhsT=on1[0:1, :], rhs=fn1[0:1, nt * 512:(nt + 1) * 512],
                                                    start=True, stop=True), reads=[b_on1, b_fn1], writes=[b_pF])
            S.op("dve", lambda e: e.tensor_copy(out=fnrow[:], in_=pF[:]), reads=[b_pF], writes=[b_fnrow])
            emit_rstd(K, ssqF, b_ssqF, rstdF, b_rstdF, D)
            xf = [K.sb(ph, [128, D], F32, "xf") for _ in range(3)]
            for t in range(NT):
                xt, b_xt = xf[t % 3]
                S.dma("sp", lambda e, xt=xt, t=t: e.dma_start(out=xt[:], in_=x3_d[t * 128:(t + 1) * 128, :]), reads=[b_x3[t]], writes=[b_xt])
                S.op("dve", lambda e, xt=xt, t=t: e.scalar_tensor_tensor(out=xt[:], in0=xt[:], scalar=rstdF[:, t:t + 1], in1=fnrow[:],
                                                                          op0=ALU.mult, op1=ALU.mult), reads=[b_xt, b_rstdF, b_fnrow], writes=[b_xt])
                S.dma("sp", lambda e, xt=xt, t=t: e.dma_start(out=out[t * 128:(t + 1) * 128, :], in_=xt[:]), reads=[b_xt])
            S.barrier()
            S.finish()
            S.flush()
    return nc


def build_p2():
    nc = bass.Bass("TRN2", target_bir_lowering=False)

    def din(name, shape, dt=F32):
        return nc.dram_tensor(name, list(shape), dt, kind="ExternalInput").ap()

    def dout(name, shape, dt=F32):
        return nc.dram_tensor(name, list(shape), dt, kind="ExternalOutput").ap()

    dm = dict(qpad=din("qpad", [128, S_LEN + 3]), kpad=din("kpad", [128, S_LEN + 3]), vaug=din("vaug", [64, NCH, 129], BF16),
              ig=din("ig", [64, NCH]), fg=din("fg", [64, NCH]), gb=din("gb", [64, 2]), cw=din("cw", [128, 8]), cb=din("cb", [128, 2]),
              triu=din("triu", [64, 64]), hmT=dout("hmT", [128, S_LEN], BF16))
    da = dict(ckvT=din("ckvT", [128, 2, S_LEN], BF16), rstd=din("rstd", [128, 64]), kidx2=din("kidx2", [128, S_LEN], BF16),
              qaT=din("qaT", [128, 4, TOK], BF16), qidxT=din("qidxT", [128, 4, TOK], BF16), widx=din("widx", [128, NT, 8]),
              wuk=din("wuk", [128, 2, 512]), wuv=din("wuv", [128, 2, 512]), gkv=din("gkv", [128, 2]),
              bt=din("bt", [128, 8, 640]), b31=din("b31", [128, 8]), cm=din("cm", [128, 512]),
              yaT=dout("yaT", [64, 8, TOK], BF16))
    da["vscr"] = nc.dram_tensor("vscr", [64, 128, 520], BF16, kind="Internal").ap()
    ident = din("ident", [128, 128])
    with ExitStack() as es:
        K = Ctx(nc, es)
        S = K.S
        idf, b_idf = K.sb(es, [128, 128], F32, "idf")
        idb, b_idb = K.sb(es, [128, 128], BF16, "idb")
        S.dma("sp", lambda e: e.dma_start(out=idf[:], in_=ident[:, :]), writes=[b_idf])
        S.op("dve", lambda e: e.tensor_copy(out=idb[:], in_=idf[:]), reads=[b_idf], writes=[b_idb])
        with ExitStack() as ph:
            emit_mlstm(K, ph, dm, idb, b_idb)
            S.barrier()
            S.flush()
        with ExitStack() as ph:
            emit_attn(K, ph, da, idf, b_idf, idb, b_idb)
            S.barrier()
            S.finish()
            S.flush()
    return nc


def gather_tokens(parts, axis):
    outs = []
    for jj in range(4):
        a = np.moveaxis(np.asarray(parts[jj]), axis, -1)
        outs.append(a.reshape(a.shape[:-1] + (NT, 1, 128)))
    g = np.concatenate(outs, axis=-2)
    g = g.reshape(g.shape[:-3] + (S_LEN,))
    return np.moveaxis(g, -1, axis)


def p2_inputs(inp, r1, core):
    b, j = divmod(core, 4)
    grp = [r1[b * 4 + jj] for jj in range(4)]
    hd = j
    qk = gather_tokens([g["qkmT"] for g in grp], 2)
    vm = gather_tokens([np.asarray(g["vm"]).transpose(1, 0, 2).reshape(TOK, 512) for g in grp], 0)
    sm = gather_tokens([np.asarray(g["small"]).transpose(1, 0, 2).reshape(TOK, 16) for g in grp], 0)
    m = mlstm_inputs_from(qk[:, hd, :], qk[:, 4 + hd, :], vm[:, hd * 128:(hd + 1) * 128], sm[:, 8 + hd], sm[:, 12 + hd], inp, hd)
    rs = gather_tokens([np.asarray(g["rstdkv"]).T.reshape(TOK) for g in grp], 0)
    kid = gather_tokens([g["kidxT"] for g in grp], 1)
    own = r1[core]
    a = {
        "ckvT": np.ascontiguousarray(gather_tokens([g["ckvT"] for g in grp], 2)),
        "rstd": np.ascontiguousarray(rs.reshape(64, 128).T, np.float32),
        "kidx2": np.ascontiguousarray(np.concatenate([kid, kid], axis=0)),
        "qaT": np.ascontiguousarray(own["qaT"]),
        "qidxT": np.ascontiguousarray(own["qidxT"]),
        "widx": np.ascontiguousarray(np.asarray(own["small"])[:, :, 0:8], np.float32),
        "ident": np.eye(128, dtype=np.float32),
    }
    a.update(attn_consts(inp, j))
    a.update(m)
    return a


def p3_inputs(inp, r1, r2, core):
    b, j = divmod(core, 4)
    hm = []
    for hd in range(4):
        h = np.asarray(r2[b * 4 + hd]["hmT"])
        hm.append(h.reshape(128, NT, 4, 128)[:, :, j, :].reshape(128, TOK))
    w_in = np.asarray(inp["w_in"][0], np.float32)
    wa = np.asarray(inp["w_branch_attn"][0], np.float32).reshape(8, 64, D).transpose(1, 0, 2)
    wm = np.asarray(inp["w_branch_mlstm"][0], np.float32).reshape(4, 128, D).transpose(1, 0, 2)
    return {
        "x1": np.ascontiguousarray(r1[core]["x1"]),
        "h2T": np.ascontiguousarray(r1[core]["h2T"]),
        "yaT": np.ascontiguousarray(r2[core]["yaT"]),
        "hmT": np.ascontiguousarray(np.stack(hm, axis=1)),
        "wg": np.ascontiguousarray(w_in[:, C_O:DIN]),
        "wa": np.ascontiguousarray(wa),
        "wm": np.ascontiguousarray(wm),
        "hnT": colT(inp["mlstm_head_norm"][0], 4),
        "wout": np.ascontiguousarray(inp["w_out"][0], np.float32),
        "cT": colT(inp["c"][b], 8),
        "ada_w": np.ascontiguousarray(inp["ada_w"][0], np.float32),
        "ada_b": np.ascontiguousarray(inp["ada_b"][0], np.float32),
        "ada_bT": colT(inp["ada_b"][0], 72),
        "n3T": colT(inp["ffn2_norm"][0], 8),
        "fn": np.ascontiguousarray(inp["final_norm"], np.float32),
        "w1": np.ascontiguousarray(inp["ffn2_w1"][0], np.float32),
        "w3": np.ascontiguousarray(inp["ffn2_w3"][0], np.float32),
        "w2": np.ascontiguousarray(inp["ffn2_w2"][0], np.float32),
        "ident": np.eye(128, dtype=np.float32),
    }


_NC_CACHE = {}


def _prog(name, builder):
    if name not in _NC_CACHE:
        _NC_CACHE[name] = builder()
    return _NC_CACHE[name]


def kernel(**inputs):
    inp = {k: np.asarray(v) for k, v in inputs.items()}
    cores = list(range(NCORES))
    r1 = run_bass_kernel_spmd(_prog("p1", build_p1), [p1_inputs(inp, c) for c in cores], core_ids=cores).results
    r2 = run_bass_kernel_spmd(_prog("p2", build_p2), [p2_inputs(inp, r1, c) for c in cores], core_ids=cores).results
    r3 = run_bass_kernel_spmd(_prog("p3", build_p3), [p3_inputs(inp, r1, r2, c) for c in cores], core_ids=cores).results
    out = np.zeros((2, S_LEN, D), np.float32)
    for c in cores:
        b, j = divmod(c, 4)
        out[b].reshape(NT, 4, 128, D)[:, j] = np.asarray(r3[c]["out"], np.float32).reshape(NT, 128, D)
    return out


NTA = S_LEN // 128
NCA = 1928
NCO = 1032


def emit_proj2(K, ph, ntiles, x1_of, b_x1, wpk, ncols, ssq2, b_ssq2, col0, A2, B2, idb, b_idb, spec):
    S = K.S
    wb, b_wb = K.sb(ph, [128, 8, ncols], BF16, "winb")
    S.dma("pool", lambda e: e.dma_start(out=wb[:], in_=wpk.rearrange("(kc p) n -> p kc n", p=128)), writes=[b_wb])
    rstd2, b_rstd2 = K.sb(ph, [128, ntiles], F32, "rstd2")
    S.op("dve", lambda e: e.tensor_scalar(out=rstd2[:], in0=ssq2[:, col0:col0 + ntiles], scalar1=1.0 / D, scalar2=EPS, op0=ALU.mult, op1=ALU.add),
         reads=[b_ssq2], writes=[b_rstd2])
    S.op("act", lambda e: e.activation(out=rstd2[:], in_=rstd2[:], func=AF.Sqrt), reads=[b_rstd2], writes=[b_rstd2])
    S.op("dve", lambda e: e.reciprocal(out=rstd2[:], in_=rstd2[:]), reads=[b_rstd2], writes=[b_rstd2])
    xts = [K.sb(ph, [128, D], F32, "xt") for _ in range(4)]
    xns = [K.sb(ph, [128, D], BF16, "xn") for _ in range(2)]
    tmpH, b_tmpH = K.sb(ph, [128, 8, 128], F32, "tmpH")
    hTs = [K.sb(ph, [128, 2, 8, 128], BF16, "hT") for _ in range(2)]
    nb, nf = spec["nb"], spec["nf"]
    fm16 = [K.sb(ph, [128, nb, 256], BF16, "fm16") for _ in range(2)]
    fm32 = [K.sb(ph, [128, max(nf, 1), 256], F32, "fm32") for _ in range(2)]
    has_sq = bool(spec.get("sq"))
    if has_sq:
        ones_f, b_ones = K.sb(ph, [128, 1], F32, "ones")
        S.op("dve", lambda e: e.memset(ones_f[:], 1.0), writes=[b_ones])
        sq = [K.sb(ph, [128, 2, 256], F32, "sq") for _ in range(2)]
        ssqkv, b_ssqkv = K.sb(ph, [128, ntiles], F32, "ssqkv")
    vms = [K.sb(ph, [128, 512], BF16, "vms") for _ in range(2)]
    pF = [K.ps(ph, [128, 512], F32, "pF") for _ in range(3)]
    pV = [K.ps(ph, [128, 512], F32, "pV") for _ in range(2)]
    pS, b_pS = K.ps(ph, [128, 512], F32, "pS")
    pT, b_pT = K.ps(ph, [128, 8, 128], BF16, "pT")
    b_AB = Buf("AB2")
    NG = ntiles // 2
    small, b_small = spec["small_sb"]
    cnt = [0]

    def prep(g):
        hT, b_hT = hTs[g % 2]
        for tt in range(2):
            t = 2 * g + tt
            xt, b_xt = xts[t % 4]
            xn, b_xn = xns[tt]
            if spec.get("xsel"):
                spec["xsel"](t, xt, b_xt)
            else:
                S.dma("sp", lambda e, xt=xt, t=t: e.dma_start(out=xt[:], in_=x1_of(t)), reads=[b_x1[t]], writes=[b_xt])
            emit_norm_T(K, xt, b_xt, rstd2[:, t:t + 1], b_rstd2, xn, b_xn, pT, b_pT, idb, b_idb, tmpH, b_tmpH,
                        A2, B2, b_AB, hT[:, tt, :, :], b_hT)
            if spec.get("out_h2T"):
                spec["out_h2T"](t, hT, tt, b_hT)

    def body(g):
        hT, b_hT = hTs[g % 2]
        f16, b_f16 = fm16[g % 2]
        f32, b_f32 = fm32[g % 2]
        if has_sq:
            sqt, b_sq = sq[g % 2]
        for (c0, kind, di) in spec["fm"]:
            pf, b_pf = pF[cnt[0] % 3]
            cnt[0] += 1
            for kc in range(8):
                S.op("pe", lambda e, pf=pf, c0=c0, kc=kc, hT=hT: e.matmul(
                    pf[:, 0:256].rearrange("p (a b) -> p a b", a=2), lhsT=wb[:, kc, c0:c0 + 128], rhs=hT[:, :, kc, :],
                    start=(kc == 0), stop=(kc == 7)), reads=[b_wb, b_hT], writes=[b_pf])
            if kind == "b":
                rd = [b_pf]
                if has_sq and di in spec["sq"]:
                    S.op("act", lambda e, pf=pf, di=di, sqt=sqt: e.activation(out=sqt[:, spec["sq"].index(di), :], in_=pf[:, 0:256], func=AF.Square),
                         reads=[b_pf], writes=[b_sq])
                    rd = [b_pf, b_sq]
                S.op("dve", lambda e, pf=pf, di=di, f16=f16: e.tensor_copy(out=f16[:, di, :], in_=pf[:, 0:256]), reads=rd, writes=[b_f16])
            else:
                S.op("act", lambda e, pf=pf, di=di, f32=f32: e.activation(out=f32[:, di, :], in_=pf[:, 0:256], func=AF.Identity),
                     reads=[b_pf], writes=[b_f32])
        if g + 1 < NG:
            prep(g + 1)
        for tt in range(2):
            t = 2 * g + tt
            if spec.get("vm_col") is not None:
                pv, b_pv = pV[tt]
                vc = spec["vm_col"]
                for kc in range(8):
                    S.op("pe", lambda e, pv=pv, kc=kc, hT=hT, tt=tt, vc=vc: e.matmul(
                        pv[:, :], lhsT=hT[:, tt, kc, :], rhs=wb[:, kc, vc:vc + 512], start=(kc == 0), stop=(kc == 7)),
                        reads=[b_wb, b_hT], writes=[b_pv])
                vmt, b_vmt = vms[tt]
                S.op("act", lambda e, pv=pv, vmt=vmt: e.activation(out=vmt[:], in_=pv[:], func=AF.Identity), reads=[b_pv], writes=[b_vmt])
                spec["out_vm"](t, vmt, b_vmt)
            sc0, sn = spec["small"]
            for kc in range(8):
                S.op("pe", lambda e, kc=kc, hT=hT, tt=tt, sc0=sc0, sn=sn: e.matmul(
                    pS[:, 0:sn], lhsT=hT[:, tt, kc, :], rhs=wb[:, kc, sc0:sc0 + sn], start=(kc == 0), stop=(kc == 7)),
                    reads=[b_wb, b_hT], writes=[b_pS])
            if has_sq:
                for c in range(2):
                    S.op("pe", lambda e, c=c, tt=tt, sqt=sqt: e.matmul(
                        pS[:, 16:17], lhsT=sqt[:, c, tt * 128:(tt + 1) * 128], rhs=ones_f[:, 0:1], start=(c == 0), stop=(c == 1)),
                        reads=[b_sq, b_ones], writes=[b_pS])
            S.op("dve", lambda e, t=t, sn=sn: e.tensor_copy(out=small[:, t, 0:sn], in_=pS[:, 0:sn]), reads=[b_pS], writes=[b_small])
            if has_sq:
                S.op("dve", lambda e, t=t: e.tensor_copy(out=ssqkv[:, t:t + 1], in_=pS[:, 16:17]), reads=[b_pS], writes=[b_ssqkv])
        spec["out_fm"](g, f16, b_f16, f32, b_f32)

    prep(0)
    for g in range(NG):
        body(g)
    if has_sq:
        rk, b_rk = spec["rstdkv"]
        emit_rstd(K, ssqkv, b_ssqkv, rk, b_rk, 256)


def build_fused():
    nc = bass.Bass("TRN2", target_bir_lowering=False)

    def din(name, shape, dt=F32):
        return nc.dram_tensor(name, list(shape), dt, kind="ExternalInput").ap()

    def scr(name, shape, dt=F32):
        return nc.dram_tensor(name, list(shape), dt, kind="Internal").ap()

    x_all = din("x_all", [S_LEN, D])
    cT = din("cT", [128, 8])
    ada_w = din("ada_w", [D, 9 * D])
    ada_b = din("ada_b", [9 * D])
    ada_bT = din("ada_bT", [128, 72])
    n1T, n2T, n3T = din("n1T", [128, 8]), din("n2T", [128, 8]), din("n3T", [128, 8])
    fn = din("fn", [D])
    w1, w3, w2 = din("w1", [D, DFF]), din("w3", [D, DFF]), din("w2", [DFF, D])
    f2w1, f2w3, f2w2 = din("f2w1", [D, DFF]), din("f2w3", [D, DFF]), din("f2w2", [DFF, D])
    wA, wO = din("wA", [D, NCA]), din("wO", [D, NCO])
    ident = din("ident", [128, 128])
    oh_d = din("oh", [128, 4])
    dm_c = dict(gb4=din("gb4", [64, 8]), cw4=din("cw4", [128, 4, 8]), cb4=din("cb4", [128, 4, 2]), triu=din("triu", [64, 64]))
    da = dict(wuk=din("wuk", [128, 2, 512]), wuv=din("wuv", [128, 2, 512]), gkv=din("gkv", [128, 2]),
              bt=din("bt", [128, 8, 640]), b31=din("b31", [128, 8]), cm=din("cm", [128, 512]))
    d3 = dict(wg=din("wg", [D, 2560]), wa=din("wa", [64, 8, D]), wm=din("wm", [128, 4, D]), hnT=din("hnT", [128, 4]), wout=din("wout", [D, D]))
    out = nc.dram_tensor("out", [TOK, D], F32, kind="ExternalOutput").ap()

    NTF = NTA + NT
    x1s = scr("x1s", [NTF * 128, D])
    ckvT_s = scr("ckvT_s", [128, 2, S_LEN], BF16)
    kidx2_s = scr("kidx2_s", [128, S_LEN], BF16)
    qkm_s = scr("qkm_s", [128, 8, S_LEN + 3], BF16)
    vm_s = scr("vm_s", [NTA, 128, 512], BF16)
    hm_s = scr("hm_s", [4, 128, S_LEN], BF16)
    qaT_s = scr("qaT_s", [128, 4, TOK], BF16)
    qidxT_s = scr("qidxT_s", [128, 4, TOK], BF16)
    h2T_s = scr("h2T_s", [NT, 128, 8, 128], BF16)
    ya_s = scr("ya_s", [64, 8, TOK], BF16)
    x2_d = scr("x2s", [TOK, D])
    x3_d = scr("x3s", [TOK, D])
    da["vscr"] = scr("vscr", [64, 128, 520], BF16)

    with ExitStack() as es:
        K = Ctx(nc, es)
        S = K.S
        idf, b_idf = K.sb(es, [128, 128], F32, "idf")
        idb, b_idb = K.sb(es, [128, 128], BF16, "idb")
        cT_sb, b_cT = K.sb(es, [128, 8], F32, "cT")
        abT_sb, b_abT = K.sb(es, [128, 72], F32, "abT")
        nsb = [K.sb(es, [128, 8], F32, "nrm") for _ in range(3)]
        modT, b_modT = K.sb(es, [128, 72], F32, "modT")
        As = [K.sb(es, [128, 8], F32, "Amod") for _ in range(3)]
        ssq2, b_ssq2 = K.sb(es, [128, NTF], F32, "ssq2")
        oh, b_oh = K.sb(es, [128, 4], F32, "oh")
        ssqF, b_ssqF = K.sb(es, [128, NT], F32, "ssqF")
        rstdF, b_rstdF = K.sb(es, [128, NT], F32, "rstdF")
        for (t_, b_, src) in ((idf, b_idf, ident), (cT_sb, b_cT, cT), (abT_sb, b_abT, ada_bT), (nsb[0][0], nsb[0][1], n1T),
                              (nsb[1][0], nsb[1][1], n2T), (nsb[2][0], nsb[2][1], n3T), (oh, b_oh, oh_d)):
            S.dma("sp", lambda e, t_=t_, src=src: e.dma_start(out=t_[:], in_=src), writes=[b_])
        S.op("dve", lambda e: e.tensor_copy(out=idb[:], in_=idf[:]), reads=[b_idf], writes=[b_idb])

        st_g1 = ExitStack()
        g1row, b_g1row = K.sb(st_g1, [128, D], F32, "g1row")
        S.barrier()
        wts1 = alloc_ffn_weights(K, st_g1, w1, w3, w2)
        with ExitStack() as ph:
            emit_mod(K, ph, ada_w, abT_sb, cT_sb, modT, b_modT, cols=[0, 1, 3, 4, 6, 7],
                     rows={2: (g1row, b_g1row, 0.5)}, ada_b_dram=ada_b, ident_f=idf, b_ident=b_idf)
            for q, (sc0, _) in enumerate(((8, 0), (32, 0), (56, 0))):
                S.op("dve", lambda e, q=q, sc0=sc0: e.scalar_tensor_tensor(out=As[q][0][:], in0=modT[:, sc0:sc0 + 8], scalar=1.0, in1=nsb[q][0][:],
                                                                            op0=ALU.add, op1=ALU.mult), reads=[b_modT, nsb[q][1]], writes=[As[q][1]])
            S.flush()
        b_x1 = [Buf("x1_%d" % t) for t in range(NTF)]
        with ExitStack() as ph:
            junk2, b_junk2 = K.sb(ph, [128, D], BF16, "junk2")

            def epi1(t, xo, b_xo):
                S.dma("sp", lambda e: e.dma_start(out=x1s[t * 128:(t + 1) * 128, :], in_=xo[:]), reads=[b_xo], writes=[b_x1[t]])
                S.op("act", lambda e: e.activation(out=junk2[:], in_=xo[:], func=AF.Square, accum_out=ssq2[:, t:t + 1]),
                     reads=[b_xo], writes=[b_junk2, b_ssq2])

            emit_ffn(K, ph, lambda t: x_all[t * 128:(t + 1) * 128, :], w1, w3, w2, As[0][0][:, :], modT[:, 0:8], Buf("AB1"),
                     g1row, b_g1row, idb, b_idb, epi1, ntiles=NTA, wts=wts1)
            S.barrier()
            S.flush()
        st_g1.close()

        IFt, b_IFt = K.sb(es, [128, 8, NTA], F32, "IFt")
        IFall, b_IFall = IFt[:, :, :].rearrange("p c t -> p t c"), b_IFt
        widx_sb, b_widx_sb = K.sb(es, [128, NT, 8], F32, "widx_sb")
        rstdkv, b_rstdkv = K.sb(es, [128, NTA], F32, "rstdkv")
        zpad, b_zpad = K.sb(es, [128, 8, 3], BF16, "zpad")
        S.op("dve", lambda e: e.memset(zpad[:], 0.0), writes=[b_zpad])
        S.dma("sp", lambda e: e.dma_start(out=qkm_s[:, :, 0:3], in_=zpad[:]), reads=[b_zpad])

        with ExitStack() as ph:
            spec = dict(fm=[(0, "b", 0), (128, "b", 1), (256, "b", 2)] + [(384 + 128 * i, "b", 3 + i) for i in range(8)],
                        nb=11, nf=0, sq=[0, 1], vm_col=1408, small=(1920, 8), small_sb=(IFall, b_IFall), rstdkv=(rstdkv, b_rstdkv))

            def out_fm(g, f16, b_f16, f32, b_f32):
                tk = slice(g * 256, (g + 1) * 256)
                S.dma("sp", lambda e: e.dma_start(out=ckvT_s[:, :, tk], in_=f16[:, 0:2, :]), reads=[b_f16])
                S.dma("sp", lambda e: e.dma_start(out=kidx2_s[:, tk], in_=f16[:, 2, :]), reads=[b_f16])
                S.dma("sp", lambda e: e.dma_start(out=qkm_s[:, :, 3 + g * 256:3 + (g + 1) * 256], in_=f16[:, 3:11, :]), reads=[b_f16])

            def out_vm(t, vmt, b_vmt):
                S.dma("sp", lambda e: e.dma_start(out=vm_s[t], in_=vmt[:]), reads=[b_vmt])

            spec["out_fm"], spec["out_vm"] = out_fm, out_vm
            emit_proj2(K, ph, NTA, lambda t: x1s[t * 128:(t + 1) * 128, :], b_x1[0:NTA], wA, NCA, ssq2, b_ssq2, 0,
                       As[1][0][:, :], modT[:, 24:32], idb, b_idb, spec)
            S.barrier()
            S.flush()
        with ExitStack() as ph:
            spec = dict(fm=[(128 * i, "b", i) for i in range(8)], nb=8, nf=0, vm_col=None, small=(1024, 8), small_sb=(widx_sb, b_widx_sb))

            def out_fm2(g, f16, b_f16, f32, b_f32):
                tk = slice(g * 256, (g + 1) * 256)
                S.dma("sp", lambda e: e.dma_start(out=qaT_s[:, :, tk], in_=f16[:, 0:4, :]), reads=[b_f16])
                S.dma("sp", lambda e: e.dma_start(out=qidxT_s[:, :, tk], in_=f16[:, 4:8, :]), reads=[b_f16])

            def out_h2T(t, hT, tt, b_hT):
                S.dma("sp", lambda e: e.dma_start(out=h2T_s[t], in_=hT[:, tt, :, :]), reads=[b_hT])

            cands1 = [K.sb(ph, [128, 4, D], F32, "x1cand") for _ in range(2)]
            for jj in range(4):
                v = ssq2[:, 0:NTA].rearrange("p (t j) -> p t j", j=4)[:, :, jj]
                if jj == 0:
                    S.op("dve", lambda e, v=v: e.tensor_scalar(out=ssq2[:, NTA:NTF], in0=v, scalar1=oh[:, 0:1], scalar2=None, op0=ALU.mult),
                         reads=[b_ssq2, b_oh], writes=[b_ssq2])
                else:
                    S.op("dve", lambda e, v=v, jj=jj: e.scalar_tensor_tensor(out=ssq2[:, NTA:NTF], in0=v, scalar=oh[:, jj:jj + 1], in1=ssq2[:, NTA:NTF],
                                                                              op0=ALU.mult, op1=ALU.add), reads=[b_ssq2, b_oh], writes=[b_ssq2])

            def xsel(t, xt, b_xt):
                cd, b_cd = cands1[t % 2]
                S.dma("sp", lambda e: e.dma_start(out=cd[:], in_=x1s[4 * t * 128:(4 * t + 4) * 128, :].rearrange("(j p) d -> p j d", p=128)),
                      reads=b_x1[4 * t:4 * t + 4], writes=[b_cd])
                S.op("dve", lambda e: e.tensor_scalar(out=xt[:], in0=cd[:, 0, :], scalar1=oh[:, 0:1], scalar2=None, op0=ALU.mult),
                     reads=[b_cd, b_oh], writes=[b_xt])
                for jj in range(1, 4):
                    S.op("dve", lambda e, jj=jj: e.scalar_tensor_tensor(out=xt[:], in0=cd[:, jj, :], scalar=oh[:, jj:jj + 1], in1=xt[:],
                                                                         op0=ALU.mult, op1=ALU.add), reads=[b_cd, b_oh, b_xt], writes=[b_xt])
                S.dma("sp", lambda e: e.dma_start(out=x1s[(NTA + t) * 128:(NTA + t + 1) * 128, :], in_=xt[:]), reads=[b_xt], writes=[b_x1[NTA + t]])

            spec["xsel"] = xsel
            spec["out_fm"], spec["out_h2T"] = out_fm2, out_h2T
            emit_proj2(K, ph, NT, lambda t: x1s[(NTA + t) * 128:(NTA + t + 1) * 128, :], b_x1[NTA:NTF], wO, NCO, ssq2, b_ssq2, NTA,
                       As[1][0][:, :], modT[:, 24:32], idb, b_idb, spec)
            S.barrier()
            S.flush()

        with ExitStack() as ph:
            emit_mlstm4(K, ph, dm_c, qkm_s, vm_s, hm_s, IFt, b_IFt, idb, b_idb)
            S.barrier()
            S.flush()

        with ExitStack() as ph:
            da.update(ckvT=ckvT_s, rstd=rstdkv[:, :], kidx2=kidx2_s, qaT=qaT_s, qidxT=qidxT_s, widx=widx_sb[:, :, 0:8], yaT=ya_s)
            emit_attn(K, ph, da, idf, b_idf, idb, b_idb)
            S.barrier()
            S.flush()

        g2row, b_g2row = K.sb(es, [128, D], F32, "g2row")
        g3row, b_g3row = K.sb(es, [128, D], F32, "g3row")
        with ExitStack() as ph:
            emit_mod(K, ph, ada_w, abT_sb, cT_sb, modT, b_modT, cols=[],
                     rows={5: (g2row, b_g2row, 1.0), 8: (g3row, b_g3row, 0.5)}, ada_b_dram=ada_b, ident_f=idf, b_ident=b_idf)
            S.barrier()
            S.flush()
        b_x2 = [Buf("x2_%d" % t) for t in range(NT)]
        b_x3 = [Buf("x3_%d" % t) for t in range(NT)]
        with ExitStack() as ph:
            d3.update(x1=x1s[NTA * 128:NTF * 128, :], h2T=h2T_s, yaT=ya_s)
            cands = [K.sb(ph, [128, 4, 512], BF16, "cand") for _ in range(2)]
            emit_merge(K, ph, d3, g2row, b_g2row, x2_d, b_x2, sel=(oh, b_oh, hm_s, cands))
            S.barrier()
            S.flush()
        with ExitStack() as ph:
            junk2, b_junk2 = K.sb(ph, [128, D], BF16, "junk2")

            def epi2(t, xo, b_xo):
                S.dma("sp", lambda e: e.dma_start(out=x3_d[t * 128:(t + 1) * 128, :], in_=xo[:]), reads=[b_xo], writes=[b_x3[t]])
                S.op("act", lambda e: e.activation(out=junk2[:], in_=xo[:], func=AF.Square, accum_out=ssqF[:, t:t + 1]),
                     reads=[b_xo], writes=[b_junk2, b_ssqF])

            emit_ffn(K, ph, lambda t: x2_d[t * 128:(t + 1) * 128, :], f2w1, f2w3, f2w2, As[2][0][:, :], modT[:, 48:56], Buf("AB3"),
                     g3row, b_g3row, idb, b_idb, epi2, src_bufs=b_x2)
            S.barrier()
            S.flush()
        with ExitStack() as ph:
            fnrow, b_fnrow = K.sb(ph, [128, D], F32, "fnrow")
            fn1, b_fn1 = K.sb(ph, [1, D], F32, "fn1")
            on1, b_on1 = K.sb(ph, [1, 128], F32, "on1")
            pF, b_pF = K.ps(ph, [128, D], F32, "pFn")
            S.op("dve", lambda e: e.memset(on1[:], 1.0), writes=[b_on1])
            S.dma("sp", lambda e: e.dma_start(out=fn1[:], in_=fn.rearrange("(a n) -> a n", a=1)), writes=[b_fn1])
            for nt in range(2):
                S.op("pe", lambda e, nt=nt: e.matmul(pF[:, nt * 512:(nt + 1) * 512], lhsT=on1[0:1, :], rhs=fn1[0:1, nt * 512:(nt + 1) * 512],
                                                    start=True, stop=True), reads=[b_on1, b_fn1], writes=[b_pF])
            S.op("dve", lambda e: e.tensor_copy(out=fnrow[:], in_=pF[:]), reads=[b_pF], writes=[b_fnrow])
            emit_rstd(K, ssqF, b_ssqF, rstdF, b_rstdF, D)
            xf = [K.sb(ph, [128, D], F32, "xf") for _ in range(3)]
            for t in range(NT):
                xt, b_xt = xf[t % 3]
                S.dma("sp", lambda e, xt=xt, t=t: e.dma_start(out=xt[:], in_=x3_d[t * 128:(t + 1) * 128, :]), reads=[b_x3[t]], writes=[b_xt])
                S.op("dve", lambda e, xt=xt, t=t: e.scalar_tensor_tensor(out=xt[:], in0=xt[:], scalar=rstdF[:, t:t + 1], in1=fnrow[:],
                                                                          op0=ALU.mult, op1=ALU.mult), reads=[b_xt, b_rstdF, b_fnrow], writes=[b_xt])
                S.dma("sp", lambda e, xt=xt, t=t: e.dma_start(out=out[t * 128:(t + 1) * 128, :], in_=xt[:]), reads=[b_xt])
            S.barrier()
            S.finish()
            S.flush()
        print("fused program: %d instructions" % S.ninst, {k: S.cnt[k] for k in S.cnt})
        nc._phase_marks = S.marks
    return nc


def fused_inputs(inp, core):
    b, j = divmod(core, 4)
    w_in = np.asarray(inp["w_in"][0], np.float32)
    colsA = np.concatenate([np.arange(C_CKV, C_CKV + 256), np.arange(C_KIDX, C_KIDX + 64), np.arange(C_KIDX, C_KIDX + 64),
                            np.arange(C_QM, C_VM + 512), np.arange(C_I, C_I + 8)])
    colsO = np.concatenate([np.arange(C_QA, C_QA + 512), np.arange(C_QIDX, C_QIDX + 512), np.arange(C_WIDX, C_WIDX + 8)])
    assert colsA.size == NCA and colsO.size == NCO
    gbv = np.asarray(inp["mlstm_gate_bias"][0], np.float32)
    cwv = np.asarray(inp["conv_w"][0], np.float32)
    cbv = np.asarray(inp["conv_b"][0], np.float32)
    gb4 = np.zeros((64, 8), np.float32)
    cw4 = np.zeros((128, 4, 8), np.float32)
    cb4 = np.zeros((128, 4, 2), np.float32)
    for hd in range(4):
        gb4[:, 2 * hd] = gbv[hd]
        gb4[:, 2 * hd + 1] = gbv[4 + hd]
        cw4[:, hd, 0:4] = cwv[:, hd * 128:(hd + 1) * 128].T
        cw4[:, hd, 4:8] = cwv[:, 512 + hd * 128:512 + (hd + 1) * 128].T
        cb4[:, hd, 0] = cbv[hd * 128:(hd + 1) * 128]
        cb4[:, hd, 1] = cbv[512 + hd * 128:512 + (hd + 1) * 128]
    ohv = np.zeros((128, 4), np.float32)
    ohv[:, j] = 1.0
    wa = np.asarray(inp["w_branch_attn"][0], np.float32).reshape(8, 64, D).transpose(1, 0, 2)
    wm = np.asarray(inp["w_branch_mlstm"][0], np.float32).reshape(4, 128, D).transpose(1, 0, 2)
    xb = np.ascontiguousarray(inp["x"][b], np.float32)
    r = {
        "x_all": xb,
        "cT": colT(inp["c"][b], 8),
        "ada_w": np.ascontiguousarray(inp["ada_w"][0], np.float32),
        "ada_b": np.ascontiguousarray(inp["ada_b"][0], np.float32),
        "ada_bT": colT(inp["ada_b"][0], 72),
        "n1T": colT(inp["ffn1_norm"][0], 8), "n2T": colT(inp["mix_norm"][0], 8), "n3T": colT(inp["ffn2_norm"][0], 8),
        "fn": np.ascontiguousarray(inp["final_norm"], np.float32),
        "w1": np.ascontiguousarray(inp["ffn1_w1"][0], np.float32), "w3": np.ascontiguousarray(inp["ffn1_w3"][0], np.float32),
        "w2": np.ascontiguousarray(inp["ffn1_w2"][0], np.float32),
        "f2w1": np.ascontiguousarray(inp["ffn2_w1"][0], np.float32), "f2w3": np.ascontiguousarray(inp["ffn2_w3"][0], np.float32),
        "f2w2": np.ascontiguousarray(inp["ffn2_w2"][0], np.float32),
        "wA": np.ascontiguousarray(w_in[:, colsA]), "wO": np.ascontiguousarray(w_in[:, colsO]),
        "wg": np.ascontiguousarray(w_in[:, C_O:DIN]),
        "ident": np.eye(128, dtype=np.float32), "oh": ohv,
        "gb4": gb4, "cw4": cw4, "cb4": cb4, "triu": np.triu(np.ones((64, 64), np.float32)),
        "wa": np.ascontiguousarray(wa), "wm": np.ascontiguousarray(wm), "hnT": colT(inp["mlstm_head_norm"][0], 4),
        "wout": np.ascontiguousarray(inp["w_out"][0], np.float32),
    }
    r.update(attn_consts(inp, j))
    return r


def kernel(**inputs):
    inp = {k: np.asarray(v) for k, v in inputs.items()}
    cores = list(range(NCORES))
    res = run_bass_kernel_spmd(_prog("fused", build_fused), [fused_inputs(inp, c) for c in cores], core_ids=cores).results
    out = np.zeros((2, S_LEN, D), np.float32)
    for c in cores:
        b, j = divmod(c, 4)
        out[b].reshape(NT, 4, 128, D)[:, j] = np.asarray(res[c]["out"], np.float32).reshape(NT, 128, D)
    return out


def emit_mlstm4(K, ph, dmc, qkm_s, vm_s, hm_s, IFt, b_IFt, ident_b, b_identb):
    S = K.S
    GS = 16
    NSEG = NCH // GS
    SEG = GS * 64
    col = lambda c: (c % 2) * 64 + c // 2
    lcol = lambda cl: (cl % 2) * 8 + cl // 2
    triu, b_triu = K.sb(ph, [64, 64], F32, "triu")
    ones64, b_ones64 = K.sb(ph, [64, 128], F32, "ones64")
    S.dma("sp", lambda e: e.dma_start(out=triu[:], in_=dmc["triu"]), writes=[b_triu])
    S.op("dve", lambda e: e.memset(ones64[:], 1.0), writes=[b_ones64])
    gb4, b_gb4 = K.sb(ph, [64, 8], F32, "gb4")
    cw4, b_cw4 = K.sb(ph, [128, 4, 8], F32, "cw4")
    cb4, b_cb4 = K.sb(ph, [128, 4, 2], F32, "cb4")
    for (t, b, src) in ((gb4, b_gb4, "gb4"), (cw4, b_cw4, "cw4"), (cb4, b_cb4, "cb4")):
        S.dma("sp", lambda e, t=t, src=src: e.dma_start(out=t[:], in_=dmc[src]), writes=[b])
    lnk, b_lnk = K.sb(ph, [64, 1], F32, "lnk")
    S.op("dve", lambda e: e.memset(lnk[:], float(np.log(128 ** -0.5))), writes=[b_lnk])
    diagW, b_diagW = K.sb(ph, [128, 32, 128], BF16, "diagW")
    pG1, b_pG1 = K.ps(ph, [128, 512], F32, "pG1")
    pG2, b_pG2 = K.ps(ph, [128, 512], F32, "pG2")
    pSt = [K.ps(ph, [64, 512], F32, "pSt")] * 2
    pCv, b_pCv = K.ps(ph, [128, 512], F32, "pCv")
    pND = [K.ps(ph, [64, 512], F32, "pND") for _ in range(2)]
    pU = [(pG1, b_pG1), (pG2, b_pG2)]
    pKt, b_pKt = K.ps(ph, [64, 8, 128], BF16, "pKt")
    pTr2, b_pTr2 = K.ps(ph, [128, GS * 64], BF16, "pTr2")
    H = []
    for hd in range(4):
        h = {}
        for n_ in ("Ig", "Fg", "Am", "Gm"):
            h[n_] = K.sb(ph, [64, NCH], F32, n_)
        h["GL"] = K.sb(ph, [128, NCH], F32, "GL")
        h["ngb"] = K.sb(ph, [64, 1], F32, "ngb")
        h["QT"] = [K.sb(ph, [128, SEG], BF16, "QTs") for _ in range(2)]
        h["KT"] = [K.sb(ph, [128, SEG], BF16, "KTs") for _ in range(2)]
        h["Ktok"] = K.sb(ph, [64, GS, 128], BF16, "Ktok")
        h["Vaug"] = K.sb(ph, [64, GS, 129], BF16, "Vaug")
        h["aV"] = K.sb(ph, [64, GS, 129], BF16, "aV")
        h["Hs"] = K.sb(ph, [64, GS, 128], F32, "Hs")
        h["C"] = [K.sb(ph, [128, 129], F32, "Cst") for _ in range(2)]
        h["Cb"] = [K.sb(ph, [128, 129], BF16, "Cbf") for _ in range(2)]
        h["t1"] = [K.sb(ph, [64, 2], F32, "t1") for _ in range(2)]
        H.append(h)
        Ig, b_Ig = h["Ig"]
        Fg, b_Fg = h["Fg"]
        Am, b_A = h["Am"]
        Gm, b_G = h["Gm"]
        GL, b_GL = h["GL"]
        ngb, b_ngb = h["ngb"]
        Vaug, b_Vaug = h["Vaug"]
        for (t, b, q) in ((Ig, b_Ig, hd), (Fg, b_Fg, 4 + hd)):
            S.op("dve", lambda e, t=t, q=q: e.tensor_copy(out=t[:, 0:64], in_=IFt[0:64, q, :]), reads=[b_IFt], writes=[b])
            S.dma("sp", lambda e, t=t, q=q: e.dma_start(out=t[:, 64:128], in_=IFt[64:128, q, :]), reads=[b_IFt], writes=[b])
        S.op("dve", lambda e, Vaug=Vaug: e.memset(Vaug[:], 1.0), writes=[b_Vaug])
        S.op("dve", lambda e, ngb=ngb, hd=hd: e.tensor_scalar(out=ngb[:], in0=gb4[:, 2 * hd + 1:2 * hd + 2], scalar1=-1.0, scalar2=None, op0=ALU.mult),
             reads=[b_gb4], writes=[b_ngb])
        S.op("act", lambda e, Fg=Fg, ngb=ngb: e.activation(out=Fg[:], in_=Fg[:], func=AF.Exp, scale=-1.0, bias=ngb[:, 0:1]),
             reads=[b_Fg, b_ngb], writes=[b_Fg])
        S.op("act", lambda e, Fg=Fg: e.activation(out=Fg[:], in_=Fg[:], func=AF.Ln, bias=1.0), reads=[b_Fg], writes=[b_Fg])
        S.op("dve", lambda e, Fg=Fg: e.tensor_scalar(out=Fg[:], in0=Fg[:], scalar1=-1.0, scalar2=None, op0=ALU.mult), reads=[b_Fg], writes=[b_Fg])
        S.op("pe", lambda e, Fg=Fg: e.matmul(pG1[0:64, 0:NCH], lhsT=triu[:, :], rhs=Fg[:, :], start=True, stop=True),
             reads=[b_triu, b_Fg], writes=[b_pG1])
        S.op("pe", lambda e, Fg=Fg: e.matmul(pG2[:, 0:NCH], lhsT=ones64[:, :], rhs=Fg[:, :], start=True, stop=True),
             reads=[b_ones64, b_Fg], writes=[b_pG2])
        S.op("dve", lambda e, Am=Am, Ig=Ig, hd=hd: e.scalar_tensor_tensor(out=Am[:], in0=Ig[:], scalar=gb4[:, 2 * hd:2 * hd + 1], in1=pG1[0:64, 0:NCH],
                                                                          op0=ALU.add, op1=ALU.subtract), reads=[b_Ig, b_gb4, b_pG1], writes=[b_A])
        S.op("act", lambda e, Am=Am: e.activation(out=Am[:], in_=Am[:], func=AF.Exp, bias=lnk[:, 0:1]), reads=[b_A, b_lnk], writes=[b_A])
        S.op("act", lambda e, Gm=Gm: e.activation(out=Gm[:], in_=pG1[0:64, 0:NCH], func=AF.Exp), reads=[b_pG1], writes=[b_G])
        S.op("act", lambda e, GL=GL: e.activation(out=GL[:], in_=pG2[:, 0:NCH], func=AF.Exp), reads=[b_pG2], writes=[b_GL])
        S.op("dve", lambda e, h=h: e.memset(h["C"][1][0][:], 0.0), writes=[h["C"][1][1]])

    for hd in range(4):
        for q in range(8):
            S.op("dve", lambda e, hd=hd, q=q: e.tensor_scalar(out=diagW[:, hd * 8 + q, :], in0=ident_b[:, :], scalar1=cw4[:, hd, q:q + 1], scalar2=None,
                                                             op0=ALU.mult), reads=[b_identb, b_cw4], writes=[b_diagW])
    xs = [K.sb(ph, [128, SEG + 3], BF16, "xs") for _ in range(3)]
    Hq, b_Hq = K.sb(ph, [64, GS, 128], F32, "Hq")
    Hn, b_Hn = K.sb(ph, [64, GS, 128], BF16, "Hn")
    mu, b_mu = K.sb(ph, [64, GS], F32, "mu")
    var, b_var = K.sb(ph, [64, GS], F32, "var")
    hseg, b_hseg = K.sb(ph, [128, GS * 64], BF16, "hseg")
    n = [0]

    def prep_seg(sg):
        par = sg % 2
        for hd in range(4):
            h = H[hd]
            for which in range(2):
                dst, b_dst = (h["QT"] if which == 0 else h["KT"])[par]
                x_, b_x = xs[n[0] % 3]
                n[0] += 1
                S.dma("sp", lambda e, x_=x_, which=which, hd=hd: e.dma_start(out=x_[:], in_=qkm_s[:, 4 * which + hd, sg * SEG:sg * SEG + SEG + 3]),
                      writes=[b_x])
                for hf in range(SEG // 512):
                    for w in range(4):
                        S.op("pe", lambda e, x_=x_, which=which, hd=hd, w=w, hf=hf: e.matmul(
                            pCv[:, :], lhsT=diagW[:, hd * 8 + 4 * which + w, :], rhs=x_[:, hf * 512 + w:hf * 512 + w + 512],
                            start=(w == 0), stop=(w == 3)), reads=[b_diagW, b_x], writes=[b_pCv])
                    S.op("act", lambda e, dst=dst, hd=hd, which=which, hf=hf: e.activation(
                        out=dst[:, hf * 512:(hf + 1) * 512], in_=pCv[:, :], func=AF.Silu, bias=cb4[:, hd, which:which + 1]),
                        reads=[b_pCv, b_cb4], writes=[b_dst])

    def load_seg(sg):
        par = sg % 2
        for hd in range(4):
            h = H[hd]
            KT, b_KT = h["KT"][par]
            Ktok, b_Ktok = h["Ktok"]
            Vaug, b_Vaug = h["Vaug"]
            aV, b_aV = h["aV"]
            Am, b_A = h["Am"]
            for g8 in range(GS // 8):
                for cc in range(8):
                    cl = g8 * 8 + cc
                    S.op("pe", lambda e, cl=cl, cc=cc, KT=KT: e.transpose(out=pKt[:, cc, :], in_=KT[:, cl * 64:(cl + 1) * 64], identity=ident_b[:]),
                         reads=[b_KT, b_identb], writes=[b_pKt])
                S.op("dve", lambda e, g8=g8, Ktok=Ktok: e.tensor_copy(out=Ktok[:, g8 * 8:(g8 + 1) * 8, :], in_=pKt[:]), reads=[b_pKt], writes=[b_Ktok])
            for h2 in range(2):
                S.dma("sp", lambda e, h2=h2, Vaug=Vaug, hd=hd: e.dma_start(
                    out=Vaug[:, h2 * 8:(h2 + 1) * 8, 0:128],
                    in_=vm_s[sg * 8:(sg + 1) * 8, h2 * 64:(h2 + 1) * 64, hd * 128:(hd + 1) * 128].rearrange("t s c -> s t c")), writes=[b_Vaug])
            for h2 in range(2):
                S.op("dve", lambda e, h2=h2, aV=aV, Vaug=Vaug, Am=Am: e.tensor_tensor(
                    out=aV[:, h2 * 8:(h2 + 1) * 8, :], in0=Vaug[:, h2 * 8:(h2 + 1) * 8, :],
                    in1=bc_last(Am[:, h2 * 64 + sg * 8:h2 * 64 + sg * 8 + 8], 129), op=ALU.mult), reads=[b_Vaug, b_A], writes=[b_aV])

    scnt = [0]
    Gm4, b_Gm4 = K.sb(ph, [64, NCH, 4], F32, "Gm4")
    for hd in range(4):
        S.op("dve", lambda e, hd=hd: e.tensor_copy(out=Gm4[:, :, hd], in_=H[hd]["Gm"][0][:, :]), reads=[H[hd]["Gm"][1]], writes=[b_Gm4])
    t1all, b_t1all = K.sb(ph, [64, 2, 2, 4], F32, "t1all")
    Sps = [K.sb(ph, [64, 64], BF16, "Sp8") for _ in range(8)]
    tmpCs = [K.sb(ph, [128, 129], F32, "tmpC4") for _ in range(4)]

    def step4(c, cl, par):
        gc = col(c)
        vc = lcol(cl)
        cs = slice(cl * 64, (cl + 1) * 64)
        st_, b_st = pSt[c % 2]
        sps = []
        for hd in range(4):
            h = H[hd]
            QT, b_QT = h["QT"][par]
            KT, b_KT = h["KT"][par]
            S.op("pe", lambda e, hd=hd, QT=QT, KT=KT: e.matmul(st_[:, hd * 64:(hd + 1) * 64], lhsT=KT[:, cs], rhs=QT[:, cs], start=True, stop=True),
                 reads=[b_KT, b_QT], writes=[b_st])
        for hd in range(4):
            h = H[hd]
            Am, b_A = h["Am"]
            sp_, b_sp = Sps[scnt[0] % 8]
            scnt[0] += 1
            sps.append((sp_, b_sp))
            S.op("dve", lambda e, hd=hd, sp_=sp_, Am=Am: e.scalar_tensor_tensor(out=sp_[:], in0=st_[:, hd * 64:(hd + 1) * 64], scalar=Am[:, gc:gc + 1],
                                                                                 in1=triu[:, :], op0=ALU.mult, op1=ALU.mult),
                 reads=[b_st, b_A, b_triu], writes=[b_sp])
        for hd in range(4):
            h = H[hd]
            u_, b_u = pU[hd // 2]
            us = slice((hd % 2) * 129, (hd % 2) * 129 + 129)
            Ktok, b_Ktok = h["Ktok"]
            aV, b_aV = h["aV"]
            S.op("pe", lambda e, u_=u_, us=us, Ktok=Ktok, aV=aV: e.matmul(u_[:, us], lhsT=Ktok[:, cl, :], rhs=aV[:, vc, :], start=True, stop=True),
                 reads=[b_Ktok, b_aV], writes=[b_u])
        for hd in range(4):
            h = H[hd]
            nd_, b_nd = pND[hd // 2]
            us = slice((hd % 2) * 129, (hd % 2) * 129 + 129)
            QT, b_QT = h["QT"][par]
            Vaug, b_Vaug = h["Vaug"]
            Cbp, b_Cbp = h["Cb"][(c + 1) % 2]
            sp_, b_sp = sps[hd]
            if c > 0:
                S.op("pe", lambda e, nd_=nd_, us=us, QT=QT, Cbp=Cbp: e.matmul(nd_[:, us], lhsT=QT[:, cs], rhs=Cbp[:, :], start=True, stop=False),
                     reads=[b_QT, b_Cbp], writes=[b_nd])
            S.op("pe", lambda e, nd_=nd_, us=us, sp_=sp_, Vaug=Vaug: e.matmul(nd_[:, us], lhsT=sp_[:, :], rhs=Vaug[:, vc, :], start=(c == 0), stop=True),
                 reads=[b_sp, b_Vaug], writes=[b_nd])
        for hd in range(4):
            h = H[hd]
            u_, b_u = pU[hd // 2]
            us = slice((hd % 2) * 129, (hd % 2) * 129 + 129)
            GL, b_GL = h["GL"]
            Cp, b_Cp = h["C"][(c + 1) % 2]
            Cn, b_Cn = h["C"][c % 2]
            tmpC, b_tmpC = tmpCs[hd]
            S.op("dve", lambda e, u_=u_, us=us, Cp=Cp, tmpC=tmpC: e.tensor_tensor(out=tmpC[:], in0=u_[:, us], in1=Cp[:], op=ALU.add),
                 reads=[b_u, b_Cp], writes=[b_tmpC])
            S.op("dve", lambda e, Cn=Cn, tmpC=tmpC, GL=GL: e.tensor_scalar(out=Cn[:], in0=tmpC[:], scalar1=GL[:, gc:gc + 1], scalar2=None, op0=ALU.mult),
                 reads=[b_tmpC, b_GL], writes=[b_Cn])
        for hd in range(4):
            h = H[hd]
            Cn, b_Cn = h["C"][c % 2]
            Cbn, b_Cbn = h["Cb"][c % 2]
            S.op("act", lambda e, Cn=Cn, Cbn=Cbn: e.activation(out=Cbn[:], in_=Cn[:], func=AF.Identity), reads=[b_Cn], writes=[b_Cbn])
        pr = c % 2
        for hd in range(4):
            h = H[hd]
            nd_, b_nd = pND[hd // 2]
            Gm, b_G = h["Gm"]
            o = (hd % 2) * 129 + 128
            S.op("act", lambda e, nd_=nd_, Gm=Gm, o=o, hd=hd: e.activation(out=t1all[:, pr, 0, hd:hd + 1], in_=nd_[:, o:o + 1], func=AF.Abs,
                                                                          scale=Gm[:, gc:gc + 1]), reads=[b_nd, b_G], writes=[b_t1all])
        S.op("dve", lambda e: e.tensor_scalar(out=t1all[:, pr, 0, :], in0=t1all[:, pr, 0, :], scalar1=1.0, scalar2=None, op0=ALU.max),
             reads=[b_t1all], writes=[b_t1all])
        S.op("dve", lambda e: e.reciprocal(out=t1all[:, pr, 0, :], in_=t1all[:, pr, 0, :]), reads=[b_t1all], writes=[b_t1all])
        S.op("dve", lambda e: e.tensor_tensor(out=t1all[:, pr, 1, :], in0=t1all[:, pr, 0, :], in1=Gm4[:, gc, :], op=ALU.mult),
             reads=[b_t1all, b_Gm4], writes=[b_t1all])
        for hd in range(4):
            h = H[hd]
            nd_, b_nd = pND[hd // 2]
            Hs, b_Hs = h["Hs"]
            o = (hd % 2) * 129
            S.op("dve", lambda e, nd_=nd_, Hs=Hs, o=o, hd=hd: e.tensor_scalar(out=Hs[:, cl, :], in0=nd_[:, o:o + 128], scalar1=t1all[:, pr, 1, hd:hd + 1],
                                                                             scalar2=None, op0=ALU.mult), reads=[b_nd, b_t1all], writes=[b_Hs])

    def finish_seg(sg):
        for hd in range(4):
            Hs, b_Hs = H[hd]["Hs"]
            S.op("dve", lambda e, Hs=Hs: e.tensor_reduce(out=mu[:], in_=Hs[:], axis=AX.X, op=ALU.add), reads=[b_Hs], writes=[b_mu])
            S.op("dve", lambda e: e.tensor_scalar(out=mu[:], in0=mu[:], scalar1=1.0 / 128, scalar2=None, op0=ALU.mult), reads=[b_mu], writes=[b_mu])
            S.op("dve", lambda e, Hs=Hs: e.tensor_tensor(out=Hs[:], in0=Hs[:], in1=bc_last(mu[:, :], 128), op=ALU.subtract),
                 reads=[b_Hs, b_mu], writes=[b_Hs])
            S.op("dve", lambda e, Hs=Hs: e.tensor_tensor(out=Hq[:], in0=Hs[:], in1=Hs[:], op=ALU.mult), reads=[b_Hs], writes=[b_Hq])
            S.op("dve", lambda e: e.tensor_reduce(out=var[:], in_=Hq[:], axis=AX.X, op=ALU.add), reads=[b_Hq], writes=[b_var])
            emit_rstd(K, var, b_var, var, b_var, 128)
            S.op("dve", lambda e, Hs=Hs: e.tensor_tensor(out=Hn[:], in0=Hs[:], in1=bc_last(var[:, :], 128), op=ALU.mult),
                 reads=[b_Hs, b_var], writes=[b_Hn])
            for cc in range(GS):
                S.op("pe", lambda e, cc=cc: e.transpose(out=pTr2[:, cc * 64:(cc + 1) * 64], in_=Hn[:, cc, :], identity=ident_b[0:64, 0:64]),
                     reads=[b_Hn, b_identb], writes=[b_pTr2])
            S.op("dve", lambda e: e.tensor_copy(out=hseg[:], in_=pTr2[:]), reads=[b_pTr2], writes=[b_hseg])
            S.dma("sp", lambda e, hd=hd: e.dma_start(out=hm_s[hd][:, sg * SEG:(sg + 1) * SEG], in_=hseg[:]), reads=[b_hseg])

    prep_seg(0)
    for sg in range(NSEG):
        load_seg(sg)
        if sg + 1 < NSEG:
            prep_seg(sg + 1)
        for cl in range(GS):
            step4(sg * GS + cl, cl, sg % 2)
        finish_seg(sg)
```

```python
import numpy as np
import ml_dtypes
import concourse.bass as bass
import concourse.mybir as mybir
from concourse.bass_utils import run_bass_kernel_spmd
from contextlib import ExitStack

F32 = mybir.dt.float32
BF16 = mybir.dt.bfloat16
ALU = mybir.AluOpType
AF = mybir.ActivationFunctionType
AX = mybir.AxisListType

D = 1024
DFF = 2816
NF = DFF // 128
DIN = 5456
NT = 16
TOK = NT * 128
S_LEN = 8192
EPS = 1e-6
NCORES = 8
DBG_STOP = 0
EPI_ENG = "dve"
DBG_EVAC = 0

P_QA, P_CKV, P_QIDX, P_KIDX, P_QM, P_KM, P_VM, P_WIDX, P_IF, NCOL1 = 0, 512, 768, 1280, 1408, 1920, 2432, 2944, 2952, 2960
C_QA, C_CKV, C_QIDX, C_KIDX, C_WIDX, C_QM, C_KM, C_VM, C_I, C_F, C_O, C_GA, C_GM = (
    0, 512, 768, 1280, 1344, 1352, 1864, 2376, 2888, 2892, 2896, 3408, 4432)


class Buf:
    __slots__ = ("name", "w", "r", "ps")

    def __init__(self, name="", ps=False):
        self.name = name
        self.w = None
        self.r = []
        self.ps = ps


class Sched:
    NS = 8

    def __init__(self, nc, es):
        self.nc = nc
        self.names = ["pe", "act", "dve", "pool", "sp"]
        self.ops = {k: [] for k in self.names}
        self.sem = {k: es.enter_context(nc.semaphore("s_" + k)) for k in ["pe", "act", "dve", "pool"]}
        self.cnt = {k: 0 for k in self.sem}
        self.dsem = {q: [es.enter_context(nc.semaphore("d_%s%d" % (q, i))) for i in range(self.NS)]
                     for q in ["sp", "act", "pool"]}
        self.dcnt = {q: 0 for q in self.dsem}
        self.seen = {k: {} for k in self.names}
        self.dlast = {}
        self.ninst = 0

    def _wait(self, e, tok):
        key, val, _ = tok
        if key == ("c", "pe") and getattr(self, "_pend", None) is not None and val > self.cnt["pe"]:
            self._pe_close(inc=True)
        if self.seen[e].get(key, 0) >= val:
            return
        self.seen[e][key] = val
        sem = self.sem[key[1]] if key[0] == "c" else self.dsem[key[1]][key[2]]
        self.ops[e].append(lambda eng, sem=sem, val=val: eng.wait_ge(sem, val))

    def _deps(self, e, reads, writes, is_dma):
        deps = []
        for b in reads:
            t = b.w
            if t is not None and not (t[2] == e and t[0][0] == "c" and e == "pe"):
                deps.append(t)
            if b.ps:
                for t in b.r:
                    if t[2] != e:
                        deps.append(t)
        for b in writes:
            t = b.w
            if t is not None and not (t[2] == e and t[0][0] == "c" and e == "pe"):
                deps.append(t)
            for t in b.r:
                if t[2] == e and t[0][0] == "c" and e == "pe":
                    continue
                deps.append(t)
        return deps

    def _pe_close(self, inc=True):
        p = getattr(self, "_pend", None)
        if p is None:
            return
        self._pend = None
        fn, wset = p
        if inc:
            self.cnt["pe"] += 1
            sem = self.sem["pe"]
            self.ops["pe"].append(lambda eng, fn=fn, sem=sem: fn(eng).then_inc(sem, 1))
        else:
            self.ops["pe"].append(lambda eng, fn=fn: fn(eng))

    def op(self, e, fn, reads=(), writes=()):
        if e == "pe":
            p = getattr(self, "_pend", None)
            wset = frozenset(id(b) for b in writes)
            if p is not None:
                self._pe_close(inc=(p[1] != wset))
            for t in self._deps(e, reads, writes, False):
                self._wait(e, t)
            self.ninst += 1
            tok = (("c", e), self.cnt[e] + 1, e)
            self._pend = (fn, wset)
        else:
            for t in self._deps(e, reads, writes, False):
                self._wait(e, t)
            self.cnt[e] += 1
            self.ninst += 1
            tok = (("c", e), self.cnt[e], e)
            sem = self.sem[e]
            self.ops[e].append(lambda eng, fn=fn, sem=sem: fn(eng).then_inc(sem, 1))
        for b in reads:
            b.r = [t for t in b.r if t[0] != tok[0]] + [tok]
        for b in writes:
            b.w = tok
            b.r = []
        return tok

    def dma(self, q, fn, reads=(), writes=()):
        for t in self._deps(q, reads, writes, True):
            self._wait(q, t)
        n = self.dcnt[q]
        self.dcnt[q] += 1
        self.ninst += 1
        slot, rnd = n % self.NS, n // self.NS
        key = ("d", q, slot)
        if rnd > 0:
            self._wait(q, (key, 16 * rnd, q))
        tok = (key, 16 * (rnd + 1), q)
        sem = self.dsem[q][slot]
        self.ops[q].append(lambda eng, fn=fn, sem=sem: fn(eng).then_inc(sem, 16))
        for b in reads:
            b.r = b.r + [tok]
        for b in writes:
            b.w = tok
            b.r = []
        self.dlast[key] = tok[1]
        return tok

    def barrier(self):
        self._pe_close(inc=True)
        for e in self.names:
            for k in self.sem:
                if k != e and self.cnt[k] > 0:
                    self._wait(e, (("c", k), self.cnt[k], None))
            for key, val in self.dlast.items():
                self._wait(e, (key, val, None))

    def finish(self):
        self._pe_close(inc=True)
        for key, val in self.dlast.items():
            self._wait("sp", (key, val, None))
        for k in self.sem:
            if self.cnt[k] > 0:
                self._wait("sp", (("c", k), self.cnt[k], None))

    def flush(self):
        nc = self.nc
        self._pe_close(inc=True)
        if not hasattr(self, "marks"):
            self.marks = []
        self.marks.append(dict(self.cnt))
        ops = self.ops
        self.ops = {k: [] for k in self.names}
        with nc.Block() as block:
            @block.sync
            def _(eng):
                for f in ops["sp"]:
                    f(eng)

            @block.tensor
            def _(eng):
                for f in ops["pe"]:
                    f(eng)

            @block.scalar
            def _(eng):
                for f in ops["act"]:
                    f(eng)

            @block.vector
            def _(eng):
                for f in ops["dve"]:
                    f(eng)

            @block.gpsimd
            def _(eng):
                for f in ops["pool"]:
                    f(eng)


class Ctx:
    def __init__(self, nc, es):
        self.nc = nc
        self.es = es
        self.S = Sched(nc, es)
        self.n = 0

    def sb(self, es, shape, dt, name=None):
        self.n += 1
        t = es.enter_context(self.nc.sbuf_tensor("%s_%d" % (name or "t", self.n), list(shape), dt))
        return t, Buf(name or "t")

    def ps(self, es, shape, dt, name=None):
        self.n += 1
        t = es.enter_context(self.nc.psum_tensor("%s_%d" % (name or "p", self.n), list(shape), dt))
        return t, Buf(name or "p", ps=True)


def bc_last(ap2d, n):
    p, c = ap2d.shape
    return ap2d.unsqueeze(2).to_broadcast([p, c, n])


def emit_mod(K, ph, ada_w, abT_sb, cT_sb, modT, b_modT, cols, rows, ada_b_dram, ident_f, b_ident):
    S = K.S
    cond, b_cond = K.sb(ph, [128, 8], F32, "cond")
    condB, b_condB = K.sb(ph, [128, 8, 128], F32, "condB")
    pieces = [K.sb(ph, [128, 8, 512], F32, "adapc") for _ in range(2)]
    abrow, b_abrow = K.sb(ph, [1, 512], F32, "abrow")
    ones1, b_ones1 = K.sb(ph, [1, 128], F32, "ones1")
    S.op("dve", lambda e: e.memset(ones1[:], 1.0), writes=[b_ones1])
    ab2 = ada_b_dram.rearrange("(a n) -> a n", a=1)
    pCol, b_pCol = K.ps(ph, [128, 512], F32, "pCol")
    pRow, b_pRow = K.ps(ph, [128, 512], F32, "pRow")
    b_cT = Buf("cT")
    S.op("act", lambda e: e.activation(out=cond[:], in_=cT_sb[:], func=AF.Silu), reads=[b_cT], writes=[b_cond])
    for kc in range(8):
        S.op("act", lambda e, kc=kc: e.activation(out=condB[:, kc, :], in_=cT_sb[:, kc:kc + 1].to_broadcast([128, 128]),
                                                  func=AF.Silu), reads=[b_cT], writes=[b_condB])
    aw = ada_w.rearrange("(kc p) n -> p kc n", p=128)
    order = sorted(set(cols) | set(rows.keys()))
    i = 0
    for v in order:
        for hh in range(2):
            pc, b_pc = pieces[i % 2]
            i += 1
            c0 = v * 1024 + hh * 512
            S.dma("sp", lambda e, pc=pc, c0=c0: e.dma_start(out=pc[:], in_=aw[:, :, c0:c0 + 512]), writes=[b_pc])
            if v in cols:
                for c in range(4):
                    cc = v * 8 + hh * 4 + c
                    for kc in range(8):
                        S.op("pe", lambda e, pc=pc, cc=cc, c=c, kc=kc: e.matmul(
                            pCol[:, cc:cc + 1], lhsT=pc[:, kc, c * 128:(c + 1) * 128], rhs=cond[:, kc:kc + 1],
                            start=(kc == 0), stop=(kc == 7)), reads=[b_pc, b_cond], writes=[b_pCol])
                q0 = v * 8 + hh * 4
                S.op("dve", lambda e, q0=q0: e.tensor_tensor(out=modT[:, q0:q0 + 4], in0=pCol[:, q0:q0 + 4], in1=abT_sb[:, q0:q0 + 4], op=ALU.add),
                     reads=[b_pCol], writes=[b_modT])
            if v in rows:
                row_sb, b_row, factor = rows[v]
                S.dma("sp", lambda e, c0=c0: e.dma_start(out=abrow[:], in_=ab2[:, c0:c0 + 512]), writes=[b_abrow])
                for kc in range(8):
                    S.op("pe", lambda e, pc=pc, kc=kc: e.matmul(pRow[:, :], lhsT=condB[:, kc, :], rhs=pc[:, kc, :], start=(kc == 0), stop=False),
                         reads=[b_pc, b_condB], writes=[b_pRow])
                S.op("pe", lambda e: e.matmul(pRow[:, :], lhsT=ones1[0:1, :], rhs=abrow[0:1, :], start=False, stop=True),
                     reads=[b_abrow, b_ones1], writes=[b_pRow])
                S.op("dve", lambda e, row_sb=row_sb, factor=factor, hh=hh: e.tensor_scalar(
                    out=row_sb[:, hh * 512:(hh + 1) * 512], in0=pRow[:], scalar1=float(factor), scalar2=None, op0=ALU.mult),
                    reads=[b_pRow], writes=[b_row])


def emit_rstd(K, ssq, b_ssq, rstd, b_rstd, n):
    S = K.S
    S.op("dve", lambda e: e.tensor_scalar(out=rstd[:], in0=ssq[:], scalar1=1.0 / n, scalar2=EPS, op0=ALU.mult, op1=ALU.add),
         reads=[b_ssq], writes=[b_rstd])
    S.op("act", lambda e: e.activation(out=rstd[:], in_=rstd[:], func=AF.Sqrt), reads=[b_rstd], writes=[b_rstd])
    S.op("dve", lambda e: e.reciprocal(out=rstd[:], in_=rstd[:]), reads=[b_rstd], writes=[b_rstd])


def emit_norm_T(K, xt, b_xt, rstd_col, b_rstd, xn, b_xn, pT, b_pT, idb, b_idb, tmpH, b_tmpH, A, Bv, b_AB, dst, b_dst):
    S = K.S
    S.op("dve", lambda e: e.tensor_scalar(out=xn[:], in0=xt[:], scalar1=rstd_col, scalar2=None, op0=ALU.mult),
         reads=[b_xt, b_rstd], writes=[b_xn])
    for c in range(8):
        S.op("pe", lambda e, c=c: e.transpose(out=pT[:, c, :], in_=xn[:, c * 128:(c + 1) * 128], identity=idb[:]),
             reads=[b_xn, b_idb], writes=[b_pT])
    S.op("dve", lambda e: e.tensor_tensor(out=tmpH[:], in0=pT[:], in1=bc_last(A, 128), op=ALU.mult),
         reads=[b_pT, b_AB], writes=[b_tmpH])
    S.op("dve", lambda e: e.tensor_tensor(out=dst, in0=tmpH[:], in1=bc_last(Bv, 128), op=ALU.add),
         reads=[b_tmpH, b_AB], writes=[b_dst])


def alloc_ffn_weights(K, ph, w1, w3, w2):
    S = K.S
    w1b, b_w1 = K.sb(ph, [128, 8, DFF], BF16, "w1b")
    w3b, b_w3 = K.sb(ph, [128, 8, DFF], BF16, "w3b")
    w2b, b_w2 = K.sb(ph, [128, NF, D], BF16, "w2b")
    S.dma("pool", lambda e: e.dma_start(out=w1b[:], in_=w1.rearrange("(kc p) n -> p kc n", p=128)), writes=[b_w1])
    S.dma("pool", lambda e: e.dma_start(out=w3b[:], in_=w3.rearrange("(kc p) n -> p kc n", p=128)), writes=[b_w3])
    S.dma("pool", lambda e: e.dma_start(out=w2b[:], in_=w2.rearrange("(fc p) n -> p fc n", p=128)), writes=[b_w2])
    return (w1b, b_w1, w3b, b_w3, w2b, b_w2)


def emit_ffn(K, ph, x_src, w1, w3, w2, A, Bv, b_AB, grow, b_grow, idb, b_idb, epilogue, src_bufs=None, ntiles=None, wts=None):
    S = K.S
    NT = ntiles or globals()["NT"]
    if wts is None:
        wts = alloc_ffn_weights(K, ph, w1, w3, w2)
    w1b, b_w1, w3b, b_w3, w2b, b_w2 = wts

    ssq, b_ssq = K.sb(ph, [128, NT], F32, "ssq")
    rstd, b_rstd = K.sb(ph, [128, NT], F32, "rstd")
    xts = [K.sb(ph, [128, D], F32, "xt") for _ in range(4)]
    xns = [K.sb(ph, [128, D], BF16, "xn") for _ in range(2)]
    junk, b_junk = xns[0]
    tmpH, b_tmpH = K.sb(ph, [128, 8, 128], F32, "tmpH")
    hTs = [K.sb(ph, [128, 2, 8, 128], BF16, "hT") for _ in range(2)]
    actTs = [(K.sb(ph, [128, NF, 256], BF16, "actT")[0], [Buf("actT%d" % f) for f in range(NF)]) for _ in range(2)]
    sil = [K.sb(ph, [128, 256], F32, "sil") for _ in range(2)]
    tmpO = [K.sb(ph, [128, D], F32, "tmpO")] * 2
    pAB = [K.ps(ph, [128, 512], F32, "pAB") for _ in range(3)]
    pO = [K.ps(ph, [128, D], F32, "pO") for _ in range(2)]
    pT, b_pT = K.ps(ph, [128, 8, 128], BF16, "pT")

    for t in range(NT):
        xt, b_xt = xts[t % 4]
        S.dma("sp", lambda e, xt=xt, t=t: e.dma_start(out=xt[:], in_=x_src(t)), reads=([src_bufs[t]] if src_bufs else []), writes=[b_xt])
        S.op("act", lambda e, xt=xt, t=t: e.activation(out=junk[:], in_=xt[:], func=AF.Square, accum_out=ssq[:, t:t + 1]),
             reads=[b_xt], writes=[b_junk, b_ssq])
    emit_rstd(K, ssq, b_ssq, rstd, b_rstd, D)
    if DBG_STOP == 1:
        return

    NG = NT // 2

    def prep(g):
        hT, b_hT = hTs[g % 2]
        for tt in range(2):
            t = 2 * g + tt
            xt, b_xt = xts[t % 4]
            xn, b_xn = xns[tt]
            S.dma("sp", lambda e, xt=xt, t=t: e.dma_start(out=xt[:], in_=x_src(t)), reads=([src_bufs[t]] if src_bufs else []), writes=[b_xt])
            emit_norm_T(K, xt, b_xt, rstd[:, t:t + 1], b_rstd, xn, b_xn, pT, b_pT, idb, b_idb, tmpH, b_tmpH,
                        A, Bv, b_AB, hT[:, tt, :, :], b_hT)

    def up(g, f):
        hT, b_hT = hTs[g % 2]
        pab, b_pab = pAB[f % 3]
        for wi, (wb, b_w) in enumerate(((w1b, b_w1), (w3b, b_w3))):
            for kc in range(8):
                S.op("pe", lambda e, wb=wb, kc=kc, f=f, wi=wi, hT=hT, pab=pab: e.matmul(
                    pab[:, wi * 256:(wi + 1) * 256].rearrange("p (a b) -> p a b", a=2),
                    lhsT=wb[:, kc, f * 128:(f + 1) * 128], rhs=hT[:, :, kc, :],
                    start=(kc == 0), stop=(kc == 7)), reads=[b_w, b_hT], writes=[b_pab])
        sl, b_sl = sil[f % 2]
        actT, b_actTs = actTs[g % 2]
        b_actT = b_actTs[f]
        S.op("act", lambda e, pab=pab, sl=sl: e.activation(out=sl[:], in_=pab[:, 0:256], func=AF.Silu),
             reads=[b_pab], writes=[b_sl])
        S.op("dve", lambda e, pab=pab, sl=sl, actT=actT, f=f: e.tensor_tensor(out=actT[:, f, :], in0=sl[:], in1=pab[:, 256:512],
                                                                               op=ALU.mult),
             reads=[b_sl, b_pab], writes=[b_actT])

    def down(g, f):
        actT, b_actTs = actTs[g % 2]
        b_actT = b_actTs[f]
        for tt in range(2):
            po, b_po = pO[tt]
            for dh in range(2):
                S.op("pe", lambda e, actT=actT, tt=tt, f=f, dh=dh, po=po: e.matmul(
                    po[:, dh * 512:(dh + 1) * 512], lhsT=actT[:, f, tt * 128:(tt + 1) * 128],
                    rhs=w2b[:, f, dh * 512:(dh + 1) * 512], start=(f == 0), stop=(f == NF - 1)),
                    reads=[b_actT, b_w2], writes=[b_po])

    def epi(g):
        for tt in range(2):
            t = 2 * g + tt
            xt, b_xt = xts[t % 4]
            po, b_po = pO[tt]
            to, b_to = tmpO[tt]
            S.op("dve", lambda e, to=to, po=po: e.tensor_tensor(out=to[:], in0=po[:], in1=grow[:], op=ALU.mult),
                 reads=[b_po, b_grow], writes=[b_to])
            S.op(EPI_ENG, lambda e, to=to, xt=xt: e.tensor_tensor(out=to[:], in0=to[:], in1=xt[:], op=ALU.add),
                 reads=[b_to, b_xt], writes=[b_to])
            epilogue(t, to, b_to)

    prep(0)
    for g in range(NG):
        for f in range(NF):
            up(g, f)
            if f >= 2:
                down(g, f - 2)
            if f == 6 and g + 1 < NG:
                prep(g + 1)
        down(g, NF - 2)
        down(g, NF - 1)
        epi(g)


def build_p1():
    nc = bass.Bass("TRN2", target_bir_lowering=False)

    def din(name, shape, dt=F32):
        return nc.dram_tensor(name, list(shape), dt, kind="ExternalInput").ap()

    def dout(name, shape, dt=F32):
        return nc.dram_tensor(name, list(shape), dt, kind="ExternalOutput").ap()

    x = din("x", [TOK, D])
    cT = din("cT", [128, 8])
    ada_w = din("ada_w", [D, 9 * D])
    ada_b = din("ada_b", [9 * D])
    ada_bT = din("ada_bT", [128, 72])
    n1T = din("n1T", [128, 8])
    n2T = din("n2T", [128, 8])
    w1 = din("w1", [D, DFF])
    w3 = din("w3", [D, DFF])
    w2 = din("w2", [DFF, D])
    w_in = din("w_in", [D, NCOL1])
    ident = din("ident", [128, 128])

    x1 = dout("x1", [TOK, D])
    h2T_o = dout("h2T", [NT, 128, 8, 128], BF16)
    qaT_o = dout("qaT", [128, 4, TOK], BF16)
    qidxT_o = dout("qidxT", [128, 4, TOK], BF16)
    kidxT_o = dout("kidxT", [64, TOK], BF16)
    ckvT_o = dout("ckvT", [128, 2, TOK], BF16)
    rstdkv_o = dout("rstdkv", [128, NT])
    small_o = dout("small", [128, NT, 16])
    qkmT_o = dout("qkmT", [128, 8, TOK])
    vm_o = dout("vm", [128, NT, 512], BF16)

    with ExitStack() as es:
        K = Ctx(nc, es)
        S = K.S
        idf, b_idf = K.sb(es, [128, 128], F32, "idf")
        idb, b_idb = K.sb(es, [128, 128], BF16, "idb")
        cT_sb, b_cT = K.sb(es, [128, 8], F32, "cT")
        abT_sb, b_abT = K.sb(es, [128, 72], F32, "abT")
        n1_sb, b_n1 = K.sb(es, [128, 8], F32, "n1")
        n2_sb, b_n2 = K.sb(es, [128, 8], F32, "n2")
        modT, b_modT = K.sb(es, [128, 72], F32, "modT")
        g1row, b_g1row = K.sb(es, [128, D], F32, "g1row")
        A1, b_A1 = K.sb(es, [128, 8], F32, "A1")
        A2, b_A2 = K.sb(es, [128, 8], F32, "A2")
        ssq2, b_ssq2 = K.sb(es, [128, NT], F32, "ssq2")
        rstd2, b_rstd2 = K.sb(es, [128, NT], F32, "rstd2")
        junk2, b_junk2 = K.sb(es, [128, D], BF16, "junk2")
        S.dma("sp", lambda e: e.dma_start(out=idf[:], in_=ident[:, :]), writes=[b_idf])
        S.dma("sp", lambda e: e.dma_start(out=cT_sb[:], in_=cT[:, :]), writes=[b_cT])
        S.dma("sp", lambda e: e.dma_start(out=abT_sb[:], in_=ada_bT[:, :]), writes=[b_abT])
        S.dma("sp", lambda e: e.dma_start(out=n1_sb[:], in_=n1T[:, :]), writes=[b_n1])
        S.dma("sp", lambda e: e.dma_start(out=n2_sb[:], in_=n2T[:, :]), writes=[b_n2])
        S.op("dve", lambda e: e.tensor_copy(out=idb[:], in_=idf[:]), reads=[b_idf], writes=[b_idb])
        with ExitStack() as ph:
            S.barrier()
            emit_mod(K, ph, ada_w, abT_sb, cT_sb, modT, b_modT, cols=[0, 1, 3, 4],
                     rows={2: (g1row, b_g1row, 0.5)}, ada_b_dram=ada_b, ident_f=idf, b_ident=b_idf)
            S.op("dve", lambda e: e.scalar_tensor_tensor(out=A1[:], in0=modT[:, 8:16], scalar=1.0, in1=n1_sb[:],
                                                         op0=ALU.add, op1=ALU.mult), reads=[b_modT, b_n1], writes=[b_A1])
            S.op("dve", lambda e: e.scalar_tensor_tensor(out=A2[:], in0=modT[:, 32:40], scalar=1.0, in1=n2_sb[:],
                                                         op0=ALU.add, op1=ALU.mult), reads=[b_modT, b_n2], writes=[b_A2])
            S.barrier()
            S.flush()
        b_x1 = [Buf("x1_%d" % t) for t in range(NT)]
        with ExitStack() as ph:
            b_AB = Buf("AB1")

            def epilogue(t, xo, b_xo):
                S.dma("sp", lambda e: e.dma_start(out=x1[t * 128:(t + 1) * 128, :], in_=xo[:]), reads=[b_xo], writes=[b_x1[t]])
                S.op("act", lambda e: e.activation(out=junk2[:], in_=xo[:], func=AF.Square, accum_out=ssq2[:, t:t + 1]),
                     reads=[b_xo], writes=[b_junk2, b_ssq2])

            emit_ffn(K, ph, lambda t: x[t * 128:(t + 1) * 128, :], w1, w3, w2, A1[:, :], modT[:, 0:8], b_AB,
                     g1row, b_g1row, idb, b_idb, epilogue)
            S.barrier()
            S.flush()
        with ExitStack() as ph:
            emit_proj(K, ph, x1, b_x1, w_in, ssq2, b_ssq2, rstd2, b_rstd2, A2, modT[:, 24:32], idb, b_idb,
                      dict(h2T=h2T_o, qaT=qaT_o, qidxT=qidxT_o, kidxT=kidxT_o, ckvT=ckvT_o, rstdkv=rstdkv_o,
                           small=small_o, qkmT=qkmT_o, vm=vm_o), ncols=NCOL1)
            S.barrier()
            S.finish()
            S.flush()
    return nc


def emit_proj(K, ph, x1, b_x1, w_in, ssq2, b_ssq2, rstd2, b_rstd2, A2, B2, idb, b_idb, outs, ncols):
    S = K.S
    wb, b_wb = K.sb(ph, [128, 8, ncols], BF16, "winb")
    S.dma("pool", lambda e: e.dma_start(out=wb[:], in_=w_in.rearrange("(kc p) n -> p kc n", p=128)), writes=[b_wb])
    emit_rstd(K, ssq2, b_ssq2, rstd2, b_rstd2, D)
    xts = [K.sb(ph, [128, D], F32, "xt") for _ in range(4)]
    xns = [K.sb(ph, [128, D], BF16, "xn") for _ in range(2)]
    tmpH, b_tmpH = K.sb(ph, [128, 8, 128], F32, "tmpH")
    hTs = [K.sb(ph, [128, 2, 8, 128], BF16, "hT") for _ in range(2)]
    ones_b, b_ones = K.sb(ph, [128, 1], F32, "ones")
    S.op("dve", lambda e: e.memset(ones_b[:], 1.0), writes=[b_ones])
    ssqkv, b_ssqkv = K.sb(ph, [128, NT], F32, "ssqkv")
    rstdkv, b_rstdkv = K.sb(ph, [128, NT], F32, "rstdkv")
    small, b_small = K.sb(ph, [128, NT, 16], F32, "small")
    fm16 = [K.sb(ph, [128, 11, 256], BF16, "fm16") for _ in range(2)]
    fm32 = [K.sb(ph, [128, 8, 256], F32, "fm32") for _ in range(2)]
    sq = [K.sb(ph, [128, 2, 256], F32, "sq") for _ in range(2)]
    vms = [K.sb(ph, [128, 512], BF16, "vms") for _ in range(2)]
    pF = [K.ps(ph, [128, 512], F32, "pF") for _ in range(3)]
    pV = [K.ps(ph, [128, 512], F32, "pV") for _ in range(2)]
    pS, b_pS = K.ps(ph, [128, 512], F32, "pS")
    pT, b_pT = K.ps(ph, [128, 8, 128], BF16, "pT")
    b_AB = Buf("AB2")
    NG = NT // 2
    b_h2T_d = Buf("h2T_d")

    def prep(g):
        hT, b_hT = hTs[g % 2]
        for tt in range(2):
            t = 2 * g + tt
            xt, b_xt = xts[t % 4]
            xn, b_xn = xns[tt]
            S.dma("sp", lambda e, xt=xt, t=t: e.dma_start(out=xt[:], in_=x1[t * 128:(t + 1) * 128, :]),
                  reads=[b_x1[t]], writes=[b_xt])
            emit_norm_T(K, xt, b_xt, rstd2[:, t:t + 1], b_rstd2, xn, b_xn, pT, b_pT, idb, b_idb, tmpH, b_tmpH,
                        A2[:, :], B2, b_AB, hT[:, tt, :, :], b_hT)
            S.dma("sp", lambda e, hT=hT, tt=tt, t=t: e.dma_start(out=outs["h2T"][t], in_=hT[:, tt, :, :]),
                  reads=[b_hT], writes=[b_h2T_d])

    chunks = []
    for i in range(4):
        chunks.append((P_QA + 128 * i, 128, "b", i))
    for i in range(2):
        chunks.append((P_CKV + 128 * i, 128, "b", 4 + i))
    for i in range(4):
        chunks.append((P_QIDX + 128 * i, 128, "b", 6 + i))
    chunks.append((P_KIDX, 128, "b", 10))
    for i in range(8):
        chunks.append((P_QM + 128 * i, 128, "f", i))

    cnt = [0]

    def body(g):
        hT, b_hT = hTs[g % 2]
        f16, b_f16 = fm16[g % 2]
        f32, b_f32 = fm32[g % 2]
        sqt, b_sq = sq[g % 2]
        for (c0, M, kind, di) in chunks:
            pf, b_pf = pF[cnt[0] % 3]
            cnt[0] += 1
            for kc in range(8):
                S.op("pe", lambda e, pf=pf, c0=c0, M=M, kc=kc, hT=hT: e.matmul(
                    pf[0:M, 0:256].rearrange("p (a b) -> p a b", a=2), lhsT=wb[:, kc, c0:c0 + M], rhs=hT[:, :, kc, :],
                    start=(kc == 0), stop=(kc == 7)), reads=[b_wb, b_hT], writes=[b_pf])
            if DBG_EVAC == 1:
                dst = f16[:, di, :] if kind == "b" else f32[:, di, :]
                S.op("dve", lambda e, pf=pf, dst=dst: e.tensor_copy(out=dst, in_=pf[:, 0:256]),
                     reads=[b_pf], writes=[b_f16 if kind == "b" else b_f32])
            elif kind == "b":
                if di in (4, 5):
                    S.op("act", lambda e, pf=pf, di=di, sqt=sqt: e.activation(out=sqt[:, di - 4, :], in_=pf[:, 0:256], func=AF.Square),
                         reads=[b_pf], writes=[b_sq])
                    S.op("dve", lambda e, pf=pf, di=di, f16=f16: e.tensor_copy(out=f16[:, di, :], in_=pf[:, 0:256]),
                         reads=[b_pf, b_sq], writes=[b_f16])
                else:
                    S.op("dve", lambda e, pf=pf, di=di, f16=f16, M=M: e.tensor_copy(out=f16[0:M, di, :], in_=pf[0:M, 0:256]),
                         reads=[b_pf], writes=[b_f16])
            else:
                S.op("act", lambda e, pf=pf, di=di, f32=f32: e.activation(out=f32[:, di, :], in_=pf[:, 0:256], func=AF.Identity),
                     reads=[b_pf], writes=[b_f32])
        if DBG_STOP == 12:
            return
        for tt in range(2):
            t = 2 * g + tt
            pv, b_pv = pV[tt]
            for kc in range(8):
                S.op("pe", lambda e, pv=pv, kc=kc, hT=hT, tt=tt: e.matmul(
                    pv[:, :], lhsT=hT[:, tt, kc, :], rhs=wb[:, kc, P_VM:P_VM + 512], start=(kc == 0), stop=(kc == 7)),
                    reads=[b_wb, b_hT], writes=[b_pv])
            vmt, b_vmt = vms[tt]
            S.op("act", lambda e, pv=pv, vmt=vmt: e.activation(out=vmt[:], in_=pv[:], func=AF.Identity), reads=[b_pv], writes=[b_vmt])
            S.dma("sp", lambda e, vmt=vmt, t=t: e.dma_start(out=outs["vm"][:, t, :], in_=vmt[:]), reads=[b_vmt])
            for kc in range(8):
                S.op("pe", lambda e, kc=kc, hT=hT, tt=tt: e.matmul(
                    pS[:, 0:8], lhsT=hT[:, tt, kc, :], rhs=wb[:, kc, P_WIDX:P_WIDX + 8], start=(kc == 0), stop=(kc == 7)),
                    reads=[b_wb, b_hT], writes=[b_pS])
            for kc in range(8):
                S.op("pe", lambda e, kc=kc, hT=hT, tt=tt: e.matmul(
                    pS[:, 8:16], lhsT=hT[:, tt, kc, :], rhs=wb[:, kc, P_IF:P_IF + 8], start=(kc == 0), stop=(kc == 7)),
                    reads=[b_wb, b_hT], writes=[b_pS])
            for c in range(2):
                S.op("pe", lambda e, c=c, tt=tt, sqt=sqt: e.matmul(
                    pS[:, 16:17], lhsT=sqt[:, c, tt * 128:(tt + 1) * 128], rhs=ones_b[:, 0:1], start=(c == 0), stop=(c == 1)),
                    reads=[b_sq, b_ones], writes=[b_pS])
            S.op("dve", lambda e, t=t: e.tensor_copy(out=small[:, t, :], in_=pS[:, 0:16]), reads=[b_pS], writes=[b_small])
            S.op("dve", lambda e, t=t: e.tensor_copy(out=ssqkv[:, t:t + 1], in_=pS[:, 16:17]), reads=[b_pS], writes=[b_ssqkv])
        tok0 = g * 256
        if DBG_STOP == 13:
            return
        S.dma("sp", lambda e, f16=f16: e.dma_start(out=outs["qaT"][:, :, tok0:tok0 + 256], in_=f16[:, 0:4, :]), reads=[b_f16])
        S.dma("sp", lambda e, f16=f16: e.dma_start(out=outs["ckvT"][:, :, tok0:tok0 + 256], in_=f16[:, 4:6, :]), reads=[b_f16])
        S.dma("sp", lambda e, f16=f16: e.dma_start(out=outs["qidxT"][:, :, tok0:tok0 + 256], in_=f16[:, 6:10, :]), reads=[b_f16])
        S.dma("sp", lambda e, f16=f16: e.dma_start(out=outs["kidxT"][:, tok0:tok0 + 256], in_=f16[0:64, 10, :]), reads=[b_f16])
        S.dma("sp", lambda e, f32=f32: e.dma_start(out=outs["qkmT"][:, :, tok0:tok0 + 256], in_=f32[:, :, :]), reads=[b_f32])

    prep(0)
    if DBG_STOP == 11:
        return
    for g in range(NG):
        if g + 1 < NG:
            prep(g + 1)
        body(g)
    if DBG_STOP == 14:
        return
    emit_rstd(K, ssqkv, b_ssqkv, rstdkv, b_rstdkv, 256)
    S.dma("sp", lambda e: e.dma_start(out=outs["rstdkv"][:, :], in_=rstdkv[:]), reads=[b_rstdkv])
    S.dma("sp", lambda e: e.dma_start(out=outs["small"][:, :, :], in_=small[:]), reads=[b_small])


def colT(v, n):
    return np.ascontiguousarray(np.asarray(v, np.float32).reshape(n, 128).T)


def core_tokens(x_b, j):
    S_, Dd = x_b.shape
    return np.ascontiguousarray(x_b.reshape(S_ // 512, 4, 128, Dd)[:, j].reshape(-1, Dd))


def p1_inputs(inp, core):
    b, j = divmod(core, 4)
    return {
        "x": core_tokens(np.asarray(inp["x"][b], np.float32), j),
        "cT": colT(inp["c"][b], 8),
        "ada_w": np.ascontiguousarray(inp["ada_w"][0], np.float32),
        "ada_b": np.ascontiguousarray(inp["ada_b"][0], np.float32),
        "ada_bT": colT(inp["ada_b"][0], 72),
        "n1T": colT(inp["ffn1_norm"][0], 8),
        "n2T": colT(inp["mix_norm"][0], 8),
        "w1": np.ascontiguousarray(inp["ffn1_w1"][0], np.float32),
        "w3": np.ascontiguousarray(inp["ffn1_w3"][0], np.float32),
        "w2": np.ascontiguousarray(inp["ffn1_w2"][0], np.float32),
        "w_in": pack_w_in(inp["w_in"][0]),
        "ident": np.eye(128, dtype=np.float32),
    }


def pack_w_in(w):
    w = np.asarray(w, np.float32)
    cols = np.concatenate([np.arange(0, 1344), np.arange(1280, 1344), np.arange(C_QM, C_VM + 512),
                           np.arange(C_WIDX, C_WIDX + 8), np.arange(C_I, C_I + 8)])
    assert cols.size == NCOL1
    return np.ascontiguousarray(w[:, cols])


NCH = S_LEN // 64


def emit_mlstm(K, ph, d, ident_b, b_identb, fused=None):
    S = K.S
    triu, b_triu = K.sb(ph, [64, 64], F32, "triu")
    ones64, b_ones64 = K.sb(ph, [64, 128], F32, "ones64")
    Ig, b_Ig = K.sb(ph, [64, NCH], F32, "Ig")
    Fg, b_Fg = K.sb(ph, [64, NCH], F32, "Fg")
    gb, b_gb = K.sb(ph, [64, 2], F32, "gb")
    ngb, b_ngb = K.sb(ph, [64, 1], F32, "ngb")
    cw, b_cw = K.sb(ph, [128, 8], F32, "cw")
    cb, b_cb = K.sb(ph, [128, 2], F32, "cb")
    LOGF, b_LOGF = K.sb(ph, [64, NCH], F32, "LOGF")
    Am, b_A = K.sb(ph, [64, NCH], F32, "Am")
    Gm, b_G = K.sb(ph, [64, NCH], F32, "Gm")
    GL, b_GL = K.sb(ph, [128, NCH], F32, "GL")
    QT, b_QT = K.sb(ph, [128, S_LEN], BF16, "QT")
    KT, b_KT = K.sb(ph, [128, S_LEN], BF16, "KT")
    Ktok, b_Ktok = K.sb(ph, [64, NCH, 128], BF16, "Ktok")
    Vaug, b_Vaug = K.sb(ph, [64, NCH, 129], BF16, "Vaug")
    aV, b_aV = K.sb(ph, [64, NCH, 129], BF16, "aV")
    if fused is None:
        col = lambda c: c
        for (t, b, src) in ((triu, b_triu, "triu"), (Ig, b_Ig, "ig"), (Fg, b_Fg, "fg"), (gb, b_gb, "gb"), (cw, b_cw, "cw"),
                            (cb, b_cb, "cb"), (Vaug, b_Vaug, "vaug")):
            S.dma("sp", lambda e, t=t, src=src: e.dma_start(out=t[:], in_=d[src]), writes=[b])
    else:
        col = lambda c: (c % 2) * 64 + c // 2
        hd, IFt, b_IFt, vm_s = fused["hd"], fused["IFt"], fused["b_IFt"], fused["vm_s"]
        for (t, b, src) in ((triu, b_triu, "triu"), (gb, b_gb, "gb"), (cw, b_cw, "cw"), (cb, b_cb, "cb")):
            S.dma("sp", lambda e, t=t, src=src: e.dma_start(out=t[:], in_=d[src]), writes=[b])
        for (t, b, q) in ((Ig, b_Ig, hd), (Fg, b_Fg, 4 + hd)):
            S.op("dve", lambda e, t=t, q=q: e.tensor_copy(out=t[:, 0:64], in_=IFt[0:64, q, :]), reads=[b_IFt], writes=[b])
            S.dma("sp", lambda e, t=t, q=q: e.dma_start(out=t[:, 64:128], in_=IFt[64:128, q, :]), reads=[b_IFt], writes=[b])
        for h2 in range(2):
            S.dma("sp", lambda e, h2=h2: e.dma_start(
                out=Vaug[:, h2 * 64:(h2 + 1) * 64, 0:128],
                in_=vm_s[:, h2 * 64:(h2 + 1) * 64, hd * 128:(hd + 1) * 128].rearrange("t s c -> s t c")), writes=[b_Vaug])
    S.op("dve", lambda e: e.memset(ones64[:], 1.0), writes=[b_ones64])
    S.op("dve", lambda e: e.memset(Vaug[:, :, 128:129], 1.0), reads=[b_Vaug], writes=[b_Vaug])
    S.op("dve", lambda e: e.tensor_scalar(out=ngb[:], in0=gb[:, 1:2], scalar1=-1.0, scalar2=None, op0=ALU.mult),
         reads=[b_gb], writes=[b_ngb])
    pG1, b_pG1 = K.ps(ph, [128, 512], F32, "pG1")
    pG2, b_pG2 = K.ps(ph, [128, 512], F32, "pG2")
    S.op("act", lambda e: e.activation(out=LOGF[:], in_=Fg[:], func=AF.Exp, scale=-1.0, bias=ngb[:, 0:1]),
         reads=[b_Fg, b_ngb], writes=[b_LOGF])
    S.op("act", lambda e: e.activation(out=LOGF[:], in_=LOGF[:], func=AF.Ln, bias=1.0), reads=[b_LOGF], writes=[b_LOGF])
    S.op("dve", lambda e: e.tensor_scalar(out=LOGF[:], in0=LOGF[:], scalar1=-1.0, scalar2=None, op0=ALU.mult),
         reads=[b_LOGF], writes=[b_LOGF])
    S.op("pe", lambda e: e.matmul(pG1[0:64, 0:NCH], lhsT=triu[:, :], rhs=LOGF[:, :], start=True, stop=True),
         reads=[b_triu, b_LOGF], writes=[b_pG1])
    S.op("pe", lambda e: e.matmul(pG2[:, 0:NCH], lhsT=ones64[:, :], rhs=LOGF[:, :], start=True, stop=True),
         reads=[b_ones64, b_LOGF], writes=[b_pG2])
    S.op("dve", lambda e: e.scalar_tensor_tensor(out=Am[:], in0=Ig[:], scalar=gb[:, 0:1], in1=pG1[0:64, 0:NCH],
                                                 op0=ALU.add, op1=ALU.subtract), reads=[b_Ig, b_gb, b_pG1], writes=[b_A])
    S.op("act", lambda e: e.activation(out=Am[:], in_=Am[:], func=AF.Exp), reads=[b_A], writes=[b_A])
    S.op("act", lambda e: e.activation(out=Gm[:], in_=pG1[0:64, 0:NCH], func=AF.Exp), reads=[b_pG1], writes=[b_G])
    S.op("act", lambda e: e.activation(out=GL[:], in_=pG2[:, 0:NCH], func=AF.Exp), reads=[b_pG2], writes=[b_GL])
    S.op("dve", lambda e: e.tensor_tensor(out=aV[:], in0=Vaug[:], in1=bc_last(Am[:, :], 129), op=ALU.mult),
         reads=[b_Vaug, b_A], writes=[b_aV])

    SEG = 1024
    xs = [K.sb(ph, [128, SEG + 3], F32, "xs") for _ in range(2)]
    accs = [K.sb(ph, [128, SEG], F32, "acc") for _ in range(2)]
    kscale = float(128 ** -0.5)
    n = 0
    for seg in range(S_LEN // SEG):
        for which, (src, dst, b_dst) in enumerate((("qpad", QT, b_QT), ("kpad", KT, b_KT))):
            x_, b_x = xs[n % 2]
            a_, b_a = accs[n % 2]
            n += 1
            S.dma("sp", lambda e, x_=x_, src=src, seg=seg: e.dma_start(out=x_[:], in_=d[src][:, seg * SEG:seg * SEG + SEG + 3]),
                  writes=[b_x])
            S.op("dve", lambda e, x_=x_, a_=a_, which=which: e.tensor_scalar(
                out=a_[:], in0=x_[:, 0:SEG], scalar1=cw[:, 4 * which:4 * which + 1], scalar2=None, op0=ALU.mult),
                reads=[b_x, b_cw], writes=[b_a])
            for w in range(1, 4):
                S.op("dve", lambda e, x_=x_, a_=a_, which=which, w=w: e.scalar_tensor_tensor(
                    out=a_[:], in0=x_[:, w:w + SEG], scalar=cw[:, 4 * which + w:4 * which + w + 1], in1=a_[:],
                    op0=ALU.mult, op1=ALU.add), reads=[b_x, b_cw, b_a], writes=[b_a])
            if which == 0:
                S.op("act", lambda e, a_=a_, dst=dst, seg=seg: e.activation(
                    out=dst[:, seg * SEG:(seg + 1) * SEG], in_=a_[:], func=AF.Silu, bias=cb[:, 0:1]),
                    reads=[b_a, b_cb], writes=[b_dst])
            else:
                S.op("act", lambda e, a_=a_: e.activation(out=a_[:], in_=a_[:], func=AF.Silu, bias=cb[:, 1:2]),
                     reads=[b_a, b_cb], writes=[b_a])
                S.op("dve", lambda e, a_=a_, dst=dst, seg=seg: e.tensor_scalar(
                    out=dst[:, seg * SEG:(seg + 1) * SEG], in0=a_[:], scalar1=kscale, scalar2=None, op0=ALU.mult),
                    reads=[b_a], writes=[b_dst])
    pKt, b_pKt = K.ps(ph, [64, 8, 128], BF16, "pKt")
    for g8 in range(NCH // 8):
        for cc in range(8):
            c = g8 * 8 + cc
            S.op("pe", lambda e, c=c, cc=cc: e.transpose(out=pKt[:, cc, :], in_=KT[:, c * 64:(c + 1) * 64], identity=ident_b[:]),
                 reads=[b_KT, b_identb], writes=[b_pKt])
        S.op("dve", lambda e, g8=g8: e.tensor_copy(out=Ktok[:, g8 * 8:(g8 + 1) * 8, :], in_=pKt[:]), reads=[b_pKt], writes=[b_Ktok])

    Cs = [K.sb(ph, [128, 129], F32, "Cst") for _ in range(2)]
    Cbs = [K.sb(ph, [128, 129], BF16, "Cbf") for _ in range(2)]
    tmpC, b_tmpC = K.sb(ph, [128, 129], F32, "tmpC")
    Sps = [K.sb(ph, [64, 64], BF16, "Sp") for _ in range(2)]
    t1s = [K.sb(ph, [64, 2], F32, "t1") for _ in range(2)]
    GSEG = 16
    Hs, b_Hs = K.sb(ph, [64, GSEG, 128], F32, "Hs")
    Hq, b_Hq = K.sb(ph, [64, GSEG, 128], F32, "Hq")
    Hn, b_Hn = K.sb(ph, [64, GSEG, 128], BF16, "Hn")
    mu, b_mu = K.sb(ph, [64, GSEG], F32, "mu")
    var, b_var = K.sb(ph, [64, GSEG], F32, "var")
    hseg, b_hseg = K.sb(ph, [128, GSEG * 64], BF16, "hseg")
    pSt = [K.ps(ph, [64, 512], F32, "pSt") for _ in range(2)]
    pU = [(pG1, b_pG1), (pG2, b_pG2)]
    pND = [K.ps(ph, [64, 512], F32, "pND") for _ in range(2)]
    pTr, b_pTr = pKt, b_pKt
    pTr2, b_pTr2 = K.ps(ph, [128, GSEG * 64], BF16, "pTr2")
    S.op("dve", lambda e: e.memset(Cs[1][0][:], 0.0), writes=[Cs[1][1]])
    for c in range(NCH):
        st_, b_st = pSt[c % 2]
        sp_, b_sp = Sps[c % 2]
        u_, b_u = pU[c % 2]
        nd_, b_nd = pND[c % 2]
        Cp, b_Cp = Cs[(c + 1) % 2]
        Cn, b_Cn = Cs[c % 2]
        Cbp, b_Cbp = Cbs[(c + 1) % 2]
        Cbn, b_Cbn = Cbs[c % 2]
        t1, b_t1 = t1s[c % 2]
        cs = slice(c * 64, (c + 1) * 64)
        S.op("pe", lambda e, st_=st_, cs=cs: e.matmul(st_[:, 0:64], lhsT=KT[:, cs], rhs=QT[:, cs], start=True, stop=True),
             reads=[b_KT, b_QT], writes=[b_st])
        S.op("dve", lambda e, st_=st_, sp_=sp_, c=c: e.scalar_tensor_tensor(
            out=sp_[:], in0=st_[:, 0:64], scalar=Am[:, col(c):col(c) + 1], in1=triu[:, :], op0=ALU.mult, op1=ALU.mult),
            reads=[b_st, b_A, b_triu], writes=[b_sp])
        S.op("pe", lambda e, u_=u_, c=c: e.matmul(u_[:, 0:129], lhsT=Ktok[:, c, :], rhs=aV[:, col(c), :], start=True, stop=True),
             reads=[b_Ktok, b_aV], writes=[b_u])
        if c > 0:
            S.op("pe", lambda e, nd_=nd_, cs=cs, Cbp=Cbp: e.matmul(nd_[:, 0:129], lhsT=QT[:, cs], rhs=Cbp[:, :], start=True, stop=False),
                 reads=[b_QT, b_Cbp], writes=[b_nd])
        S.op("pe", lambda e, nd_=nd_, sp_=sp_, c=c: e.matmul(nd_[:, 0:129], lhsT=sp_[:, :], rhs=Vaug[:, col(c), :], start=(c == 0), stop=True),
             reads=[b_sp, b_Vaug], writes=[b_nd])
        S.op("dve", lambda e, u_=u_, Cp=Cp: e.tensor_tensor(out=tmpC[:], in0=u_[:, 0:129], in1=Cp[:], op=ALU.add),
             reads=[b_u, b_Cp], writes=[b_tmpC])
        S.op("dve", lambda e, Cn=Cn, c=c: e.tensor_scalar(out=Cn[:], in0=tmpC[:], scalar1=GL[:, col(c):col(c) + 1], scalar2=None, op0=ALU.mult),
             reads=[b_tmpC, b_GL], writes=[b_Cn])
        S.op("act", lambda e, Cn=Cn, Cbn=Cbn: e.activation(out=Cbn[:], in_=Cn[:], func=AF.Identity), reads=[b_Cn], writes=[b_Cbn])
        S.op("act", lambda e, nd_=nd_, t1=t1, c=c: e.activation(out=t1[:, 0:1], in_=nd_[:, 128:129], func=AF.Abs, scale=Gm[:, col(c):col(c) + 1]),
             reads=[b_nd, b_G], writes=[b_t1])
        S.op("dve", lambda e, t1=t1: e.tensor_scalar(out=t1[:, 0:1], in0=t1[:, 0:1], scalar1=1.0, scalar2=None, op0=ALU.max),
             reads=[b_t1], writes=[b_t1])
        S.op("dve", lambda e, t1=t1: e.reciprocal(out=t1[:, 0:1], in_=t1[:, 0:1]), reads=[b_t1], writes=[b_t1])
        S.op("dve", lambda e, t1=t1, c=c: e.tensor_tensor(out=t1[:, 1:2], in0=t1[:, 0:1], in1=Gm[:, col(c):col(c) + 1], op=ALU.mult),
             reads=[b_t1, b_G], writes=[b_t1])
        S.op("dve", lambda e, nd_=nd_, t1=t1, c=c: e.tensor_scalar(out=Hs[:, c % GSEG, :], in0=nd_[:, 0:128], scalar1=t1[:, 1:2], scalar2=None,
                                                                    op0=ALU.mult), reads=[b_nd, b_t1], writes=[b_Hs])
        if c % GSEG == GSEG - 1:
            sg = c // GSEG
            S.op("dve", lambda e: e.tensor_reduce(out=mu[:], in_=Hs[:], axis=AX.X, op=ALU.add), reads=[b_Hs], writes=[b_mu])
            S.op("dve", lambda e: e.tensor_scalar(out=mu[:], in0=mu[:], scalar1=1.0 / 128, scalar2=None, op0=ALU.mult),
                 reads=[b_mu], writes=[b_mu])
            S.op("dve", lambda e: e.tensor_tensor(out=Hs[:], in0=Hs[:], in1=bc_last(mu[:, :], 128), op=ALU.subtract),
                 reads=[b_Hs, b_mu], writes=[b_Hs])
            S.op("dve", lambda e: e.tensor_tensor(out=Hq[:], in0=Hs[:], in1=Hs[:], op=ALU.mult), reads=[b_Hs], writes=[b_Hq])
            S.op("dve", lambda e: e.tensor_reduce(out=var[:], in_=Hq[:], axis=AX.X, op=ALU.add), reads=[b_Hq], writes=[b_var])
            emit_rstd(K, var, b_var, var, b_var, 128)
            S.op("dve", lambda e: e.tensor_tensor(out=Hn[:], in0=Hs[:], in1=bc_last(var[:, :], 128), op=ALU.mult),
                 reads=[b_Hs, b_var], writes=[b_Hn])
            for cc in range(GSEG):
                S.op("pe", lambda e, cc=cc: e.transpose(out=pTr2[:, cc * 64:(cc + 1) * 64], in_=Hn[:, cc, :], identity=ident_b[0:64, 0:64]),
                     reads=[b_Hn, b_identb], writes=[b_pTr2])
            S.op("dve", lambda e: e.tensor_copy(out=hseg[:], in_=pTr2[:]), reads=[b_pTr2], writes=[b_hseg])
            S.dma("sp", lambda e, sg=sg: e.dma_start(out=d["hmT"][:, sg * GSEG * 64:(sg + 1) * GSEG * 64], in_=hseg[:]), reads=[b_hseg])


def mlstm_inputs_from(qT, kT, v, ig, fg, inp, hd):
    z3 = np.zeros((128, 3), np.float32)
    vaug = np.zeros((64, NCH, 129), ml_dtypes.bfloat16)
    vaug[:, :, 0:128] = np.asarray(v).reshape(NCH, 64, 128).transpose(1, 0, 2)
    gbv = np.asarray(inp["mlstm_gate_bias"][0], np.float32)
    cwv = np.asarray(inp["conv_w"][0], np.float32)
    cbv = np.asarray(inp["conv_b"][0], np.float32)
    cw = np.concatenate([cwv[:, hd * 128:(hd + 1) * 128].T, cwv[:, 512 + hd * 128:512 + (hd + 1) * 128].T], axis=1)
    cb = np.stack([cbv[hd * 128:(hd + 1) * 128], cbv[512 + hd * 128:512 + (hd + 1) * 128]], axis=1)
    return {
        "qpad": np.ascontiguousarray(np.concatenate([z3, qT], axis=1), np.float32),
        "kpad": np.ascontiguousarray(np.concatenate([z3, kT], axis=1), np.float32),
        "vaug": vaug,
        "ig": np.ascontiguousarray(np.asarray(ig, np.float32).reshape(NCH, 64).T),
        "fg": np.ascontiguousarray(np.asarray(fg, np.float32).reshape(NCH, 64).T),
        "gb": np.ascontiguousarray(np.broadcast_to(np.array([gbv[hd], gbv[4 + hd]], np.float32)[None, :], (64, 2))),
        "cw": np.ascontiguousarray(cw, np.float32),
        "cb": np.ascontiguousarray(cb, np.float32),
        "triu": np.triu(np.ones((64, 64), np.float32)),
    }


NIT_BISECT = 15
NEG_MASK = -32768.0


def emit_attn(K, ph, d, ident_f, b_identf, ident_b, b_identb, nslots=NT):
    S = K.S
    nc = K.nc
    KpT, b_KpT = K.sb(ph, [128, 4, S_LEN], BF16, "KpT")
    kidx2, b_kidx2 = K.sb(ph, [128, S_LEN], BF16, "kidx2")
    rstd, b_rstd = K.sb(ph, [128, 64], F32, "rstdk")
    widx, b_widx = K.sb(ph, [128, NT, 8], F32, "widx")
    wabs, b_wabs = K.sb(ph, [128, NT, 8], F32, "wabs")
    wsgn, b_wsgn = K.sb(ph, [128, NT, 8], F32, "wsgn")
    E4, b_E4 = K.sb(ph, [128, 512], BF16, "E4")
    zer, b_zer = K.sb(ph, [128, 65], BF16, "zer")
    onesr, b_onesr = K.sb(ph, [128, 64], F32, "onesr")
    cm, b_cm = K.sb(ph, [128, 512], F32, "cm")
    btb, b_btb = K.sb(ph, [128, 8, 640], BF16, "btb")
    b31, b_b31 = K.sb(ph, [128, 8], F32, "b31")
    ring = [K.ps(ph, [128, 512], F32, "ring") for _ in range(4)]
    pY = [K.ps(ph, [128, 512], F32, "pY") for _ in range(2)]
    pT, b_pT = K.ps(ph, [128, 4, 128], BF16, "pTa")
    S.dma("sp", lambda e: e.dma_start(out=kidx2[:], in_=d["kidx2"]), writes=[b_kidx2])
    S.dma("sp", lambda e: e.dma_start(out=rstd[:], in_=d["rstd"]), writes=[b_rstd])
    S.dma("sp", lambda e: e.dma_start(out=widx[:], in_=d["widx"]), writes=[b_widx])
    S.dma("sp", lambda e: e.dma_start(out=cm[:], in_=d["cm"]), writes=[b_cm])
    S.dma("sp", lambda e: e.dma_start(out=b31[:], in_=d["b31"]), writes=[b_b31])
    S.op("dve", lambda e: e.memset(zer[:], 0.0), writes=[b_zer])
    S.op("dve", lambda e: e.memset(onesr[:], 1.0), writes=[b_onesr])
    for hh in range(4):
        S.op("dve", lambda e, hh=hh: e.tensor_copy(out=E4[:, hh * 128:(hh + 1) * 128], in_=ident_f[:]), reads=[b_identf], writes=[b_E4])
    S.op("act", lambda e: e.activation(out=wabs[:], in_=widx[:], func=AF.Abs), reads=[b_widx], writes=[b_wabs])
    S.op("act", lambda e: e.activation(out=wsgn[:], in_=widx[:], func=AF.Sign), reads=[b_widx], writes=[b_wsgn])

    with ExitStack() as pp:
        ckvT, b_ckvT = K.sb(pp, [128, 2, S_LEN], BF16, "ckvT")
        wst, b_wst = K.sb(pp, [128, 2, 512], F32, "wst")
        gkv, b_gkv = K.sb(pp, [128, 2], F32, "gkv")
        wukb, b_wukb = K.sb(pp, [128, 2, 512], BF16, "wukb")
        wuvb, b_wuvb = K.sb(pp, [128, 2, 512], BF16, "wuvb")
        bst, b_bst = K.sb(pp, [128, 640], F32, "bst")
        S.dma("sp", lambda e: e.dma_start(out=ckvT[:], in_=d["ckvT"]), writes=[b_ckvT])
        S.dma("sp", lambda e: e.dma_start(out=gkv[:], in_=d["gkv"]), writes=[b_gkv])
        for (src, dst, b_dst, fac) in (("wuk", wukb, b_wukb, 0.125), ("wuv", wuvb, b_wuvb, 1.0)):
            S.dma("sp", lambda e, src=src: e.dma_start(out=wst[:], in_=d[src]), writes=[b_wst])
            for cc in range(2):
                S.op("dve", lambda e, cc=cc, dst=dst, fac=fac: e.tensor_scalar(
                    out=dst[:, cc, :], in0=wst[:, cc, :], scalar1=gkv[:, cc:cc + 1], scalar2=float(fac), op0=ALU.mult, op1=ALU.mult),
                    reads=[b_wst, b_gkv], writes=[b_dst])
        for h in range(8):
            S.dma("sp", lambda e, h=h: e.dma_start(out=bst[:], in_=d["bt"][:, h, :]), writes=[b_bst])
            S.op("dve", lambda e, h=h: e.tensor_scalar(out=btb[:, h, :], in0=bst[:], scalar1=b31[:, h:h + 1], scalar2=None, op0=ALU.subtract),
                 reads=[b_bst, b_b31], writes=[b_btb])
        ktoks = [K.sb(pp, [128, 512], BF16, "ktok") for _ in range(2)]
        vts = [K.sb(pp, [128, 8, 65], BF16, "vt") for _ in range(3)]
        for (vt, b_vt) in vts:
            S.op("dve", lambda e, vt=vt: e.memset(vt[:], 1.0), writes=[b_vt])
        b_vscr = [Buf("vscr%d" % T) for T in range(64)]
        for T in range(64):
            pk, b_pk = ring[(2 * T) % 4]
            pv, b_pv = ring[(2 * T + 1) % 4]
            kt, b_kt = ktoks[T % 2]
            vt, b_vt = vts[T % 3]
            ts = slice(T * 128, (T + 1) * 128)
            for cc in range(2):
                S.op("pe", lambda e, pk=pk, cc=cc, ts=ts: e.matmul(pk[:, :], lhsT=ckvT[:, cc, ts], rhs=wukb[:, cc, :], start=(cc == 0), stop=(cc == 1)),
                     reads=[b_ckvT, b_wukb], writes=[b_pk])
            for cc in range(2):
                S.op("pe", lambda e, pv=pv, cc=cc, ts=ts: e.matmul(pv[:, :], lhsT=ckvT[:, cc, ts], rhs=wuvb[:, cc, :], start=(cc == 0), stop=(cc == 1)),
                     reads=[b_ckvT, b_wuvb], writes=[b_pv])
            S.op("dve", lambda e, pk=pk, kt=kt, T=T: e.tensor_scalar(out=kt[:], in0=pk[:], scalar1=rstd[:, T:T + 1], scalar2=None, op0=ALU.mult),
                 reads=[b_pk, b_rstd], writes=[b_kt])
            S.op("dve", lambda e, pv=pv, vt=vt, T=T: e.tensor_scalar(
                out=vt[:, :, 0:64], in0=pv[:].rearrange("p (h d) -> p h d", h=8), scalar1=rstd[:, T:T + 1], scalar2=None, op0=ALU.mult),
                reads=[b_pv, b_rstd], writes=[b_vt])
            S.dma("sp", lambda e, vt=vt, T=T: e.dma_start(out=d["vscr"][T], in_=vt[:].rearrange("p h d -> p (h d)")),
                  reads=[b_vt], writes=[b_vscr[T]])
            for q in range(4):
                S.op("pe", lambda e, q=q, kt=kt: e.transpose(out=pT[:, q, :], in_=kt[:, q * 128:(q + 1) * 128], identity=ident_b[:]),
                     reads=[b_kt, b_identb], writes=[b_pT])
            S.op("act", lambda e, ts=ts: e.copy(out=KpT[:, :, ts], in_=pT[:]), reads=[b_pT], writes=[b_KpT])
        S.barrier()

    sc, b_sc = K.sb(ph, [128, S_LEN], F32, "sc")
    nms = [K.sb(ph, [128, S_LEN], BF16, "nm") for _ in range(2)]
    nmbs = [K.sb(ph, [128, 8, 640], BF16, "nmb") for _ in range(2)]
    nmA_bufs = [Buf("nmA0"), Buf("nmA1")]
    rts = [K.sb(ph, [128, 512], F32, "rt") for _ in range(2)]
    qas = [K.sb(ph, [128, 4, 256], BF16, "qa") for _ in range(2)]
    for (qa_, b_qa_) in qas:
        S.op("dve", lambda e, qa_=qa_: e.memset(qa_[:], 0.0), writes=[b_qa_])
    qis = [K.sb(ph, [128, 4, 128], BF16, "qi") for _ in range(2)]
    vbufs = [K.sb(ph, [128, 520], BF16, "vbuf") for _ in range(3)]
    PTs = [K.sb(ph, [128, 512], BF16, "PT") for _ in range(4)]
    bs = {n_: K.sb(ph, [128, 1], F32, n_) for n_ in ("amax", "w0", "lo", "mid", "cnt", "gw", "nmid", "sa")}
    rd, b_rd = K.sb(ph, [128, 512], F32, "rd")
    bsb, b_bsb = K.sb(ph, [64, 512], F32, "bsb")
    yo, b_yo = K.sb(ph, [64, 1024], BF16, "yo")
    rcnt = [0]
    vcnt = [0]
    pcnt = [0]

    def stage_a(i):
        nk = (i + 1) * 512
        qa, b_qa = qas[i % 2]
        qi, b_qi = qis[i % 2]
        nm, b_nm = nms[i % 2]
        nmb, b_nmb = nmbs[i % 2]
        b_nmA = nmA_bufs[i % 2]
        S.dma("sp", lambda e: e.dma_start(out=qa[0:64, :, 0:128], in_=d["qaT"][0:64, :, i * 128:(i + 1) * 128]), writes=[b_qa])
        S.dma("sp", lambda e: e.dma_start(out=qa[64:128, :, 128:256], in_=d["qaT"][64:128, :, i * 128:(i + 1) * 128]), writes=[b_qa])
        S.dma("sp", lambda e: e.dma_start(out=qi[:], in_=d["qidxT"][:, :, i * 128:(i + 1) * 128]), writes=[b_qi])
        for kc in range(i + 1):
            ks = slice(kc * 512, (kc + 1) * 512)
            for h in range(8):
                pz, b_pz = ring[rcnt[0] % 4]
                rt, b_rt = rts[rcnt[0] % 2]
                rcnt[0] += 1
                hp = slice((h % 2) * 64, (h % 2) * 64 + 64)
                S.op("pe", lambda e, pz=pz, hp=hp, h=h, ks=ks: e.matmul(pz[:, :], lhsT=qi[hp, h // 2, :], rhs=kidx2[hp, ks], start=True, stop=True),
                     reads=[b_qi, b_kidx2], writes=[b_pz])
                S.op("act", lambda e, pz=pz, rt=rt, h=h: e.activation(out=rt[:], in_=pz[:], func=AF.Relu, scale=wabs[:, i, h:h + 1]),
                     reads=[b_pz, b_wabs], writes=[b_rt])
                if h == 0:
                    S.op("dve", lambda e, rt=rt, ks=ks, h=h: e.tensor_scalar(out=sc[:, ks], in0=rt[:], scalar1=wsgn[:, i, h:h + 1], scalar2=None, op0=ALU.mult),
                         reads=[b_rt, b_wsgn], writes=[b_sc])
                else:
                    S.op("dve", lambda e, rt=rt, ks=ks, h=h: e.scalar_tensor_tensor(out=sc[:, ks], in0=rt[:], scalar=wsgn[:, i, h:h + 1], in1=sc[:, ks],
                                                                                     op0=ALU.mult, op1=ALU.add), reads=[b_rt, b_wsgn, b_sc], writes=[b_sc])
        amax, b_amax = bs["amax"]
        w0, b_w0 = bs["w0"]
        lo, b_lo = bs["lo"]
        mid, b_mid = bs["mid"]
        cnt, b_cnt = bs["cnt"]
        gw, b_gw = bs["gw"]
        S.op("dve", lambda e: e.tensor_reduce(out=amax[:], in_=sc[:, 0:nk], axis=AX.X, op=ALU.max, apply_absolute_value=True),
             reads=[b_sc], writes=[b_amax])
        S.op("dve", lambda e: e.tensor_tensor(out=sc[:, nk - 512:nk], in0=sc[:, nk - 512:nk], in1=cm[:], op=ALU.add),
             reads=[b_sc, b_cm], writes=[b_sc])
        S.op("dve", lambda e: e.tensor_scalar(out=lo[:], in0=amax[:], scalar1=-1.0, scalar2=-1.0, op0=ALU.mult, op1=ALU.add),
             reads=[b_amax], writes=[b_lo])
        S.op("dve", lambda e: e.tensor_scalar(out=w0[:], in0=amax[:], scalar1=2.0, scalar2=2.0, op0=ALU.mult, op1=ALU.add),
             reads=[b_amax], writes=[b_w0])
        split = False
        nd_ = (nk * 9 // 16) // 512 * 512 if split else nk
        na_ = nk - nd_
        thr = 255.5 - 0.5 * na_
        nmid, b_nmid = bs["nmid"]
        sa, b_sa = bs["sa"]
        for it in range(1, NIT_BISECT + 1):
            f = float(2.0 ** -it)
            S.op("dve", lambda e, f=f: e.scalar_tensor_tensor(out=mid[:], in0=w0[:], scalar=f, in1=lo[:], op0=ALU.mult, op1=ALU.add),
                 reads=[b_w0, b_lo], writes=[b_mid])
            if split:
                S.op("dve", lambda e: e.tensor_scalar(out=nmid[:], in0=mid[:], scalar1=-1.0, scalar2=None, op0=ALU.mult), reads=[b_mid], writes=[b_nmid])
                S.op("act", lambda e: e.activation(out=nm[:, nd_:nk], in_=sc[:, nd_:nk], func=AF.Sign, bias=nmid[:, 0:1], accum_out=sa[:]),
                     reads=[b_sc, b_nmid], writes=[b_nmA, b_sa])
            S.op("dve", lambda e: e.tensor_scalar(out=nm[:, 0:nd_], in0=sc[:, 0:nd_], scalar1=mid[:, 0:1], scalar2=0.0, op0=ALU.is_ge, op1=ALU.add,
                                                  accum_out=cnt[:]), reads=[b_sc, b_mid], writes=[b_nm, b_cnt])
            if split:
                S.op("dve", lambda e: e.scalar_tensor_tensor(out=cnt[:], in0=sa[:], scalar=0.5, in1=cnt[:], op0=ALU.mult, op1=ALU.add),
                     reads=[b_sa, b_cnt], writes=[b_cnt])
            S.op("dve", lambda e: e.scalar_tensor_tensor(out=gw[:], in0=cnt[:], scalar=float(thr), in1=w0[:], op0=ALU.is_ge, op1=ALU.mult),
                 reads=[b_cnt, b_w0], writes=[b_gw])
            S.op("dve", lambda e, f=f: e.scalar_tensor_tensor(out=lo[:], in0=gw[:], scalar=f, in1=lo[:], op0=ALU.mult, op1=ALU.add),
                 reads=[b_gw, b_lo], writes=[b_lo])
        S.op("dve", lambda e: e.tensor_scalar(out=nm[:, 0:nk], in0=sc[:, 0:nk], scalar1=lo[:, 0:1], scalar2=NEG_MASK, op0=ALU.is_lt, op1=ALU.mult),
             reads=[b_sc, b_lo], writes=[b_nm, b_nmA])
        t0 = 1 if i == 0 else 0
        for h in range(8):
            S.op("dve", lambda e, h=h: e.tensor_tensor(
                out=nmb[:, h, t0 * 128:640], in0=nm[:, nk - 640 + t0 * 128:nk], in1=btb[:, h, t0 * 128:640], op=ALU.add),
                reads=[b_nm, b_btb], writes=[b_nmb])

    def stage_b(i):
        ntile = 4 * i + 4
        qa, b_qa = qas[i % 2]
        nm, b_nm = nms[i % 2]
        nmb, b_nmb = nmbs[i % 2]
        for g in range(2):
            py, b_py = pY[g]
            S.op("pe", lambda e, py=py: e.matmul(py[0:65, :], lhsT=zer[:, :], rhs=E4[:, :], start=True, stop=False),
                 reads=[b_zer, b_E4], writes=[b_py])
        units = [(st, g) for st in range(ntile) for g in range(2)]
        state = {}

        def qk(u):
            st, g = units[u]
            if g == 0:
                vb, b_vb = vbufs[vcnt[0] % 3]
                vcnt[0] += 1
                S.dma("sp", lambda e: e.dma_start(out=vb[:], in_=d["vscr"][st]), reads=[b_vscr[st]], writes=[b_vb])
                state[("vb", st)] = (vb, b_vb)
            ti = st - (4 * i - 1)
            ss = slice(st * 128, (st + 1) * 128)
            pl, b_pl = ring[rcnt[0] % 4]
            rcnt[0] += 1
            pt, b_pt = PTs[pcnt[0] % 4]
            pcnt[0] += 1
            if ti < 0:
                S.op("pe", lambda e: e.matmul(pl[:, :], lhsT=nm[:, ss], rhs=E4[:, :], start=True, stop=False),
                     reads=[b_nm, b_E4], writes=[b_pl])
                for pp in range(2):
                    pair = 2 * g + pp
                    S.op("pe", lambda e, pp=pp, pair=pair: e.matmul(pl[:, pp * 256:(pp + 1) * 256], lhsT=KpT[:, pair, ss], rhs=qa[:, pair, :],
                                                                    start=False, stop=(pp == 1)), reads=[b_KpT, b_qa], writes=[b_pl])
            for hh in (range(4) if ti >= 0 else ()):
                h = 4 * g + hh
                hp = slice((h % 2) * 64, (h % 2) * 64 + 64)
                os_ = slice(hh * 128, (hh + 1) * 128)
                if ti >= 0:
                    S.op("pe", lambda e, os_=os_, h=h: e.matmul(pl[:, os_], lhsT=nmb[:, h, ti * 128:(ti + 1) * 128], rhs=ident_b[:, :],
                                                                start=True, stop=False), reads=[b_nmb, b_identb], writes=[b_pl])
                qc = slice((h % 2) * 128, (h % 2) * 128 + 128)
                S.op("pe", lambda e, os_=os_, hp=hp, h=h, qc=qc: e.matmul(pl[:, os_], lhsT=KpT[hp, h // 2, ss], rhs=qa[hp, h // 2, qc],
                                                                            start=False, stop=True), reads=[b_KpT, b_qa], writes=[b_pl])
            S.op("act", lambda e: e.activation(out=pt[:], in_=pl[:], func=AF.Exp), reads=[b_pl], writes=[b_pt])
            state[u] = (pt, b_pt)

        def pv(u):
            st, g = units[u]
            pt, b_pt = state.pop(u)
            vb, b_vb = state[("vb", st)]
            py, b_py = pY[g]
            for hh in range(4):
                h = 4 * g + hh
                os_ = slice(hh * 128, (hh + 1) * 128)
                S.op("pe", lambda e, os_=os_, h=h, hh=hh: e.matmul(
                    py[0:65, os_], lhsT=vb[:, h * 65:(h + 1) * 65], rhs=pt[:, os_], start=False, stop=(st == ntile - 1 and hh == 3)),
                    reads=[b_vb, b_pt], writes=[b_py])

        qk(0)
        for u in range(len(units)):
            if u + 1 < len(units):
                qk(u + 1)
            pv(u)
        for g in range(2):
            pb, b_pb = ring[rcnt[0] % 4]
            rcnt[0] += 1
            py, b_py = pY[g]
            S.op("dve", lambda e, py=py: e.reciprocal(out=rd[64:65, :], in_=py[64:65, :]), reads=[b_py], writes=[b_rd])
            S.op("pe", lambda e, pb=pb: e.matmul(pb[0:64, :], lhsT=onesr[64:65, 0:64], rhs=rd[64:65, :], start=True, stop=True),
                 reads=[b_onesr, b_rd], writes=[b_pb])
            S.op("act", lambda e, pb=pb: e.activation(out=bsb[:, :], in_=pb[0:64, :], func=AF.Identity), reads=[b_pb], writes=[b_bsb])
            S.op("dve", lambda e, py=py, g=g: e.tensor_tensor(out=yo[:, g * 512:(g + 1) * 512], in0=py[0:64, :], in1=bsb[:, :], op=ALU.mult),
                 reads=[b_py, b_bsb], writes=[b_yo])
        S.dma("sp", lambda e: e.dma_start(out=d["yaT"][:, :, i * 128:(i + 1) * 128], in_=yo[:].rearrange("p (h t) -> p h t", h=8)),
              reads=[b_yo])

    stage_a(0)
    for i in range(nslots):
        if i + 1 < nslots:
            stage_a(i + 1)
        stage_b(i)


def t5_bucket_np(dist):
    n = np.maximum(dist, 0)
    nf = np.maximum(n, 1).astype(np.float32)
    large = 16 + (np.log(nf / np.float32(16)) / np.float32(np.log(128 / 16)) * np.float32(16)).astype(np.int32)
    large = np.minimum(large, 31)
    return np.where(n < 16, n, large)


def attn_consts(inp, j):
    rb = np.asarray(inp["rel_bias"], np.float32)
    t = np.arange(128)[:, None, None]
    ti = np.arange(5)[None, :, None]
    sl = np.arange(128)[None, None, :]
    dist = (j + 1 - ti) * 128 + t - sl
    bk = t5_bucket_np(dist)
    bt = rb[bk]
    bt = np.ascontiguousarray(bt.transpose(0, 3, 1, 2).reshape(128, 8, 640), np.float32)
    b31 = np.ascontiguousarray(np.broadcast_to(rb[31][None, :], (128, 8)), np.float32)
    jp = np.arange(4)[None, :, None]
    t2 = np.arange(128)[:, None, None]
    valid = ((jp - j) * 128 + sl - t2) <= 0
    cm = np.where(valid, 0.0, -1e30).astype(np.float32).reshape(128, 512)
    wuk = np.asarray(inp["w_uk"][0], np.float32).transpose(1, 0, 2).reshape(256, 512)
    wuv = np.asarray(inp["w_uv"][0], np.float32).transpose(1, 0, 2).reshape(256, 512)
    return {
        "bt": bt, "b31": b31, "cm": cm,
        "wuk": np.ascontiguousarray(wuk.reshape(2, 128, 512).transpose(1, 0, 2)),
        "wuv": np.ascontiguousarray(wuv.reshape(2, 128, 512).transpose(1, 0, 2)),
        "gkv": colT(inp["kv_norm"][0], 2),
    }


def emit_merge(K, ph, d, g2row, b_g2row, x2_d, b_x2, sel=None):
    S = K.S
    wg, b_wg = K.sb(ph, [128, 8, 2560], BF16, "wg")
    wa, b_wa = K.sb(ph, [64, 8, D], BF16, "wa")
    wm, b_wm = K.sb(ph, [128, 4, D], BF16, "wm")
    wo, b_wo = K.sb(ph, [128, 8, D], BF16, "wo")
    wms, b_wms = K.sb(ph, [128, 4, D], F32, "wms")
    hn, b_hn = K.sb(ph, [128, 4], F32, "hn")
    S.dma("pool", lambda e: e.dma_start(out=wg[:], in_=d["wg"].rearrange("(kc p) n -> p kc n", p=128)), writes=[b_wg])
    S.dma("pool", lambda e: e.dma_start(out=wa[:], in_=d["wa"]), writes=[b_wa])
    S.dma("pool", lambda e: e.dma_start(out=wo[:], in_=d["wout"].rearrange("(kc p) n -> p kc n", p=128)), writes=[b_wo])
    S.dma("sp", lambda e: e.dma_start(out=wms[:], in_=d["wm"]), writes=[b_wms])
    S.dma("sp", lambda e: e.dma_start(out=hn[:], in_=d["hnT"]), writes=[b_hn])
    for hd in range(4):
        S.op("dve", lambda e, hd=hd: e.tensor_scalar(out=wm[:, hd, :], in0=wms[:, hd, :], scalar1=hn[:, hd:hd + 1], scalar2=None, op0=ALU.mult),
             reads=[b_wms, b_hn], writes=[b_wm])
    h2s = [K.sb(ph, [128, 2, 8, 128], BF16, "h2g") for _ in range(2)]
    yas = [K.sb(ph, [64, 8, 256], BF16, "yag") for _ in range(2)]
    hms = [K.sb(ph, [128, 4, 256], BF16, "hmg") for _ in range(2)]
    x1s = [K.sb(ph, [128, D], F32, "x1t") for _ in range(4)]
    sig, b_sig = K.sb(ph, [128, 20, 256], BF16, "sig")
    hmo, b_hmo = K.sb(ph, [128, 4, 256], BF16, "hmo")
    t1s = [K.sb(ph, [128, 256], F32, "mt1") for _ in range(2)]
    t2s = [K.sb(ph, [128, 256], F32, "mt2") for _ in range(2)]
    mgs = [K.sb(ph, [128, 8, 256], BF16, "mg") for _ in range(2)]
    tos = [K.sb(ph, [128, D], F32, "mto") for _ in range(2)]
    pG = [K.ps(ph, [128, 512], F32, "pGm") for _ in range(2)]
    pZ = [K.ps(ph, [128, 512], F32, "pZ") for _ in range(2)]
    pO = [K.ps(ph, [128, D], F32, "pOm") for _ in range(2)]
    NG = NT // 2
    for g in range(NG):
        h2, b_h2 = h2s[g % 2]
        ya, b_ya = yas[g % 2]
        hm, b_hm = hms[g % 2]
        mg, b_mg = mgs[g % 2]
        tk = slice(g * 256, (g + 1) * 256)
        for tt in range(2):
            S.dma("sp", lambda e, h2=h2, tt=tt, g=g: e.dma_start(out=h2[:, tt, :, :], in_=d["h2T"][2 * g + tt]), writes=[b_h2])
        S.dma("sp", lambda e, ya=ya, tk=tk: e.dma_start(out=ya[:], in_=d["yaT"][:, :, tk]), writes=[b_ya])
        if sel is None:
            S.dma("sp", lambda e, hm=hm, tk=tk: e.dma_start(out=hm[:], in_=d["hmT"][:, :, tk]), writes=[b_hm])
        else:
            oh, b_oh, hm_s, cands = sel
            for tt in range(2):
                i = 2 * g + tt
                cd, b_cd = cands[tt]
                S.dma("sp", lambda e, cd=cd, i=i: e.dma_start(out=cd[:], in_=hm_s.rearrange("h e t -> e h t")[:, :, i * 512:(i + 1) * 512]),
                      writes=[b_cd])
                ts_ = slice(tt * 128, (tt + 1) * 128)
                S.op("dve", lambda e, cd=cd, hm=hm, ts_=ts_: e.tensor_scalar(out=hm[:, :, ts_], in0=cd[:, :, 0:128], scalar1=oh[:, 0:1], scalar2=None,
                                                                            op0=ALU.mult), reads=[b_cd, b_oh], writes=[b_hm])
                for jj in range(1, 4):
                    S.op("dve", lambda e, cd=cd, hm=hm, ts_=ts_, jj=jj: e.scalar_tensor_tensor(
                        out=hm[:, :, ts_], in0=cd[:, :, jj * 128:(jj + 1) * 128], scalar=oh[:, jj:jj + 1], in1=hm[:, :, ts_],
                        op0=ALU.mult, op1=ALU.add), reads=[b_cd, b_oh, b_hm], writes=[b_hm])
        for c in range(20):
            pg, b_pg = pG[c % 2]
            for kc in range(8):
                S.op("pe", lambda e, pg=pg, c=c, kc=kc, h2=h2: e.matmul(
                    pg[:, 0:256].rearrange("p (a b) -> p a b", a=2), lhsT=wg[:, kc, c * 128:(c + 1) * 128], rhs=h2[:, :, kc, :],
                    start=(kc == 0), stop=(kc == 7)), reads=[b_wg, b_h2], writes=[b_pg])
            S.op("act", lambda e, pg=pg, c=c: e.activation(out=sig[:, c, :], in_=pg[:, 0:256], func=AF.Sigmoid), reads=[b_pg], writes=[b_sig])
        S.op("dve", lambda e, hm=hm: e.tensor_tensor(out=hmo[:], in0=hm[:], in1=sig[:, 0:4, :], op=ALU.mult), reads=[b_hm, b_sig], writes=[b_hmo])
        for n in range(8):
            pz, b_pz = pZ[n % 2]
            t1, b_t1 = t1s[n % 2]
            t2, b_t2 = t2s[n % 2]
            ns = slice(n * 128, (n + 1) * 128)
            for h in range(8):
                S.op("pe", lambda e, pz=pz, h=h, ns=ns, ya=ya: e.matmul(pz[:, 0:256], lhsT=wa[:, h, ns], rhs=ya[:, h, :], start=(h == 0), stop=(h == 7)),
                     reads=[b_wa, b_ya], writes=[b_pz])
            for hd in range(4):
                S.op("pe", lambda e, pz=pz, hd=hd, ns=ns: e.matmul(pz[:, 256:512], lhsT=wm[:, hd, ns], rhs=hmo[:, hd, :], start=(hd == 0), stop=(hd == 3)),
                     reads=[b_wm, b_hmo], writes=[b_pz])
            S.op("dve", lambda e, pz=pz, t1=t1, n=n: e.tensor_tensor(out=t1[:], in0=pz[:, 0:256], in1=sig[:, 4 + n, :], op=ALU.mult),
                 reads=[b_pz, b_sig], writes=[b_t1])
            S.op("dve", lambda e, pz=pz, t2=t2, n=n: e.tensor_tensor(out=t2[:], in0=pz[:, 256:512], in1=sig[:, 12 + n, :], op=ALU.mult),
                 reads=[b_pz, b_sig], writes=[b_t2])
            S.op("dve", lambda e, t1=t1, t2=t2, mg=mg, n=n: e.tensor_tensor(out=mg[:, n, :], in0=t1[:], in1=t2[:], op=ALU.add),
                 reads=[b_t1, b_t2], writes=[b_mg])
        for tt in range(2):
            t = 2 * g + tt
            po, b_po = pO[tt]
            x1t, b_x1t = x1s[t % 4]
            to, b_to = tos[tt]
            S.dma("sp", lambda e, x1t=x1t, t=t: e.dma_start(out=x1t[:], in_=d["x1"][t * 128:(t + 1) * 128, :]), writes=[b_x1t])
            for dh in range(2):
                for kc in range(8):
                    S.op("pe", lambda e, po=po, dh=dh, kc=kc, mg=mg, tt=tt: e.matmul(
                        po[:, dh * 512:(dh + 1) * 512], lhsT=mg[:, kc, tt * 128:(tt + 1) * 128], rhs=wo[:, kc, dh * 512:(dh + 1) * 512],
                        start=(kc == 0), stop=(kc == 7)), reads=[b_mg, b_wo], writes=[b_po])
            S.op("dve", lambda e, po=po, to=to: e.tensor_tensor(out=to[:], in0=po[:], in1=g2row[:], op=ALU.mult), reads=[b_po, b_g2row], writes=[b_to])
            S.op("dve", lambda e, to=to, x1t=x1t: e.tensor_tensor(out=to[:], in0=to[:], in1=x1t[:], op=ALU.add), reads=[b_to, b_x1t], writes=[b_to])
            S.dma("sp", lambda e, to=to, t=t: e.dma_start(out=x2_d[t * 128:(t + 1) * 128, :], in_=to[:]), reads=[b_to], writes=[b_x2[t]])


def build_p3():
    nc = bass.Bass("TRN2", target_bir_lowering=False)

    def din(name, shape, dt=F32):
        return nc.dram_tensor(name, list(shape), dt, kind="ExternalInput").ap()

    d = dict(x1=din("x1", [TOK, D]), h2T=din("h2T", [NT, 128, 8, 128], BF16), yaT=din("yaT", [64, 8, TOK], BF16),
             hmT=din("hmT", [128, 4, TOK], BF16), wg=din("wg", [D, 2560]), wa=din("wa", [64, 8, D]), wm=din("wm", [128, 4, D]),
             hnT=din("hnT", [128, 4]), wout=din("wout", [D, D]))
    cT = din("cT", [128, 8])
    ada_w = din("ada_w", [D, 9 * D])
    ada_b = din("ada_b", [9 * D])
    ada_bT = din("ada_bT", [128, 72])
    n3T = din("n3T", [128, 8])
    fn = din("fn", [D])
    w1 = din("w1", [D, DFF])
    w3 = din("w3", [D, DFF])
    w2 = din("w2", [DFF, D])
    ident = din("ident", [128, 128])
    out = nc.dram_tensor("out", [TOK, D], F32, kind="ExternalOutput").ap()
    x2_d = nc.dram_tensor("x2s", [TOK, D], F32, kind="Internal").ap()
    x3_d = nc.dram_tensor("x3s", [TOK, D], F32, kind="Internal").ap()
    with ExitStack() as es:
        K = Ctx(nc, es)
        S = K.S
        idf, b_idf = K.sb(es, [128, 128], F32, "idf")
        idb, b_idb = K.sb(es, [128, 128], BF16, "idb")
        cT_sb, b_cT = K.sb(es, [128, 8], F32, "cT")
        abT_sb, b_abT = K.sb(es, [128, 72], F32, "abT")
        n3_sb, b_n3 = K.sb(es, [128, 8], F32, "n3")
        modT, b_modT = K.sb(es, [128, 72], F32, "modT")
        g3row, b_g3row = K.sb(es, [128, D], F32, "g3row")
        A3, b_A3 = K.sb(es, [128, 8], F32, "A3")
        ssqF, b_ssqF = K.sb(es, [128, NT], F32, "ssqF")
        rstdF, b_rstdF = K.sb(es, [128, NT], F32, "rstdF")
        junkF, b_junkF = K.sb(es, [128, D], BF16, "junkF")
        st_g2 = ExitStack()
        g2row, b_g2row = K.sb(st_g2, [128, D], F32, "g2row")
        S.dma("sp", lambda e: e.dma_start(out=idf[:], in_=ident[:, :]), writes=[b_idf])
        S.dma("sp", lambda e: e.dma_start(out=cT_sb[:], in_=cT[:, :]), writes=[b_cT])
        S.dma("sp", lambda e: e.dma_start(out=abT_sb[:], in_=ada_bT[:, :]), writes=[b_abT])
        S.dma("sp", lambda e: e.dma_start(out=n3_sb[:], in_=n3T[:, :]), writes=[b_n3])
        S.op("dve", lambda e: e.tensor_copy(out=idb[:], in_=idf[:]), reads=[b_idf], writes=[b_idb])
        with ExitStack() as ph:
            S.barrier()
            emit_mod(K, ph, ada_w, abT_sb, cT_sb, modT, b_modT, cols=[6, 7],
                     rows={5: (g2row, b_g2row, 1.0), 8: (g3row, b_g3row, 0.5)}, ada_b_dram=ada_b, ident_f=idf, b_ident=b_idf)
            S.op("dve", lambda e: e.scalar_tensor_tensor(out=A3[:], in0=modT[:, 56:64], scalar=1.0, in1=n3_sb[:],
                                                         op0=ALU.add, op1=ALU.mult), reads=[b_modT, b_n3], writes=[b_A3])
            S.barrier()
            S.flush()
        b_x2 = [Buf("x2_%d" % t) for t in range(NT)]
        b_x3 = [Buf("x3_%d" % t) for t in range(NT)]
        with ExitStack() as ph:
            emit_merge(K, ph, d, g2row, b_g2row, x2_d, b_x2)
            S.barrier()
            S.flush()
        st_g2.close()
        with ExitStack() as ph:
            b_AB = Buf("AB3")

            def epilogue(t, xo, b_xo):
                S.dma("sp", lambda e: e.dma_start(out=x3_d[t * 128:(t + 1) * 128, :], in_=xo[:]), reads=[b_xo], writes=[b_x3[t]])
                S.op("act", lambda e: e.activation(out=junkF[:], in_=xo[:], func=AF.Square, accum_out=ssqF[:, t:t + 1]),
                     reads=[b_xo], writes=[b_junkF, b_ssqF])

            emit_ffn(K, ph, lambda t: x2_d[t * 128:(t + 1) * 128, :], w1, w3, w2, A3[:, :], modT[:, 48:56], b_AB,
                     g3row, b_g3row, idb, b_idb, epilogue, src_bufs=b_x2)
            S.barrier()
            S.flush()
        with ExitStack() as ph:
            fnrow, b_fnrow = K.sb(ph, [128, D], F32, "fnrow")
            fn1, b_fn1 = K.sb(ph, [1, D], F32, "fn1")
            on1, b_on1 = K.sb(ph, [1, 128], F32, "on1")
            pF, b_pF = K.ps(ph, [128, D], F32, "pFn")
            S.op("dve", lambda e: e.memset(on1[:], 1.0), writes=[b_on1])
            S.dma("sp", lambda e: e.dma_start(out=fn1[:], in_=fn.rearrange("(a n) -> a n", a=1)), writes=[b_fn1])
            for nt in range(2):
                S.op("pe", lambda e, nt=nt: e.matmul(pF[:, nt * 512:(nt + 1) * 512], lhsT=on1[0:1, :], rhs=fn1[0:1, nt * 512:(nt + 1) * 512],
                                                    start=True, stop=True), reads=[b_on1, b_fn1], writes=[b_pF])
            S.op("dve", lambda e: e.tensor_copy(out=fnrow[:], in_=pF[:]), reads=[b_pF], writes=[b_fnrow])
            emit_rstd(K, ssqF, b_ssqF, rstdF, b_rstdF, D)
            xf = [K.sb(ph, [128, D], F32, "xf") for _ in range(3)]
            for t in range(NT):
                xt, b_xt = xf[t % 3]
                S.dma("sp", lambda e, xt=xt, t=t: e.dma_start(out=xt[:], in_=x3_d[t * 128:(t + 1) * 128, :]), reads=[b_x3[t]], writes=[b_xt])
                S.op("dve", lambda e, xt=xt, t=t: e.scalar_tensor_tensor(out=xt[:], in0=xt[:], scalar=rstdF[:, t:t + 1], in1=fnrow[:],
                                                                          op0=ALU.mult, op1=ALU.mult), reads=[b_xt, b_rstdF, b_fnrow], writes=[b_xt])
                S.dma("sp", lambda e, xt=xt, t=t: e.dma_start(out=out[t * 128:(t + 1) * 128, :], in_=xt[:]), reads=[b_xt])
            S.barrier()
            S.finish()
            S.flush()
    return nc


def build_p2():
    nc = bass.Bass("TRN2", target_bir_lowering=False)

    def din(name, shape, dt=F32):
        return nc.dram_tensor(name, list(shape), dt, kind="ExternalInput").ap()

    def dout(name, shape, dt=F32):
        return nc.dram_tensor(name, list(shape), dt, kind="ExternalOutput").ap()

    dm = dict(qpad=din("qpad", [128, S_LEN + 3]), kpad=din("kpad", [128, S_LEN + 3]), vaug=din("vaug", [64, NCH, 129], BF16),
              ig=din("ig", [64, NCH]), fg=din("fg", [64, NCH]), gb=din("gb", [64, 2]), cw=din("cw", [128, 8]), cb=din("cb", [128, 2]),
              triu=din("triu", [64, 64]), hmT=dout("hmT", [128, S_LEN], BF16))
    da = dict(ckvT=din("ckvT", [128, 2, S_LEN], BF16), rstd=din("rstd", [128, 64]), kidx2=din("kidx2", [128, S_LEN], BF16),
              qaT=din("qaT", [128, 4, TOK], BF16), qidxT=din("qidxT", [128, 4, TOK], BF16), widx=din("widx", [128, NT, 8]),
              wuk=din("wuk", [128, 2, 512]), wuv=din("wuv", [128, 2, 512]), gkv=din("gkv", [128, 2]),
              bt=din("bt", [128, 8, 640]), b31=din("b31", [128, 8]), cm=din("cm", [128, 512]),
              yaT=dout("yaT", [64, 8, TOK], BF16))
    da["vscr"] = nc.dram_tensor("vscr", [64, 128, 520], BF16, kind="Internal").ap()
    ident = din("ident", [128, 128])
    with ExitStack() as es:
        K = Ctx(nc, es)
        S = K.S
        idf, b_idf = K.sb(es, [128, 128], F32, "idf")
        idb, b_idb = K.sb(es, [128, 128], BF16, "idb")
        S.dma("sp", lambda e: e.dma_start(out=idf[:], in_=ident[:, :]), writes=[b_idf])
        S.op("dve", lambda e: e.tensor_copy(out=idb[:], in_=idf[:]), reads=[b_idf], writes=[b_idb])
        with ExitStack() as ph:
            emit_mlstm(K, ph, dm, idb, b_idb)
            S.barrier()
            S.flush()
        with ExitStack() as ph:
            emit_attn(K, ph, da, idf, b_idf, idb, b_idb)
            S.barrier()
            S.finish()
            S.flush()
    return nc


def gather_tokens(parts, axis):
    outs = []
    for jj in range(4):
        a = np.moveaxis(np.asarray(parts[jj]), axis, -1)
        outs.append(a.reshape(a.shape[:-1] + (NT, 1, 128)))
    g = np.concatenate(outs, axis=-2)
    g = g.reshape(g.shape[:-3] + (S_LEN,))
    return np.moveaxis(g, -1, axis)


def p2_inputs(inp, r1, core):
    b, j = divmod(core, 4)
    grp = [r1[b * 4 + jj] for jj in range(4)]
    hd = j
    qk = gather_tokens([g["qkmT"] for g in grp], 2)
    vm = gather_tokens([np.asarray(g["vm"]).transpose(1, 0, 2).reshape(TOK, 512) for g in grp], 0)
    sm = gather_tokens([np.asarray(g["small"]).transpose(1, 0, 2).reshape(TOK, 16) for g in grp], 0)
    m = mlstm_inputs_from(qk[:, hd, :], qk[:, 4 + hd, :], vm[:, hd * 128:(hd + 1) * 128], sm[:, 8 + hd], sm[:, 12 + hd], inp, hd)
    rs = gather_tokens([np.asarray(g["rstdkv"]).T.reshape(TOK) for g in grp], 0)
    kid = gather_tokens([g["kidxT"] for g in grp], 1)
    own = r1[core]
    a = {
        "ckvT": np.ascontiguousarray(gather_tokens([g["ckvT"] for g in grp], 2)),
        "rstd": np.ascontiguousarray(rs.reshape(64, 128).T, np.float32),
        "kidx2": np.ascontiguousarray(np.concatenate([kid, kid], axis=0)),
        "qaT": np.ascontiguousarray(own["qaT"]),
        "qidxT": np.ascontiguousarray(own["qidxT"]),
        "widx": np.ascontiguousarray(np.asarray(own["small"])[:, :, 0:8], np.float32),
        "ident": np.eye(128, dtype=np.float32),
    }
    a.update(attn_consts(inp, j))
    a.update(m)
    return a


def p3_inputs(inp, r1, r2, core):
    b, j = divmod(core, 4)
    hm = []
    for hd in range(4):
        h = np.asarray(r2[b * 4 + hd]["hmT"])
        hm.append(h.reshape(128, NT, 4, 128)[:, :, j, :].reshape(128, TOK))
    w_in = np.asarray(inp["w_in"][0], np.float32)
    wa = np.asarray(inp["w_branch_attn"][0], np.float32).reshape(8, 64, D).transpose(1, 0, 2)
    wm = np.asarray(inp["w_branch_mlstm"][0], np.float32).reshape(4, 128, D).transpose(1, 0, 2)
    return {
        "x1": np.ascontiguousarray(r1[core]["x1"]),
        "h2T": np.ascontiguousarray(r1[core]["h2T"]),
        "yaT": np.ascontiguousarray(r2[core]["yaT"]),
        "hmT": np.ascontiguousarray(np.stack(hm, axis=1)),
        "wg": np.ascontiguousarray(w_in[:, C_O:DIN]),
        "wa": np.ascontiguousarray(wa),
        "wm": np.ascontiguousarray(wm),
        "hnT": colT(inp["mlstm_head_norm"][0], 4),
        "wout": np.ascontiguousarray(inp["w_out"][0], np.float32),
        "cT": colT(inp["c"][b], 8),
        "ada_w": np.ascontiguousarray(inp["ada_w"][0], np.float32),
        "ada_b": np.ascontiguousarray(inp["ada_b"][0], np.float32),
        "ada_bT": colT(inp["ada_b"][0], 72),
        "n3T": colT(inp["ffn2_norm"][0], 8),
        "fn": np.ascontiguousarray(inp["final_norm"], np.float32),
        "w1": np.ascontiguousarray(inp["ffn2_w1"][0], np.float32),
        "w3": np.ascontiguousarray(inp["ffn2_w3"][0], np.float32),
        "w2": np.ascontiguousarray(inp["ffn2_w2"][0], np.float32),
        "ident": np.eye(128, dtype=np.float32),
    }


_NC_CACHE = {}


def _prog(name, builder):
    if name not in _NC_CACHE:
        _NC_CACHE[name] = builder()
    return _NC_CACHE[name]


def kernel(**inputs):
    inp = {k: np.asarray(v) for k, v in inputs.items()}
    cores = list(range(NCORES))
    r1 = run_bass_kernel_spmd(_prog("p1", build_p1), [p1_inputs(inp, c) for c in cores], core_ids=cores).results
    r2 = run_bass_kernel_spmd(_prog("p2", build_p2), [p2_inputs(inp, r1, c) for c in cores], core_ids=cores).results
    r3 = run_bass_kernel_spmd(_prog("p3", build_p3), [p3_inputs(inp, r1, r2, c) for c in cores], core_ids=cores).results
    out = np.zeros((2, S_LEN, D), np.float32)
    for c in cores:
        b, j = divmod(c, 4)
        out[b].reshape(NT, 4, 128, D)[:, j] = np.asarray(r3[c]["out"], np.float32).reshape(NT, 128, D)
    return out


NTA = S_LEN // 128
NCA = 1928
NCO = 1032


def emit_proj2(K, ph, ntiles, x1_of, b_x1, wpk, ncols, ssq2, b_ssq2, col0, A2, B2, idb, b_idb, spec):
    S = K.S
    wb, b_wb = K.sb(ph, [128, 8, ncols], BF16, "winb")
    S.dma("pool", lambda e: e.dma_start(out=wb[:], in_=wpk.rearrange("(kc p) n -> p kc n", p=128)), writes=[b_wb])
    rstd2, b_rstd2 = K.sb(ph, [128, ntiles], F32, "rstd2")
    S.op("dve", lambda e: e.tensor_scalar(out=rstd2[:], in0=ssq2[:, col0:col0 + ntiles], scalar1=1.0 / D, scalar2=EPS, op0=ALU.mult, op1=ALU.add),
         reads=[b_ssq2], writes=[b_rstd2])
    S.op("act", lambda e: e.activation(out=rstd2[:], in_=rstd2[:], func=AF.Sqrt), reads=[b_rstd2], writes=[b_rstd2])
    S.op("dve", lambda e: e.reciprocal(out=rstd2[:], in_=rstd2[:]), reads=[b_rstd2], writes=[b_rstd2])
    xts = [K.sb(ph, [128, D], F32, "xt") for _ in range(4)]
    xns = [K.sb(ph, [128, D], BF16, "xn") for _ in range(2)]
    tmpH, b_tmpH = K.sb(ph, [128, 8, 128], F32, "tmpH")
    hTs = [K.sb(ph, [128, 2, 8, 128], BF16, "hT") for _ in range(2)]
    nb, nf = spec["nb"], spec["nf"]
    fm16 = [K.sb(ph, [128, nb, 256], BF16, "fm16") for _ in range(2)]
    fm32 = [K.sb(ph, [128, max(nf, 1), 256], F32, "fm32") for _ in range(2)]
    has_sq = bool(spec.get("sq"))
    if has_sq:
        ones_f, b_ones = K.sb(ph, [128, 1], F32, "ones")
        S.op("dve", lambda e: e.memset(ones_f[:], 1.0), writes=[b_ones])
        sq = [K.sb(ph, [128, 2, 256], F32, "sq") for _ in range(2)]
        ssqkv, b_ssqkv = K.sb(ph, [128, ntiles], F32, "ssqkv")
    vms = [K.sb(ph, [128, 512], BF16, "vms") for _ in range(2)]
    pF = [K.ps(ph, [128, 512], F32, "pF") for _ in range(3)]
    pV = [K.ps(ph, [128, 512], F32, "pV") for _ in range(2)]
    pS, b_pS = K.ps(ph, [128, 512], F32, "pS")
    pT, b_pT = K.ps(ph, [128, 8, 128], BF16, "pT")
    b_AB = Buf("AB2")
    NG = ntiles // 2
    small, b_small = spec["small_sb"]
    cnt = [0]

    def prep(g):
        hT, b_hT = hTs[g % 2]
        for tt in range(2):
            t = 2 * g + tt
            xt, b_xt = xts[t % 4]
            xn, b_xn = xns[tt]
            if spec.get("xsel"):
                spec["xsel"](t, xt, b_xt)
            else:
                S.dma("sp", lambda e, xt=xt, t=t: e.dma_start(out=xt[:], in_=x1_of(t)), reads=[b_x1[t]], writes=[b_xt])
            emit_norm_T(K, xt, b_xt, rstd2[:, t:t + 1], b_rstd2, xn, b_xn, pT, b_pT, idb, b_idb, tmpH, b_tmpH,
                        A2, B2, b_AB, hT[:, tt, :, :], b_hT)
            if spec.get("out_h2T"):
                spec["out_h2T"](t, hT, tt, b_hT)

    def body(g):
        hT, b_hT = hTs[g % 2]
        f16, b_f16 = fm16[g % 2]
        f32, b_f32 = fm32[g % 2]
        if has_sq:
            sqt, b_sq = sq[g % 2]
        for (c0, kind, di) in spec["fm"]:
            pf, b_pf = pF[cnt[0] % 3]
            cnt[0] += 1
            for kc in range(8):
                S.op("pe", lambda e, pf=pf, c0=c0, kc=kc, hT=hT: e.matmul(
                    pf[:, 0:256].rearrange("p (a b) -> p a b", a=2), lhsT=wb[:, kc, c0:c0 + 128], rhs=hT[:, :, kc, :],
                    start=(kc == 0), stop=(kc == 7)), reads=[b_wb, b_hT], writes=[b_pf])
            if kind == "b":
                rd = [b_pf]
                if has_sq and di in spec["sq"]:
                    S.op("act", lambda e, pf=pf, di=di, sqt=sqt: e.activation(out=sqt[:, spec["sq"].index(di), :], in_=pf[:, 0:256], func=AF.Square),
                         reads=[b_pf], writes=[b_sq])
                    rd = [b_pf, b_sq]
                S.op("dve", lambda e, pf=pf, di=di, f16=f16: e.tensor_copy(out=f16[:, di, :], in_=pf[:, 0:256]), reads=rd, writes=[b_f16])
            else:
                S.op("act", lambda e, pf=pf, di=di, f32=f32: e.activation(out=f32[:, di, :], in_=pf[:, 0:256], func=AF.Identity),
                     reads=[b_pf], writes=[b_f32])
        if g + 1 < NG:
            prep(g + 1)
        for tt in range(2):
            t = 2 * g + tt
            if spec.get("vm_col") is not None:
                pv, b_pv = pV[tt]
                vc = spec["vm_col"]
                for kc in range(8):
                    S.op("pe", lambda e, pv=pv, kc=kc, hT=hT, tt=tt, vc=vc: e.matmul(
                        pv[:, :], lhsT=hT[:, tt, kc, :], rhs=wb[:, kc, vc:vc + 512], start=(kc == 0), stop=(kc == 7)),
                        reads=[b_wb, b_hT], writes=[b_pv])
                vmt, b_vmt = vms[tt]
                S.op("act", lambda e, pv=pv, vmt=vmt: e.activation(out=vmt[:], in_=pv[:], func=AF.Identity), reads=[b_pv], writes=[b_vmt])
                spec["out_vm"](t, vmt, b_vmt)
            sc0, sn = spec["small"]
            for kc in range(8):
                S.op("pe", lambda e, kc=kc, hT=hT, tt=tt, sc0=sc0, sn=sn: e.matmul(
                    pS[:, 0:sn], lhsT=hT[:, tt, kc, :], rhs=wb[:, kc, sc0:sc0 + sn], start=(kc == 0), stop=(kc == 7)),
                    reads=[b_wb, b_hT], writes=[b_pS])
            if has_sq:
                for c in range(2):
                    S.op("pe", lambda e, c=c, tt=tt, sqt=sqt: e.matmul(
                        pS[:, 16:17], lhsT=sqt[:, c, tt * 128:(tt + 1) * 128], rhs=ones_f[:, 0:1], start=(c == 0), stop=(c == 1)),
                        reads=[b_sq, b_ones], writes=[b_pS])
            S.op("dve", lambda e, t=t, sn=sn: e.tensor_copy(out=small[:, t, 0:sn], in_=pS[:, 0:sn]), reads=[b_pS], writes=[b_small])
            if has_sq:
                S.op("dve", lambda e, t=t: e.tensor_copy(out=ssqkv[:, t:t + 1], in_=pS[:, 16:17]), reads=[b_pS], writes=[b_ssqkv])
        spec["out_fm"](g, f16, b_f16, f32, b_f32)

    prep(0)
    for g in range(NG):
        body(g)
    if has_sq:
        rk, b_rk = spec["rstdkv"]
        emit_rstd(K, ssqkv, b_ssqkv, rk, b_rk, 256)


def build_fused():
    nc = bass.Bass("TRN2", target_bir_lowering=False)

    def din(name, shape, dt=F32):
        return nc.dram_tensor(name, list(shape), dt, kind="ExternalInput").ap()

    def scr(name, shape, dt=F32):
        return nc.dram_tensor(name, list(shape), dt, kind="Internal").ap()

    x_all = din("x_all", [S_LEN, D])
    cT = din("cT", [128, 8])
    ada_w = din("ada_w", [D, 9 * D])
    ada_b = din("ada_b", [9 * D])
    ada_bT = din("ada_bT", [128, 72])
    n1T, n2T, n3T = din("n1T", [128, 8]), din("n2T", [128, 8]), din("n3T", [128, 8])
    fn = din("fn", [D])
    w1, w3, w2 = din("w1", [D, DFF]), din("w3", [D, DFF]), din("w2", [DFF, D])
    f2w1, f2w3, f2w2 = din("f2w1", [D, DFF]), din("f2w3", [D, DFF]), din("f2w2", [DFF, D])
    wA, wO = din("wA", [D, NCA]), din("wO", [D, NCO])
    ident = din("ident", [128, 128])
    oh_d = din("oh", [128, 4])
    dm_c = dict(gb4=din("gb4", [64, 8]), cw4=din("cw4", [128, 4, 8]), cb4=din("cb4", [128, 4, 2]), triu=din("triu", [64, 64]))
    da = dict(wuk=din("wuk", [128, 2, 512]), wuv=din("wuv", [128, 2, 512]), gkv=din("gkv", [128, 2]),
              bt=din("bt", [128, 8, 640]), b31=din("b31", [128, 8]), cm=din("cm", [128, 512]))
    d3 = dict(wg=din("wg", [D, 2560]), wa=din("wa", [64, 8, D]), wm=din("wm", [128, 4, D]), hnT=din("hnT", [128, 4]), wout=din("wout", [D, D]))
    out = nc.dram_tensor("out", [TOK, D], F32, kind="ExternalOutput").ap()

    NTF = NTA + NT
    x1s = scr("x1s", [NTF * 128, D])
    ckvT_s = scr("ckvT_s", [128, 2, S_LEN], BF16)
    kidx2_s = scr("kidx2_s", [128, S_LEN], BF16)
    qkm_s = scr("qkm_s", [128, 8, S_LEN + 3], BF16)
    vm_s = scr("vm_s", [NTA, 128, 512], BF16)
    hm_s = scr("hm_s", [4, 128, S_LEN], BF16)
    qaT_s = scr("qaT_s", [128, 4, TOK], BF16)
    qidxT_s = scr("qidxT_s", [128, 4, TOK], BF16)
    h2T_s = scr("h2T_s", [NT, 128, 8, 128], BF16)
    ya_s = scr("ya_s", [64, 8, TOK], BF16)
    x2_d = scr("x2s", [TOK, D])
    x3_d = scr("x3s", [TOK, D])
    da["vscr"] = scr("vscr", [64, 128, 520], BF16)

    with ExitStack() as es:
        K = Ctx(nc, es)
        S = K.S
        idf, b_idf = K.sb(es, [128, 128], F32, "idf")
        idb, b_idb = K.sb(es, [128, 128], BF16, "idb")
        cT_sb, b_cT = K.sb(es, [128, 8], F32, "cT")
        abT_sb, b_abT = K.sb(es, [128, 72], F32, "abT")
        nsb = [K.sb(es, [128, 8], F32, "nrm") for _ in range(3)]
        modT, b_modT = K.sb(es, [128, 72], F32, "modT")
        As = [K.sb(es, [128, 8], F32, "Amod") for _ in range(3)]
        ssq2, b_ssq2 = K.sb(es, [128, NTF], F32, "ssq2")
        oh, b_oh = K.sb(es, [128, 4], F32, "oh")
        ssqF, b_ssqF = K.sb(es, [128, NT], F32, "ssqF")
        rstdF, b_rstdF = K.sb(es, [128, NT], F32, "rstdF")
        for (t_, b_, src) in ((idf, b_idf, ident), (cT_sb, b_cT, cT), (abT_sb, b_abT, ada_bT), (nsb[0][0], nsb[0][1], n1T),
                              (nsb[1][0], nsb[1][1], n2T), (nsb[2][0], nsb[2][1], n3T), (oh, b_oh, oh_d)):
            S.dma("sp", lambda e, t_=t_, src=src: e.dma_start(out=t_[:], in_=src), writes=[b_])
        S.op("dve", lambda e: e.tensor_copy(out=idb[:], in_=idf[:]), reads=[b_idf], writes=[b_idb])

        st_g1 = ExitStack()
        g1row, b_g1row = K.sb(st_g1, [128, D], F32, "g1row")
        S.barrier()
        wts1 = alloc_ffn_weights(K, st_g1, w1, w3, w2)
        with ExitStack() as ph:
            emit_mod(K, ph, ada_w, abT_sb, cT_sb, modT, b_modT, cols=[0, 1, 3, 4, 6, 7],
                     rows={2: (g1row, b_g1row, 0.5)}, ada_b_dram=ada_b, ident_f=idf, b_ident=b_idf)
            for q, (sc0, _) in enumerate(((8, 0), (32, 0), (56, 0))):
                S.op("dve", lambda e, q=q, sc0=sc0: e.scalar_tensor_tensor(out=As[q][0][:], in0=modT[:, sc0:sc0 + 8], scalar=1.0, in1=nsb[q][0][:],
                                                                            op0=ALU.add, op1=ALU.mult), reads=[b_modT, nsb[q][1]], writes=[As[q][1]])
            S.flush()
        b_x1 = [Buf("x1_%d" % t) for t in range(NTF)]
        with ExitStack() as ph:
            junk2, b_junk2 = K.sb(ph, [128, D], BF16, "junk2")

            def epi1(t, xo, b_xo):
                S.dma("sp", lambda e: e.dma_start(out=x1s[t * 128:(t + 1) * 128, :], in_=xo[:]), reads=[b_xo], writes=[b_x1[t]])
                S.op("act", lambda e: e.activation(out=junk2[:], in_=xo[:], func=AF.Square, accum_out=ssq2[:, t:t + 1]),
                     reads=[b_xo], writes=[b_junk2, b_ssq2])

            emit_ffn(K, ph, lambda t: x_all[t * 128:(t + 1) * 128, :], w1, w3, w2, As[0][0][:, :], modT[:, 0:8], Buf("AB1"),
                     g1row, b_g1row, idb, b_idb, epi1, ntiles=NTA, wts=wts1)
            S.barrier()
            S.flush()
        st_g1.close()

        IFt, b_IFt = K.sb(es, [128, 8, NTA], F32, "IFt")
        IFall, b_IFall = IFt[:, :, :].rearrange("p c t -> p t c"), b_IFt
        widx_sb, b_widx_sb = K.sb(es, [128, NT, 8], F32, "widx_sb")
        rstdkv, b_rstdkv = K.sb(es, [128, NTA], F32, "rstdkv")
        zpad, b_zpad = K.sb(es, [128, 8, 3], BF16, "zpad")
        S.op("dve", lambda e: e.memset(zpad[:], 0.0), writes=[b_zpad])
        S.dma("sp", lambda e: e.dma_start(out=qkm_s[:, :, 0:3], in_=zpad[:]), reads=[b_zpad])

        with ExitStack() as ph:
            spec = dict(fm=[(0, "b", 0), (128, "b", 1), (256, "b", 2)] + [(384 + 128 * i, "b", 3 + i) for i in range(8)],
                        nb=11, nf=0, sq=[0, 1], vm_col=1408, small=(1920, 8), small_sb=(IFall, b_IFall), rstdkv=(rstdkv, b_rstdkv))

            def out_fm(g, f16, b_f16, f32, b_f32):
                tk = slice(g * 256, (g + 1) * 256)
                S.dma("sp", lambda e: e.dma_start(out=ckvT_s[:, :, tk], in_=f16[:, 0:2, :]), reads=[b_f16])
                S.dma("sp", lambda e: e.dma_start(out=kidx2_s[:, tk], in_=f16[:, 2, :]), reads=[b_f16])
                S.dma("sp", lambda e: e.dma_start(out=qkm_s[:, :, 3 + g * 256:3 + (g + 1) * 256], in_=f16[:, 3:11, :]), reads=[b_f16])

            def out_vm(t, vmt, b_vmt):
                S.dma("sp", lambda e: e.dma_start(out=vm_s[t], in_=vmt[:]), reads=[b_vmt])

            spec["out_fm"], spec["out_vm"] = out_fm, out_vm
            emit_proj2(K, ph, NTA, lambda t: x1s[t * 128:(t + 1) * 128, :], b_x1[0:NTA], wA, NCA, ssq2, b_ssq2, 0,
                       As[1][0][:, :], modT[:, 24:32], idb, b_idb, spec)
            S.barrier()
            S.flush()
        with ExitStack() as ph:
            spec = dict(fm=[(128 * i, "b", i) for i in range(8)], nb=8, nf=0, vm_col=None, small=(1024, 8), small_sb=(widx_sb, b_widx_sb))

            def out_fm2(g, f16, b_f16, f32, b_f32):
                tk = slice(g * 256, (g + 1) * 256)
                S.dma("sp", lambda e: e.dma_start(out=qaT_s[:, :, tk], in_=f16[:, 0:4, :]), reads=[b_f16])
                S.dma("sp", lambda e: e.dma_start(out=qidxT_s[:, :, tk], in_=f16[:, 4:8, :]), reads=[b_f16])

            def out_h2T(t, hT, tt, b_hT):
                S.dma("sp", lambda e: e.dma_start(out=h2T_s[t], in_=hT[:, tt, :, :]), reads=[b_hT])

            cands1 = [K.sb(ph, [128, 4, D], F32, "x1cand") for _ in range(2)]
            for jj in range(4):
                v = ssq2[:, 0:NTA].rearrange("p (t j) -> p t j", j=4)[:, :, jj]
                if jj == 0:
                    S.op("dve", lambda e, v=v: e.tensor_scalar(out=ssq2[:, NTA:NTF], in0=v, scalar1=oh[:, 0:1], scalar2=None, op0=ALU.mult),
                         reads=[b_ssq2, b_oh], writes=[b_ssq2])
                else:
                    S.op("dve", lambda e, v=v, jj=jj: e.scalar_tensor_tensor(out=ssq2[:, NTA:NTF], in0=v, scalar=oh[:, jj:jj + 1], in1=ssq2[:, NTA:NTF],
                                                                              op0=ALU.mult, op1=ALU.add), reads=[b_ssq2, b_oh], writes=[b_ssq2])

            def xsel(t, xt, b_xt):
                cd, b_cd = cands1[t % 2]
                S.dma("sp", lambda e: e.dma_start(out=cd[:], in_=x1s[4 * t * 128:(4 * t + 4) * 128, :].rearrange("(j p) d -> p j d", p=128)),
                      reads=b_x1[4 * t:4 * t + 4], writes=[b_cd])
                S.op("dve", lambda e: e.tensor_scalar(out=xt[:], in0=cd[:, 0, :], scalar1=oh[:, 0:1], scalar2=None, op0=ALU.mult),
                     reads=[b_cd, b_oh], writes=[b_xt])
                for jj in range(1, 4):
                    S.op("dve", lambda e, jj=jj: e.scalar_tensor_tensor(out=xt[:], in0=cd[:, jj, :], scalar=oh[:, jj:jj + 1], in1=xt[:],
                                                                         op0=ALU.mult, op1=ALU.add), reads=[b_cd, b_oh, b_xt], writes=[b_xt])
                S.dma("sp", lambda e: e.dma_start(out=x1s[(NTA + t) * 128:(NTA + t + 1) * 128, :], in_=xt[:]), reads=[b_xt], writes=[b_x1[NTA + t]])

            spec["xsel"] = xsel
            spec["out_fm"], spec["out_h2T"] = out_fm2, out_h2T
            emit_proj2(K, ph, NT, lambda t: x1s[(NTA + t) * 128:(NTA + t + 1) * 128, :], b_x1[NTA:NTF], wO, NCO, ssq2, b_ssq2, NTA,
                       As[1][0][:, :], modT[:, 24:32], idb, b_idb, spec)
            S.barrier()
            S.flush()

        with ExitStack() as ph:
            emit_mlstm4(K, ph, dm_c, qkm_s, vm_s, hm_s, IFt, b_IFt, idb, b_idb)
            S.barrier()
            S.flush()

        with ExitStack() as ph:
            da.update(ckvT=ckvT_s, rstd=rstdkv[:, :], kidx2=kidx2_s, qaT=qaT_s, qidxT=qidxT_s, widx=widx_sb[:, :, 0:8], yaT=ya_s)
            emit_attn(K, ph, da, idf, b_idf, idb, b_idb)
            S.barrier()
            S.flush()

        g2row, b_g2row = K.sb(es, [128, D], F32, "g2row")
        g3row, b_g3row = K.sb(es, [128, D], F32, "g3row")
        with ExitStack() as ph:
            emit_mod(K, ph, ada_w, abT_sb, cT_sb, modT, b_modT, cols=[],
                     rows={5: (g2row, b_g2row, 1.0), 8: (g3row, b_g3row, 0.5)}, ada_b_dram=ada_b, ident_f=idf, b_ident=b_idf)
            S.barrier()
            S.flush()
        b_x2 = [Buf("x2_%d" % t) for t in range(NT)]
        b_x3 = [Buf("x3_%d" % t) for t in range(NT)]
        with ExitStack() as ph:
            d3.update(x1=x1s[NTA * 128:NTF * 128, :], h2T=h2T_s, yaT=ya_s)
            cands = [K.sb(ph, [128, 4, 512], BF16, "cand") for _ in range(2)]
            emit_merge(K, ph, d3, g2row, b_g2row, x2_d, b_x2, sel=(oh, b_oh, hm_s, cands))
            S.barrier()
            S.flush()
        with ExitStack() as ph:
            junk2, b_junk2 = K.sb(ph, [128, D], BF16, "junk2")

            def epi2(t, xo, b_xo):
                S.dma("sp", lambda e: e.dma_start(out=x3_d[t * 128:(t + 1) * 128, :], in_=xo[:]), reads=[b_xo], writes=[b_x3[t]])
                S.op("act", lambda e: e.activation(out=junk2[:], in_=xo[:], func=AF.Square, accum_out=ssqF[:, t:t + 1]),
                     reads=[b_xo], writes=[b_junk2, b_ssqF])

            emit_ffn(K, ph, lambda t: x2_d[t * 128:(t + 1) * 128, :], f2w1, f2w3, f2w2, As[2][0][:, :], modT[:, 48:56], Buf("AB3"),
                     g3row, b_g3row, idb, b_idb, epi2, src_bufs=b_x2)
            S.barrier()
            S.flush()
        with ExitStack() as ph:
            fnrow, b_fnrow = K.sb(ph, [128, D], F32, "fnrow")
            fn1, b_fn1 = K.sb(ph, [1, D], F32, "fn1")
            on1, b_on1 = K.sb(ph, [1, 128], F32, "on1")
            pF, b_pF = K.ps(ph, [128, D], F32, "pFn")
            S.op("dve", lambda e: e.memset(on1[:], 1.0), writes=[b_on1])
            S.dma("sp", lambda e: e.dma_start(out=fn1[:], in_=fn.rearrange("(a n) -> a n", a=1)), writes=[b_fn1])
            for nt in range(2):
                S.op("pe", lambda e, nt=nt: e.matmul(pF[:, nt * 512:(nt + 1) * 512], lhsT=on1[0:1, :], rhs=fn1[0:1, nt * 512:(nt + 1) * 512],
                                                    start=True, stop=True), reads=[b_on1, b_fn1], writes=[b_pF])
            S.op("dve", lambda e: e.tensor_copy(out=fnrow[:], in_=pF[:]), reads=[b_pF], writes=[b_fnrow])
            emit_rstd(K, ssqF, b_ssqF, rstdF, b_rstdF, D)
            xf = [K.sb(ph, [128, D], F32, "xf") for _ in range(3)]
            for t in range(NT):
                xt, b_xt = xf[t % 3]
                S.dma("sp", lambda e, xt=xt, t=t: e.dma_start(out=xt[:], in_=x3_d[t * 128:(t + 1) * 128, :]), reads=[b_x3[t]], writes=[b_xt])
                S.op("dve", lambda e, xt=xt, t=t: e.scalar_tensor_tensor(out=xt[:], in0=xt[:], scalar=rstdF[:, t:t + 1], in1=fnrow[:],
                                                                          op0=ALU.mult, op1=ALU.mult), reads=[b_xt, b_rstdF, b_fnrow], writes=[b_xt])
                S.dma("sp", lambda e, xt=xt, t=t: e.dma_start(out=out[t * 128:(t + 1) * 128, :], in_=xt[:]), reads=[b_xt])
            S.barrier()
            S.finish()
            S.flush()
        print("fused program: %d instructions" % S.ninst, {k: S.cnt[k] for k in S.cnt})
        nc._phase_marks = S.marks
    return nc


def fused_inputs(inp, core):
    b, j = divmod(core, 4)
    w_in = np.asarray(inp["w_in"][0], np.float32)
    colsA = np.concatenate([np.arange(C_CKV, C_CKV + 256), np.arange(C_KIDX, C_KIDX + 64), np.arange(C_KIDX, C_KIDX + 64),
                            np.arange(C_QM, C_VM + 512), np.arange(C_I, C_I + 8)])
    colsO = np.concatenate([np.arange(C_QA, C_QA + 512), np.arange(C_QIDX, C_QIDX + 512), np.arange(C_WIDX, C_WIDX + 8)])
    assert colsA.size == NCA and colsO.size == NCO
    gbv = np.asarray(inp["mlstm_gate_bias"][0], np.float32)
    cwv = np.asarray(inp["conv_w"][0], np.float32)
    cbv = np.asarray(inp["conv_b"][0], np.float32)
    gb4 = np.zeros((64, 8), np.float32)
    cw4 = np.zeros((128, 4, 8), np.float32)
    cb4 = np.zeros((128, 4, 2), np.float32)
    for hd in range(4):
        gb4[:, 2 * hd] = gbv[hd]
        gb4[:, 2 * hd + 1] = gbv[4 + hd]
        cw4[:, hd, 0:4] = cwv[:, hd * 128:(hd + 1) * 128].T
        cw4[:, hd, 4:8] = cwv[:, 512 + hd * 128:512 + (hd + 1) * 128].T
        cb4[:, hd, 0] = cbv[hd * 128:(hd + 1) * 128]
        cb4[:, hd, 1] = cbv[512 + hd * 128:512 + (hd + 1) * 128]
    ohv = np.zeros((128, 4), np.float32)
    ohv[:, j] = 1.0
    wa = np.asarray(inp["w_branch_attn"][0], np.float32).reshape(8, 64, D).transpose(1, 0, 2)
    wm = np.asarray(inp["w_branch_mlstm"][0], np.float32).reshape(4, 128, D).transpose(1, 0, 2)
    xb = np.ascontiguousarray(inp["x"][b], np.float32)
    r = {
        "x_all": xb,
        "cT": colT(inp["c"][b], 8),
        "ada_w": np.ascontiguousarray(inp["ada_w"][0], np.float32),
        "ada_b": np.ascontiguousarray(inp["ada_b"][0], np.float32),
        "ada_bT": colT(inp["ada_b"][0], 72),
        "n1T": colT(inp["ffn1_norm"][0], 8), "n2T": colT(inp["mix_norm"][0], 8), "n3T": colT(inp["ffn2_norm"][0], 8),
        "fn": np.ascontiguousarray(inp["final_norm"], np.float32),
        "w1": np.ascontiguousarray(inp["ffn1_w1"][0], np.float32), "w3": np.ascontiguousarray(inp["ffn1_w3"][0], np.float32),
        "w2": np.ascontiguousarray(inp["ffn1_w2"][0], np.float32),
        "f2w1": np.ascontiguousarray(inp["ffn2_w1"][0], np.float32), "f2w3": np.ascontiguousarray(inp["ffn2_w3"][0], np.float32),
        "f2w2": np.ascontiguousarray(inp["ffn2_w2"][0], np.float32),
        "wA": np.ascontiguousarray(w_in[:, colsA]), "wO": np.ascontiguousarray(w_in[:, colsO]),
        "wg": np.ascontiguousarray(w_in[:, C_O:DIN]),
        "ident": np.eye(128, dtype=np.float32), "oh": ohv,
        "gb4": gb4, "cw4": cw4, "cb4": cb4, "triu": np.triu(np.ones((64, 64), np.float32)),
        "wa": np.ascontiguousarray(wa), "wm": np.ascontiguousarray(wm), "hnT": colT(inp["mlstm_head_norm"][0], 4),
        "wout": np.ascontiguousarray(inp["w_out"][0], np.float32),
    }
    r.update(attn_consts(inp, j))
    return r


def kernel(**inputs):
    inp = {k: np.asarray(v) for k, v in inputs.items()}
    cores = list(range(NCORES))
    res = run_bass_kernel_spmd(_prog("fused", build_fused), [fused_inputs(inp, c) for c in cores], core_ids=cores).results
    out = np.zeros((2, S_LEN, D), np.float32)
    for c in cores:
        b, j = divmod(c, 4)
        out[b].reshape(NT, 4, 128, D)[:, j] = np.asarray(res[c]["out"], np.float32).reshape(NT, 128, D)
    return out


def emit_mlstm4(K, ph, dmc, qkm_s, vm_s, hm_s, IFt, b_IFt, ident_b, b_identb):
    S = K.S
    GS = 16
    NSEG = NCH // GS
    SEG = GS * 64
    col = lambda c: (c % 2) * 64 + c // 2
    lcol = lambda cl: (cl % 2) * 8 + cl // 2
    triu, b_triu = K.sb(ph, [64, 64], F32, "triu")
    ones64, b_ones64 = K.sb(ph, [64, 128], F32, "ones64")
    S.dma("sp", lambda e: e.dma_start(out=triu[:], in_=dmc["triu"]), writes=[b_triu])
    S.op("dve", lambda e: e.memset(ones64[:], 1.0), writes=[b_ones64])
    gb4, b_gb4 = K.sb(ph, [64, 8], F32, "gb4")
    cw4, b_cw4 = K.sb(ph, [128, 4, 8], F32, "cw4")
    cb4, b_cb4 = K.sb(ph, [128, 4, 2], F32, "cb4")
    for (t, b, src) in ((gb4, b_gb4, "gb4"), (cw4, b_cw4, "cw4"), (cb4, b_cb4, "cb4")):
        S.dma("sp", lambda e, t=t, src=src: e.dma_start(out=t[:], in_=dmc[src]), writes=[b])
    lnk, b_lnk = K.sb(ph, [64, 1], F32, "lnk")
    S.op("dve", lambda e: e.memset(lnk[:], float(np.log(128 ** -0.5))), writes=[b_lnk])
    diagW, b_diagW = K.sb(ph, [128, 32, 128], BF16, "diagW")
    pG1, b_pG1 = K.ps(ph, [128, 512], F32, "pG1")
    pG2, b_pG2 = K.ps(ph, [128, 512], F32, "pG2")
    pSt = [K.ps(ph, [64, 512], F32, "pSt")] * 2
    pCv, b_pCv = K.ps(ph, [128, 512], F32, "pCv")
    pND = [K.ps(ph, [64, 512], F32, "pND") for _ in range(2)]
    pU = [(pG1, b_pG1), (pG2, b_pG2)]
    pKt, b_pKt = K.ps(ph, [64, 8, 128], BF16, "pKt")
    pTr2, b_pTr2 = K.ps(ph, [128, GS * 64], BF16, "pTr2")
    H = []
    for hd in range(4):
        h = {}
        for n_ in ("Ig", "Fg", "Am", "Gm"):
            h[n_] = K.sb(ph, [64, NCH], F32, n_)
        h["GL"] = K.sb(ph, [128, NCH], F32, "GL")
        h["ngb"] = K.sb(ph, [64, 1], F32, "ngb")
        h["QT"] = [K.sb(ph, [128, SEG], BF16, "QTs") for _ in range(2)]
        h["KT"] = [K.sb(ph, [128, SEG], BF16, "KTs") for _ in range(2)]
        h["Ktok"] = K.sb(ph, [64, GS, 128], BF16, "Ktok")
        h["Vaug"] = K.sb(ph, [64, GS, 129], BF16, "Vaug")
        h["aV"] = K.sb(ph, [64, GS, 129], BF16, "aV")
        h["Hs"] = K.sb(ph, [64, GS, 128], F32, "Hs")
        h["C"] = [K.sb(ph, [128, 129], F32, "Cst") for _ in range(2)]
        h["Cb"] = [K.sb(ph, [128, 129], BF16, "Cbf") for _ in range(2)]
        h["t1"] = [K.sb(ph, [64, 2], F32, "t1") for _ in range(2)]
        H.append(h)
        Ig, b_Ig = h["Ig"]
        Fg, b_Fg = h["Fg"]
        Am, b_A = h["Am"]
        Gm, b_G = h["Gm"]
        GL, b_GL = h["GL"]
        ngb, b_ngb = h["ngb"]
        Vaug, b_Vaug = h["Vaug"]
        for (t, b, q) in ((Ig, b_Ig, hd), (Fg, b_Fg, 4 + hd)):
            S.op("dve", lambda e, t=t, q=q: e.tensor_copy(out=t[:, 0:64], in_=IFt[0:64, q, :]), reads=[b_IFt], writes=[b])
            S.dma("sp", lambda e, t=t, q=q: e.dma_start(out=t[:, 64:128], in_=IFt[64:128, q, :]), reads=[b_IFt], writes=[b])
        S.op("dve", lambda e, Vaug=Vaug: e.memset(Vaug[:], 1.0), writes=[b_Vaug])
        S.op("dve", lambda e, ngb=ngb, hd=hd: e.tensor_scalar(out=ngb[:], in0=gb4[:, 2 * hd + 1:2 * hd + 2], scalar1=-1.0, scalar2=None, op0=ALU.mult),
             reads=[b_gb4], writes=[b_ngb])
        S.op("act", lambda e, Fg=Fg, ngb=ngb: e.activation(out=Fg[:], in_=Fg[:], func=AF.Exp, scale=-1.0, bias=ngb[:, 0:1]),
             reads=[b_Fg, b_ngb], writes=[b_Fg])
        S.op("act", lambda e, Fg=Fg: e.activation(out=Fg[:], in_=Fg[:], func=AF.Ln, bias=1.0), reads=[b_Fg], writes=[b_Fg])
        S.op("dve", lambda e, Fg=Fg: e.tensor_scalar(out=Fg[:], in0=Fg[:], scalar1=-1.0, scalar2=None, op0=ALU.mult), reads=[b_Fg], writes=[b_Fg])
        S.op("pe", lambda e, Fg=Fg: e.matmul(pG1[0:64, 0:NCH], lhsT=triu[:, :], rhs=Fg[:, :], start=True, stop=True),
             reads=[b_triu, b_Fg], writes=[b_pG1])
        S.op("pe", lambda e, Fg=Fg: e.matmul(pG2[:, 0:NCH], lhsT=ones64[:, :], rhs=Fg[:, :], start=True, stop=True),
             reads=[b_ones64, b_Fg], writes=[b_pG2])
        S.op("dve", lambda e, Am=Am, Ig=Ig, hd=hd: e.scalar_tensor_tensor(out=Am[:], in0=Ig[:], scalar=gb4[:, 2 * hd:2 * hd + 1], in1=pG1[0:64, 0:NCH],
                                                                          op0=ALU.add, op1=ALU.subtract), reads=[b_Ig, b_gb4, b_pG1], writes=[b_A])
        S.op("act", lambda e, Am=Am: e.activation(out=Am[:], in_=Am[:], func=AF.Exp, bias=lnk[:, 0:1]), reads=[b_A, b_lnk], writes=[b_A])
        S.op("act", lambda e, Gm=Gm: e.activation(out=Gm[:], in_=pG1[0:64, 0:NCH], func=AF.Exp), reads=[b_pG1], writes=[b_G])
        S.op("act", lambda e, GL=GL: e.activation(out=GL[:], in_=pG2[:, 0:NCH], func=AF.Exp), reads=[b_pG2], writes=[b_GL])
        S.op("dve", lambda e, h=h: e.memset(h["C"][1][0][:], 0.0), writes=[h["C"][1][1]])

    for hd in range(4):
        for q in range(8):
            S.op("dve", lambda e, hd=hd, q=q: e.tensor_scalar(out=diagW[:, hd * 8 + q, :], in0=ident_b[:, :], scalar1=cw4[:, hd, q:q + 1], scalar2=None,
                                                             op0=ALU.mult), reads=[b_identb, b_cw4], writes=[b_diagW])
    xs = [K.sb(ph, [128, SEG + 3], BF16, "xs") for _ in range(3)]
    Hq, b_Hq = K.sb(ph, [64, GS, 128], F32, "Hq")
    Hn, b_Hn = K.sb(ph, [64, GS, 128], BF16, "Hn")
    mu, b_mu = K.sb(ph, [64, GS], F32, "mu")
    var, b_var = K.sb(ph, [64, GS], F32, "var")
    hseg, b_hseg = K.sb(ph, [128, GS * 64], BF16, "hseg")
    n = [0]

    def prep_seg(sg):
        par = sg % 2
        for hd in range(4):
            h = H[hd]
            for which in range(2):
                dst, b_dst = (h["QT"] if which == 0 else h["KT"])[par]
                x_, b_x = xs[n[0] % 3]
                n[0] += 1
                S.dma("sp", lambda e, x_=x_, which=which, hd=hd: e.dma_start(out=x_[:], in_=qkm_s[:, 4 * which + hd, sg * SEG:sg * SEG + SEG + 3]),
                      writes=[b_x])
                for hf in range(SEG // 512):
                    for w in range(4):
                        S.op("pe", lambda e, x_=x_, which=which, hd=hd, w=w, hf=hf: e.matmul(
                            pCv[:, :], lhsT=diagW[:, hd * 8 + 4 * which + w, :], rhs=x_[:, hf * 512 + w:hf * 512 + w + 512],
                            start=(w == 0), stop=(w == 3)), reads=[b_diagW, b_x], writes=[b_pCv])
                    S.op("act", lambda e, dst=dst, hd=hd, which=which, hf=hf: e.activation(
                        out=dst[:, hf * 512:(hf + 1) * 512], in_=pCv[:, :], func=AF.Silu, bias=cb4[:, hd, which:which + 1]),
                        reads=[b_pCv, b_cb4], writes=[b_dst])

    def load_seg(sg):
        par = sg % 2
        for hd in range(4):
            h = H[hd]
            KT, b_KT = h["KT"][par]
            Ktok, b_Ktok = h["Ktok"]
            Vaug, b_Vaug = h["Vaug"]
            aV, b_aV = h["aV"]
            Am, b_A = h["Am"]
            for g8 in range(GS // 8):
                for cc in range(8):
                    cl = g8 * 8 + cc
                    S.op("pe", lambda e, cl=cl, cc=cc, KT=KT: e.transpose(out=pKt[:, cc, :], in_=KT[:, cl * 64:(cl + 1) * 64], identity=ident_b[:]),
                         reads=[b_KT, b_identb], writes=[b_pKt])
                S.op("dve", lambda e, g8=g8, Ktok=Ktok: e.tensor_copy(out=Ktok[:, g8 * 8:(g8 + 1) * 8, :], in_=pKt[:]), reads=[b_pKt], writes=[b_Ktok])
            for h2 in range(2):
                S.dma("sp", lambda e, h2=h2, Vaug=Vaug, hd=hd: e.dma_start(
                    out=Vaug[:, h2 * 8:(h2 + 1) * 8, 0:128],
                    in_=vm_s[sg * 8:(sg + 1) * 8, h2 * 64:(h2 + 1) * 64, hd * 128:(hd + 1) * 128].rearrange("t s c -> s t c")), writes=[b_Vaug])
            for h2 in range(2):
                S.op("dve", lambda e, h2=h2, aV=aV, Vaug=Vaug, Am=Am: e.tensor_tensor(
                    out=aV[:, h2 * 8:(h2 + 1) * 8, :], in0=Vaug[:, h2 * 8:(h2 + 1) * 8, :],
                    in1=bc_last(Am[:, h2 * 64 + sg * 8:h2 * 64 + sg * 8 + 8], 129), op=ALU.mult), reads=[b_Vaug, b_A], writes=[b_aV])

    scnt = [0]
    Gm4, b_Gm4 = K.sb(ph, [64, NCH, 4], F32, "Gm4")
    for hd in range(4):
        S.op("dve", lambda e, hd=hd: e.tensor_copy(out=Gm4[:, :, hd], in_=H[hd]["Gm"][0][:, :]), reads=[H[hd]["Gm"][1]], writes=[b_Gm4])
    t1all, b_t1all = K.sb(ph, [64, 2, 2, 4], F32, "t1all")
    Sps = [K.sb(ph, [64, 64], BF16, "Sp8") for _ in range(8)]
    tmpCs = [K.sb(ph, [128, 129], F32, "tmpC4") for _ in range(4)]

    def step4(c, cl, par):
        gc = col(c)
        vc = lcol(cl)
        cs = slice(cl * 64, (cl + 1) * 64)
        st_, b_st = pSt[c % 2]
        sps = []
        for hd in range(4):
            h = H[hd]
            QT, b_QT = h["QT"][par]
            KT, b_KT = h["KT"][par]
            S.op("pe", lambda e, hd=hd, QT=QT, KT=KT: e.matmul(st_[:, hd * 64:(hd + 1) * 64], lhsT=KT[:, cs], rhs=QT[:, cs], start=True, stop=True),
                 reads=[b_KT, b_QT], writes=[b_st])
        for hd in range(4):
            h = H[hd]
            Am, b_A = h["Am"]
            sp_, b_sp = Sps[scnt[0] % 8]
            scnt[0] += 1
            sps.append((sp_, b_sp))
            S.op("dve", lambda e, hd=hd, sp_=sp_, Am=Am: e.scalar_tensor_tensor(out=sp_[:], in0=st_[:, hd * 64:(hd + 1) * 64], scalar=Am[:, gc:gc + 1],
                                                                                 in1=triu[:, :], op0=ALU.mult, op1=ALU.mult),
                 reads=[b_st, b_A, b_triu], writes=[b_sp])
        for hd in range(4):
            h = H[hd]
            u_, b_u = pU[hd // 2]
            us = slice((hd % 2) * 129, (hd % 2) * 129 + 129)
            Ktok, b_Ktok = h["Ktok"]
            aV, b_aV = h["aV"]
            S.op("pe", lambda e, u_=u_, us=us, Ktok=Ktok, aV=aV: e.matmul(u_[:, us], lhsT=Ktok[:, cl, :], rhs=aV[:, vc, :], start=True, stop=True),
                 reads=[b_Ktok, b_aV], writes=[b_u])
        for hd in range(4):
            h = H[hd]
            nd_, b_nd = pND[hd // 2]
            us = slice((hd % 2) * 129, (hd % 2) * 129 + 129)
            QT, b_QT = h["QT"][par]
            Vaug, b_Vaug = h["Vaug"]
            Cbp, b_Cbp = h["Cb"][(c + 1) % 2]
            sp_, b_sp = sps[hd]
            if c > 0:
                S.op("pe", lambda e, nd_=nd_, us=us, QT=QT, Cbp=Cbp: e.matmul(nd_[:, us], lhsT=QT[:, cs], rhs=Cbp[:, :], start=True, stop=False),
                     reads=[b_QT, b_Cbp], writes=[b_nd])
            S.op("pe", lambda e, nd_=nd_, us=us, sp_=sp_, Vaug=Vaug: e.matmul(nd_[:, us], lhsT=sp_[:, :], rhs=Vaug[:, vc, :], start=(c == 0), stop=True),
                 reads=[b_sp, b_Vaug], writes=[b_nd])
        for hd in range(4):
            h = H[hd]
            u_, b_u = pU[hd // 2]
            us = slice((hd % 2) * 129, (hd % 2) * 129 + 129)
            GL, b_GL = h["GL"]
            Cp, b_Cp = h["C"][(c + 1) % 2]
            Cn, b_Cn = h["C"][c % 2]
            tmpC, b_tmpC = tmpCs[hd]
            S.op("dve", lambda e, u_=u_, us=us, Cp=Cp, tmpC=tmpC: e.tensor_tensor(out=tmpC[:], in0=u_[:, us], in1=Cp[:], op=ALU.add),
                 reads=[b_u, b_Cp], writes=[b_tmpC])
            S.op("dve", lambda e, Cn=Cn, tmpC=tmpC, GL=GL: e.tensor_scalar(out=Cn[:], in0=tmpC[:], scalar1=GL[:, gc:gc + 1], scalar2=None, op0=ALU.mult),
                 reads=[b_tmpC, b_GL], writes=[b_Cn])
        for hd in range(4):
            h = H[hd]
            Cn, b_Cn = h["C"][c % 2]
            Cbn, b_Cbn = h["Cb"][c % 2]
            S.op("act", lambda e, Cn=Cn, Cbn=Cbn: e.activation(out=Cbn[:], in_=Cn[:], func=AF.Identity), reads=[b_Cn], writes=[b_Cbn])
        pr = c % 2
        for hd in range(4):
            h = H[hd]
            nd_, b_nd = pND[hd // 2]
            Gm, b_G = h["Gm"]
            o = (hd % 2) * 129 + 128
            S.op("act", lambda e, nd_=nd_, Gm=Gm, o=o, hd=hd: e.activation(out=t1all[:, pr, 0, hd:hd + 1], in_=nd_[:, o:o + 1], func=AF.Abs,
                                                                          scale=Gm[:, gc:gc + 1]), reads=[b_nd, b_G], writes=[b_t1all])
        S.op("dve", lambda e: e.tensor_scalar(out=t1all[:, pr, 0, :], in0=t1all[:, pr, 0, :], scalar1=1.0, scalar2=None, op0=ALU.max),
             reads=[b_t1all], writes=[b_t1all])
        S.op("dve", lambda e: e.reciprocal(out=t1all[:, pr, 0, :], in_=t1all[:, pr, 0, :]), reads=[b_t1all], writes=[b_t1all])
        S.op("dve", lambda e: e.tensor_tensor(out=t1all[:, pr, 1, :], in0=t1all[:, pr, 0, :], in1=Gm4[:, gc, :], op=ALU.mult),
             reads=[b_t1all, b_Gm4], writes=[b_t1all])
        for hd in range(4):
            h = H[hd]
            nd_, b_nd = pND[hd // 2]
            Hs, b_Hs = h["Hs"]
            o = (hd % 2) * 129
            S.op("dve", lambda e, nd_=nd_, Hs=Hs, o=o, hd=hd: e.tensor_scalar(out=Hs[:, cl, :], in0=nd_[:, o:o + 128], scalar1=t1all[:, pr, 1, hd:hd + 1],
                                                                             scalar2=None, op0=ALU.mult), reads=[b_nd, b_t1all], writes=[b_Hs])

    def finish_seg(sg):
        for hd in range(4):
            Hs, b_Hs = H[hd]["Hs"]
            S.op("dve", lambda e, Hs=Hs: e.tensor_reduce(out=mu[:], in_=Hs[:], axis=AX.X, op=ALU.add), reads=[b_Hs], writes=[b_mu])
            S.op("dve", lambda e: e.tensor_scalar(out=mu[:], in0=mu[:], scalar1=1.0 / 128, scalar2=None, op0=ALU.mult), reads=[b_mu], writes=[b_mu])
            S.op("dve", lambda e, Hs=Hs: e.tensor_tensor(out=Hs[:], in0=Hs[:], in1=bc_last(mu[:, :], 128), op=ALU.subtract),
                 reads=[b_Hs, b_mu], writes=[b_Hs])
            S.op("dve", lambda e, Hs=Hs: e.tensor_tensor(out=Hq[:], in0=Hs[:], in1=Hs[:], op=ALU.mult), reads=[b_Hs], writes=[b_Hq])
            S.op("dve", lambda e: e.tensor_reduce(out=var[:], in_=Hq[:], axis=AX.X, op=ALU.add), reads=[b_Hq], writes=[b_var])
            emit_rstd(K, var, b_var, var, b_var, 128)
            S.op("dve", lambda e, Hs=Hs: e.tensor_tensor(out=Hn[:], in0=Hs[:], in1=bc_last(var[:, :], 128), op=ALU.mult),
                 reads=[b_Hs, b_var], writes=[b_Hn])
            for cc in range(GS):
                S.op("pe", lambda e, cc=cc: e.transpose(out=pTr2[:, cc * 64:(cc + 1) * 64], in_=Hn[:, cc, :], identity=ident_b[0:64, 0:64]),
                     reads=[b_Hn, b_identb], writes=[b_pTr2])
            S.op("dve", lambda e: e.tensor_copy(out=hseg[:], in_=pTr2[:]), reads=[b_pTr2], writes=[b_hseg])
            S.dma("sp", lambda e, hd=hd: e.dma_start(out=hm_s[hd][:, sg * SEG:(sg + 1) * SEG], in_=hseg[:]), reads=[b_hseg])

    prep_seg(0)
    for sg in range(NSEG):
        load_seg(sg)
        if sg + 1 < NSEG:
            prep_seg(sg + 1)
        for cl in range(GS):
            step4(sg * GS + cl, cl, sg % 2)
        finish_seg(sg)
```
